# Optimizing a Trainium2 kernel written in Bass

```python
import math
import jax, jax.numpy as jnp
from jax import lax
import numpy as np

D_MODEL = 1024
BATCH = 8
SEQ = 4096
DEPTH = 4

D_FF = 2816
HEAD_DIM = 64
Q_BLOCK = 128
N_DIFF_HEADS = (D_MODEL // 2) // (2 * HEAD_DIM)
N_FOX_HEADS = (D_MODEL // 2) // HEAD_DIM
DIFF_V_DIM = 2 * HEAD_DIM
N_REL_BUCKETS = 32
REL_MAX_DIST = 128
S5_GROUP = 16
S5_GROUPS = D_MODEL // S5_GROUP
S5_STATE = 64
NORM_EPS = 1e-6
N_ATTN_LAYERS = (DEPTH + 1) // 2
N_SSM_LAYERS = DEPTH // 2

DIFF_QK_COLS = N_DIFF_HEADS * 2 * HEAD_DIM
DIFF_V_COLS = N_DIFF_HEADS * DIFF_V_DIM
FOX_COLS = N_FOX_HEADS * HEAD_DIM
IN_SPLITS = (DIFF_QK_COLS, DIFF_QK_COLS, DIFF_V_COLS, FOX_COLS, FOX_COLS, FOX_COLS, N_FOX_HEADS)
IN_COLS = sum(IN_SPLITS)
MIX_COLS = DIFF_V_COLS + FOX_COLS

kernel_name = "hybrid_diff_fox_s5_macaron"


def rmsnorm(x, g):
    xf = x.astype(jnp.float32)
    y = xf * lax.rsqrt(jnp.mean(xf * xf, axis=-1, keepdims=True) + NORM_EPS)
    return (y * g.astype(jnp.float32)).astype(x.dtype)


def swiglu(h, w_gate, w_up, w_down):
    return (jax.nn.silu(h @ w_gate) * (h @ w_up)) @ w_down


def rel_bucket(q_pos, k_pos):
    n = jnp.maximum(q_pos[:, None] - k_pos[None, :], 0)
    max_exact = N_REL_BUCKETS // 2
    nf = jnp.maximum(n, 1).astype(jnp.float32)
    large = max_exact + (jnp.log(nf / max_exact) / math.log(REL_MAX_DIST / max_exact)
                         * (N_REL_BUCKETS - max_exact)).astype(jnp.int32)
    large = jnp.minimum(large, N_REL_BUCKETS - 1)
    return jnp.where(n < max_exact, n, large)


def diff_fox_mixer(h, w_in, w_out, fg_bias, dq_g, dk_g, lq1, lk1, lq2, lk2, subln_g,
                   fq_g, fk_g, rel_bias, layer_idx):
    f32 = jnp.float32
    b, s, _ = h.shape
    proj = h @ w_in
    dq, dk, dv, fq, fk, fv, fl = jnp.split(proj, np.cumsum(IN_SPLITS)[:-1], axis=-1)
    dq = rmsnorm(dq.reshape(b, s, N_DIFF_HEADS, 2, HEAD_DIM), dq_g).transpose(0, 2, 3, 1, 4)
    dk = rmsnorm(dk.reshape(b, s, N_DIFF_HEADS, 2, HEAD_DIM), dk_g).transpose(0, 2, 3, 1, 4)
    dv = dv.reshape(b, s, N_DIFF_HEADS, DIFF_V_DIM).transpose(0, 2, 1, 3)
    fq = rmsnorm(fq.reshape(b, s, N_FOX_HEADS, HEAD_DIM), fq_g).transpose(0, 2, 1, 3)
    fk = rmsnorm(fk.reshape(b, s, N_FOX_HEADS, HEAD_DIM), fk_g).transpose(0, 2, 1, 3)
    fv = fv.reshape(b, s, N_FOX_HEADS, HEAD_DIM).transpose(0, 2, 1, 3)
    log_f = jax.nn.log_sigmoid(fl.astype(f32) + fg_bias.astype(f32))
    cum = jnp.cumsum(log_f, axis=1).transpose(0, 2, 1)

    lam_init = 0.8 - 0.6 * math.exp(-0.3 * layer_idx)
    lam = (jnp.exp(jnp.sum(lq1.astype(f32) * lk1.astype(f32)))
           - jnp.exp(jnp.sum(lq2.astype(f32) * lk2.astype(f32))) + lam_init)
    scale = HEAD_DIM ** -0.5
    pos = jnp.arange(s, dtype=jnp.int32)
    od_blocks, of_blocks = [], []
    for blk in range(s // Q_BLOCK):
        q0, q1 = blk * Q_BLOCK, (blk + 1) * Q_BLOCK
        qpos, kpos = pos[q0:q1], pos[:q1]
        causal = kpos[None, :] <= qpos[:, None]
        bias = rel_bias[rel_bucket(qpos, kpos)].transpose(2, 0, 1).astype(f32)
        lg = (jnp.einsum('bhmqd,bhmkd->bhmqk', dq[:, :, :, q0:q1], dk[:, :, :, :q1]).astype(f32) * scale
              + bias[None, :, None])
        p = jax.nn.softmax(jnp.where(causal, lg, -jnp.inf), axis=-1)
        pd = p[:, :, 0] - lam * p[:, :, 1]
        od_blocks.append(jnp.einsum('bhqk,bhkv->bhqv', pd.astype(dv.dtype), dv[:, :, :q1]))
        lf = (jnp.einsum('bhqd,bhkd->bhqk', fq[:, :, q0:q1], fk[:, :, :q1]).astype(f32) * scale
              + cum[:, :, q0:q1, None] - cum[:, :, None, :q1])
        pf = jax.nn.softmax(jnp.where(causal, lf, -jnp.inf), axis=-1)
        of_blocks.append(jnp.einsum('bhqk,bhkv->bhqv', pf.astype(fv.dtype), fv[:, :, :q1]))
    od = jnp.concatenate(od_blocks, axis=2)
    od = rmsnorm(od, subln_g) * (1.0 - lam_init)
    od = od.transpose(0, 2, 1, 3).reshape(b, s, DIFF_V_COLS)
    of = jnp.concatenate(of_blocks, axis=2).transpose(0, 2, 1, 3).reshape(b, s, FOX_COLS)
    return jnp.concatenate([od, of], axis=-1) @ w_out


def _complex_linear_combine(left, right):
    ar1, ai1, br1, bi1 = left
    ar2, ai2, br2, bi2 = right
    return (ar2 * ar1 - ai2 * ai1,
            ar2 * ai1 + ai2 * ar1,
            ar2 * br1 - ai2 * bi1 + br2,
            ar2 * bi1 + ai2 * br1 + bi2)


def s5_mixer(h, a_re, a_im, log_step, b_re, b_im, c_re, c_im, d_skip, w_glu_a, w_glu_b):
    f32 = jnp.float32
    b, s, _ = h.shape
    u = h.reshape(b, s, S5_GROUPS, S5_GROUP).astype(f32)
    step = jnp.exp(log_step.astype(f32))[:, None]
    ar, ai = a_re.astype(f32), a_im.astype(f32)
    mag = jnp.exp(ar * step)
    lr, li = mag * jnp.cos(ai * step), mag * jnp.sin(ai * step)
    den = ar * ar + ai * ai
    cr = ((lr - 1.0) * ar + li * ai) / den
    ci = (li * ar - (lr - 1.0) * ai) / den
    br, bi = b_re.astype(f32), b_im.astype(f32)
    bbr = cr[..., None] * br - ci[..., None] * bi
    bbi = cr[..., None] * bi + ci[..., None] * br
    xr = jnp.einsum('bsgc,gpc->bsgp', u, bbr)
    xi = jnp.einsum('bsgc,gpc->bsgp', u, bbi)
    lr_s = jnp.broadcast_to(lr, (1, s, S5_GROUPS, S5_STATE))
    li_s = jnp.broadcast_to(li, (1, s, S5_GROUPS, S5_STATE))
    _, _, xr, xi = lax.associative_scan(_complex_linear_combine, (lr_s, li_s, xr, xi), axis=1)
    y = (jnp.einsum('bsgp,gcp->bsgc', xr, c_re.astype(f32))
         - jnp.einsum('bsgp,gcp->bsgc', xi, c_im.astype(f32)))
    y = y.reshape(b, s, D_MODEL) + d_skip.astype(f32) * h.astype(f32)
    y = jax.nn.gelu(y).astype(h.dtype)
    return (y @ w_glu_a) * jax.nn.sigmoid(y @ w_glu_b)


def setup_inputs(seed: int = 0) -> dict:
    key = jax.random.key(seed)
    keys = list(jax.random.split(key, 48))
    it = iter(keys)

    def nrm(shape, scale):
        return jax.random.normal(next(it), shape, jnp.float32) * scale

    def gain(shape):
        return 1.0 + nrm(shape, 0.05)

    L, La, Ls = DEPTH, N_ATTN_LAYERS, N_SSM_LAYERS
    x = nrm((BATCH, SEQ, D_MODEL), 1.0)
    ffn1_norm = gain((L, D_MODEL))
    ffn1_gate = nrm((L, D_MODEL, D_FF), D_MODEL ** -0.5)
    ffn1_up = nrm((L, D_MODEL, D_FF), D_MODEL ** -0.5)
    ffn1_down = nrm((L, D_FF, D_MODEL), D_FF ** -0.5)
    mix_norm = gain((L, D_MODEL))
    ffn2_norm = gain((L, D_MODEL))
    ffn2_gate = nrm((L, D_MODEL, D_FF), D_MODEL ** -0.5)
    ffn2_up = nrm((L, D_MODEL, D_FF), D_MODEL ** -0.5)
    ffn2_down = nrm((L, D_FF, D_MODEL), D_FF ** -0.5)
    attn_w_in = nrm((La, D_MODEL, IN_COLS), D_MODEL ** -0.5)
    attn_w_out = nrm((La, MIX_COLS, D_MODEL), MIX_COLS ** -0.5)
    fg_bias = jax.random.uniform(next(it), (La, N_FOX_HEADS), jnp.float32, 1.0, 5.0)
    diff_q_norm = gain((La, HEAD_DIM))
    diff_k_norm = gain((La, HEAD_DIM))
    diff_lambda_q1 = nrm((La, HEAD_DIM), 0.1)
    diff_lambda_k1 = nrm((La, HEAD_DIM), 0.1)
    diff_lambda_q2 = nrm((La, HEAD_DIM), 0.1)
    diff_lambda_k2 = nrm((La, HEAD_DIM), 0.1)
    diff_subln = gain((La, DIFF_V_DIM))
    fox_q_norm = gain((La, HEAD_DIM))
    fox_k_norm = gain((La, HEAD_DIM))
    rel_bias = nrm((N_REL_BUCKETS, N_DIFF_HEADS), 0.5)
    s5_a_re = -0.5 + nrm((Ls, S5_GROUPS, S5_STATE), 0.01)
    s5_a_im = (math.pi * jnp.arange(S5_STATE, dtype=jnp.float32))[None, None, :] \
        + nrm((Ls, S5_GROUPS, S5_STATE), 0.01)
    s5_log_step = jax.random.uniform(next(it), (Ls, S5_GROUPS), jnp.float32,
                                     math.log(1e-3), math.log(1e-1))
    s5_b_re = nrm((Ls, S5_GROUPS, S5_STATE, S5_GROUP), (2 * S5_GROUP) ** -0.5)
    s5_b_im = nrm((Ls, S5_GROUPS, S5_STATE, S5_GROUP), (2 * S5_GROUP) ** -0.5)
    s5_c_re = nrm((Ls, S5_GROUPS, S5_GROUP, S5_STATE), S5_STATE ** -0.5)
    s5_c_im = nrm((Ls, S5_GROUPS, S5_GROUP, S5_STATE), S5_STATE ** -0.5)
    s5_d = nrm((Ls, D_MODEL), 1.0)
    s5_glu_a = nrm((Ls, D_MODEL, D_MODEL), D_MODEL ** -0.5)
    s5_glu_b = nrm((Ls, D_MODEL, D_MODEL), D_MODEL ** -0.5)
    return {"x": x,
            "ffn1_norm": ffn1_norm, "ffn1_gate": ffn1_gate, "ffn1_up": ffn1_up, "ffn1_down": ffn1_down,
            "mix_norm": mix_norm,
            "ffn2_norm": ffn2_norm, "ffn2_gate": ffn2_gate, "ffn2_up": ffn2_up, "ffn2_down": ffn2_down,
            "attn_w_in": attn_w_in, "attn_w_out": attn_w_out, "fg_bias": fg_bias,
            "diff_q_norm": diff_q_norm, "diff_k_norm": diff_k_norm,
            "diff_lambda_q1": diff_lambda_q1, "diff_lambda_k1": diff_lambda_k1,
            "diff_lambda_q2": diff_lambda_q2, "diff_lambda_k2": diff_lambda_k2,
            "diff_subln": diff_subln, "fox_q_norm": fox_q_norm, "fox_k_norm": fox_k_norm,
            "rel_bias": rel_bias,
            "s5_a_re": s5_a_re, "s5_a_im": s5_a_im, "s5_log_step": s5_log_step,
            "s5_b_re": s5_b_re, "s5_b_im": s5_b_im, "s5_c_re": s5_c_re, "s5_c_im": s5_c_im,
            "s5_d": s5_d, "s5_glu_a": s5_glu_a, "s5_glu_b": s5_glu_b}


def reference(x, ffn1_norm, ffn1_gate, ffn1_up, ffn1_down, mix_norm,
              ffn2_norm, ffn2_gate, ffn2_up, ffn2_down,
              attn_w_in, attn_w_out, fg_bias, diff_q_norm, diff_k_norm,
              diff_lambda_q1, diff_lambda_k1, diff_lambda_q2, diff_lambda_k2,
              diff_subln, fox_q_norm, fox_k_norm, rel_bias,
              s5_a_re, s5_a_im, s5_log_step, s5_b_re, s5_b_im, s5_c_re, s5_c_im,
              s5_d, s5_glu_a, s5_glu_b):
    for i in range(DEPTH):
        x = x + 0.5 * swiglu(rmsnorm(x, ffn1_norm[i]), ffn1_gate[i], ffn1_up[i], ffn1_down[i])
        h = rmsnorm(x, mix_norm[i])
        j = i // 2
        if i % 2 == 0:
            x = x + diff_fox_mixer(h, attn_w_in[j], attn_w_out[j], fg_bias[j],
                                   diff_q_norm[j], diff_k_norm[j],
                                   diff_lambda_q1[j], diff_lambda_k1[j],
                                   diff_lambda_q2[j], diff_lambda_k2[j],
                                   diff_subln[j], fox_q_norm[j], fox_k_norm[j],
                                   rel_bias, i)
        else:
            x = x + s5_mixer(h, s5_a_re[j], s5_a_im[j], s5_log_step[j],
                             s5_b_re[j], s5_b_im[j], s5_c_re[j], s5_c_im[j],
                             s5_d[j], s5_glu_a[j], s5_glu_b[j])
        x = x + 0.5 * swiglu(rmsnorm(x, ffn2_norm[i]), ffn2_gate[i], ffn2_up[i], ffn2_down[i])
    return x
```

```python
import bisect
import contextlib
import math

import numpy as np
import concourse.bass as bass
import concourse.mybir as mybir
from concourse.bass_utils import run_bass_kernel_spmd

F32 = mybir.dt.float32
BF16 = mybir.dt.bfloat16
AF = mybir.ActivationFunctionType
ALU = mybir.AluOpType
AX = mybir.AxisListType

D = 1024
S = 4096
DFF = 2816
DEPTH = 4
NCORES = 8
EPS = 1e-6
IN_COLS = 3080
NEG = -80.0


class Op:
    __slots__ = ("eng", "fn", "dma", "deps", "marked", "cum", "seq")

    def __init__(self, eng, fn, dma):
        self.eng = eng
        self.fn = fn
        self.dma = dma
        self.deps = []
        self.marked = False
        self.cum = 0
        self.seq = 0


class Prog:
    ENGS = ["sync", "act", "dve", "pe", "pool"]
    BLK = {"sync": "sync", "act": "scalar", "dve": "vector", "pe": "tensor", "pool": "gpsimd"}
    CH = 30000

    def __init__(self, nc):
        self.nc = nc
        self.ops = []
        self.lastw = {}
        self.readers = {}
        self.last_on = {}

    @staticmethod
    def stream(o):
        return o.dma if o.dma else o.eng

    def op(self, eng, name, kw, reads=(), writes=(), dma=None):
        args = ()
        if isinstance(kw, tuple):
            args, kw = kw
        fn = (lambda e, name=name, args=args, kw=kw: getattr(e, name)(*args, **kw))
        o = Op(eng, fn, dma)
        o.seq = len(self.ops)
        deps = {}

        def add(d, raw=False):
            if d is None:
                return
            if (not d.dma) and (not o.dma) and d.eng == o.eng:
                if not (raw and o.eng in ("act", "pool", "dve")):
                    return
            st = self.stream(d)
            if st not in deps or deps[st].seq < d.seq:
                deps[st] = d

        for k in reads:
            add(self.lastw.get(k), raw=True)
        for k in writes:
            add(self.lastw.get(k))
            for d in self.readers.get(k, {}).values():
                add(d)
        o.deps = list(deps.values())
        for k in writes:
            self.lastw[k] = o
            self.readers[k] = {}
        for k in reads:
            self.readers.setdefault(k, {})[self.stream(o)] = o
        self.ops.append(o)
        if fn is not None:
            self.last_on[self.stream(o)] = o
        return o

    def barrier(self, final=False):
        lasts = {st: d for st, d in self.last_on.items() if final or not st.startswith("cast_")}
        keep = {k: v for k, v in self.lastw.items() if isinstance(k, tuple) and isinstance(k[0], str) and k[0].startswith("w_")}
        for e in self.ENGS:
            o = Op(e, None, None)
            o.seq = len(self.ops)
            o.deps = [d for st, d in lasts.items() if not ((not d.dma) and d.eng == e)]
            self.ops.append(o)
        self.lastw = {} if final else keep
        self.readers = {}

    def emit(self):
        nc = self.nc
        for o in self.ops:
            for d in o.deps:
                d.marked = True
        cnt = {}
        dma_seqs = {}
        for o in self.ops:
            if o.fn is None:
                continue
            if o.dma:
                cnt[o.dma] = cnt.get(o.dma, 0) + 1
                o.cum = cnt[o.dma]
                dma_seqs.setdefault(o.dma, []).append(o.seq)
            elif o.marked:
                cnt[o.eng] = cnt.get(o.eng, 0) + 1
                o.cum = cnt[o.eng]
        for k, v in cnt.items():
            if k not in self.ENGS:
                assert v * 16 < 60000, (k, v)
        with contextlib.ExitStack() as es:
            sems = {}

            def sem(name):
                if name not in sems:
                    sems[name] = es.enter_context(nc.semaphore("s_" + name))
                return sems[name]

            for k, v in cnt.items():
                if k in self.ENGS:
                    for c in range((v - 1) // self.CH + 1):
                        sem(f"{k}{c}")
                else:
                    sem(k)

            bar_seqs = [o.seq for o in self.ops if o.fn is None]
            self.partial = {}

            def resolve(d, o):
                if d.dma:
                    n = bisect.bisect_left(dma_seqs[d.dma], o.seq)
                    nb = bar_seqs[bisect.bisect_left(bar_seqs, o.seq)] if bisect.bisect_left(bar_seqs, o.seq) < len(bar_seqs) else 1 << 60
                    n_epoch = bisect.bisect_left(dma_seqs[d.dma], nb)
                    if n_epoch > n:
                        self.partial[d.dma] = self.partial.get(d.dma, 0) + 1
                    assert n >= d.cum
                    return d.dma, n * 16
                c = (d.cum - 1) // self.CH
                return f"{d.eng}{c}", (d.cum - 1) % self.CH + 1

            block = es.enter_context(nc.Block())
            for eng in self.ENGS:
                ops_e = [o for o in self.ops if o.eng == eng]

                def body(e, ops_e=ops_e):
                    waited = {}
                    for o in ops_e:
                        for d in o.deps:
                            sn, val = resolve(d, o)
                            if waited.get(sn, 0) < val:
                                e.wait_ge(sem(sn), val)
                                waited[sn] = val
                        if o.fn is None:
                            continue
                        inst = o.fn(e)
                        if o.dma:
                            inst.then_inc(sem(o.dma), 16)
                        elif o.marked:
                            c = (o.cum - 1) // self.CH
                            inst.then_inc(sem(f"{o.eng}{c}"), 1)

                getattr(block, self.BLK[eng])(body)


class Builder:
    def __init__(self, phases=None):
        self.nc = bass.Bass("TRN2", target_bir_lowering=False)
        self.P = Prog(self.nc)
        self.phases = phases
        self.inputs = {}
        self.wkeys = {}

    uid = 0
    debug = False

    def dump(self, name, ap, reads):
        if not self.debug:
            return
        o = self.nc.dram_tensor("dbg_" + name, list(ap.shape), ap.dtype, kind="ExternalOutput").ap()
        self.P.op("sync", "dma_start", dict(out=o, in_=ap), reads=list(reads), writes=[("dbg", name)], dma="dbg")

    def sb(self, name, shape, dt=F32):
        return self.nc.sbuf_tensor(f"{name}__u{self.uid}", shape, dt)

    def pp(self, name, shape, dt=F32):
        return self.nc.psum_tensor(f"{name}__u{self.uid}", shape, dt)

    def din(self, name, shape, dt=F32):
        ap = self.nc.dram_tensor(name, list(shape), dt, kind="ExternalInput").ap()
        self.inputs[name] = ap
        return ap

    def dscr(self, name, shape, dt):
        return self.nc.dram_tensor(name, list(shape), dt, kind="Internal").ap()

    def cast_w(self, src, dst, semname, key, nsplit=4):
        P = self.P
        rows, cols = src.shape
        b = cols
        for cand in range(1, 9):
            if cols % cand == 0 and cols // cand <= 1024:
                b = cols // cand
                break
        rs = rows // nsplit
        for i in range(nsplit):
            s_ap = src[i * rs:(i + 1) * rs, :].rearrange("k (a b) -> k a b", b=b)
            d_ap = dst[i * rs:(i + 1) * rs, :].rearrange("k (a b) -> k a b", b=b)
            self.wkeys.setdefault(key, [])
            kk = (key, len(self.wkeys[key]))
            self.wkeys[key].append(kk)
            P.op("pool", "dma_start", dict(out=d_ap, in_=s_ap), writes=[kk], dma=semname)

    def ffn_phase(self, es, xsrc, xdst, gnorm_ap, wg, wu, wd, wkey, ident_b):
        nc, P = self.nc, self.P
        TB = 1024
        NB = S // TB
        KT = D // 128
        MT = DFF // 128
        CG = 256
        NCG = DFF // CG
        A = lambda *a: es.enter_context(self.sb(*a))
        PS = lambda *a: es.enter_context(self.pp(*a))
        xt = [A(f"f_xt{i}", [128, 8, D], F32) for i in range(2)]
        hb = [A(f"f_hb{i}", [128, D], BF16) for i in range(2)]
        hT = A("f_hT", [128, KT, TB], BF16)
        aT = A("f_aT", [128, MT, TB], BF16)
        wdt = A("f_wd", [128, MT, D], BF16)
        wgt = [A(f"f_wg{i}", [128, KT, CG], BF16) for i in range(2)]
        wut = [A(f"f_wu{i}", [128, KT, CG], BF16) for i in range(2)]
        gn = A("f_gn", [128, D], F32)
        sq = A("f_sq", [128, D], BF16)
        ss = A("f_ss", [128, 8], F32)
        rstd = A("f_rstd", [128, 8], F32)
        sg = [A(f"f_sg{i}", [128, 512], F32) for i in range(2)]
        pT = [PS(f"f_pT{i}", [128, 8, 128], BF16) for i in range(2)]
        pg = [PS(f"f_pg{i}", [128, 512], F32) for i in range(2)]
        pu = [PS(f"f_pu{i}", [128, 512], F32) for i in range(2)]
        po = [PS(f"f_po{i}", [128, 512], F32) for i in range(2)]

        P.op("sync", "dma_start", dict(out=gn[:], in_=gnorm_ap.partition_broadcast(128)),
             writes=["f_gn"], dma="f_misc")
        wd_v = wd.rearrange("(k p) n -> p k n", p=128)
        for h in range(2):
            P.op("sync", "dma_start", dict(out=wdt[:, h * 11:(h + 1) * 11, :], in_=wd_v[:, h * 11:(h + 1) * 11, :]),
                 reads=self.wkeys[wkey], writes=["f_wd"], dma="f_misc")
        wg_v = wg.rearrange("(k p) n -> p k n", p=128)
        wu_v = wu.rearrange("(k p) n -> p k n", p=128)

        def xv(ap, b):
            return ap[b * TB:(b + 1) * TB, :].rearrange("(p t) d -> p t d", t=8)

        def load_x(b):
            P.op("sync", "dma_start", dict(out=xt[b % 2][:], in_=xv(xsrc, b)),
                 reads=[("xres", b)], writes=[("f_xt", b % 2)], dma=f"f_x{b % 2}")

        load_x(0)
        wl = 0
        for b in range(NB):
            X = xt[b % 2]
            xk = ("f_xt", b % 2)
            if b + 1 < NB:
                load_x(b + 1)
            for t in range(8):
                P.op("act", "activation", dict(out=sq[:], in_=X[:, t, :], func=AF.Square, accum_out=ss[:, t:t + 1]),
                     reads=[xk], writes=["f_sq", ("f_ss", t)])
            P.op("dve", "tensor_scalar", dict(out=rstd[:], in0=ss[:], scalar1=1.0 / D, scalar2=EPS,
                                              op0=ALU.mult, op1=ALU.add),
                 reads=[("f_ss", t) for t in range(8)], writes=["f_rstd"])
            P.op("act", "activation", dict(out=rstd[:], in_=rstd[:], func=AF.Sqrt), reads=["f_rstd"], writes=["f_rstd"])
            P.op("dve", "reciprocal", dict(out=rstd[:], in_=rstd[:]), reads=["f_rstd"], writes=["f_rstd"])
            for t in range(8):
                H = hb[t % 2]
                P.op("dve", "scalar_tensor_tensor", dict(out=H[:], in0=X[:, t, :], scalar=rstd[:, t:t + 1], in1=gn[:],
                                                         op0=ALU.mult, op1=ALU.mult),
                     reads=[xk, "f_rstd", "f_gn"], writes=[("f_hb", t % 2)])
                pt = pT[t % 2]
                for k in range(KT):
                    P.op("pe", "transpose", dict(out=pt[:, k, :], in_=H[:, k * 128:(k + 1) * 128], identity=ident_b[:]),
                         reads=[("f_hb", t % 2), "ident"], writes=[("f_pT", t % 2)])
                if t % 2 == 0:
                    P.op("dve", "tensor_copy", dict(out=hT[:, :, t * 128:(t + 1) * 128], in_=pt[:]),
                         reads=[("f_pT", t % 2)], writes=[("f_hT", t)])
                else:
                    P.op("act", "copy", dict(out=hT[:, :, t * 128:(t + 1) * 128], in_=pt[:]),
                         reads=[("f_pT", t % 2)], writes=[("f_hT", t)])
            hT_keys = [("f_hT", t) for t in range(8)]
            ev = 0
            for cg in range(NCG):
                sl = wl % 2
                wl += 1
                P.op("sync", "dma_start", dict(out=wgt[sl][:], in_=wg_v[:, :, cg * CG:(cg + 1) * CG]),
                     reads=self.wkeys[wkey], writes=[("f_wg", sl)], dma=f"f_w{sl}")
                P.op("sync", "dma_start", dict(out=wut[sl][:], in_=wu_v[:, :, cg * CG:(cg + 1) * CG]),
                     reads=self.wkeys[wkey], writes=[("f_wu", sl)], dma=f"f_w{sl}")
                for mi in range(CG // 128):
                    m = cg * (CG // 128) + mi
                    for hf in range(TB // 512):
                        q = ev % 2
                        ev += 1
                        for k in range(KT):
                            P.op("pe", "matmul", ((pg[q][:],), dict(lhsT=wgt[sl][:, k, mi * 128:(mi + 1) * 128],
                                                                   rhs=hT[:, k, hf * 512:(hf + 1) * 512],
                                                                   start=(k == 0), stop=(k == KT - 1))),
                                 reads=[("f_wg", sl)] + hT_keys, writes=[("f_pg", q)])
                        for k in range(KT):
                            P.op("pe", "matmul", ((pu[q][:],), dict(lhsT=wut[sl][:, k, mi * 128:(mi + 1) * 128],
                                                                   rhs=hT[:, k, hf * 512:(hf + 1) * 512],
                                                                   start=(k == 0), stop=(k == KT - 1))),
                                 reads=[("f_wu", sl)] + hT_keys, writes=[("f_pu", q)])
                        P.op("act", "activation", dict(out=sg[q][:], in_=pg[q][:], func=AF.Silu),
                             reads=[("f_pg", q)], writes=[("f_sg", q)])
                        P.op("dve", "tensor_tensor", dict(out=aT[:, m, hf * 512:(hf + 1) * 512], in0=sg[q][:],
                                                          in1=pu[q][:], op=ALU.mult),
                             reads=[("f_sg", q), ("f_pu", q)], writes=[("f_aT", m)])
            aT_keys = [("f_aT", m) for m in range(MT)]
            for t in range(8):
                for nh in range(2):
                    q = (t * 2 + nh) % 2
                    for k in range(MT):
                        P.op("pe", "matmul", ((po[q][:],), dict(lhsT=aT[:, k, t * 128:(t + 1) * 128],
                                                               rhs=wdt[:, k, nh * 512:(nh + 1) * 512],
                                                               start=(k == 0), stop=(k == MT - 1))),
                             reads=aT_keys + ["f_wd"], writes=[("f_po", q)])
                    P.op("dve", "scalar_tensor_tensor", dict(out=X[:, t, nh * 512:(nh + 1) * 512], in0=po[q][:],
                                                             scalar=0.5, in1=X[:, t, nh * 512:(nh + 1) * 512],
                                                             op0=ALU.mult, op1=ALU.add),
                         reads=[("f_po", q), xk], writes=[xk])
            P.op("sync", "dma_start", dict(out=xv(xdst, b), in_=X[:]), reads=[xk], writes=[("xres", b)], dma="f_st")

    def s5_phase(self, es, xsrc, xdst, L, wn, wb, ident_f, ident_b):
        nc, P = self.nc, self.P
        jj = L // 2
        A = lambda *a: es.enter_context(self.sb(*a))
        toep = A("s_toep", [128, 64, 128], BF16)
        wtr = A("s_wtr", [128, 64, 64], BF16)
        wti = A("s_wti", [128, 64, 64], BF16)
        vre = A("s_vre", [128, 32, 128], BF16)
        vim = A("s_vim", [128, 32, 128], BF16)
        dvec = A("s_dvec", [128, 64], F32)
        mcat = A("s_mcat", [128, 2, 64], F32)
        gn = A("s_gn", [128, D], F32)
        P.op("sync", "dma_start", dict(out=gn[:], in_=wn["mix_norm"][L].partition_broadcast(128)),
             writes=["s_gn"], dma="s_misc")
        with contextlib.ExitStack() as es1:
            self.s5_setup(es1, jj, wn, ident_f, toep, wtr, wti, vre, vim, dvec, mcat)
        P.barrier()
        self.s5_main(es, xsrc, xdst, jj, wb, ident_b, toep, wtr, wti, vre, vim, dvec, mcat, gn)

    def s5_setup(self, es, jj, wn, ident_f, toep, wtr, wti, vre, vim, dvec, mcat):
        nc, P = self.nc, self.P
        A = lambda n, s, d=F32: es.enter_context(self.sb(n, s, d))
        PS = lambda n, s, d=F32: es.enter_context(self.pp(n, s, d))
        kn = lambda ap: ap.tensor.name.split("__u")[0]

        def TT(out, in0, in1, op, eng="dve"):
            P.op(eng, "tensor_tensor", dict(out=out, in0=in0, in1=in1, op=op), reads=[kn(in0), kn(in1)], writes=[kn(out)])

        def TS(out, in0, s1, op0, s2=None, op1=None, eng="dve"):
            kw = dict(out=out, in0=in0, scalar1=s1, scalar2=s2, op0=op0)
            if op1 is not None:
                kw["op1"] = op1
            rd = [kn(in0)] + ([kn(s1)] if not isinstance(s1, (int, float)) else [])
            P.op(eng, "tensor_scalar", kw, reads=rd, writes=[kn(out)])

        def ACT(out, in_, func, **kw):
            P.op("act", "activation", dict(out=out, in_=in_, func=func, **kw), reads=[kn(in_)], writes=[kn(out)])

        def CP(out, in_, eng="dve"):
            P.op(eng, "tensor_copy", dict(out=out, in_=in_), reads=[kn(in_)], writes=[kn(out)])

        def DMA(out, in_, wr):
            P.op("sync", "dma_start", dict(out=out, in_=in_), writes=[wr], dma="z_ld")

        are, aim, lst = wn["s5_a_re"][jj], wn["s5_a_im"][jj], wn["s5_log_step"][jj]
        bre, bim, cre, cim, dsk = wn["s5_b_re"][jj], wn["s5_b_im"][jj], wn["s5_c_re"][jj], wn["s5_c_im"][jj], wn["s5_d"][jj]
        smask = self.consts["s5mask"]
        AR = A("z_ar", [64, 128]); AI = A("z_ai", [64, 128]); LS = A("z_ls", [64, 1])
        for h in range(2):
            DMA(AR[:, h * 64:(h + 1) * 64], are, "z_ar")
            DMA(AI[:, h * 64:(h + 1) * 64], aim, "z_ai")
        DMA(LS[:], lst.rearrange("(g o) -> g o", o=1), "z_ls")
        mask = A("z_mask", [128, 128])
        DMA(mask[:], smask, "z_mask")
        BR = A("z_br", [128, 64, 16]); BI = A("z_bi", [128, 64, 16])
        for h in range(2):
            DMA(BR[h * 64:(h + 1) * 64], bre.rearrange("g p c -> p g c"), "z_br")
            DMA(BI[h * 64:(h + 1) * 64], bim.rearrange("g p c -> p g c"), "z_bi")
        tC = [A("z_tcr", [128, 8, 128]), A("z_tci", [128, 8, 128])]
        for t, src in zip(tC, (cre, cim)):
            for h in range(2):
                DMA(t[:, :, h * 64:(h + 1) * 64], src.rearrange("(o g) c p -> (g c) o p", o=8), kn(t[:]))
        Dg = A("z_dg", [64, 16]); Dg8 = A("z_dg8", [64, 8, 16])
        DMA(Dg[:], dsk.rearrange("(g c) -> g c", c=16), "z_dg")
        CP(Dg8[:], Dg[:].unsqueeze(1).broadcast_to([64, 8, 16]))

        step = A("z_step", [64, 1])
        ACT(step[:], LS[:], AF.Exp)
        ARS = A("z_ars", [64, 128]); TH = A("z_th", [64, 128]); MAG = A("z_mag", [64, 128])
        Cc = A("z_c", [64, 128]); Sn = A("z_s", [64, 128])
        t1 = A("z_t1", [64, 128]); t2 = A("z_t2", [64, 128]); t3 = A("z_t3", [64, 128])
        TS(ARS[:], AR[:], step[:, 0:1], ALU.mult)
        TS(TH[:], AI[:], step[:, 0:1], ALU.mult)
        ACT(MAG[:], ARS[:], AF.Exp)
        ACT(Sn[:], TH[:], AF.Sin, scale=1.0 / 32)
        hpi = A("z_hpi", [64, 1])
        P.op("dve", "memset", ((hpi[:], math.pi / 2), {}), writes=["z_hpi"])
        P.op("act", "activation", dict(out=Cc[:], in_=TH[:], func=AF.Sin, scale=1.0 / 32, bias=hpi[:, 0:1]),
             reads=["z_th", "z_hpi"], writes=["z_c"])
        for it in range(5):
            TT(t1[:], Cc[:], Cc[:], ALU.mult)
            TT(t2[:], Sn[:], Sn[:], ALU.mult)
            TT(t3[:], Cc[:], Sn[:], ALU.mult)
            TT(Cc[:], t1[:], t2[:], ALU.subtract)
            TS(Sn[:], t3[:], 2.0, ALU.mult)
        PWr = A("z_pwr", [64, 16, 128]); PWi = A("z_pwi", [64, 16, 128])

        def cmul(orr, oi, ar, ai, br, bi):
            TT(t1[:], ar, br, ALU.mult)
            TT(t2[:], ai, bi, ALU.mult)
            TT(t3[:], ar, bi, ALU.mult)
            TT(orr, t1[:], t2[:], ALU.subtract)
            TT(t1[:], ai, br, ALU.mult)
            TT(oi, t3[:], t1[:], ALU.add)

        P.op("dve", "memset", ((PWr[:, 7, :], 1.0), {}), writes=["z_pwr"])
        P.op("dve", "memset", ((PWi[:, 7, :], 0.0), {}), writes=["z_pwi"])
        TT(PWr[:, 8, :], MAG[:], Cc[:], ALU.mult)
        TT(PWi[:, 8, :], MAG[:], Sn[:], ALU.mult)
        for k in range(9, 16):
            cmul(PWr[:, k, :], PWi[:, k, :], PWr[:, k - 1, :], PWi[:, k - 1, :], PWr[:, 8, :], PWi[:, 8, :])
        m2 = A("z_m2", [64, 128])
        TT(t1[:], PWr[:, 8, :], PWr[:, 8, :], ALU.mult)
        TT(t2[:], PWi[:, 8, :], PWi[:, 8, :], ALU.mult)
        TT(m2[:], t1[:], t2[:], ALU.add)
        P.op("dve", "reciprocal", dict(out=m2[:], in_=m2[:]), reads=["z_m2"], writes=["z_m2"])
        TT(PWr[:, 6, :], PWr[:, 8, :], m2[:], ALU.mult)
        TT(t1[:], PWi[:, 8, :], m2[:], ALU.mult)
        TS(PWi[:, 6, :], t1[:], -1.0, ALU.mult)
        for k in range(5, -1, -1):
            cmul(PWr[:, k, :], PWi[:, k, :], PWr[:, k + 1, :], PWi[:, k + 1, :], PWr[:, 6, :], PWi[:, 6, :])
        CRg_ = A("z_crg", [64, 128]); CIg_ = A("z_cig", [64, 128]); den = A("z_den", [64, 128]); lm1 = A("z_lm1", [64, 128])
        TT(t1[:], AR[:], AR[:], ALU.mult)
        TT(t2[:], AI[:], AI[:], ALU.mult)
        TT(den[:], t1[:], t2[:], ALU.add)
        P.op("dve", "reciprocal", dict(out=den[:], in_=den[:]), reads=["z_den"], writes=["z_den"])
        TS(lm1[:], PWr[:, 8, :], -1.0, ALU.add)
        TT(t1[:], lm1[:], AR[:], ALU.mult)
        TT(t2[:], PWi[:, 8, :], AI[:], ALU.mult)
        TT(t1[:], t1[:], t2[:], ALU.add)
        TT(CRg_[:], t1[:], den[:], ALU.mult)
        TT(t1[:], PWi[:, 8, :], AR[:], ALU.mult)
        TT(t2[:], lm1[:], AI[:], ALU.mult)
        TT(t1[:], t1[:], t2[:], ALU.subtract)
        TT(CIg_[:], t1[:], den[:], ALU.mult)

        TABr = A("z_tabr", [128, 64, 25]); TABi = A("z_tabi", [128, 64, 25])
        CRp = A("z_crp", [128, 64]); CIp = A("z_cip", [128, 64])
        pz = [PS("z_pz0", [128, 8, 64]), PS("z_pz1", [128, 8, 64])]
        slots = [7 - s for s in range(8)] + [7 + k for k in range(9)] + [14 - s for s in range(8)]
        idf64 = ident_f[0:64, 0:64]
        nb = 0
        for TAB, PW in ((TABr, PWr), (TABi, PWi)):
            for b0 in range(0, 25, 8):
                n = min(8, 25 - b0)
                pt = pz[nb % 2]
                nb += 1
                for q in range(n):
                    P.op("pe", "transpose", dict(out=pt[:, q, :], in_=PW[:, slots[b0 + q], :], identity=idf64),
                         reads=[kn(PW[:]), "ident_f"], writes=[kn(pt[:])])
                CP(TAB[:, :, b0:b0 + n].rearrange("p g s -> p s g"), pt[:, 0:n, :])
        pt = pz[nb % 2]
        P.op("pe", "transpose", dict(out=pt[:, 0, :], in_=CRg_[:], identity=idf64), reads=["z_crg", "ident_f"], writes=[kn(pt[:])])
        P.op("pe", "transpose", dict(out=pt[:, 1, :], in_=CIg_[:], identity=idf64), reads=["z_cig", "ident_f"], writes=[kn(pt[:])])
        P.op("pe", "transpose", dict(out=pt[:, 2, :], in_=Dg8[:].rearrange("g a c -> g (a c)"), identity=idf64),
             reads=["z_dg8", "ident_f"], writes=[kn(pt[:])])
        CP(CRp[:], pt[:, 0, :])
        CP(CIp[:], pt[:, 1, :])
        CP(dvec[:], pt[:, 2, :])
        for h in range(2):
            rs = slice(h * 64, (h + 1) * 64)
            mr = TABr[rs, h::2, 16]
            mi = TABi[rs, h::2, 16]
            CP(mcat[rs, 0, 0:32], mr)
            CP(mcat[rs, 0, 32:64], mr)
            TS(mcat[rs, 1, 0:32], mi, -1.0, ALU.mult)
            CP(mcat[rs, 1, 32:64], mi)
        CRg = A("z_crpg", [128, 64, 16]); CIn = A("z_cinpg", [128, 64, 16])
        pc = [PS("z_pc0", [128, 4, 128]), PS("z_pc1", [128, 4, 128])]
        nb = 0
        for t, dstC in zip(tC, (CRg, CIn)):
            for q4 in range(2):
                pt = pc[nb % 2]
                nb += 1
                for q in range(4):
                    P.op("pe", "transpose", dict(out=pt[:, q, :], in_=t[:, q4 * 4 + q, :], identity=ident_f[:]),
                         reads=[kn(t[:]), "ident_f"], writes=[kn(pt[:])])
                CP(dstC[:, q4 * 32:(q4 + 1) * 32, :].rearrange("p (a g) c -> p a (g c)", a=4), pt[:])
        TS(CIn[:], CIn[:], -1.0, ALU.mult)
        BBr = A("z_bbr", [128, 64, 16]); BBi = A("z_bbi", [128, 64, 16])
        u1 = A("z_u1", [128, 64, 16]); u2 = A("z_u2", [128, 64, 16])
        crb = CRp[:].unsqueeze(2).broadcast_to([128, 64, 16])
        cib = CIp[:].unsqueeze(2).broadcast_to([128, 64, 16])
        TT(u1[:], crb, BR[:], ALU.mult)
        TT(u2[:], cib, BI[:], ALU.mult)
        TT(BBr[:], u1[:], u2[:], ALU.subtract)
        TT(u1[:], crb, BI[:], ALU.mult)
        TT(u2[:], cib, BR[:], ALU.mult)
        TT(BBi[:], u1[:], u2[:], ALU.add)

        GH = 16
        X1 = A("z_x1", [128, GH, 8, 16]); X2 = A("z_x2", [128, GH, 8, 16])
        X3 = A("z_x3", [128, GH, 8, 16]); X4 = A("z_x4", [128, GH, 8, 16])
        T1 = A("z_T1", [128, GH, 8, 16]); T2 = A("z_T2", [128, GH, 8, 16])
        ptp = [PS("z_ptp0", [128, 4, 128]), PS("z_ptp1", [128, 4, 128])]
        pw8 = [PS("z_pw0", [128, 8, 64]), PS("z_pw1", [128, 8, 64])]

        def cprod(oa, ob, s0, Xr, Xi, g0, opa, opb, e1="dve", e2="dve"):
            pr = TABr[:, g0:g0 + GH, s0:s0 + 8].unsqueeze(3).broadcast_to([128, GH, 8, 16])
            pi = TABi[:, g0:g0 + GH, s0:s0 + 8].unsqueeze(3).broadcast_to([128, GH, 8, 16])
            xr = Xr[:, g0:g0 + GH, :].unsqueeze(2).broadcast_to([128, GH, 8, 16])
            xi = Xi[:, g0:g0 + GH, :].unsqueeze(2).broadcast_to([128, GH, 8, 16])
            TT(T1[:], pr, xr, ALU.mult, e1)
            TT(T2[:], pi, xi, ALU.mult, e1)
            TT(oa[:], T1[:], T2[:], opa, e1)
            TT(T1[:], pr, xi, ALU.mult, e2)
            TT(T2[:], pi, xr, ALU.mult, e2)
            TT(ob[:], T1[:], T2[:], opb, e2)

        nb = 0
        for gh in range(64 // GH):
            g0 = gh * GH
            cprod(X1, X2, 0, BBr, BBi, g0, ALU.subtract, ALU.add)
            cprod(X3, X4, 8, CRg, CIn, g0, ALU.add, ALU.subtract)
            for q4 in range(GH // 4):
                pt = ptp[nb % 2]
                nb += 1
                for q in range(4):
                    gl = q4 * 4 + q
                    P.op("pe", "matmul", ((pt[:, q, :],), dict(lhsT=X1[0:64, gl].rearrange("p s c -> p (s c)"),
                                                              rhs=X3[0:64, gl].rearrange("p s c -> p (s c)"),
                                                              start=True, stop=False)),
                         reads=["z_x1", "z_x3"], writes=[kn(pt[:])])
                    P.op("pe", "matmul", ((pt[:, q, :],), dict(lhsT=X2[0:64, gl].rearrange("p s c -> p (s c)"),
                                                              rhs=X4[0:64, gl].rearrange("p s c -> p (s c)"),
                                                              start=False, stop=True)),
                         reads=["z_x2", "z_x4"], writes=[kn(pt[:])])
                TT(toep[:, g0 + q4 * 4:g0 + q4 * 4 + 4, :], pt[:], mask[:].unsqueeze(1).broadcast_to([128, 4, 128]), ALU.mult)
            cprod(X1, X2, 17, BBr, BBi, g0, ALU.subtract, ALU.add)
            for Xs, wt in ((X1, wtr), (X2, wti)):
                for q8 in range(GH // 8):
                    pt = pw8[nb % 2]
                    nb += 1
                    for q in range(8):
                        gl = q8 * 8 + q
                        P.op("pe", "transpose", dict(out=pt[:, q, :], in_=Xs[0:64, gl].rearrange("p s c -> p (s c)"),
                                                     identity=idf64),
                             reads=[kn(Xs[:]), "ident_f"], writes=[kn(pt[:])])
                    CP(wt[:, g0 + q8 * 8:g0 + q8 * 8 + 8, :], pt[:])
            cprod(X3, X4, 9, CRg, CIn, g0, ALU.add, ALU.subtract)
            for Xs, vt in ((X3, vre), (X4, vim)):
                for h in range(2):
                    rs = slice(h * 64, (h + 1) * 64)
                    P.op("dve", "tensor_copy", dict(out=vt[rs, gh * (GH // 2):(gh + 1) * (GH // 2), :],
                                                     in_=Xs[rs, h::2].rearrange("p g s c -> p g (s c)")),
                         reads=[kn(Xs[:])], writes=[kn(vt[:])])

    def s5_main(self, es, xsrc, xdst, jj, wb, ident_b, toep, wtr, wti, vre, vim, dvec, mcat, gn):
        nc, P = self.nc, self.P
        A = lambda n, s, d=F32: es.enter_context(self.sb(n, s, d))
        PS = lambda n, s, d=F32: es.enter_context(self.pp(n, s, d))
        xq = A("m_xq", [128, 8, D])
        hy = A("m_hy", [128, 8192], BF16)
        uy = A("m_uy", [128, 8192], BF16)
        beta = A("m_beta", [128, 128, 64])
        XS = A("m_xs", [128, 129, 64], BF16)
        Z = [A("m_z0", [128, 2, 64]), A("m_z1", [128, 2, 64])]
        AB = A("m_ab", [128, 2, 64]); Ssum = A("m_s", [128, 64])
        wa = A("m_wa", [128, 8, 512], BF16); wbt = A("m_wb", [128, 8, 512], BF16)
        sq = A("m_sq", [128, D], BF16); ss = A("m_ss", [128, 8]); rstd = A("m_rstd", [128, 8])
        ytmp = [A(f"m_yt{i}", [128, 128]) for i in range(2)]
        yact = [A(f"m_ya{i}", [128, 128], BF16) for i in range(2)]
        sgt = A("m_sg", [128, 512]); tt = A("m_tt", [128, 512])
        pT = [PS(f"m_pT{i}", [128, 8, 128], BF16) for i in range(2)]
        pb = [PS(f"m_pb{i}", [128, 2, 128]) for i in range(2)]
        py = [PS(f"m_py{i}", [128, 128]) for i in range(2)]
        pa = PS("m_pa", [128, 512]); pbb = PS("m_pbb", [128, 512])
        hbp = hy[:].rearrange("p (g s c) -> p g s c", g=64, s=8)
        yt = hy[:].rearrange("p (s ch) -> p s ch", s=8)
        U = uy[:].rearrange("p (g j) -> p g j", g=64)
        yT = uy[:].rearrange("p (k t) -> p k t", k=8)
        wa_v = wb["s5_glu_a"][jj].rearrange("(k p) n -> p k n", p=128)
        wb_v = wb["s5_glu_b"][jj].rearrange("(k p) n -> p k n", p=128)
        wkey = f"w_s5_{jj}"

        def xv(ap, q):
            return ap[q * 1024:(q + 1) * 1024, :].rearrange("(p t) d -> p t d", t=8)

        P.op("dve", "memset", ((Z[0][:], 0.0), {}), writes=[("Z", 0)])
        P.op("dve", "memset", ((XS[:, 0, :], 0.0), {}), writes=["XS"])
        cnt = 0
        ncp = 0
        for q in range(4):
            P.op("sync", "dma_start", dict(out=xq[:], in_=xv(xsrc, q)), reads=[("xres", q)], writes=["xq"], dma="m_x")
            for t in range(8):
                P.op("act", "activation", dict(out=sq[:], in_=xq[:, t, :], func=AF.Square, accum_out=ss[:, t:t + 1]),
                     reads=["xq"], writes=["m_sq", "m_ss"])
            P.op("dve", "tensor_scalar", dict(out=rstd[:], in0=ss[:], scalar1=1.0 / D, scalar2=EPS, op0=ALU.mult, op1=ALU.add),
                 reads=["m_ss"], writes=["m_rstd"])
            P.op("act", "activation", dict(out=rstd[:], in_=rstd[:], func=AF.Sqrt), reads=["m_rstd"], writes=["m_rstd"])
            P.op("dve", "reciprocal", dict(out=rstd[:], in_=rstd[:]), reads=["m_rstd"], writes=["m_rstd"])
            for t in range(8):
                P.op("dve", "scalar_tensor_tensor", dict(out=hbp[:, :, t, :], in0=xq[:, t, :].rearrange("p (g c) -> p g c", c=16),
                                                         scalar=rstd[:, t:t + 1], in1=gn[:].rearrange("p (g c) -> p g c", c=16),
                                                         op0=ALU.mult, op1=ALU.mult),
                     reads=["xq", "m_rstd", "s_gn"], writes=["hy"])
            for g8 in range(8):
                pt = pT[g8 % 2]
                for gq in range(8):
                    g = g8 * 8 + gq
                    P.op("pe", "transpose", dict(out=pt[:, gq, :], in_=hy[:, g * 128:(g + 1) * 128], identity=ident_b[:]),
                         reads=["hy", "ident"], writes=[("m_pT", g8 % 2)])
                if g8 % 2 == 0:
                    P.op("dve", "tensor_copy", dict(out=U[:, g8 * 8:(g8 + 1) * 8, :], in_=pt[:]), reads=[("m_pT", 0)], writes=["uy"])
                else:
                    P.op("act", "copy", dict(out=U[:, g8 * 8:(g8 + 1) * 8, :], in_=pt[:]), reads=[("m_pT", 1)], writes=["uy"])
            for pr in range(32):
                pbt = pb[pr % 2]
                for ri, wt in enumerate((wtr, wti)):
                    for g2 in range(2):
                        g = 2 * pr + g2
                        P.op("pe", "matmul", ((pbt[g2 * 64:(g2 + 1) * 64, ri, :],),
                                              dict(lhsT=wt[:, g, :], rhs=U[:, g, :], start=True, stop=True)),
                             reads=["uy", "s_w"], writes=[("m_pb", pr % 2)])
                bt = beta[:, :, :]
                ov = bass.AP(tensor=bt.tensor, offset=bt.offset + pr, ap=[list(bt.ap[0]), [32, 2], [64, 128]])
                if pr % 2 == 0:
                    P.op("dve", "tensor_copy", dict(out=ov, in_=pbt[:]), reads=[("m_pb", 0)], writes=["beta"])
                else:
                    P.op("act", "copy", dict(out=ov, in_=pbt[:]), reads=[("m_pb", 1)], writes=["beta"])
            for j in range(128):
                zc, zn = Z[cnt % 2], Z[(cnt + 1) % 2]
                kc, kn_ = ("Z", cnt % 2), ("Z", (cnt + 1) % 2)
                cnt += 1
                zt = zc[:, :, :]
                win = bass.AP(tensor=zt.tensor, offset=zt.offset, ap=[list(zt.ap[0]), [32, 2], [1, 64]])
                P.op("dve", "tensor_tensor", dict(out=AB[:], in0=mcat[:], in1=win, op=ALU.mult), reads=[kc, "s_mcat"], writes=["AB"])
                P.op("dve", "tensor_tensor", dict(out=Ssum[:], in0=AB[:, 0, :], in1=AB[:, 1, :], op=ALU.add), reads=["AB"], writes=["Ssum"])
                P.op("dve", "tensor_tensor", dict(out=zn[:], in0=Ssum[:].unsqueeze(1).broadcast_to([128, 2, 64]),
                                                  in1=beta[:, j, :].unsqueeze(1).broadcast_to([128, 2, 64]), op=ALU.add),
                     reads=["Ssum", "beta"], writes=[kn_])
                P.op("act", "copy", dict(out=XS[:, j + 1, :], in_=zn[:, 0, :]), reads=[kn_], writes=["XS"])
            for g in range(64):
                pr, g2 = g // 2, g % 2
                pyt = py[g % 2]
                rs = slice(g2 * 64, (g2 + 1) * 64)
                P.op("pe", "matmul", ((pyt[:],), dict(lhsT=toep[:, g, :], rhs=U[:, g, :], start=True, stop=False)),
                     reads=["uy", "s_toep"], writes=[("m_py", g % 2)])
                P.op("pe", "matmul", ((pyt[:],), dict(lhsT=vre[rs, pr, :], rhs=XS[rs, 0:128, pr], start=False, stop=False)),
                     reads=["XS", "s_v"], writes=[("m_py", g % 2)])
                P.op("pe", "matmul", ((pyt[:],), dict(lhsT=vim[rs, pr, :], rhs=XS[rs, 0:128, 32 + pr], start=False, stop=True)),
                     reads=["XS", "s_v"], writes=[("m_py", g % 2)])
                P.op("dve", "scalar_tensor_tensor", dict(out=ytmp[g % 2][:], in0=U[:, g, :], scalar=dvec[:, g:g + 1], in1=pyt[:],
                                                         op0=ALU.mult, op1=ALU.add),
                     reads=["uy", ("m_py", g % 2), "s_dvec"], writes=[("m_ytmp", g % 2)])
                P.op("act", "activation", dict(out=yact[g % 2][:], in_=ytmp[g % 2][:], func=AF.Gelu),
                     reads=[("m_ytmp", g % 2)], writes=[("m_yact", g % 2)])
                g8 = g // 8
                pt = pT[g8 % 2]
                P.op("pe", "transpose", dict(out=pt[:, g % 8, :], in_=yact[g % 2][:], identity=ident_b[:]),
                     reads=[("m_yact", g % 2), "ident"], writes=[("m_pT", g8 % 2)])
                if g % 8 == 7:
                    ov = yt[:, :, g8 * 128:(g8 + 1) * 128].rearrange("p i (g c) -> p i g c", c=16)
                    iv = pt[:].rearrange("p g (i c) -> p i g c", c=16)
                    P.op("dve", "tensor_copy", dict(out=ov, in_=iv), reads=[("m_pT", g8 % 2)], writes=["hy"])
            P.op("act", "copy", dict(out=XS[:, 0, :], in_=XS[:, 128, :]), reads=["XS"], writes=["XS"])
            for s in range(8):
                pt = pT[s % 2]
                for k in range(8):
                    P.op("pe", "transpose", dict(out=pt[:, k, :], in_=yt[:, s, k * 128:(k + 1) * 128], identity=ident_b[:]),
                         reads=["hy", "ident"], writes=[("m_pT", s % 2)])
                if s % 2 == 0:
                    P.op("dve", "tensor_copy", dict(out=yT[:, :, s * 128:(s + 1) * 128], in_=pt[:]), reads=[("m_pT", 0)], writes=["uy"])
                else:
                    P.op("act", "copy", dict(out=yT[:, :, s * 128:(s + 1) * 128], in_=pt[:]), reads=[("m_pT", 1)], writes=["uy"])
            for nh in range(2):
                P.op("sync", "dma_start", dict(out=wa[:], in_=wa_v[:, :, nh * 512:(nh + 1) * 512]), reads=self.wkeys[wkey], writes=["m_wa"], dma="m_w")
                P.op("sync", "dma_start", dict(out=wbt[:], in_=wb_v[:, :, nh * 512:(nh + 1) * 512]), reads=self.wkeys[wkey], writes=["m_wb"], dma="m_w")
                for s in range(8):
                    for k in range(8):
                        P.op("pe", "matmul", ((pa[:],), dict(lhsT=yT[:, k, s * 128:(s + 1) * 128], rhs=wa[:, k, :],
                                                            start=(k == 0), stop=(k == 7))), reads=["uy", "m_wa"], writes=["m_pa"])
                    for k in range(8):
                        P.op("pe", "matmul", ((pbb[:],), dict(lhsT=yT[:, k, s * 128:(s + 1) * 128], rhs=wbt[:, k, :],
                                                             start=(k == 0), stop=(k == 7))), reads=["uy", "m_wb"], writes=["m_pbb"])
                    P.op("act", "activation", dict(out=sgt[:], in_=pbb[:], func=AF.Sigmoid), reads=["m_pbb"], writes=["m_sg"])
                    P.op("dve", "tensor_tensor", dict(out=tt[:], in0=sgt[:], in1=pa[:], op=ALU.mult), reads=["m_sg", "m_pa"], writes=["m_tt"])
                    xs_ = xq[:, s, nh * 512:(nh + 1) * 512]
                    P.op("dve", "tensor_tensor", dict(out=xs_, in0=xs_, in1=tt[:], op=ALU.add), reads=["m_tt", "xq"], writes=["xq"])
            P.op("sync", "dma_start", dict(out=xv(xdst, q), in_=xq[:]), reads=["xq"], writes=[("xres", q)], dma="m_st")

    def attn_phase(self, es, xsrc, xdst, L, wn, wb, ident_f, ident_b):
        nc, P = self.nc, self.P
        jj = L // 2
        lam_init = 0.8 - 0.6 * math.exp(-0.3 * L)
        A = lambda n, s, d=F32: es.enter_context(self.sb(n, s, d))
        QT, KT, HD = self.scr["QT"], self.scr["KT"], self.scr["HD"]
        win_v = wb["attn_w_in"][jj].rearrange("(k p) n -> p k n", p=128)
        wout_v = wb["attn_w_out"][jj].rearrange("(k p) n -> p k n", p=128)
        wkey = f"w_attn_{jj}"
        kn = lambda ap: ap.tensor.name.split("__u")[0]
        Vd = A("a_vd", [128, 32, 4, 129], BF16)
        Vf = A("a_vf", [128, 32, 8, 65], BF16)
        cposk = A("a_cposk", [128, 32, 8])
        P.op("dve", "memset", ((Vd[:, :, :, 128:129], 1.0), {}), writes=["Vd"])
        P.op("dve", "memset", ((Vf[:, :, :, 64:65], 1.0), {}), writes=["Vf"])

        LSP = A("a_lsp", [128, 32, 8]); R = A("a_R", [128, 33, 8])
        tri = A("a_tri", [128, 128]); onesf = A("a_onesf", [128, 128])
        with contextlib.ExitStack() as e1:
            B = lambda n, s, d=F32: e1.enter_context(self.sb(n, s, d))
            PS = lambda n, s, d=F32: e1.enter_context(self.pp(n, s, d))
            xt = [B(f"a_xt{i}", [128, 4, D]) for i in range(2)]
            hb = [B(f"a_hb{i}", [128, D], BF16) for i in range(2)]
            hT = B("a_hT", [128, 8, 512], BF16)
            win = B("a_win", [128, 8, IN_COLS], BF16)
            gn = B("a_gn", [128, D])
            sq = B("a_sq", [128, D], BF16); ss = B("a_ss", [128, 4]); rstd = B("a_rstd", [128, 4])
            qsq = [B(f"a_qsq{i}", [128, 512]) for i in range(2)]
            lnv = [B(f"a_lnv{i}", [128, 512]) for i in range(2)]
            qo = [B(f"a_qo{i}", [128, 512], BF16) for i in range(2)]
            G = B("a_G", [128, 4]); epsc = B("a_eps", [128, 1]); ones2 = B("a_ones2", [128, 128])
            fgb = B("a_fgb", [128, 8]); zt = B("a_zt", [128, 8])
            pT = PS("a_pT", [128, 8, 128], BF16)
            pq = [PS(f"a_pq{i}", [128, 512]) for i in range(2)]
            pms = [PS(f"a_pms{i}", [128, 512]) for i in range(2)]
            pv = [PS(f"a_pv{i}", [128, 512]) for i in range(2)]
            pfl = PS("a_pfl", [128, 512])

            def DMA(out, in_, wr, rd=()):
                P.op("sync", "dma_start", dict(out=out, in_=in_), reads=list(rd), writes=[wr], dma="a_ld")

            DMA(gn[:], wn["mix_norm"][L].partition_broadcast(128), "a_gn")
            for h in range(2):
                DMA(win[:, h * 4:(h + 1) * 4, :], win_v[:, h * 4:(h + 1) * 4, :], "a_win", self.wkeys[wkey])
            for c, nm in enumerate(("diff_q_norm", "diff_k_norm", "fox_q_norm", "fox_k_norm")):
                for h in range(2):
                    DMA(G[h * 64:(h + 1) * 64, c:c + 1], wn[nm][jj].rearrange("(d o) -> d o", o=1), "a_G")
            DMA(fgb[:], wn["fg_bias"][jj].partition_broadcast(128), "a_fgb")
            DMA(ones2[:], self.consts["ones2"], "a_ones2")
            DMA(tri[:], self.consts["tri"], "a_tri")
            P.op("dve", "memset", ((epsc[:], EPS), {}), writes=["a_eps"])
            P.op("dve", "memset", ((onesf[:], 1.0), {}), writes=["a_onesf"])
            P.op("dve", "memset", ((R[:, 0, :], 0.0), {}), writes=["a_R"])
            qk_tiles = []
            for h in range(4):
                qk_tiles.append((h * 128, 0, QT, 2 * h))
            for h in range(4):
                qk_tiles.append((512 + h * 128, 1, KT, 2 * h))
            for h in range(4):
                qk_tiles.append((1536 + h * 128, 2, QT, 8 + 2 * h))
            for h in range(4):
                qk_tiles.append((2048 + h * 128, 3, KT, 8 + 2 * h))

            def xv(ap, b):
                return ap[b * 512:(b + 1) * 512, :].rearrange("(t p) d -> p t d", p=128)

            def load_x(b):
                P.op("sync", "dma_start", dict(out=xt[b % 2][:], in_=xv(xsrc, b)), reads=[("xres", b)],
                     writes=[("a_xt", b % 2)], dma=f"a_x{b % 2}")

            load_x(0)
            ev = 0
            for b in range(8):
                X = xt[b % 2]
                xk = ("a_xt", b % 2)
                if b + 1 < 8:
                    load_x(b + 1)
                for t in range(4):
                    P.op("act", "activation", dict(out=sq[:], in_=X[:, t, :], func=AF.Square, accum_out=ss[:, t:t + 1]),
                         reads=[xk], writes=["a_sq", "a_ss"])
                P.op("dve", "tensor_scalar", dict(out=rstd[:], in0=ss[:], scalar1=1.0 / D, scalar2=EPS, op0=ALU.mult, op1=ALU.add),
                     reads=["a_ss"], writes=["a_rstd"])
                P.op("act", "activation", dict(out=rstd[:], in_=rstd[:], func=AF.Sqrt), reads=["a_rstd"], writes=["a_rstd"])
                P.op("dve", "reciprocal", dict(out=rstd[:], in_=rstd[:]), reads=["a_rstd"], writes=["a_rstd"])
                for t in range(4):
                    H = hb[t % 2]
                    P.op("dve", "scalar_tensor_tensor", dict(out=H[:], in0=X[:, t, :], scalar=rstd[:, t:t + 1], in1=gn[:],
                                                             op0=ALU.mult, op1=ALU.mult),
                         reads=[xk, "a_rstd", "a_gn"], writes=[("a_hb", t % 2)])
                    for k in range(8):
                        P.op("pe", "transpose", dict(out=pT[:, k, :], in_=H[:, k * 128:(k + 1) * 128], identity=ident_b[:]),
                             reads=[("a_hb", t % 2), "ident"], writes=["a_pT"])
                    P.op("dve", "tensor_copy", dict(out=hT[:, :, t * 128:(t + 1) * 128], in_=pT[:]), reads=["a_pT"], writes=["a_hT"])
                for (c0, gc, dstT, m0) in qk_tiles:
                    q = ev % 2
                    ev += 1
                    for k in range(8):
                        P.op("pe", "matmul", ((pq[q][:],), dict(lhsT=win[:, k, c0:c0 + 128], rhs=hT[:, k, :],
                                                                start=(k == 0), stop=(k == 7))),
                             reads=["a_win", "a_hT"], writes=[("a_pq", q)])
                    P.op("act", "activation", dict(out=qsq[q][:], in_=pq[q][:], func=AF.Square), reads=[("a_pq", q)], writes=[("a_qsq", q)])
                    P.op("pe", "matmul", ((pms[q][:],), dict(lhsT=ones2[:], rhs=qsq[q][:], start=True, stop=True)),
                         reads=["a_ones2", ("a_qsq", q)], writes=[("a_pms", q)])
                    P.op("act", "activation", dict(out=lnv[q][:], in_=pms[q][:], func=AF.Ln, bias=epsc[:, 0:1]),
                         reads=[("a_pms", q), "a_eps"], writes=[("a_lnv", q)])
                    P.op("act", "activation", dict(out=lnv[q][:], in_=lnv[q][:], func=AF.Exp, scale=-0.5),
                         reads=[("a_lnv", q)], writes=[("a_lnv", q)])
                    P.op("dve", "scalar_tensor_tensor", dict(out=qo[q][:], in0=pq[q][:], scalar=G[:, gc:gc + 1], in1=lnv[q][:],
                                                             op0=ALU.mult, op1=ALU.mult),
                         reads=[("a_pq", q), ("a_lnv", q), "a_G"], writes=[("a_qo", q)])
                    for hh in range(2):
                        P.op("sync", "dma_start", dict(out=dstT[m0 + hh, 0:64, b * 512:(b + 1) * 512],
                                                       in_=qo[q][hh * 64:(hh + 1) * 64, :]),
                             reads=[("a_qo", q)], writes=[(kn(dstT), m0 + hh)], dma="a_qst")
                for t in range(4):
                    blk = b * 4 + t
                    for vi, (c0, Vt, nh, vd) in enumerate(((1024, Vd, 4, 128), (2560, Vf, 8, 64))):
                        q = vi
                        for k in range(8):
                            P.op("pe", "matmul", ((pv[q][:],), dict(lhsT=hT[:, k, t * 128:(t + 1) * 128], rhs=win[:, k, c0:c0 + 512],
                                                                    start=(k == 0), stop=(k == 7))),
                                 reads=["a_win", "a_hT"], writes=[("a_pv", q)])
                        P.op("act" if vi == 0 else "dve", "copy" if vi == 0 else "tensor_copy",
                             dict(out=Vt[:, blk, :, 0:vd], in_=pv[q][:].rearrange("p (h v) -> p h v", h=nh)),
                             reads=[("a_pv", q)], writes=["Vd" if vi == 0 else "Vf"])
                    for k in range(8):
                        P.op("pe", "matmul", ((pfl[:, 0:8],), dict(lhsT=hT[:, k, t * 128:(t + 1) * 128], rhs=win[:, k, 3072:3080],
                                                                   start=(k == 0), stop=(k == 7))),
                             reads=["a_win", "a_hT"], writes=["a_pfl"])
                    P.op("dve", "tensor_tensor", dict(out=zt[:], in0=pfl[:, 0:8], in1=fgb[:], op=ALU.add),
                         reads=["a_pfl", "a_fgb"], writes=["a_zt"])
                    P.op("act", "activation", dict(out=zt[:], in_=zt[:], func=AF.Exp, scale=-1.0), reads=["a_zt"], writes=["a_zt"])
                    P.op("act", "activation", dict(out=LSP[:, blk, :], in_=zt[:], func=AF.Ln, bias=1.0), reads=["a_zt"], writes=["a_lsp"])
                    P.op("dve", "tensor_tensor", dict(out=R[:, blk + 1, :], in0=R[:, blk, :], in1=LSP[:, blk, :], op=ALU.add),
                         reads=["a_lsp", "a_R"], writes=["a_R"])
        P.barrier()
        self.uid += 1
        with contextlib.ExitStack() as e1:
            B = lambda n, s, d=F32: e1.enter_context(self.sb(n, s, d))
            PS = lambda n, s, d=F32: e1.enter_context(self.pp(n, s, d))
            cT = B("a_cT", [8, S]); rT = B("a_rT", [8, S])
            a123 = [B(f"a_a{i}", [8, S], BF16) for i in range(3)]
            onesb = B("a_onesb", [8, S], BF16)
            pv = [PS(f"a_pv{i}", [128, 512]) for i in range(2)]
            pq = [PS(f"a_pq{i}", [128, 512]) for i in range(2)]
            P.op("dve", "memset", ((onesb[:], 1.0), {}), writes=["a_onesb"])
            for blk in range(32):
                q = blk % 2
                P.op("pe", "matmul", ((pv[q][:, 0:8],), dict(lhsT=tri[:], rhs=LSP[:, blk, :], start=True, stop=False)),
                     reads=["a_tri", "a_lsp"], writes=[("a_pv", q)])
                P.op("pe", "matmul", ((pv[q][:, 0:8],), dict(lhsT=onesf[:], rhs=R[:, blk, :], start=False, stop=True)),
                     reads=["a_onesf", "a_R"], writes=[("a_pv", q)])
                P.op("dve", "tensor_copy", dict(out=cposk[:, blk, :], in_=pv[q][:, 0:8]), reads=[("a_pv", q)], writes=["cposk"])
                P.op("pe", "matmul", ((pq[q][0:8, 0:128],), dict(lhsT=LSP[:, blk, :], rhs=tri[:], start=True, stop=False)),
                     reads=["a_tri", "a_lsp"], writes=[("a_pq", q)])
                P.op("pe", "matmul", ((pq[q][0:8, 0:128],), dict(lhsT=R[:, blk, :], rhs=onesf[:], start=False, stop=True)),
                     reads=["a_onesf", "a_R"], writes=[("a_pq", q)])
                P.op("act", "activation", dict(out=cT[:, blk * 128:(blk + 1) * 128], in_=pq[q][0:8, 0:128], func=AF.Copy, scale=-8.0),
                     reads=[("a_pq", q)], writes=["a_cT"])
            P.op("dve", "tensor_copy", dict(out=a123[0][:], in_=cT[:]), reads=["a_cT"], writes=["a_a0"])
            P.op("dve", "tensor_tensor", dict(out=rT[:], in0=cT[:], in1=a123[0][:], op=ALU.subtract), reads=["a_cT", "a_a0"], writes=["a_rT"])
            P.op("dve", "tensor_copy", dict(out=a123[1][:], in_=rT[:]), reads=["a_rT"], writes=["a_a1"])
            P.op("dve", "tensor_tensor", dict(out=cT[:], in0=rT[:], in1=a123[1][:], op=ALU.subtract), reads=["a_rT", "a_a1"], writes=["a_cT"])
            P.op("dve", "tensor_copy", dict(out=a123[2][:], in_=cT[:]), reads=["a_cT"], writes=["a_a2"])
            for i in range(3):
                P.op("sync", "dma_start", dict(out=QT[8:16, 64 + i, :], in_=a123[i][:]), reads=[f"a_a{i}"],
                     writes=[("QTaug", i)], dma="a_qst")
                P.op("sync", "dma_start", dict(out=KT[8:16, 64 + i, :], in_=onesb[:]), reads=["a_onesb"],
                     writes=[("KTaug", i)], dma="a_qst")
        self.dump("lsp", LSP[:], [])
        self.dump("cposk", cposk[:], [])
        self.dump("vd", Vd[:, 0:2], [])
        self.dump("vf", Vf[:, 30:32], [])
        self.dump("qt", QT[:, :, 0:512], [])
        self.dump("kt", KT[:, :, 3584:4096], [])
        P.barrier()

        Ocat = A("b_ocat", [128, 32, D], BF16)
        with contextlib.ExitStack() as e2:
            B = lambda n, s, d=F32: e2.enter_context(self.sb(n, s, d))
            PS = lambda n, s, d=F32: e2.enter_context(self.pp(n, s, d))
            qT = [B(f"b_qT{i}", [67, S], BF16) for i in range(2)]
            kT = [B(f"b_kT{i}", [67, S], BF16) for i in range(2)]
            Pe = [B(f"b_pe{i}", [128, 512], BF16) for i in range(3)]
            tmpn = [B(f"b_tmp{i}", [128, 128]) for i in range(2)]
            BT = B("b_BT", [128, 5, 2, 128])
            b31 = B("b_b31", [128, 4])
            n0 = B("b_n0", [128, 32, 128])
            rb33 = B("b_rb33", [33, 5]); rbl = B("b_rbl", [33, 128]); OH = B("b_oh", [33, 384]); hrep = B("b_hrep", [128, 384])
            lq = [B(f"b_lq{i}", [128, 64]) for i in range(4)]
            lp = B("b_lp", [128, 64]); e12 = B("b_e12", [128, 2]); nlam = B("b_nlam", [128, 1])
            SW = B("b_sw", [128, 128])
            rinv = B("b_rinv", [128, 1]); odt = B("b_odt", [128, 128]); ssq = B("b_ssq", [128, 1]); junk = B("b_junk", [128, 128], BF16)
            ps = [PS(f"b_ps{i}", [128, 512]) for i in range(2)]
            po = [PS(f"b_po{i}", [128, 4, 256]) for i in range(2)]

            def DMA(out, in_, wr, rd=(), sem="b_ld"):
                P.op("sync", "dma_start", dict(out=out, in_=in_), reads=list(rd), writes=[wr], dma=sem)

            P.op("dve", "memset", ((rb33[:], 0.0), {}), writes=["b_rb33"])
            P.op("dve", "memset", ((rb33[32:33, :], NEG), {}), writes=["b_rb33"])
            DMA(rb33[0:32, 0:4], wn["rel_bias"], "b_rb33")
            DMA(OH[:], self.consts["relOH"], "b_oh")
            for i, nm in enumerate(("diff_lambda_q1", "diff_lambda_k1", "diff_lambda_q2", "diff_lambda_k2")):
                DMA(lq[i][:], wn[nm][jj].partition_broadcast(128), f"b_lq{i}")
            DMA(SW[:], wn["diff_subln"][jj].partition_broadcast(128), "b_sw")
            for h in range(5):
                P.op("dve", "tensor_copy", dict(out=rbl[:], in_=rb33[:, h:h + 1].broadcast_to([33, 128])), reads=["b_rb33"], writes=["b_rbl"])
                P.op("pe", "matmul", ((ps[0][:, 0:384],), dict(lhsT=rbl[:], rhs=OH[:], start=True, stop=True)),
                     reads=["b_rbl", "b_oh"], writes=[("b_ps", 0)])
                P.op("dve", "tensor_copy", dict(out=hrep[:], in_=ps[0][:, 0:384]), reads=[("b_ps", 0)], writes=["b_hrep"])
                if h < 4:
                    P.op("dve", "tensor_copy", dict(out=b31[:, h:h + 1], in_=hrep[:, 383:384]), reads=["b_hrep"], writes=["b_b31"])
                DMA(HD[h], hrep[:], ("HD", h), ["b_hrep"], sem="b_hd")
                hd = HD[h]
                src = bass.AP(tensor=hd.tensor, offset=hd.offset + 127, ap=[[383, 128], [128, 2], [1, 128]])
                DMA(BT[:, h, :, :], src, "b_BT", [("HD", h)], sem="b_hd2")
            for i in range(2):
                P.op("dve", "tensor_tensor", dict(out=lp[:], in0=lq[2 * i][:], in1=lq[2 * i + 1][:], op=ALU.mult),
                     reads=[f"b_lq{2 * i}", f"b_lq{2 * i + 1}"], writes=["b_lp"])
                P.op("dve", "tensor_reduce", dict(out=e12[:, i:i + 1], in_=lp[:], op=ALU.add, axis=AX.X), reads=["b_lp"], writes=["b_e12"])
            P.op("act", "activation", dict(out=e12[:], in_=e12[:], func=AF.Exp), reads=["b_e12"], writes=["b_e12"])
            P.op("dve", "scalar_tensor_tensor", dict(out=nlam[:], in0=e12[:, 1:2], scalar=-lam_init, in1=e12[:, 0:1],
                                                     op0=ALU.add, op1=ALU.subtract), reads=["b_e12"], writes=["b_nlam"])
            P.op("dve", "tensor_scalar", dict(out=SW[:], in0=SW[:], scalar1=1.0 - lam_init, scalar2=None, op0=ALU.mult),
                 reads=["b_sw"], writes=["b_sw"])

            maps = [(2 * h + m, "d", h, m) for h in range(4) for m in range(2)] + [(8 + f, "f", f, 0) for f in range(8)]
            pe_i = 0
            sp_i = 0
            tn_i = 0
            for mi, (mapi, kind, hh, mm) in enumerate(maps):
                sl = mi % 2
                K = 64 if kind == "d" else 67
                for c4 in range(2):
                    cs = slice(c4 * 2048, (c4 + 1) * 2048)
                    DMA(qT[sl][0:K, cs], QT[mapi, 0:K, cs], ("b_qT", sl), [("QT", mapi)] + [("QTaug", i) for i in range(3)], sem=f"b_q{sl}")
                    DMA(kT[sl][0:K, cs], KT[mapi, 0:K, cs], ("b_kT", sl), [("KT", mapi)] + [("KTaug", i) for i in range(3)], sem=f"b_q{sl}")
                Vt, vd = (Vd, 128) if kind == "d" else (Vf, 64)
                vkey = "Vd" if kind == "d" else "Vf"
                bth = hh if kind == "d" else 4
                for I in range(8):
                    pot = po[sp_i % 2]
                    pok = ("b_po", sp_i % 2)
                    sp_i += 1
                    started = [False, False]
                    for J in range(4 * I + 4):
                        qlo = max(4 * I, J)
                        c0 = (qlo - 4 * I) * 128
                        pst = ps[pe_i % 2]
                        psk = ("b_ps", pe_i % 2)
                        pet = Pe[pe_i % 3]
                        pek = ("b_pe", pe_i % 3)
                        pe_i += 1
                        P.op("pe", "matmul", ((pst[:, c0:512],), dict(lhsT=kT[sl][0:K, J * 128:(J + 1) * 128],
                                                                      rhs=qT[sl][0:K, I * 512 + c0:(I + 1) * 512],
                                                                      start=True, stop=True)),
                             reads=[("b_qT", sl), ("b_kT", sl)], writes=[psk])
                        if kind == "d":
                            fbias, frd = b31[:, hh:hh + 1], "b_b31"
                        else:
                            fbias, frd = cposk[:, J, hh:hh + 1], "cposk"
                        nnear = 2 if kind == "d" else 1
                        cfar = c0
                        for dist in range(nnear):
                            qt = J + dist
                            if qt < qlo or qt >= 4 * I + 4:
                                continue
                            cc = (qt - 4 * I) * 128
                            tn = tmpn[tn_i % 2]
                            tnk = ("b_tmp", tn_i % 2)
                            tn_i += 1
                            P.op("dve", "scalar_tensor_tensor", dict(out=tn[:], in0=pst[:, cc:cc + 128], scalar=0.125,
                                                                     in1=BT[:, bth, dist, :], op0=ALU.mult, op1=ALU.add),
                                 reads=[psk, "b_BT"], writes=[tnk])
                            if kind == "d":
                                P.op("act", "activation", dict(out=pet[:, cc:cc + 128], in_=tn[:], func=AF.Exp), reads=[tnk], writes=[pek])
                            else:
                                P.op("act", "activation", dict(out=pet[:, cc:cc + 128], in_=tn[:], func=AF.Exp, bias=fbias),
                                     reads=[tnk, frd], writes=[pek])
                            cfar = cc + 128
                        if cfar < 512:
                            P.op("act", "activation", dict(out=pet[:, cfar:512], in_=pst[:, cfar:512], func=AF.Exp, scale=0.125, bias=fbias),
                                 reads=[psk, frd], writes=[pek])
                        for qt in range(qlo, 4 * I + 4):
                            ql = qt - 4 * I
                            cc = ql * 128
                            bank = ql // 2
                            st = not started[bank]
                            started[bank] = True
                            P.op("pe", "matmul", ((pot[:, ql, 0:vd + 1],), dict(lhsT=pet[:, cc:cc + 128], rhs=Vt[:, J, hh, 0:vd + 1],
                                                                               start=st, stop=(J == qt), skip_group_check=True)),
                                 reads=[pek, vkey], writes=[pok])
                    for ql in range(4):
                        qt = 4 * I + ql
                        P.op("dve", "reciprocal", dict(out=rinv[:], in_=pot[:, ql, vd:vd + 1]), reads=[pok], writes=["b_rinv"])
                        if kind == "f":
                            P.op("dve", "tensor_scalar", dict(out=Ocat[:, qt, 512 + hh * 64:512 + (hh + 1) * 64], in0=pot[:, ql, 0:64],
                                                              scalar1=rinv[:, 0:1], scalar2=None, op0=ALU.mult),
                                 reads=[pok, "b_rinv"], writes=[("b_ocat", qt)])
                        elif mm == 0:
                            P.op("dve", "tensor_scalar", dict(out=n0[:, qt, :], in0=pot[:, ql, 0:128], scalar1=rinv[:, 0:1], scalar2=None,
                                                              op0=ALU.mult), reads=[pok, "b_rinv"], writes=["b_n0"])
                        else:
                            P.op("dve", "tensor_tensor", dict(out=rinv[:], in0=rinv[:], in1=nlam[:], op=ALU.mult),
                                 reads=["b_rinv", "b_nlam"], writes=["b_rinv"])
                            P.op("dve", "scalar_tensor_tensor", dict(out=odt[:], in0=pot[:, ql, 0:128], scalar=rinv[:, 0:1], in1=n0[:, qt, :],
                                                                     op0=ALU.mult, op1=ALU.add), reads=[pok, "b_rinv", "b_n0"], writes=["b_odt"])
                            P.op("act", "activation", dict(out=junk[:], in_=odt[:], func=AF.Square, accum_out=ssq[:, 0:1]),
                                 reads=["b_odt"], writes=["b_junk", "b_ssq"])
                            P.op("dve", "tensor_scalar", dict(out=ssq[:], in0=ssq[:], scalar1=1.0 / 128, scalar2=EPS, op0=ALU.mult, op1=ALU.add),
                                 reads=["b_ssq"], writes=["b_ssq"])
                            P.op("act", "activation", dict(out=ssq[:], in_=ssq[:], func=AF.Sqrt), reads=["b_ssq"], writes=["b_ssq"])
                            P.op("dve", "reciprocal", dict(out=ssq[:], in_=ssq[:]), reads=["b_ssq"], writes=["b_ssq"])
                            P.op("dve", "scalar_tensor_tensor", dict(out=Ocat[:, qt, hh * 128:(hh + 1) * 128], in0=odt[:], scalar=ssq[:, 0:1],
                                                                     in1=SW[:], op0=ALU.mult, op1=ALU.mult),
                                 reads=["b_odt", "b_ssq", "b_sw"], writes=[("b_ocat", qt)])
            self.dump("bt", BT[:], [])
            self.dump("n0", n0[:], [])
            self.dump("nlam", nlam[:], [])
            self.dump("ocat", Ocat[:, 0:2, :], [])
            self.dump("ocat2", Ocat[:, 30:32, :], [])
        P.barrier()
        self.uid += 1
        with contextlib.ExitStack() as e3:
            B = lambda n, s, d=F32: e3.enter_context(self.sb(n, s, d))
            PS = lambda n, s, d=F32: e3.enter_context(self.pp(n, s, d))
            wout = B("b_wout", [128, 8, D], BF16)
            oT = [B(f"b_oT{i}", [128, 8, 128], BF16) for i in range(2)]
            xo = [B(f"b_xo{i}", [128, D]) for i in range(2)]
            pT = PS("b_pT", [128, 8, 128], BF16)
            po2 = PS("b_po2", [128, 512])

            def DMA(out, in_, wr, rd=(), sem="b_ld"):
                P.op("sync", "dma_start", dict(out=out, in_=in_), reads=list(rd), writes=[wr], dma=sem)

            for h in range(2):
                DMA(wout[:, h * 4:(h + 1) * 4, :], wout_v[:, h * 4:(h + 1) * 4, :], "b_wout", self.wkeys[wkey])
            def xrow(ap, blk):
                return ap[blk * 128:(blk + 1) * 128, :]
            for blk in range(32):
                sl = blk % 2
                DMA(xo[sl][:], xrow(xsrc, blk), ("b_xo", sl), [("xres", blk // 4)], sem=f"b_x{sl}")
                for k in range(8):
                    P.op("pe", "transpose", dict(out=pT[:, k, :], in_=Ocat[:, blk, k * 128:(k + 1) * 128], identity=ident_b[:]),
                         reads=[("b_ocat", blk), "ident"], writes=["b_pT"])
                P.op("act", "copy", dict(out=oT[sl][:], in_=pT[:]), reads=["b_pT"], writes=[("b_oT", sl)])
                for nh in range(2):
                    for k in range(8):
                        P.op("pe", "matmul", ((po2[:],), dict(lhsT=oT[sl][:, k, :], rhs=wout[:, k, nh * 512:(nh + 1) * 512],
                                                             start=(k == 0), stop=(k == 7))),
                             reads=[("b_oT", sl), "b_wout"], writes=["b_po2"])
                    P.op("dve", "tensor_tensor", dict(out=xo[sl][:, nh * 512:(nh + 1) * 512], in0=xo[sl][:, nh * 512:(nh + 1) * 512],
                                                      in1=po2[:], op=ALU.add), reads=["b_po2", ("b_xo", sl)], writes=[("b_xo", sl)])
                P.op("sync", "dma_start", dict(out=xrow(xdst, blk), in_=xo[sl][:]), reads=[("b_xo", sl)], writes=[("xres_o", blk)], dma="b_st")

    def build(self):
        nc, P = self.nc, self.P
        x_in = self.din("x", [S, D])
        ident = self.din("ident", [128, 128])
        wn = {}
        for nm, shp in [("ffn1_norm", [DEPTH, D]), ("ffn1_gate", [DEPTH, D, DFF]), ("ffn1_up", [DEPTH, D, DFF]),
                        ("ffn1_down", [DEPTH, DFF, D]), ("mix_norm", [DEPTH, D]), ("ffn2_norm", [DEPTH, D]),
                        ("ffn2_gate", [DEPTH, D, DFF]), ("ffn2_up", [DEPTH, D, DFF]), ("ffn2_down", [DEPTH, DFF, D])]:
            wn[nm] = self.din(nm, shp)
        for nm, shp in [("s5_a_re", [2, 64, 64]), ("s5_a_im", [2, 64, 64]), ("s5_log_step", [2, 64]),
                        ("s5_b_re", [2, 64, 64, 16]), ("s5_b_im", [2, 64, 64, 16]), ("s5_c_re", [2, 64, 16, 64]),
                        ("s5_c_im", [2, 64, 16, 64]), ("s5_d", [2, D]), ("s5_glu_a", [2, D, D]), ("s5_glu_b", [2, D, D])]:
            wn[nm] = self.din(nm, shp)
        for nm, shp in [("attn_w_in", [2, D, IN_COLS]), ("attn_w_out", [2, D, D]), ("fg_bias", [2, 8]),
                        ("diff_q_norm", [2, 64]), ("diff_k_norm", [2, 64]), ("diff_lambda_q1", [2, 64]),
                        ("diff_lambda_k1", [2, 64]), ("diff_lambda_q2", [2, 64]), ("diff_lambda_k2", [2, 64]),
                        ("diff_subln", [2, 128]), ("fox_q_norm", [2, 64]), ("fox_k_norm", [2, 64]), ("rel_bias", [32, 4])]:
            wn[nm] = self.din(nm, shp)
        self.consts = {k: self.din(k, list(v.shape)) for k, v in host_consts().items() if k != "ident"}
        self.scr = {"QT": self.dscr("QT", [16, 67, S], BF16), "KT": self.dscr("KT", [16, 67, S], BF16),
                    "HD": self.dscr("HD", [5, 128, 384], F32)}
        out = nc.dram_tensor("out", [S, D], F32, kind="ExternalOutput").ap()
        xres = self.dscr("xres", [S, D], F32)
        wb = {}
        for f in ("ffn1", "ffn2"):
            wb[f + "_gate"] = self.dscr(f + "_gate_b", [DEPTH, D, DFF], BF16)
            wb[f + "_up"] = self.dscr(f + "_up_b", [DEPTH, D, DFF], BF16)
            wb[f + "_down"] = self.dscr(f + "_down_b", [DEPTH, DFF, D], BF16)
        wb["attn_w_in"] = self.dscr("attn_w_in_b", [2, D, IN_COLS], BF16)
        wb["attn_w_out"] = self.dscr("attn_w_out_b", [2, D, D], BF16)
        wb["s5_glu_a"] = self.dscr("s5_glu_a_b", [2, D, D], BF16)
        wb["s5_glu_b"] = self.dscr("s5_glu_b_b", [2, D, D], BF16)

        with contextlib.ExitStack() as es0:
            ident_f = es0.enter_context(self.sb("ident_f", [128, 128], F32))
            ident_b = es0.enter_context(self.sb("ident_b", [128, 128], BF16))
            P.op("sync", "dma_start", dict(out=ident_f[:], in_=ident), writes=["ident_f"], dma="c_misc")
            P.op("dve", "tensor_copy", dict(out=ident_b[:], in_=ident_f[:]), reads=["ident_f"], writes=["ident"])
            need = set(k for k, _ in (self.phases or [("ffn1", 0), ("ffn2", 0), ("s5", 1), ("attn", 0)]))
            for L in range(DEPTH):
                for f in ("ffn1", "ffn2"):
                    if f == "ffn2":
                        if L % 2 == 0 and "attn" in need:
                            for w in ("attn_w_in", "attn_w_out"):
                                self.cast_w(wn[w][L // 2], wb[w][L // 2], f"cast_attn_{L // 2}", f"w_attn_{L // 2}")
                        if L % 2 == 1 and "s5" in need:
                            for w in ("s5_glu_a", "s5_glu_b"):
                                self.cast_w(wn[w][L // 2], wb[w][L // 2], f"cast_s5_{L // 2}", f"w_s5_{L // 2}")
                    if f not in need:
                        continue
                    for w in ("gate", "up", "down"):
                        self.cast_w(wn[f"{f}_{w}"][L], wb[f"{f}_{w}"][L], f"cast_{f}_{L}", f"w_{f}_{L}")
            phases = self.phases
            if phases is None:
                phases = []
                for L in range(DEPTH):
                    phases += [("ffn1", L), ("attn" if L % 2 == 0 else "s5", L), ("ffn2", L)]
            src = x_in
            for i, (kind, L) in enumerate(phases):
                dst = out if i == len(phases) - 1 else xres
                P.barrier()
                self.uid += 1
                with contextlib.ExitStack() as es:
                    if kind in ("ffn1", "ffn2"):
                        f = kind
                        self.ffn_phase(es, src, dst, wn[f + "_norm"][L], wb[f + "_gate"][L], wb[f + "_up"][L],
                                       wb[f + "_down"][L], f"w_{f}_{L}", ident_b)
                    elif kind == "s5":
                        self.s5_phase(es, src, dst, L, wn, wb, ident_f, ident_b)
                    elif kind == "attn":
                        self.attn_phase(es, src, dst, L, wn, wb, ident_f, ident_b)
                src = xres
            P.barrier(final=True)
            P.emit()
        return nc


def host_consts():
    idx = np.arange(128) // 16
    s5mask = (idx[None, :] >= idx[:, None]).astype(np.float32)
    n = np.arange(384) - 127
    nn = np.maximum(n, 0)
    nf = np.maximum(nn, 1).astype(np.float32)
    large = 16 + (np.log(nf / np.float32(16)) / np.float32(math.log(128 / 16)) * np.float32(16)).astype(np.int32)
    large = np.minimum(large, 31)
    bucket = np.where(nn < 16, nn, large)
    oh = np.zeros((33, 384), np.float32)
    for i in range(384):
        if n[i] >= 0:
            oh[bucket[i], i] = 1.0
        else:
            oh[32, i] = 1.0
    ones2 = np.zeros((128, 128), np.float32)
    ones2[:64, :64] = 1.0 / 64
    ones2[64:, 64:] = 1.0 / 64
    tri = (np.arange(128)[:, None] <= np.arange(128)[None, :]).astype(np.float32)
    return {"ident": np.eye(128, dtype=np.float32), "s5mask": s5mask, "relOH": oh, "ones2": ones2, "tri": tri}


_CACHE = {}


def kernel(**inputs):
    if "b" not in _CACHE:
        b = Builder()
        b.build()
        _CACHE["b"] = b
    b = _CACHE["b"]
    consts = host_consts()
    in_maps = []
    for c in range(NCORES):
        m = {}
        for k in b.inputs:
            if k == "x":
                m[k] = np.ascontiguousarray(inputs["x"][c])
            elif k in consts:
                m[k] = consts[k]
            else:
                m[k] = np.ascontiguousarray(inputs[k])
        in_maps.append(m)
    res = run_bass_kernel_spmd(b.nc, in_maps, core_ids=list(range(NCORES)))
    return np.stack([np.asarray(r["out"]) for r in res.results], axis=0).astype(np.float32)
```

```python
import bisect
import contextlib
import math

import numpy as np
import concourse.bass as bass
import concourse.mybir as mybir
from concourse.bass_utils import run_bass_kernel_spmd

F32 = mybir.dt.float32
BF16 = mybir.dt.bfloat16
AF = mybir.ActivationFunctionType
ALU = mybir.AluOpType
AX = mybir.AxisListType

D = 1024
S = 4096
DFF = 2816
DEPTH = 4
NCORES = 8
EPS = 1e-6
IN_COLS = 3080
NEG = -80.0


class Op:
    __slots__ = ("eng", "fn", "dma", "deps", "marked", "cum", "seq")

    def __init__(self, eng, fn, dma):
        self.eng = eng
        self.fn = fn
        self.dma = dma
        self.deps = []
        self.marked = False
        self.cum = 0
        self.seq = 0


class Prog:
    ENGS = ["sync", "act", "dve", "pe", "pool"]
    BLK = {"sync": "sync", "act": "scalar", "dve": "vector", "pe": "tensor", "pool": "gpsimd"}
    CH = 30000

    def __init__(self, nc):
        self.nc = nc
        self.ops = []
        self.lastw = {}
        self.readers = {}
        self.last_on = {}

    @staticmethod
    def stream(o):
        return o.dma if o.dma else o.eng

    def op(self, eng, name, kw, reads=(), writes=(), dma=None):
        args = ()
        if isinstance(kw, tuple):
            args, kw = kw
        fn = (lambda e, name=name, args=args, kw=kw: getattr(e, name)(*args, **kw))
        o = Op(eng, fn, dma)
        o.seq = len(self.ops)
        deps = {}

        def add(d, raw=False):
            if d is None:
                return
            if (not d.dma) and (not o.dma) and d.eng == o.eng:
                if o.eng == "pe":
                    return
            st = self.stream(d)
            if st not in deps or deps[st].seq < d.seq:
                deps[st] = d

        for k in reads:
            add(self.lastw.get(k), raw=True)
        for k in writes:
            add(self.lastw.get(k))
            for d in self.readers.get(k, {}).values():
                add(d)
        o.deps = list(deps.values())
        for k in writes:
            self.lastw[k] = o
            self.readers[k] = {}
        for k in reads:
            self.readers.setdefault(k, {})[self.stream(o)] = o
        self.ops.append(o)
        if fn is not None:
            self.last_on[self.stream(o)] = o
        return o

    def barrier(self, final=False):
        lasts = {st: d for st, d in self.last_on.items() if final or not st.startswith("cast_")}
        keep = {k: v for k, v in self.lastw.items() if isinstance(k, tuple) and isinstance(k[0], str) and k[0].startswith("w_")}
        for e in self.ENGS:
            o = Op(e, None, None)
            o.seq = len(self.ops)
            o.deps = [d for st, d in lasts.items() if not ((not d.dma) and d.eng == e)]
            self.ops.append(o)
        self.lastw = {} if final else keep
        self.readers = {}

    def emit(self):
        nc = self.nc
        for o in self.ops:
            for d in o.deps:
                d.marked = True
        cnt = {}
        dma_seqs = {}
        for o in self.ops:
            if o.fn is None:
                continue
            if o.dma:
                cnt[o.dma] = cnt.get(o.dma, 0) + 1
                o.cum = cnt[o.dma]
                dma_seqs.setdefault(o.dma, []).append(o.seq)
            elif o.marked:
                cnt[o.eng] = cnt.get(o.eng, 0) + 1
                o.cum = cnt[o.eng]
        for k, v in cnt.items():
            if k not in self.ENGS:
                assert v * 16 < 60000, (k, v)
        with contextlib.ExitStack() as es:
            sems = {}

            def sem(name):
                if name not in sems:
                    sems[name] = es.enter_context(nc.semaphore("s_" + name))
                return sems[name]

            for k, v in cnt.items():
                if k in self.ENGS:
                    for c in range((v - 1) // self.CH + 1):
                        sem(f"{k}{c}")
                else:
                    sem(k)

            bar_seqs = [o.seq for o in self.ops if o.fn is None]
            self.partial = {}

            def resolve(d, o):
                if d.dma:
                    n = bisect.bisect_left(dma_seqs[d.dma], o.seq)
                    nb = bar_seqs[bisect.bisect_left(bar_seqs, o.seq)] if bisect.bisect_left(bar_seqs, o.seq) < len(bar_seqs) else 1 << 60
                    n_epoch = bisect.bisect_left(dma_seqs[d.dma], nb)
                    if n_epoch > n:
                        self.partial[d.dma] = self.partial.get(d.dma, 0) + 1
                    assert n >= d.cum
                    return d.dma, n * 16
                c = (d.cum - 1) // self.CH
                return f"{d.eng}{c}", (d.cum - 1) % self.CH + 1

            block = es.enter_context(nc.Block())
            for eng in self.ENGS:
                ops_e = [o for o in self.ops if o.eng == eng]

                def body(e, ops_e=ops_e):
                    waited = {}
                    for o in ops_e:
                        for d in o.deps:
                            sn, val = resolve(d, o)
                            if waited.get(sn, 0) < val:
                                e.wait_ge(sem(sn), val)
                                waited[sn] = val
                        if o.fn is None:
                            continue
                        inst = o.fn(e)
                        if o.dma:
                            inst.then_inc(sem(o.dma), 16)
                        elif o.marked:
                            c = (o.cum - 1) // self.CH
                            inst.then_inc(sem(f"{o.eng}{c}"), 1)

                getattr(block, self.BLK[eng])(body)


class Builder:
    def __init__(self, phases=None):
        self.nc = bass.Bass("TRN2", target_bir_lowering=False)
        self.P = Prog(self.nc)
        self.phases = phases
        self.inputs = {}
        self.wkeys = {}

    uid = 0
    debug = False

    def dump(self, name, ap, reads):
        if not self.debug:
            return
        o = self.nc.dram_tensor("dbg_" + name, list(ap.shape), ap.dtype, kind="ExternalOutput").ap()
        self.P.op("sync", "dma_start", dict(out=o, in_=ap), reads=list(reads), writes=[("dbg", name)], dma="dbg")

    def sb(self, name, shape, dt=F32):
        return self.nc.sbuf_tensor(f"{name}__u{self.uid}", shape, dt)

    def pp(self, name, shape, dt=F32):
        return self.nc.psum_tensor(f"{name}__u{self.uid}", shape, dt)

    def din(self, name, shape, dt=F32):
        ap = self.nc.dram_tensor(name, list(shape), dt, kind="ExternalInput").ap()
        self.inputs[name] = ap
        return ap

    def dscr(self, name, shape, dt):
        return self.nc.dram_tensor(name, list(shape), dt, kind="Internal").ap()

    def cast_w(self, src, dst, semname, key, nsplit=4):
        P = self.P
        rows, cols = src.shape
        b = cols
        for cand in range(1, 9):
            if cols % cand == 0 and cols // cand <= 1024:
                b = cols // cand
                break
        rs = rows // nsplit
        for i in range(nsplit):
            s_ap = src[i * rs:(i + 1) * rs, :].rearrange("k (a b) -> k a b", b=b)
            d_ap = dst[i * rs:(i + 1) * rs, :].rearrange("k (a b) -> k a b", b=b)
            self.wkeys.setdefault(key, [])
            kk = (key, len(self.wkeys[key]))
            self.wkeys[key].append(kk)
            P.op("pool", "dma_start", dict(out=d_ap, in_=s_ap), writes=[kk], dma=semname)

    def ffn_phase(self, es, xsrc, xdst, gnorm_ap, wg, wu, wd, wkey, ident_b):
        nc, P = self.nc, self.P
        TB = 1024
        NB = S // TB
        KT = D // 128
        MT = DFF // 128
        CG = 256
        NCG = DFF // CG
        A = lambda *a: es.enter_context(self.sb(*a))
        PS = lambda *a: es.enter_context(self.pp(*a))
        xt = [A(f"f_xt{i}", [128, 8, D], F32) for i in range(2)]
        hb = [A(f"f_hb{i}", [128, D], BF16) for i in range(2)]
        hT = A("f_hT", [128, KT, TB], BF16)
        aT = A("f_aT", [128, MT, TB], BF16)
        wdt = A("f_wd", [128, MT, D], BF16)
        wgt = [A(f"f_wg{i}", [128, KT, CG], BF16) for i in range(2)]
        wut = [A(f"f_wu{i}", [128, KT, CG], BF16) for i in range(2)]
        gn = A("f_gn", [128, D], F32)
        sq = A("f_sq", [128, D], BF16)
        ss = A("f_ss", [128, 8], F32)
        rstd = A("f_rstd", [128, 8], F32)
        sg = [A(f"f_sg{i}", [128, 512], F32) for i in range(2)]
        pT = [PS(f"f_pT{i}", [128, 8, 128], BF16) for i in range(2)]
        pg = [PS(f"f_pg{i}", [128, 512], F32) for i in range(2)]
        pu = [PS(f"f_pu{i}", [128, 512], F32) for i in range(2)]
        po = [PS(f"f_po{i}", [128, 512], F32) for i in range(2)]

        P.op("sync", "dma_start", dict(out=gn[:], in_=gnorm_ap.partition_broadcast(128)),
             writes=["f_gn"], dma="f_misc")
        wd_v = wd.rearrange("(k p) n -> p k n", p=128)
        for h in range(2):
            P.op("sync", "dma_start", dict(out=wdt[:, h * 11:(h + 1) * 11, :], in_=wd_v[:, h * 11:(h + 1) * 11, :]),
                 reads=self.wkeys[wkey], writes=["f_wd"], dma="f_misc")
        wg_v = wg.rearrange("(k p) n -> p k n", p=128)
        wu_v = wu.rearrange("(k p) n -> p k n", p=128)

        def xv(ap, b):
            return ap[b * TB:(b + 1) * TB, :].rearrange("(p t) d -> p t d", t=8)

        def load_x(b):
            P.op("sync", "dma_start", dict(out=xt[b % 2][:], in_=xv(xsrc, b)),
                 reads=[("xres", b)], writes=[("f_xt", b % 2)], dma=f"f_x{b % 2}")

        load_x(0)
        wl = 0
        for b in range(NB):
            X = xt[b % 2]
            xk = ("f_xt", b % 2)
            if b + 1 < NB:
                load_x(b + 1)
            for t in range(8):
                P.op("act", "activation", dict(out=sq[:], in_=X[:, t, :], func=AF.Square, accum_out=ss[:, t:t + 1]),
                     reads=[xk], writes=["f_sq", ("f_ss", t)])
            P.op("dve", "tensor_scalar", dict(out=rstd[:], in0=ss[:], scalar1=1.0 / D, scalar2=EPS,
                                              op0=ALU.mult, op1=ALU.add),
                 reads=[("f_ss", t) for t in range(8)], writes=["f_rstd"])
            P.op("act", "activation", dict(out=rstd[:], in_=rstd[:], func=AF.Sqrt), reads=["f_rstd"], writes=["f_rstd"])
            P.op("dve", "reciprocal", dict(out=rstd[:], in_=rstd[:]), reads=["f_rstd"], writes=["f_rstd"])
            for t in range(8):
                H = hb[t % 2]
                P.op("dve", "scalar_tensor_tensor", dict(out=H[:], in0=X[:, t, :], scalar=rstd[:, t:t + 1], in1=gn[:],
                                                         op0=ALU.mult, op1=ALU.mult),
                     reads=[xk, "f_rstd", "f_gn"], writes=[("f_hb", t % 2)])
                pt = pT[t % 2]
                for k in range(KT):
                    P.op("pe", "transpose", dict(out=pt[:, k, :], in_=H[:, k * 128:(k + 1) * 128], identity=ident_b[:]),
                         reads=[("f_hb", t % 2), "ident"], writes=[("f_pT", t % 2)])
                if t % 2 == 0:
                    P.op("dve", "tensor_copy", dict(out=hT[:, :, t * 128:(t + 1) * 128], in_=pt[:]),
                         reads=[("f_pT", t % 2)], writes=[("f_hT", t)])
                else:
                    P.op("act", "copy", dict(out=hT[:, :, t * 128:(t + 1) * 128], in_=pt[:]),
                         reads=[("f_pT", t % 2)], writes=[("f_hT", t)])
            hT_keys = [("f_hT", t) for t in range(8)]
            ev = 0
            for cg in range(NCG):
                sl = wl % 2
                wl += 1
                P.op("sync", "dma_start", dict(out=wgt[sl][:], in_=wg_v[:, :, cg * CG:(cg + 1) * CG]),
                     reads=self.wkeys[wkey], writes=[("f_wg", sl)], dma=f"f_w{sl}")
                P.op("sync", "dma_start", dict(out=wut[sl][:], in_=wu_v[:, :, cg * CG:(cg + 1) * CG]),
                     reads=self.wkeys[wkey], writes=[("f_wu", sl)], dma=f"f_w{sl}")
                for mi in range(CG // 128):
                    m = cg * (CG // 128) + mi
                    for hf in range(TB // 512):
                        q = ev % 2
                        ev += 1
                        for k in range(KT):
                            P.op("pe", "matmul", ((pg[q][:],), dict(lhsT=wgt[sl][:, k, mi * 128:(mi + 1) * 128],
                                                                   rhs=hT[:, k, hf * 512:(hf + 1) * 512],
                                                                   start=(k == 0), stop=(k == KT - 1))),
                                 reads=[("f_wg", sl)] + hT_keys, writes=[("f_pg", q)])
                        for k in range(KT):
                            P.op("pe", "matmul", ((pu[q][:],), dict(lhsT=wut[sl][:, k, mi * 128:(mi + 1) * 128],
                                                                   rhs=hT[:, k, hf * 512:(hf + 1) * 512],
                                                                   start=(k == 0), stop=(k == KT - 1))),
                                 reads=[("f_wu", sl)] + hT_keys, writes=[("f_pu", q)])
                        P.op("act", "activation", dict(out=sg[q][:], in_=pg[q][:], func=AF.Silu),
                             reads=[("f_pg", q)], writes=[("f_sg", q)])
                        P.op("dve", "tensor_tensor", dict(out=aT[:, m, hf * 512:(hf + 1) * 512], in0=sg[q][:],
                                                          in1=pu[q][:], op=ALU.mult),
                             reads=[("f_sg", q), ("f_pu", q)], writes=[("f_aT", m)])
            aT_keys = [("f_aT", m) for m in range(MT)]
            for t in range(8):
                for nh in range(2):
                    q = (t * 2 + nh) % 2
                    for k in range(MT):
                        P.op("pe", "matmul", ((po[q][:],), dict(lhsT=aT[:, k, t * 128:(t + 1) * 128],
                                                               rhs=wdt[:, k, nh * 512:(nh + 1) * 512],
                                                               start=(k == 0), stop=(k == MT - 1))),
                             reads=aT_keys + ["f_wd"], writes=[("f_po", q)])
                    P.op("dve", "scalar_tensor_tensor", dict(out=X[:, t, nh * 512:(nh + 1) * 512], in0=po[q][:],
                                                             scalar=0.5, in1=X[:, t, nh * 512:(nh + 1) * 512],
                                                             op0=ALU.mult, op1=ALU.add),
                         reads=[("f_po", q), xk], writes=[xk])
            P.op("sync", "dma_start", dict(out=xv(xdst, b), in_=X[:]), reads=[xk], writes=[("xres", b)], dma="f_st")

    def s5_phase(self, es, xsrc, xdst, L, wn, wb, ident_f, ident_b):
        nc, P = self.nc, self.P
        jj = L // 2
        A = lambda *a: es.enter_context(self.sb(*a))
        toep = A("s_toep", [128, 64, 128], BF16)
        wtr = A("s_wtr", [128, 64, 64], BF16)
        wti = A("s_wti", [128, 64, 64], BF16)
        vre = A("s_vre", [128, 32, 128], BF16)
        vim = A("s_vim", [128, 32, 128], BF16)
        dvec = A("s_dvec", [128, 64], F32)
        mcat = A("s_mcat", [128, 2, 64], F32)
        gn = A("s_gn", [128, D], F32)
        P.op("sync", "dma_start", dict(out=gn[:], in_=wn["mix_norm"][L].partition_broadcast(128)),
             writes=["s_gn"], dma="s_misc")
        with contextlib.ExitStack() as es1:
            self.s5_setup(es1, jj, wn, ident_f, toep, wtr, wti, vre, vim, dvec, mcat)
        P.barrier()
        self.s5_main(es, xsrc, xdst, jj, wb, ident_b, toep, wtr, wti, vre, vim, dvec, mcat, gn)

    def s5_setup(self, es, jj, wn, ident_f, toep, wtr, wti, vre, vim, dvec, mcat):
        nc, P = self.nc, self.P
        A = lambda n, s, d=F32: es.enter_context(self.sb(n, s, d))
        PS = lambda n, s, d=F32: es.enter_context(self.pp(n, s, d))
        kn = lambda ap: ap.tensor.name.split("__u")[0]

        def TT(out, in0, in1, op, eng="dve"):
            P.op(eng, "tensor_tensor", dict(out=out, in0=in0, in1=in1, op=op), reads=[kn(in0), kn(in1)], writes=[kn(out)])

        def TS(out, in0, s1, op0, s2=None, op1=None, eng="dve"):
            kw = dict(out=out, in0=in0, scalar1=s1, scalar2=s2, op0=op0)
            if op1 is not None:
                kw["op1"] = op1
            rd = [kn(in0)] + ([kn(s1)] if not isinstance(s1, (int, float)) else [])
            P.op(eng, "tensor_scalar", kw, reads=rd, writes=[kn(out)])

        def ACT(out, in_, func, **kw):
            P.op("act", "activation", dict(out=out, in_=in_, func=func, **kw), reads=[kn(in_)], writes=[kn(out)])

        def CP(out, in_, eng="dve"):
            P.op(eng, "tensor_copy", dict(out=out, in_=in_), reads=[kn(in_)], writes=[kn(out)])

        def DMA(out, in_, wr):
            P.op("sync", "dma_start", dict(out=out, in_=in_), writes=[wr], dma="z_ld")

        are, aim, lst = wn["s5_a_re"][jj], wn["s5_a_im"][jj], wn["s5_log_step"][jj]
        bre, bim, cre, cim, dsk = wn["s5_b_re"][jj], wn["s5_b_im"][jj], wn["s5_c_re"][jj], wn["s5_c_im"][jj], wn["s5_d"][jj]
        smask = self.consts["s5mask"]
        AR = A("z_ar", [64, 128]); AI = A("z_ai", [64, 128]); LS = A("z_ls", [64, 1])
        for h in range(2):
            DMA(AR[:, h * 64:(h + 1) * 64], are, "z_ar")
            DMA(AI[:, h * 64:(h + 1) * 64], aim, "z_ai")
        DMA(LS[:], lst.rearrange("(g o) -> g o", o=1), "z_ls")
        mask = A("z_mask", [128, 128])
        DMA(mask[:], smask, "z_mask")
        BR = A("z_br", [128, 64, 16]); BI = A("z_bi", [128, 64, 16])
        for h in range(2):
            DMA(BR[h * 64:(h + 1) * 64], bre.rearrange("g p c -> p g c"), "z_br")
            DMA(BI[h * 64:(h + 1) * 64], bim.rearrange("g p c -> p g c"), "z_bi")
        tC = [A("z_tcr", [128, 8, 128]), A("z_tci", [128, 8, 128])]
        for t, src in zip(tC, (cre, cim)):
            for h in range(2):
                DMA(t[:, :, h * 64:(h + 1) * 64], src.rearrange("(o g) c p -> (g c) o p", o=8), kn(t[:]))
        Dg = A("z_dg", [64, 16]); Dg8 = A("z_dg8", [64, 8, 16])
        DMA(Dg[:], dsk.rearrange("(g c) -> g c", c=16), "z_dg")
        CP(Dg8[:], Dg[:].unsqueeze(1).broadcast_to([64, 8, 16]))

        step = A("z_step", [64, 1])
        ACT(step[:], LS[:], AF.Exp)
        ARS = A("z_ars", [64, 128]); TH = A("z_th", [64, 128]); MAG = A("z_mag", [64, 128])
        Cc = A("z_c", [64, 128]); Sn = A("z_s", [64, 128])
        t1 = A("z_t1", [64, 128]); t2 = A("z_t2", [64, 128]); t3 = A("z_t3", [64, 128])
        TS(ARS[:], AR[:], step[:, 0:1], ALU.mult)
        TS(TH[:], AI[:], step[:, 0:1], ALU.mult)
        ACT(MAG[:], ARS[:], AF.Exp)
        ACT(Sn[:], TH[:], AF.Sin, scale=1.0 / 32)
        hpi = A("z_hpi", [64, 1])
        P.op("dve", "memset", ((hpi[:], math.pi / 2), {}), writes=["z_hpi"])
        P.op("act", "activation", dict(out=Cc[:], in_=TH[:], func=AF.Sin, scale=1.0 / 32, bias=hpi[:, 0:1]),
             reads=["z_th", "z_hpi"], writes=["z_c"])
        for it in range(5):
            TT(t1[:], Cc[:], Cc[:], ALU.mult)
            TT(t2[:], Sn[:], Sn[:], ALU.mult)
            TT(t3[:], Cc[:], Sn[:], ALU.mult)
            TT(Cc[:], t1[:], t2[:], ALU.subtract)
            TS(Sn[:], t3[:], 2.0, ALU.mult)
        PWr = A("z_pwr", [64, 16, 128]); PWi = A("z_pwi", [64, 16, 128])

        def cmul(orr, oi, ar, ai, br, bi):
            TT(t1[:], ar, br, ALU.mult)
            TT(t2[:], ai, bi, ALU.mult)
            TT(t3[:], ar, bi, ALU.mult)
            TT(orr, t1[:], t2[:], ALU.subtract)
            TT(t1[:], ai, br, ALU.mult)
            TT(oi, t3[:], t1[:], ALU.add)

        P.op("dve", "memset", ((PWr[:, 7, :], 1.0), {}), writes=["z_pwr"])
        P.op("dve", "memset", ((PWi[:, 7, :], 0.0), {}), writes=["z_pwi"])
        TT(PWr[:, 8, :], MAG[:], Cc[:], ALU.mult)
        TT(PWi[:, 8, :], MAG[:], Sn[:], ALU.mult)
        for k in range(9, 16):
            cmul(PWr[:, k, :], PWi[:, k, :], PWr[:, k - 1, :], PWi[:, k - 1, :], PWr[:, 8, :], PWi[:, 8, :])
        m2 = A("z_m2", [64, 128])
        TT(t1[:], PWr[:, 8, :], PWr[:, 8, :], ALU.mult)
        TT(t2[:], PWi[:, 8, :], PWi[:, 8, :], ALU.mult)
        TT(m2[:], t1[:], t2[:], ALU.add)
        P.op("dve", "reciprocal", dict(out=m2[:], in_=m2[:]), reads=["z_m2"], writes=["z_m2"])
        TT(PWr[:, 6, :], PWr[:, 8, :], m2[:], ALU.mult)
        TT(t1[:], PWi[:, 8, :], m2[:], ALU.mult)
        TS(PWi[:, 6, :], t1[:], -1.0, ALU.mult)
        for k in range(5, -1, -1):
            cmul(PWr[:, k, :], PWi[:, k, :], PWr[:, k + 1, :], PWi[:, k + 1, :], PWr[:, 6, :], PWi[:, 6, :])
        CRg_ = A("z_crg", [64, 128]); CIg_ = A("z_cig", [64, 128]); den = A("z_den", [64, 128]); lm1 = A("z_lm1", [64, 128])
        TT(t1[:], AR[:], AR[:], ALU.mult)
        TT(t2[:], AI[:], AI[:], ALU.mult)
        TT(den[:], t1[:], t2[:], ALU.add)
        P.op("dve", "reciprocal", dict(out=den[:], in_=den[:]), reads=["z_den"], writes=["z_den"])
        TS(lm1[:], PWr[:, 8, :], -1.0, ALU.add)
        TT(t1[:], lm1[:], AR[:], ALU.mult)
        TT(t2[:], PWi[:, 8, :], AI[:], ALU.mult)
        TT(t1[:], t1[:], t2[:], ALU.add)
        TT(CRg_[:], t1[:], den[:], ALU.mult)
        TT(t1[:], PWi[:, 8, :], AR[:], ALU.mult)
        TT(t2[:], lm1[:], AI[:], ALU.mult)
        TT(t1[:], t1[:], t2[:], ALU.subtract)
        TT(CIg_[:], t1[:], den[:], ALU.mult)

        TABr = A("z_tabr", [128, 64, 25]); TABi = A("z_tabi", [128, 64, 25])
        CRp = A("z_crp", [128, 64]); CIp = A("z_cip", [128, 64])
        pz = [PS("z_pz0", [128, 8, 64]), PS("z_pz1", [128, 8, 64])]
        slots = [7 - s for s in range(8)] + [7 + k for k in range(9)] + [14 - s for s in range(8)]
        idf64 = ident_f[0:64, 0:64]
        nb = 0
        for TAB, PW in ((TABr, PWr), (TABi, PWi)):
            for b0 in range(0, 25, 8):
                n = min(8, 25 - b0)
                pt = pz[nb % 2]
                nb += 1
                for q in range(n):
                    P.op("pe", "transpose", dict(out=pt[:, q, :], in_=PW[:, slots[b0 + q], :], identity=idf64),
                         reads=[kn(PW[:]), "ident_f"], writes=[kn(pt[:])])
                CP(TAB[:, :, b0:b0 + n].rearrange("p g s -> p s g"), pt[:, 0:n, :])
        pt = pz[nb % 2]
        P.op("pe", "transpose", dict(out=pt[:, 0, :], in_=CRg_[:], identity=idf64), reads=["z_crg", "ident_f"], writes=[kn(pt[:])])
        P.op("pe", "transpose", dict(out=pt[:, 1, :], in_=CIg_[:], identity=idf64), reads=["z_cig", "ident_f"], writes=[kn(pt[:])])
        P.op("pe", "transpose", dict(out=pt[:, 2, :], in_=Dg8[:].rearrange("g a c -> g (a c)"), identity=idf64),
             reads=["z_dg8", "ident_f"], writes=[kn(pt[:])])
        CP(CRp[:], pt[:, 0, :])
        CP(CIp[:], pt[:, 1, :])
        CP(dvec[:], pt[:, 2, :])
        for h in range(2):
            rs = slice(h * 64, (h + 1) * 64)
            mr = TABr[rs, h::2, 16]
            mi = TABi[rs, h::2, 16]
            CP(mcat[rs, 0, 0:32], mr)
            CP(mcat[rs, 0, 32:64], mr)
            TS(mcat[rs, 1, 0:32], mi, -1.0, ALU.mult)
            CP(mcat[rs, 1, 32:64], mi)
        CRg = A("z_crpg", [128, 64, 16]); CIn = A("z_cinpg", [128, 64, 16])
        pc = [PS("z_pc0", [128, 4, 128]), PS("z_pc1", [128, 4, 128])]
        nb = 0
        for t, dstC in zip(tC, (CRg, CIn)):
            for q4 in range(2):
                pt = pc[nb % 2]
                nb += 1
                for q in range(4):
                    P.op("pe", "transpose", dict(out=pt[:, q, :], in_=t[:, q4 * 4 + q, :], identity=ident_f[:]),
                         reads=[kn(t[:]), "ident_f"], writes=[kn(pt[:])])
                CP(dstC[:, q4 * 32:(q4 + 1) * 32, :].rearrange("p (a g) c -> p a (g c)", a=4), pt[:])
        TS(CIn[:], CIn[:], -1.0, ALU.mult)
        BBr = A("z_bbr", [128, 64, 16]); BBi = A("z_bbi", [128, 64, 16])
        u1 = A("z_u1", [128, 64, 16]); u2 = A("z_u2", [128, 64, 16])
        crb = CRp[:].unsqueeze(2).broadcast_to([128, 64, 16])
        cib = CIp[:].unsqueeze(2).broadcast_to([128, 64, 16])
        TT(u1[:], crb, BR[:], ALU.mult)
        TT(u2[:], cib, BI[:], ALU.mult)
        TT(BBr[:], u1[:], u2[:], ALU.subtract)
        TT(u1[:], crb, BI[:], ALU.mult)
        TT(u2[:], cib, BR[:], ALU.mult)
        TT(BBi[:], u1[:], u2[:], ALU.add)

        GH = 16
        X1 = A("z_x1", [128, GH, 8, 16]); X2 = A("z_x2", [128, GH, 8, 16])
        X3 = A("z_x3", [128, GH, 8, 16]); X4 = A("z_x4", [128, GH, 8, 16])
        T1 = A("z_T1", [128, GH, 8, 16]); T2 = A("z_T2", [128, GH, 8, 16])
        ptp = [PS("z_ptp0", [128, 4, 128]), PS("z_ptp1", [128, 4, 128])]
        pw8 = [PS("z_pw0", [128, 8, 64]), PS("z_pw1", [128, 8, 64])]

        def cprod(oa, ob, s0, Xr, Xi, g0, opa, opb, e1="dve", e2="dve"):
            pr = TABr[:, g0:g0 + GH, s0:s0 + 8].unsqueeze(3).broadcast_to([128, GH, 8, 16])
            pi = TABi[:, g0:g0 + GH, s0:s0 + 8].unsqueeze(3).broadcast_to([128, GH, 8, 16])
            xr = Xr[:, g0:g0 + GH, :].unsqueeze(2).broadcast_to([128, GH, 8, 16])
            xi = Xi[:, g0:g0 + GH, :].unsqueeze(2).broadcast_to([128, GH, 8, 16])
            TT(T1[:], pr, xr, ALU.mult, e1)
            TT(T2[:], pi, xi, ALU.mult, e1)
            TT(oa[:], T1[:], T2[:], opa, e1)
            TT(T1[:], pr, xi, ALU.mult, e2)
            TT(T2[:], pi, xr, ALU.mult, e2)
            TT(ob[:], T1[:], T2[:], opb, e2)

        nb = 0
        for gh in range(64 // GH):
            g0 = gh * GH
            cprod(X1, X2, 0, BBr, BBi, g0, ALU.subtract, ALU.add)
            cprod(X3, X4, 8, CRg, CIn, g0, ALU.add, ALU.subtract)
            for q4 in range(GH // 4):
                pt = ptp[nb % 2]
                nb += 1
                for q in range(4):
                    gl = q4 * 4 + q
                    P.op("pe", "matmul", ((pt[:, q, :],), dict(lhsT=X1[0:64, gl].rearrange("p s c -> p (s c)"),
                                                              rhs=X3[0:64, gl].rearrange("p s c -> p (s c)"),
                                                              start=True, stop=False)),
                         reads=["z_x1", "z_x3"], writes=[kn(pt[:])])
                    P.op("pe", "matmul", ((pt[:, q, :],), dict(lhsT=X2[0:64, gl].rearrange("p s c -> p (s c)"),
                                                              rhs=X4[0:64, gl].rearrange("p s c -> p (s c)"),
                                                              start=False, stop=True)),
                         reads=["z_x2", "z_x4"], writes=[kn(pt[:])])
                TT(toep[:, g0 + q4 * 4:g0 + q4 * 4 + 4, :], pt[:], mask[:].unsqueeze(1).broadcast_to([128, 4, 128]), ALU.mult)
            cprod(X1, X2, 17, BBr, BBi, g0, ALU.subtract, ALU.add)
            for Xs, wt in ((X1, wtr), (X2, wti)):
                for q8 in range(GH // 8):
                    pt = pw8[nb % 2]
                    nb += 1
                    for q in range(8):
                        gl = q8 * 8 + q
                        P.op("pe", "transpose", dict(out=pt[:, q, :], in_=Xs[0:64, gl].rearrange("p s c -> p (s c)"),
                                                     identity=idf64),
                             reads=[kn(Xs[:]), "ident_f"], writes=[kn(pt[:])])
                    CP(wt[:, g0 + q8 * 8:g0 + q8 * 8 + 8, :], pt[:])
            cprod(X3, X4, 9, CRg, CIn, g0, ALU.add, ALU.subtract)
            for Xs, vt in ((X3, vre), (X4, vim)):
                for h in range(2):
                    rs = slice(h * 64, (h + 1) * 64)
                    P.op("dve", "tensor_copy", dict(out=vt[rs, gh * (GH // 2):(gh + 1) * (GH // 2), :],
                                                     in_=Xs[rs, h::2].rearrange("p g s c -> p g (s c)")),
                         reads=[kn(Xs[:])], writes=[kn(vt[:])])

    def s5_main(self, es, xsrc, xdst, jj, wb, ident_b, toep, wtr, wti, vre, vim, dvec, mcat, gn):
        nc, P = self.nc, self.P
        A = lambda n, s, d=F32: es.enter_context(self.sb(n, s, d))
        PS = lambda n, s, d=F32: es.enter_context(self.pp(n, s, d))
        xq = A("m_xq", [128, 8, D])
        hy = A("m_hy", [128, 8192], BF16)
        uy = A("m_uy", [128, 8192], BF16)
        beta = A("m_beta", [128, 128, 64])
        XS = A("m_xs", [128, 129, 64], BF16)
        Z = [A("m_z0", [128, 2, 64]), A("m_z1", [128, 2, 64])]
        AB = A("m_ab", [128, 2, 64]); Ssum = A("m_s", [128, 64])
        wa = A("m_wa", [128, 8, 512], BF16); wbt = A("m_wb", [128, 8, 512], BF16)
        sq = A("m_sq", [128, D], BF16); ss = A("m_ss", [128, 8]); rstd = A("m_rstd", [128, 8])
        ytmp = [A(f"m_yt{i}", [128, 128]) for i in range(4)]
        yact = [A(f"m_ya{i}", [128, 128], BF16) for i in range(4)]
        sgt = A("m_sg", [128, 512]); tt = A("m_tt", [128, 512])
        pT = [PS(f"m_pT{i}", [128, 8, 128], BF16) for i in range(2)]
        pb = [PS(f"m_pb{i}", [128, 2, 128]) for i in range(2)]
        py = [PS(f"m_py{i}", [128, 128]) for i in range(2)]
        pa = PS("m_pa", [128, 512]); pbb = PS("m_pbb", [128, 512])
        hbp = hy[:].rearrange("p (g s c) -> p g s c", g=64, s=8)
        yt = hy[:].rearrange("p (s ch) -> p s ch", s=8)
        U = uy[:].rearrange("p (g j) -> p g j", g=64)
        yT = uy[:].rearrange("p (k t) -> p k t", k=8)
        wa_v = wb["s5_glu_a"][jj].rearrange("(k p) n -> p k n", p=128)
        wb_v = wb["s5_glu_b"][jj].rearrange("(k p) n -> p k n", p=128)
        wkey = f"w_s5_{jj}"

        def xv(ap, q):
            return ap[q * 1024:(q + 1) * 1024, :].rearrange("(p t) d -> p t d", t=8)

        P.op("dve", "memset", ((Z[0][:], 0.0), {}), writes=[("Z", 0)])
        P.op("dve", "memset", ((XS[:, 0, :], 0.0), {}), writes=["XS"])
        cnt = 0
        ncp = 0
        for q in range(4):
            P.op("sync", "dma_start", dict(out=xq[:], in_=xv(xsrc, q)), reads=[("xres", q)], writes=["xq"], dma="m_x")
            for t in range(8):
                P.op("act", "activation", dict(out=sq[:], in_=xq[:, t, :], func=AF.Square, accum_out=ss[:, t:t + 1]),
                     reads=["xq"], writes=["m_sq", "m_ss"])
            P.op("dve", "tensor_scalar", dict(out=rstd[:], in0=ss[:], scalar1=1.0 / D, scalar2=EPS, op0=ALU.mult, op1=ALU.add),
                 reads=["m_ss"], writes=["m_rstd"])
            P.op("act", "activation", dict(out=rstd[:], in_=rstd[:], func=AF.Sqrt), reads=["m_rstd"], writes=["m_rstd"])
            P.op("dve", "reciprocal", dict(out=rstd[:], in_=rstd[:]), reads=["m_rstd"], writes=["m_rstd"])
            for t in range(8):
                P.op("dve", "scalar_tensor_tensor", dict(out=hbp[:, :, t, :], in0=xq[:, t, :].rearrange("p (g c) -> p g c", c=16),
                                                         scalar=rstd[:, t:t + 1], in1=gn[:].rearrange("p (g c) -> p g c", c=16),
                                                         op0=ALU.mult, op1=ALU.mult),
                     reads=["xq", "m_rstd", "s_gn"], writes=["hy"])
            for g8 in range(8):
                pt = pT[g8 % 2]
                for gq in range(8):
                    g = g8 * 8 + gq
                    P.op("pe", "transpose", dict(out=pt[:, gq, :], in_=hy[:, g * 128:(g + 1) * 128], identity=ident_b[:]),
                         reads=["hy", "ident"], writes=[("m_pT", g8 % 2)])
                if g8 % 2 == 0:
                    P.op("dve", "tensor_copy", dict(out=U[:, g8 * 8:(g8 + 1) * 8, :], in_=pt[:]), reads=[("m_pT", 0)], writes=["uy"])
                else:
                    P.op("act", "copy", dict(out=U[:, g8 * 8:(g8 + 1) * 8, :], in_=pt[:]), reads=[("m_pT", 1)], writes=["uy"])
            for pr in range(32):
                pbt = pb[pr % 2]
                for ri, wt in enumerate((wtr, wti)):
                    for g2 in range(2):
                        g = 2 * pr + g2
                        P.op("pe", "matmul", ((pbt[g2 * 64:(g2 + 1) * 64, ri, :],),
                                              dict(lhsT=wt[:, g, :], rhs=U[:, g, :], start=True, stop=True)),
                             reads=["uy", "s_w"], writes=[("m_pb", pr % 2)])
                bt = beta[:, :, :]
                ov = bass.AP(tensor=bt.tensor, offset=bt.offset + pr, ap=[list(bt.ap[0]), [32, 2], [64, 128]])
                if pr % 2 == 0:
                    P.op("dve", "tensor_copy", dict(out=ov, in_=pbt[:]), reads=[("m_pb", 0)], writes=["beta"])
                else:
                    P.op("act", "copy", dict(out=ov, in_=pbt[:]), reads=[("m_pb", 1)], writes=["beta"])
            for j in range(128):
                zc, zn = Z[cnt % 2], Z[(cnt + 1) % 2]
                kc, kn_ = ("Z", cnt % 2), ("Z", (cnt + 1) % 2)
                cnt += 1
                zt = zc[:, :, :]
                win = bass.AP(tensor=zt.tensor, offset=zt.offset, ap=[list(zt.ap[0]), [32, 2], [1, 64]])
                P.op("dve", "tensor_tensor", dict(out=AB[:], in0=mcat[:], in1=win, op=ALU.mult), reads=[kc, "s_mcat"], writes=["AB"])
                P.op("dve", "tensor_tensor", dict(out=Ssum[:], in0=AB[:, 0, :], in1=AB[:, 1, :], op=ALU.add), reads=["AB"], writes=["Ssum"])
                P.op("dve", "tensor_tensor", dict(out=zn[:], in0=Ssum[:].unsqueeze(1).broadcast_to([128, 2, 64]),
                                                  in1=beta[:, j, :].unsqueeze(1).broadcast_to([128, 2, 64]), op=ALU.add),
                     reads=["Ssum", "beta"], writes=[kn_])
                P.op("act", "copy", dict(out=XS[:, j + 1, :], in_=zn[:, 0, :]), reads=[kn_], writes=["XS"])
            def grp_front(g):
                pr, g2 = g // 2, g % 2
                pyt = [py[0][:], py[1][:], pa[:, 0:128], pbb[:, 0:128]][g % 4]
                pyk = [("m_py", 0), ("m_py", 1), "m_pa", "m_pbb"][g % 4]
                rs = slice(g2 * 64, (g2 + 1) * 64)
                P.op("pe", "matmul", ((pyt,), dict(lhsT=toep[:, g, :], rhs=U[:, g, :], start=True, stop=False)),
                     reads=["uy", "s_toep"], writes=[pyk])
                P.op("pe", "matmul", ((pyt,), dict(lhsT=vre[rs, pr, :], rhs=XS[rs, 0:128, pr], start=False, stop=False)),
                     reads=["XS", "s_v"], writes=[pyk])
                P.op("pe", "matmul", ((pyt,), dict(lhsT=vim[rs, pr, :], rhs=XS[rs, 0:128, 32 + pr], start=False, stop=True)),
                     reads=["XS", "s_v"], writes=[pyk])
                P.op("dve", "scalar_tensor_tensor", dict(out=ytmp[g % 4][:], in0=U[:, g, :], scalar=dvec[:, g:g + 1], in1=pyt,
                                                         op0=ALU.mult, op1=ALU.add),
                     reads=["uy", pyk, "s_dvec"], writes=[("m_ytmp", g % 4)])
                P.op("act", "activation", dict(out=yact[g % 4][:], in_=ytmp[g % 4][:], func=AF.Gelu),
                     reads=[("m_ytmp", g % 4)], writes=[("m_yact", g % 4)])

            def grp_back(g):
                g8 = g // 8
                pt = pT[g8 % 2]
                P.op("pe", "transpose", dict(out=pt[:, g % 8, :], in_=yact[g % 4][:], identity=ident_b[:]),
                     reads=[("m_yact", g % 4), "ident"], writes=[("m_pT", g8 % 2)])
                if g % 8 == 7:
                    ov = yt[:, :, g8 * 128:(g8 + 1) * 128].rearrange("p i (g c) -> p i g c", c=16)
                    iv = pt[:].rearrange("p g (i c) -> p i g c", c=16)
                    P.op("dve", "tensor_copy", dict(out=ov, in_=iv), reads=[("m_pT", g8 % 2)], writes=["hy"])

            for g in range(64 + 2):
                if g < 64:
                    grp_front(g)
                if g >= 2:
                    grp_back(g - 2)
            P.op("act", "copy", dict(out=XS[:, 0, :], in_=XS[:, 128, :]), reads=["XS"], writes=["XS"])
            for s in range(8):
                pt = pT[s % 2]
                for k in range(8):
                    P.op("pe", "transpose", dict(out=pt[:, k, :], in_=yt[:, s, k * 128:(k + 1) * 128], identity=ident_b[:]),
                         reads=["hy", "ident"], writes=[("m_pT", s % 2)])
                if s % 2 == 0:
                    P.op("dve", "tensor_copy", dict(out=yT[:, :, s * 128:(s + 1) * 128], in_=pt[:]), reads=[("m_pT", 0)], writes=["uy"])
                else:
                    P.op("act", "copy", dict(out=yT[:, :, s * 128:(s + 1) * 128], in_=pt[:]), reads=[("m_pT", 1)], writes=["uy"])
            for nh in range(2):
                P.op("sync", "dma_start", dict(out=wa[:], in_=wa_v[:, :, nh * 512:(nh + 1) * 512]), reads=self.wkeys[wkey], writes=["m_wa"], dma="m_w")
                P.op("sync", "dma_start", dict(out=wbt[:], in_=wb_v[:, :, nh * 512:(nh + 1) * 512]), reads=self.wkeys[wkey], writes=["m_wb"], dma="m_w")
                for s in range(8):
                    for k in range(8):
                        P.op("pe", "matmul", ((pa[:],), dict(lhsT=yT[:, k, s * 128:(s + 1) * 128], rhs=wa[:, k, :],
                                                            start=(k == 0), stop=(k == 7))), reads=["uy", "m_wa"], writes=["m_pa"])
                    for k in range(8):
                        P.op("pe", "matmul", ((pbb[:],), dict(lhsT=yT[:, k, s * 128:(s + 1) * 128], rhs=wbt[:, k, :],
                                                             start=(k == 0), stop=(k == 7))), reads=["uy", "m_wb"], writes=["m_pbb"])
                    P.op("act", "activation", dict(out=sgt[:], in_=pbb[:], func=AF.Sigmoid), reads=["m_pbb"], writes=["m_sg"])
                    P.op("dve", "tensor_tensor", dict(out=tt[:], in0=sgt[:], in1=pa[:], op=ALU.mult), reads=["m_sg", "m_pa"], writes=["m_tt"])
                    xs_ = xq[:, s, nh * 512:(nh + 1) * 512]
                    P.op("dve", "tensor_tensor", dict(out=xs_, in0=xs_, in1=tt[:], op=ALU.add), reads=["m_tt", "xq"], writes=["xq"])
            P.op("sync", "dma_start", dict(out=xv(xdst, q), in_=xq[:]), reads=["xq"], writes=[("xres", q)], dma="m_st")

    def attn_phase(self, es, xsrc, xdst, L, wn, wb, ident_f, ident_b):
        nc, P = self.nc, self.P
        jj = L // 2
        lam_init = 0.8 - 0.6 * math.exp(-0.3 * L)
        A = lambda n, s, d=F32: es.enter_context(self.sb(n, s, d))
        QT, KT, HD = self.scr["QT"], self.scr["KT"], self.scr["HD"]
        win_v = wb["attn_w_in"][jj].rearrange("(k p) n -> p k n", p=128)
        wout_v = wb["attn_w_out"][jj].rearrange("(k p) n -> p k n", p=128)
        wkey = f"w_attn_{jj}"
        kn = lambda ap: ap.tensor.name.split("__u")[0]
        Vd = A("a_vd", [128, 32, 4, 129], BF16)
        Vf = A("a_vf", [128, 32, 8, 65], BF16)
        cposk = A("a_cposk", [128, 32, 8])
        P.op("dve", "memset", ((Vd[:, :, :, 128:129], 1.0), {}), writes=["Vd"])
        P.op("dve", "memset", ((Vf[:, :, :, 64:65], 1.0), {}), writes=["Vf"])

        LSP = A("a_lsp", [128, 32, 8]); R = A("a_R", [128, 33, 8])
        tri = A("a_tri", [128, 128]); onesf = A("a_onesf", [128, 128])
        with contextlib.ExitStack() as e1:
            B = lambda n, s, d=F32: e1.enter_context(self.sb(n, s, d))
            PS = lambda n, s, d=F32: e1.enter_context(self.pp(n, s, d))
            xt = [B(f"a_xt{i}", [128, 4, D]) for i in range(2)]
            hb = [B(f"a_hb{i}", [128, D], BF16) for i in range(2)]
            hT = B("a_hT", [128, 8, 512], BF16)
            win = B("a_win", [128, 8, IN_COLS], BF16)
            gn = B("a_gn", [128, D])
            sq = B("a_sq", [128, D], BF16); ss = B("a_ss", [128, 4]); rstd = B("a_rstd", [128, 4])
            qsq = [B(f"a_qsq{i}", [128, 512]) for i in range(2)]
            lnv = [B(f"a_lnv{i}", [128, 512]) for i in range(2)]
            qo = [B(f"a_qo{i}", [128, 512], BF16) for i in range(2)]
            G = B("a_G", [128, 4]); epsc = B("a_eps", [128, 1]); ones2 = B("a_ones2", [128, 128])
            fgb = B("a_fgb", [128, 8]); zt = B("a_zt", [128, 8])
            pT = PS("a_pT", [128, 8, 128], BF16)
            pq = [PS(f"a_pq{i}", [128, 512]) for i in range(2)]
            pms = [PS(f"a_pms{i}", [128, 512]) for i in range(2)]
            pv = [PS(f"a_pv{i}", [128, 512]) for i in range(2)]
            pfl = PS("a_pfl", [128, 512])

            def DMA(out, in_, wr, rd=()):
                P.op("sync", "dma_start", dict(out=out, in_=in_), reads=list(rd), writes=[wr], dma="a_ld")

            DMA(gn[:], wn["mix_norm"][L].partition_broadcast(128), "a_gn")
            for h in range(2):
                DMA(win[:, h * 4:(h + 1) * 4, :], win_v[:, h * 4:(h + 1) * 4, :], "a_win", self.wkeys[wkey])
            for c, nm in enumerate(("diff_q_norm", "diff_k_norm", "fox_q_norm", "fox_k_norm")):
                for h in range(2):
                    DMA(G[h * 64:(h + 1) * 64, c:c + 1], wn[nm][jj].rearrange("(d o) -> d o", o=1), "a_G")
            DMA(fgb[:], wn["fg_bias"][jj].partition_broadcast(128), "a_fgb")
            DMA(ones2[:], self.consts["ones2"], "a_ones2")
            DMA(tri[:], self.consts["tri"], "a_tri")
            P.op("dve", "memset", ((epsc[:], EPS), {}), writes=["a_eps"])
            P.op("dve", "memset", ((onesf[:], 1.0), {}), writes=["a_onesf"])
            P.op("dve", "memset", ((R[:, 0, :], 0.0), {}), writes=["a_R"])
            qk_tiles = []
            for h in range(4):
                qk_tiles.append((h * 128, 0, QT, 2 * h))
            for h in range(4):
                qk_tiles.append((512 + h * 128, 1, KT, 2 * h))
            for h in range(4):
                qk_tiles.append((1536 + h * 128, 2, QT, 8 + 2 * h))
            for h in range(4):
                qk_tiles.append((2048 + h * 128, 3, KT, 8 + 2 * h))

            def xv(ap, b):
                return ap[b * 512:(b + 1) * 512, :].rearrange("(t p) d -> p t d", p=128)

            def load_x(b):
                P.op("sync", "dma_start", dict(out=xt[b % 2][:], in_=xv(xsrc, b)), reads=[("xres", b)],
                     writes=[("a_xt", b % 2)], dma=f"a_x{b % 2}")

            load_x(0)
            ev = 0
            for b in range(8):
                X = xt[b % 2]
                xk = ("a_xt", b % 2)
                if b + 1 < 8:
                    load_x(b + 1)
                for t in range(4):
                    P.op("act", "activation", dict(out=sq[:], in_=X[:, t, :], func=AF.Square, accum_out=ss[:, t:t + 1]),
                         reads=[xk], writes=["a_sq", "a_ss"])
                P.op("dve", "tensor_scalar", dict(out=rstd[:], in0=ss[:], scalar1=1.0 / D, scalar2=EPS, op0=ALU.mult, op1=ALU.add),
                     reads=["a_ss"], writes=["a_rstd"])
                P.op("act", "activation", dict(out=rstd[:], in_=rstd[:], func=AF.Sqrt), reads=["a_rstd"], writes=["a_rstd"])
                P.op("dve", "reciprocal", dict(out=rstd[:], in_=rstd[:]), reads=["a_rstd"], writes=["a_rstd"])
                for t in range(4):
                    H = hb[t % 2]
                    P.op("dve", "scalar_tensor_tensor", dict(out=H[:], in0=X[:, t, :], scalar=rstd[:, t:t + 1], in1=gn[:],
                                                             op0=ALU.mult, op1=ALU.mult),
                         reads=[xk, "a_rstd", "a_gn"], writes=[("a_hb", t % 2)])
                    for k in range(8):
                        P.op("pe", "transpose", dict(out=pT[:, k, :], in_=H[:, k * 128:(k + 1) * 128], identity=ident_b[:]),
                             reads=[("a_hb", t % 2), "ident"], writes=["a_pT"])
                    P.op("dve", "tensor_copy", dict(out=hT[:, :, t * 128:(t + 1) * 128], in_=pT[:]), reads=["a_pT"], writes=["a_hT"])
                for (c0, gc, dstT, m0) in qk_tiles:
                    q = ev % 2
                    ev += 1
                    for k in range(8):
                        P.op("pe", "matmul", ((pq[q][:],), dict(lhsT=win[:, k, c0:c0 + 128], rhs=hT[:, k, :],
                                                                start=(k == 0), stop=(k == 7))),
                             reads=["a_win", "a_hT"], writes=[("a_pq", q)])
                    P.op("act", "activation", dict(out=qsq[q][:], in_=pq[q][:], func=AF.Square), reads=[("a_pq", q)], writes=[("a_qsq", q)])
                    P.op("pe", "matmul", ((pms[q][:],), dict(lhsT=ones2[:], rhs=qsq[q][:], start=True, stop=True)),
                         reads=["a_ones2", ("a_qsq", q)], writes=[("a_pms", q)])
                    P.op("act", "activation", dict(out=lnv[q][:], in_=pms[q][:], func=AF.Ln, bias=epsc[:, 0:1]),
                         reads=[("a_pms", q), "a_eps"], writes=[("a_lnv", q)])
                    P.op("act", "activation", dict(out=lnv[q][:], in_=lnv[q][:], func=AF.Exp, scale=-0.5),
                         reads=[("a_lnv", q)], writes=[("a_lnv", q)])
                    P.op("dve", "scalar_tensor_tensor", dict(out=qo[q][:], in0=pq[q][:], scalar=G[:, gc:gc + 1], in1=lnv[q][:],
                                                             op0=ALU.mult, op1=ALU.mult),
                         reads=[("a_pq", q), ("a_lnv", q), "a_G"], writes=[("a_qo", q)])
                    for hh in range(2):
                        P.op("sync", "dma_start", dict(out=dstT[m0 + hh, 0:64, b * 512:(b + 1) * 512],
                                                       in_=qo[q][hh * 64:(hh + 1) * 64, :]),
                             reads=[("a_qo", q)], writes=[(kn(dstT), m0 + hh)], dma="a_qst")
                for t in range(4):
                    blk = b * 4 + t
                    for vi, (c0, Vt, nh, vd) in enumerate(((1024, Vd, 4, 128), (2560, Vf, 8, 64))):
                        q = vi
                        for k in range(8):
                            P.op("pe", "matmul", ((pv[q][:],), dict(lhsT=hT[:, k, t * 128:(t + 1) * 128], rhs=win[:, k, c0:c0 + 512],
                                                                    start=(k == 0), stop=(k == 7))),
                                 reads=["a_win", "a_hT"], writes=[("a_pv", q)])
                        P.op("act" if vi == 0 else "dve", "copy" if vi == 0 else "tensor_copy",
                             dict(out=Vt[:, blk, :, 0:vd], in_=pv[q][:].rearrange("p (h v) -> p h v", h=nh)),
                             reads=[("a_pv", q)], writes=["Vd" if vi == 0 else "Vf"])
                    for k in range(8):
                        P.op("pe", "matmul", ((pfl[:, 0:8],), dict(lhsT=hT[:, k, t * 128:(t + 1) * 128], rhs=win[:, k, 3072:3080],
                                                                   start=(k == 0), stop=(k == 7))),
                             reads=["a_win", "a_hT"], writes=["a_pfl"])
                    P.op("dve", "tensor_tensor", dict(out=zt[:], in0=pfl[:, 0:8], in1=fgb[:], op=ALU.add),
                         reads=["a_pfl", "a_fgb"], writes=["a_zt"])
                    P.op("act", "activation", dict(out=zt[:], in_=zt[:], func=AF.Exp, scale=-1.0), reads=["a_zt"], writes=["a_zt"])
                    P.op("act", "activation", dict(out=LSP[:, blk, :], in_=zt[:], func=AF.Ln, bias=1.0), reads=["a_zt"], writes=["a_lsp"])
                    P.op("dve", "tensor_tensor", dict(out=R[:, blk + 1, :], in0=R[:, blk, :], in1=LSP[:, blk, :], op=ALU.add),
                         reads=["a_lsp", "a_R"], writes=["a_R"])
        P.barrier()
        self.uid += 1
        with contextlib.ExitStack() as e1:
            B = lambda n, s, d=F32: e1.enter_context(self.sb(n, s, d))
            PS = lambda n, s, d=F32: e1.enter_context(self.pp(n, s, d))
            cT = B("a_cT", [8, S]); rT = B("a_rT", [8, S])
            a123 = [B(f"a_a{i}", [8, S], BF16) for i in range(3)]
            onesb = B("a_onesb", [8, S], BF16)
            pv = [PS(f"a_pv{i}", [128, 512]) for i in range(2)]
            pq = [PS(f"a_pq{i}", [128, 512]) for i in range(2)]
            P.op("dve", "memset", ((onesb[:], 1.0), {}), writes=["a_onesb"])
            for blk in range(32):
                q = blk % 2
                P.op("pe", "matmul", ((pv[q][:, 0:8],), dict(lhsT=tri[:], rhs=LSP[:, blk, :], start=True, stop=False)),
                     reads=["a_tri", "a_lsp"], writes=[("a_pv", q)])
                P.op("pe", "matmul", ((pv[q][:, 0:8],), dict(lhsT=onesf[:], rhs=R[:, blk, :], start=False, stop=True)),
                     reads=["a_onesf", "a_R"], writes=[("a_pv", q)])
                P.op("dve", "tensor_copy", dict(out=cposk[:, blk, :], in_=pv[q][:, 0:8]), reads=[("a_pv", q)], writes=["cposk"])
                P.op("pe", "matmul", ((pq[q][0:8, 0:128],), dict(lhsT=LSP[:, blk, :], rhs=tri[:], start=True, stop=False)),
                     reads=["a_tri", "a_lsp"], writes=[("a_pq", q)])
                P.op("pe", "matmul", ((pq[q][0:8, 0:128],), dict(lhsT=R[:, blk, :], rhs=onesf[:], start=False, stop=True)),
                     reads=["a_onesf", "a_R"], writes=[("a_pq", q)])
                P.op("act", "activation", dict(out=cT[:, blk * 128:(blk + 1) * 128], in_=pq[q][0:8, 0:128], func=AF.Copy, scale=-8.0),
                     reads=[("a_pq", q)], writes=["a_cT"])
            P.op("dve", "tensor_copy", dict(out=a123[0][:], in_=cT[:]), reads=["a_cT"], writes=["a_a0"])
            P.op("dve", "tensor_tensor", dict(out=rT[:], in0=cT[:], in1=a123[0][:], op=ALU.subtract), reads=["a_cT", "a_a0"], writes=["a_rT"])
            P.op("dve", "tensor_copy", dict(out=a123[1][:], in_=rT[:]), reads=["a_rT"], writes=["a_a1"])
            P.op("dve", "tensor_tensor", dict(out=cT[:], in0=rT[:], in1=a123[1][:], op=ALU.subtract), reads=["a_rT", "a_a1"], writes=["a_cT"])
            P.op("dve", "tensor_copy", dict(out=a123[2][:], in_=cT[:]), reads=["a_cT"], writes=["a_a2"])
            for i in range(3):
                P.op("sync", "dma_start", dict(out=QT[8:16, 64 + i, :], in_=a123[i][:]), reads=[f"a_a{i}"],
                     writes=[("QTaug", i)], dma="a_qst")
                P.op("sync", "dma_start", dict(out=KT[8:16, 64 + i, :], in_=onesb[:]), reads=["a_onesb"],
                     writes=[("KTaug", i)], dma="a_qst")
        self.dump("lsp", LSP[:], [])
        self.dump("cposk", cposk[:], [])
        self.dump("vd", Vd[:, 0:2], [])
        self.dump("vf", Vf[:, 30:32], [])
        self.dump("qt", QT[:, :, 0:512], [])
        self.dump("kt", KT[:, :, 3584:4096], [])
        P.barrier()

        Ocat = A("b_ocat", [128, 32, D], BF16)
        with contextlib.ExitStack() as e2:
            B = lambda n, s, d=F32: e2.enter_context(self.sb(n, s, d))
            PS = lambda n, s, d=F32: e2.enter_context(self.pp(n, s, d))
            qT = [B(f"b_qT{i}", [67, S], BF16) for i in range(2)]
            kT = [B(f"b_kT{i}", [67, S], BF16) for i in range(2)]
            NPS, NPE = 4, 6
            Pe = [B(f"b_pe{i}", [128, 512], BF16) for i in range(NPE)]
            BT = B("b_BT", [128, 5, 2, 128]); BT8 = B("b_BT8", [128, 5, 2, 128])
            b31 = B("b_b31", [128, 4])
            n0 = B("b_n0", [128, 32, 128])
            rb33 = B("b_rb33", [33, 5]); rbl = B("b_rbl", [33, 128]); OH = B("b_oh", [33, 384]); hrep = B("b_hrep", [128, 384])
            lq = [B(f"b_lq{i}", [128, 64]) for i in range(4)]
            lp = B("b_lp", [128, 64]); e12 = B("b_e12", [128, 2]); nlam = B("b_nlam", [128, 1])
            SW = B("b_sw", [128, 128])
            rinv = B("b_rinv", [128, 1]); odt = B("b_odt", [128, 128]); ssq = B("b_ssq", [128, 1]); junk = B("b_junk", [128, 128], BF16)
            ps = [PS(f"b_ps{i}", [128, 512]) for i in range(NPS)]
            po = [PS(f"b_po{i}", [128, 4, 256]) for i in range(2)]

            def DMA(out, in_, wr, rd=(), sem="b_ld"):
                P.op("sync", "dma_start", dict(out=out, in_=in_), reads=list(rd), writes=[wr], dma=sem)

            P.op("dve", "memset", ((rb33[:], 0.0), {}), writes=["b_rb33"])
            P.op("dve", "memset", ((rb33[32:33, :], NEG), {}), writes=["b_rb33"])
            DMA(rb33[0:32, 0:4], wn["rel_bias"], "b_rb33")
            DMA(OH[:], self.consts["relOH"], "b_oh")
            for i, nm in enumerate(("diff_lambda_q1", "diff_lambda_k1", "diff_lambda_q2", "diff_lambda_k2")):
                DMA(lq[i][:], wn[nm][jj].partition_broadcast(128), f"b_lq{i}")
            DMA(SW[:], wn["diff_subln"][jj].partition_broadcast(128), "b_sw")
            for h in range(5):
                P.op("dve", "tensor_copy", dict(out=rbl[:], in_=rb33[:, h:h + 1].broadcast_to([33, 128])), reads=["b_rb33"], writes=["b_rbl"])
                P.op("pe", "matmul", ((ps[0][:, 0:384],), dict(lhsT=rbl[:], rhs=OH[:], start=True, stop=True)),
                     reads=["b_rbl", "b_oh"], writes=[("b_ps", 0)])
                P.op("dve", "tensor_copy", dict(out=hrep[:], in_=ps[0][:, 0:384]), reads=[("b_ps", 0)], writes=["b_hrep"])
                if h < 4:
                    P.op("dve", "tensor_copy", dict(out=b31[:, h:h + 1], in_=hrep[:, 383:384]), reads=["b_hrep"], writes=["b_b31"])
                DMA(HD[h], hrep[:], ("HD", h), ["b_hrep"], sem="b_hd")
                hd = HD[h]
                src = bass.AP(tensor=hd.tensor, offset=hd.offset + 127, ap=[[383, 128], [128, 2], [1, 128]])
                DMA(BT[:, h, :, :], src, "b_BT", [("HD", h)], sem="b_hd2")
            for h in range(5):
                if h < 4:
                    P.op("dve", "tensor_scalar", dict(out=BT8[:, h], in0=BT[:, h], scalar1=b31[:, h:h + 1], scalar2=8.0,
                                                      op0=ALU.subtract, op1=ALU.mult), reads=["b_BT", "b_b31"], writes=["b_BT8"])
                else:
                    P.op("dve", "tensor_scalar", dict(out=BT8[:, h], in0=BT[:, h], scalar1=8.0, scalar2=None, op0=ALU.mult),
                         reads=["b_BT"], writes=["b_BT8"])
            for i in range(2):
                P.op("dve", "tensor_tensor", dict(out=lp[:], in0=lq[2 * i][:], in1=lq[2 * i + 1][:], op=ALU.mult),
                     reads=[f"b_lq{2 * i}", f"b_lq{2 * i + 1}"], writes=["b_lp"])
                P.op("dve", "tensor_reduce", dict(out=e12[:, i:i + 1], in_=lp[:], op=ALU.add, axis=AX.X), reads=["b_lp"], writes=["b_e12"])
            P.op("act", "activation", dict(out=e12[:], in_=e12[:], func=AF.Exp), reads=["b_e12"], writes=["b_e12"])
            P.op("dve", "scalar_tensor_tensor", dict(out=nlam[:], in0=e12[:, 1:2], scalar=-lam_init, in1=e12[:, 0:1],
                                                     op0=ALU.add, op1=ALU.subtract), reads=["b_e12"], writes=["b_nlam"])
            P.op("dve", "tensor_scalar", dict(out=SW[:], in0=SW[:], scalar1=1.0 - lam_init, scalar2=None, op0=ALU.mult),
                 reads=["b_sw"], writes=["b_sw"])

            maps = [(2 * h + m, "d", h, m) for h in range(4) for m in range(2)] + [(8 + f, "f", f, 0) for f in range(8)]
            steps = [(mi, mp, I, J) for mi, mp in enumerate(maps) for I in range(8) for J in range(4 * I + 4)]
            LAG = 3
            started = {}

            def front(idx):
                mi, (mapi, kind, hh, mm), I, J = steps[idx]
                sl = mi % 2
                K = 64 if kind == "d" else 67
                if I == 0 and J == 0:
                    for c4 in range(2):
                        cs = slice(c4 * 2048, (c4 + 1) * 2048)
                        DMA(qT[sl][0:K, cs], QT[mapi, 0:K, cs], ("b_qT", sl), [], sem=f"b_q{sl}")
                        DMA(kT[sl][0:K, cs], KT[mapi, 0:K, cs], ("b_kT", sl), [], sem=f"b_q{sl}")
                bth = hh if kind == "d" else 4
                qlo = max(4 * I, J)
                c0 = (qlo - 4 * I) * 128
                pst, psk = ps[idx % NPS], ("b_ps", idx % NPS)
                pet, pek = Pe[idx % NPE], ("b_pe", idx % NPE)
                P.op("pe", "matmul", ((pst[:, c0:512],), dict(lhsT=kT[sl][0:K, J * 128:(J + 1) * 128],
                                                              rhs=qT[sl][0:K, I * 512 + c0:(I + 1) * 512],
                                                              start=True, stop=True)),
                     reads=[("b_qT", sl), ("b_kT", sl)], writes=[psk])
                if kind == "d":
                    fbias, frd = b31[:, hh:hh + 1], "b_b31"
                else:
                    fbias, frd = cposk[:, J, hh:hh + 1], "cposk"
                nnear = 2 if kind == "d" else 1
                for dist in range(nnear):
                    qt = J + dist
                    if qt < qlo or qt >= 4 * I + 4:
                        continue
                    cc = (qt - 4 * I) * 128
                    P.op("dve", "tensor_tensor", dict(out=pst[:, cc:cc + 128], in0=pst[:, cc:cc + 128], in1=BT8[:, bth, dist, :],
                                                      op=ALU.add), reads=[psk, "b_BT8"], writes=[psk])
                P.op("act", "activation", dict(out=pet[:, c0:512], in_=pst[:, c0:512], func=AF.Exp, scale=0.125, bias=fbias),
                     reads=[psk, frd], writes=[pek])

            def back(idx):
                mi, (mapi, kind, hh, mm), I, J = steps[idx]
                sp = mi * 8 + I
                pot, pok = po[sp % 2], ("b_po", sp % 2)
                pet, pek = Pe[idx % NPE], ("b_pe", idx % NPE)
                Vt, vd = (Vd, 128) if kind == "d" else (Vf, 64)
                vkey = "Vd" if kind == "d" else "Vf"
                qlo = max(4 * I, J)
                for qt in range(qlo, 4 * I + 4):
                    ql = qt - 4 * I
                    cc = ql * 128
                    st = (sp, ql // 2) not in started
                    started[(sp, ql // 2)] = True
                    P.op("pe", "matmul", ((pot[:, ql, 0:vd + 1],), dict(lhsT=pet[:, cc:cc + 128], rhs=Vt[:, J, hh, 0:vd + 1],
                                                                       start=st, stop=(J == qt), skip_group_check=True)),
                         reads=[pek, vkey], writes=[pok])
                if J != 4 * I + 3:
                    return
                for ql in range(4):
                    qt = 4 * I + ql
                    P.op("dve", "reciprocal", dict(out=rinv[:], in_=pot[:, ql, vd:vd + 1]), reads=[pok], writes=["b_rinv"])
                    if kind == "f":
                        P.op("dve", "tensor_scalar", dict(out=Ocat[:, qt, 512 + hh * 64:512 + (hh + 1) * 64], in0=pot[:, ql, 0:64],
                                                          scalar1=rinv[:, 0:1], scalar2=None, op0=ALU.mult),
                             reads=[pok, "b_rinv"], writes=[("b_ocat", qt)])
                    elif mm == 0:
                        P.op("dve", "tensor_scalar", dict(out=n0[:, qt, :], in0=pot[:, ql, 0:128], scalar1=rinv[:, 0:1], scalar2=None,
                                                          op0=ALU.mult), reads=[pok, "b_rinv"], writes=["b_n0"])
                    else:
                        P.op("dve", "tensor_tensor", dict(out=rinv[:], in0=rinv[:], in1=nlam[:], op=ALU.mult),
                             reads=["b_rinv", "b_nlam"], writes=["b_rinv"])
                        P.op("dve", "scalar_tensor_tensor", dict(out=odt[:], in0=pot[:, ql, 0:128], scalar=rinv[:, 0:1], in1=n0[:, qt, :],
                                                                 op0=ALU.mult, op1=ALU.add), reads=[pok, "b_rinv", "b_n0"], writes=["b_odt"])
                        P.op("act", "activation", dict(out=junk[:], in_=odt[:], func=AF.Square, accum_out=ssq[:, 0:1]),
                             reads=["b_odt"], writes=["b_junk", "b_ssq"])
                        P.op("dve", "tensor_scalar", dict(out=ssq[:], in0=ssq[:], scalar1=1.0 / 128, scalar2=EPS, op0=ALU.mult, op1=ALU.add),
                             reads=["b_ssq"], writes=["b_ssq"])
                        P.op("act", "activation", dict(out=ssq[:], in_=ssq[:], func=AF.Sqrt), reads=["b_ssq"], writes=["b_ssq"])
                        P.op("dve", "reciprocal", dict(out=ssq[:], in_=ssq[:]), reads=["b_ssq"], writes=["b_ssq"])
                        P.op("dve", "scalar_tensor_tensor", dict(out=Ocat[:, qt, hh * 128:(hh + 1) * 128], in0=odt[:], scalar=ssq[:, 0:1],
                                                                 in1=SW[:], op0=ALU.mult, op1=ALU.mult),
                             reads=["b_odt", "b_ssq", "b_sw"], writes=[("b_ocat", qt)])

            for idx in range(len(steps) + LAG):
                if idx < len(steps):
                    front(idx)
                if idx >= LAG:
                    back(idx - LAG)
            self.dump("bt", BT[:], [])
            self.dump("n0", n0[:], [])
            self.dump("nlam", nlam[:], [])
            self.dump("ocat", Ocat[:, 0:2, :], [])
            self.dump("ocat2", Ocat[:, 30:32, :], [])
        P.barrier()
        self.uid += 1
        with contextlib.ExitStack() as e3:
            B = lambda n, s, d=F32: e3.enter_context(self.sb(n, s, d))
            PS = lambda n, s, d=F32: e3.enter_context(self.pp(n, s, d))
            wout = B("b_wout", [128, 8, D], BF16)
            oT = [B(f"b_oT{i}", [128, 8, 128], BF16) for i in range(2)]
            xo = [B(f"b_xo{i}", [128, D]) for i in range(2)]
            pT = PS("b_pT", [128, 8, 128], BF16)
            po2 = PS("b_po2", [128, 512])

            def DMA(out, in_, wr, rd=(), sem="b_ld"):
                P.op("sync", "dma_start", dict(out=out, in_=in_), reads=list(rd), writes=[wr], dma=sem)

            for h in range(2):
                DMA(wout[:, h * 4:(h + 1) * 4, :], wout_v[:, h * 4:(h + 1) * 4, :], "b_wout", self.wkeys[wkey])
            def xrow(ap, blk):
                return ap[blk * 128:(blk + 1) * 128, :]
            for blk in range(32):
                sl = blk % 2
                DMA(xo[sl][:], xrow(xsrc, blk), ("b_xo", sl), [("xres", blk // 4)], sem=f"b_x{sl}")
                for k in range(8):
                    P.op("pe", "transpose", dict(out=pT[:, k, :], in_=Ocat[:, blk, k * 128:(k + 1) * 128], identity=ident_b[:]),
                         reads=[("b_ocat", blk), "ident"], writes=["b_pT"])
                P.op("act", "copy", dict(out=oT[sl][:], in_=pT[:]), reads=["b_pT"], writes=[("b_oT", sl)])
                for nh in range(2):
                    for k in range(8):
                        P.op("pe", "matmul", ((po2[:],), dict(lhsT=oT[sl][:, k, :], rhs=wout[:, k, nh * 512:(nh + 1) * 512],
                                                             start=(k == 0), stop=(k == 7))),
                             reads=[("b_oT", sl), "b_wout"], writes=["b_po2"])
                    P.op("dve", "tensor_tensor", dict(out=xo[sl][:, nh * 512:(nh + 1) * 512], in0=xo[sl][:, nh * 512:(nh + 1) * 512],
                                                      in1=po2[:], op=ALU.add), reads=["b_po2", ("b_xo", sl)], writes=[("b_xo", sl)])
                P.op("sync", "dma_start", dict(out=xrow(xdst, blk), in_=xo[sl][:]), reads=[("b_xo", sl)], writes=[("xres_o", blk)], dma="b_st")

    def build(self):
        nc, P = self.nc, self.P
        x_in = self.din("x", [S, D])
        ident = self.din("ident", [128, 128])
        wn = {}
        for nm, shp in [("ffn1_norm", [DEPTH, D]), ("ffn1_gate", [DEPTH, D, DFF]), ("ffn1_up", [DEPTH, D, DFF]),
                        ("ffn1_down", [DEPTH, DFF, D]), ("mix_norm", [DEPTH, D]), ("ffn2_norm", [DEPTH, D]),
                        ("ffn2_gate", [DEPTH, D, DFF]), ("ffn2_up", [DEPTH, D, DFF]), ("ffn2_down", [DEPTH, DFF, D])]:
            wn[nm] = self.din(nm, shp)
        for nm, shp in [("s5_a_re", [2, 64, 64]), ("s5_a_im", [2, 64, 64]), ("s5_log_step", [2, 64]),
                        ("s5_b_re", [2, 64, 64, 16]), ("s5_b_im", [2, 64, 64, 16]), ("s5_c_re", [2, 64, 16, 64]),
                        ("s5_c_im", [2, 64, 16, 64]), ("s5_d", [2, D]), ("s5_glu_a", [2, D, D]), ("s5_glu_b", [2, D, D])]:
            wn[nm] = self.din(nm, shp)
        for nm, shp in [("attn_w_in", [2, D, IN_COLS]), ("attn_w_out", [2, D, D]), ("fg_bias", [2, 8]),
                        ("diff_q_norm", [2, 64]), ("diff_k_norm", [2, 64]), ("diff_lambda_q1", [2, 64]),
                        ("diff_lambda_k1", [2, 64]), ("diff_lambda_q2", [2, 64]), ("diff_lambda_k2", [2, 64]),
                        ("diff_subln", [2, 128]), ("fox_q_norm", [2, 64]), ("fox_k_norm", [2, 64]), ("rel_bias", [32, 4])]:
            wn[nm] = self.din(nm, shp)
        self.consts = {k: self.din(k, list(v.shape)) for k, v in host_consts().items() if k != "ident"}
        self.scr = {"QT": self.dscr("QT", [16, 67, S], BF16), "KT": self.dscr("KT", [16, 67, S], BF16),
                    "HD": self.dscr("HD", [5, 128, 384], F32)}
        out = nc.dram_tensor("out", [S, D], F32, kind="ExternalOutput").ap()
        xres = self.dscr("xres", [S, D], F32)
        wb = {}
        for f in ("ffn1", "ffn2"):
            wb[f + "_gate"] = self.dscr(f + "_gate_b", [DEPTH, D, DFF], BF16)
            wb[f + "_up"] = self.dscr(f + "_up_b", [DEPTH, D, DFF], BF16)
            wb[f + "_down"] = self.dscr(f + "_down_b", [DEPTH, DFF, D], BF16)
        wb["attn_w_in"] = self.dscr("attn_w_in_b", [2, D, IN_COLS], BF16)
        wb["attn_w_out"] = self.dscr("attn_w_out_b", [2, D, D], BF16)
        wb["s5_glu_a"] = self.dscr("s5_glu_a_b", [2, D, D], BF16)
        wb["s5_glu_b"] = self.dscr("s5_glu_b_b", [2, D, D], BF16)

        with contextlib.ExitStack() as es0:
            ident_f = es0.enter_context(self.sb("ident_f", [128, 128], F32))
            ident_b = es0.enter_context(self.sb("ident_b", [128, 128], BF16))
            P.op("sync", "dma_start", dict(out=ident_f[:], in_=ident), writes=["ident_f"], dma="c_misc")
            P.op("dve", "tensor_copy", dict(out=ident_b[:], in_=ident_f[:]), reads=["ident_f"], writes=["ident"])
            need = set(k for k, _ in (self.phases or [("ffn1", 0), ("ffn2", 0), ("s5", 1), ("attn", 0)]))
            for L in range(DEPTH):
                for f in ("ffn1", "ffn2"):
                    if f == "ffn2":
                        if L % 2 == 0 and "attn" in need:
                            for w in ("attn_w_in", "attn_w_out"):
                                self.cast_w(wn[w][L // 2], wb[w][L // 2], f"cast_attn_{L // 2}", f"w_attn_{L // 2}")
                        if L % 2 == 1 and "s5" in need:
                            for w in ("s5_glu_a", "s5_glu_b"):
                                self.cast_w(wn[w][L // 2], wb[w][L // 2], f"cast_s5_{L // 2}", f"w_s5_{L // 2}")
                    if f not in need:
                        continue
                    for w in ("gate", "up", "down"):
                        self.cast_w(wn[f"{f}_{w}"][L], wb[f"{f}_{w}"][L], f"cast_{f}_{L}", f"w_{f}_{L}")
            phases = self.phases
            if phases is None:
                phases = []
                for L in range(DEPTH):
                    phases += [("ffn1", L), ("attn" if L % 2 == 0 else "s5", L), ("ffn2", L)]
            src = x_in
            for i, (kind, L) in enumerate(phases):
                dst = out if i == len(phases) - 1 else xres
                P.barrier()
                self.uid += 1
                with contextlib.ExitStack() as es:
                    if kind in ("ffn1", "ffn2"):
                        f = kind
                        self.ffn_phase(es, src, dst, wn[f + "_norm"][L], wb[f + "_gate"][L], wb[f + "_up"][L],
                                       wb[f + "_down"][L], f"w_{f}_{L}", ident_b)
                    elif kind == "s5":
                        self.s5_phase(es, src, dst, L, wn, wb, ident_f, ident_b)
                    elif kind == "attn":
                        self.attn_phase(es, src, dst, L, wn, wb, ident_f, ident_b)
                src = xres
            P.barrier(final=True)
            P.emit()
        return nc


def host_consts():
    idx = np.arange(128) // 16
    s5mask = (idx[None, :] >= idx[:, None]).astype(np.float32)
    n = np.arange(384) - 127
    nn = np.maximum(n, 0)
    nf = np.maximum(nn, 1).astype(np.float32)
    large = 16 + (np.log(nf / np.float32(16)) / np.float32(math.log(128 / 16)) * np.float32(16)).astype(np.int32)
    large = np.minimum(large, 31)
    bucket = np.where(nn < 16, nn, large)
    oh = np.zeros((33, 384), np.float32)
    for i in range(384):
        if n[i] >= 0:
            oh[bucket[i], i] = 1.0
        else:
            oh[32, i] = 1.0
    ones2 = np.zeros((128, 128), np.float32)
    ones2[:64, :64] = 1.0 / 64
    ones2[64:, 64:] = 1.0 / 64
    tri = (np.arange(128)[:, None] <= np.arange(128)[None, :]).astype(np.float32)
    return {"ident": np.eye(128, dtype=np.float32), "s5mask": s5mask, "relOH": oh, "ones2": ones2, "tri": tri}


_CACHE = {}


def kernel(**inputs):
    if "b" not in _CACHE:
        b = Builder()
        b.build()
        _CACHE["b"] = b
    b = _CACHE["b"]
    consts = host_consts()
    in_maps = []
    for c in range(NCORES):
        m = {}
        for k in b.inputs:
            if k == "x":
                m[k] = np.ascontiguousarray(inputs["x"][c])
            elif k in consts:
                m[k] = consts[k]
            else:
                m[k] = np.ascontiguousarray(inputs[k])
        in_maps.append(m)
    res = run_bass_kernel_spmd(b.nc, in_maps, core_ids=list(range(NCORES)))
    return np.stack([np.asarray(r["out"]) for r in res.results], axis=0).astype(np.float32)
```

```python
import bisect
import contextlib
import math

import numpy as np
import concourse.bass as bass
import concourse.mybir as mybir
from concourse.bass_utils import run_bass_kernel_spmd

F32 = mybir.dt.float32
BF16 = mybir.dt.bfloat16
AF = mybir.ActivationFunctionType
ALU = mybir.AluOpType
AX = mybir.AxisListType

D = 1024
S = 4096
DFF = 2816
DEPTH = 4
NCORES = 8
EPS = 1e-6
IN_COLS = 3080
NEG = -80.0


class Op:
    __slots__ = ("eng", "fn", "dma", "deps", "marked", "cum", "seq")

    def __init__(self, eng, fn, dma):
        self.eng = eng
        self.fn = fn
        self.dma = dma
        self.deps = []
        self.marked = False
        self.cum = 0
        self.seq = 0


class Prog:
    ENGS = ["sync", "act", "dve", "pe", "pool"]
    BLK = {"sync": "sync", "act": "scalar", "dve": "vector", "pe": "tensor", "pool": "gpsimd"}
    CH = 30000

    def __init__(self, nc):
        self.nc = nc
        self.ops = []
        self.lastw = {}
        self.readers = {}
        self.last_on = {}

    @staticmethod
    def stream(o):
        return o.dma if o.dma else o.eng

    def op(self, eng, name, kw, reads=(), writes=(), dma=None):
        args = ()
        if isinstance(kw, tuple):
            args, kw = kw
        fn = (lambda e, name=name, args=args, kw=kw: getattr(e, name)(*args, **kw))
        o = Op(eng, fn, dma)
        o.seq = len(self.ops)
        deps = {}

        def add(d, raw=False):
            if d is None:
                return
            if (not d.dma) and (not o.dma) and d.eng == o.eng:
                if o.eng == "pe":
                    return
            st = self.stream(d)
            if st not in deps or deps[st].seq < d.seq:
                deps[st] = d

        for k in reads:
            add(self.lastw.get(k), raw=True)
        for k in writes:
            add(self.lastw.get(k))
            for d in self.readers.get(k, {}).values():
                add(d)
        o.deps = list(deps.values())
        for k in writes:
            self.lastw[k] = o
            self.readers[k] = {}
        for k in reads:
            self.readers.setdefault(k, {})[self.stream(o)] = o
        self.ops.append(o)
        if fn is not None:
            self.last_on[self.stream(o)] = o
        return o

    def barrier(self, final=False):
        lasts = {st: d for st, d in self.last_on.items() if final or not st.startswith("cast_")}
        keep = {k: v for k, v in self.lastw.items() if isinstance(k, tuple) and isinstance(k[0], str) and k[0].startswith("w_")}
        for e in self.ENGS:
            o = Op(e, None, None)
            o.seq = len(self.ops)
            o.deps = [d for st, d in lasts.items() if not ((not d.dma) and d.eng == e)]
            self.ops.append(o)
        self.lastw = {} if final else keep
        self.readers = {}

    def emit(self):
        nc = self.nc
        for o in self.ops:
            for d in o.deps:
                d.marked = True
        cnt = {}
        dma_seqs = {}
        for o in self.ops:
            if o.fn is None:
                continue
            if o.dma:
                cnt[o.dma] = cnt.get(o.dma, 0) + 1
                o.cum = cnt[o.dma]
                dma_seqs.setdefault(o.dma, []).append(o.seq)
            elif o.marked:
                cnt[o.eng] = cnt.get(o.eng, 0) + 1
                o.cum = cnt[o.eng]
        for k, v in cnt.items():
            if k not in self.ENGS:
                assert v * 16 < 60000, (k, v)
        with contextlib.ExitStack() as es:
            sems = {}

            def sem(name):
                if name not in sems:
                    sems[name] = es.enter_context(nc.semaphore("s_" + name))
                return sems[name]

            for k, v in cnt.items():
                if k in self.ENGS:
                    for c in range((v - 1) // self.CH + 1):
                        sem(f"{k}{c}")
                else:
                    sem(k)

            bar_seqs = [o.seq for o in self.ops if o.fn is None]
            self.partial = {}

            def resolve(d, o):
                if d.dma:
                    n = bisect.bisect_left(dma_seqs[d.dma], o.seq)
                    nb = bar_seqs[bisect.bisect_left(bar_seqs, o.seq)] if bisect.bisect_left(bar_seqs, o.seq) < len(bar_seqs) else 1 << 60
                    n_epoch = bisect.bisect_left(dma_seqs[d.dma], nb)
                    if n_epoch > n:
                        self.partial[d.dma] = self.partial.get(d.dma, 0) + 1
                    assert n >= d.cum
                    return d.dma, n * 16
                c = (d.cum - 1) // self.CH
                return f"{d.eng}{c}", (d.cum - 1) % self.CH + 1

            block = es.enter_context(nc.Block())
            for eng in self.ENGS:
                ops_e = [o for o in self.ops if o.eng == eng]

                def body(e, ops_e=ops_e):
                    waited = {}
                    for o in ops_e:
                        for d in o.deps:
                            sn, val = resolve(d, o)
                            if waited.get(sn, 0) < val:
                                e.wait_ge(sem(sn), val)
                                waited[sn] = val
                        if o.fn is None:
                            continue
                        inst = o.fn(e)
                        if o.dma:
                            inst.then_inc(sem(o.dma), 16)
                        elif o.marked:
                            c = (o.cum - 1) // self.CH
                            inst.then_inc(sem(f"{o.eng}{c}"), 1)

                getattr(block, self.BLK[eng])(body)


class Builder:
    def __init__(self, phases=None):
        self.nc = bass.Bass("TRN2", target_bir_lowering=False)
        self.P = Prog(self.nc)
        self.phases = phases
        self.inputs = {}
        self.wkeys = {}

    uid = 0
    debug = False

    def dump(self, name, ap, reads):
        if not self.debug:
            return
        o = self.nc.dram_tensor("dbg_" + name, list(ap.shape), ap.dtype, kind="ExternalOutput").ap()
        self.P.op("sync", "dma_start", dict(out=o, in_=ap), reads=list(reads), writes=[("dbg", name)], dma="dbg")

    def sb(self, name, shape, dt=F32):
        return self.nc.sbuf_tensor(f"{name}__u{self.uid}", shape, dt)

    def pp(self, name, shape, dt=F32):
        return self.nc.psum_tensor(f"{name}__u{self.uid}", shape, dt)

    def din(self, name, shape, dt=F32):
        ap = self.nc.dram_tensor(name, list(shape), dt, kind="ExternalInput").ap()
        self.inputs[name] = ap
        return ap

    def dscr(self, name, shape, dt):
        return self.nc.dram_tensor(name, list(shape), dt, kind="Internal").ap()

    def cast_w(self, src, dst, semname, key, nsplit=4):
        P = self.P
        rows, cols = src.shape
        b = cols
        for cand in range(1, 9):
            if cols % cand == 0 and cols // cand <= 1024:
                b = cols // cand
                break
        rs = rows // nsplit
        for i in range(nsplit):
            s_ap = src[i * rs:(i + 1) * rs, :].rearrange("k (a b) -> k a b", b=b)
            d_ap = dst[i * rs:(i + 1) * rs, :].rearrange("k (a b) -> k a b", b=b)
            self.wkeys.setdefault(key, [])
            kk = (key, len(self.wkeys[key]))
            self.wkeys[key].append(kk)
            P.op("pool", "dma_start", dict(out=d_ap, in_=s_ap), writes=[kk], dma=semname)

    def ffn_phase(self, es, xsrc, xdst, gnorm_ap, wg, wu, wd, wkey, ident_b):
        nc, P = self.nc, self.P
        TB = 1024
        NB = S // TB
        KT = D // 128
        MT = DFF // 128
        CG = 256
        NCG = DFF // CG
        A = lambda *a: es.enter_context(self.sb(*a))
        PS = lambda *a: es.enter_context(self.pp(*a))
        xt = [A(f"f_xt{i}", [128, 8, D], F32) for i in range(2)]
        hb = [A(f"f_hb{i}", [128, D], BF16) for i in range(2)]
        hT = A("f_hT", [128, KT, TB], BF16)
        aT = A("f_aT", [128, MT, TB], BF16)
        wdt = A("f_wd", [128, MT, D], BF16)
        wgt = [A(f"f_wg{i}", [128, KT, CG], BF16) for i in range(2)]
        wut = [A(f"f_wu{i}", [128, KT, CG], BF16) for i in range(2)]
        gn = A("f_gn", [128, D], F32)
        sq = A("f_sq", [128, D], BF16)
        ss = A("f_ss", [128, 8], F32)
        rstd = A("f_rstd", [128, 8], F32)
        sg = [A(f"f_sg{i}", [128, 512], F32) for i in range(2)]
        pT = [PS(f"f_pT{i}", [128, 8, 128], BF16) for i in range(2)]
        pg = [PS(f"f_pg{i}", [128, 512], F32) for i in range(2)]
        pu = [PS(f"f_pu{i}", [128, 512], F32) for i in range(2)]
        po = [PS(f"f_po{i}", [128, 512], F32) for i in range(2)]

        P.op("sync", "dma_start", dict(out=gn[:], in_=gnorm_ap.partition_broadcast(128)),
             writes=["f_gn"], dma="f_misc")
        wd_v = wd.rearrange("(k p) n -> p k n", p=128)
        for h in range(2):
            P.op("sync", "dma_start", dict(out=wdt[:, h * 11:(h + 1) * 11, :], in_=wd_v[:, h * 11:(h + 1) * 11, :]),
                 reads=self.wkeys[wkey], writes=["f_wd"], dma="f_misc")
        wg_v = wg.rearrange("(k p) n -> p k n", p=128)
        wu_v = wu.rearrange("(k p) n -> p k n", p=128)

        def xv(ap, b):
            return ap[b * TB:(b + 1) * TB, :].rearrange("(p t) d -> p t d", t=8)

        def load_x(b):
            P.op("sync", "dma_start", dict(out=xt[b % 2][:], in_=xv(xsrc, b)),
                 reads=[("xres", b)], writes=[("f_xt", b % 2)], dma=f"f_x{b % 2}")

        load_x(0)
        wl = 0
        for b in range(NB):
            X = xt[b % 2]
            xk = ("f_xt", b % 2)
            if b + 1 < NB:
                load_x(b + 1)
            for t in range(8):
                P.op("act", "activation", dict(out=sq[:], in_=X[:, t, :], func=AF.Square, accum_out=ss[:, t:t + 1]),
                     reads=[xk], writes=["f_sq", ("f_ss", t)])
            P.op("dve", "tensor_scalar", dict(out=rstd[:], in0=ss[:], scalar1=1.0 / D, scalar2=EPS,
                                              op0=ALU.mult, op1=ALU.add),
                 reads=[("f_ss", t) for t in range(8)], writes=["f_rstd"])
            P.op("act", "activation", dict(out=rstd[:], in_=rstd[:], func=AF.Sqrt), reads=["f_rstd"], writes=["f_rstd"])
            P.op("dve", "reciprocal", dict(out=rstd[:], in_=rstd[:]), reads=["f_rstd"], writes=["f_rstd"])
            for t in range(8):
                H = hb[t % 2]
                P.op("dve", "scalar_tensor_tensor", dict(out=H[:], in0=X[:, t, :], scalar=rstd[:, t:t + 1], in1=gn[:],
                                                         op0=ALU.mult, op1=ALU.mult),
                     reads=[xk, "f_rstd", "f_gn"], writes=[("f_hb", t % 2)])
                pt = pT[t % 2]
                for k in range(KT):
                    P.op("pe", "transpose", dict(out=pt[:, k, :], in_=H[:, k * 128:(k + 1) * 128], identity=ident_b[:]),
                         reads=[("f_hb", t % 2), "ident"], writes=[("f_pT", t % 2)])
                if t % 2 == 0:
                    P.op("dve", "tensor_copy", dict(out=hT[:, :, t * 128:(t + 1) * 128], in_=pt[:]),
                         reads=[("f_pT", t % 2)], writes=[("f_hT", t)])
                else:
                    P.op("act", "copy", dict(out=hT[:, :, t * 128:(t + 1) * 128], in_=pt[:]),
                         reads=[("f_pT", t % 2)], writes=[("f_hT", t)])
            hT_keys = [("f_hT", t) for t in range(8)]
            ev = 0
            for cg in range(NCG):
                sl = wl % 2
                wl += 1
                P.op("sync", "dma_start", dict(out=wgt[sl][:], in_=wg_v[:, :, cg * CG:(cg + 1) * CG]),
                     reads=self.wkeys[wkey], writes=[("f_wg", sl)], dma=f"f_w{sl}")
                P.op("sync", "dma_start", dict(out=wut[sl][:], in_=wu_v[:, :, cg * CG:(cg + 1) * CG]),
                     reads=self.wkeys[wkey], writes=[("f_wu", sl)], dma=f"f_w{sl}")
                for mi in range(CG // 128):
                    m = cg * (CG // 128) + mi
                    for hf in range(TB // 512):
                        q = ev % 2
                        ev += 1
                        for k in range(KT):
                            P.op("pe", "matmul", ((pg[q][:],), dict(lhsT=wgt[sl][:, k, mi * 128:(mi + 1) * 128],
                                                                   rhs=hT[:, k, hf * 512:(hf + 1) * 512],
                                                                   start=(k == 0), stop=(k == KT - 1))),
                                 reads=[("f_wg", sl)] + hT_keys, writes=[("f_pg", q)])
                        for k in range(KT):
                            P.op("pe", "matmul", ((pu[q][:],), dict(lhsT=wut[sl][:, k, mi * 128:(mi + 1) * 128],
                                                                   rhs=hT[:, k, hf * 512:(hf + 1) * 512],
                                                                   start=(k == 0), stop=(k == KT - 1))),
                                 reads=[("f_wu", sl)] + hT_keys, writes=[("f_pu", q)])
                        P.op("act", "activation", dict(out=sg[q][:], in_=pg[q][:], func=AF.Silu),
                             reads=[("f_pg", q)], writes=[("f_sg", q)])
                        P.op("dve", "tensor_tensor", dict(out=aT[:, m, hf * 512:(hf + 1) * 512], in0=sg[q][:],
                                                          in1=pu[q][:], op=ALU.mult),
                             reads=[("f_sg", q), ("f_pu", q)], writes=[("f_aT", m)])
            aT_keys = [("f_aT", m) for m in range(MT)]
            for t in range(8):
                for nh in range(2):
                    q = (t * 2 + nh) % 2
                    for k in range(MT):
                        P.op("pe", "matmul", ((po[q][:],), dict(lhsT=aT[:, k, t * 128:(t + 1) * 128],
                                                               rhs=wdt[:, k, nh * 512:(nh + 1) * 512],
                                                               start=(k == 0), stop=(k == MT - 1))),
                             reads=aT_keys + ["f_wd"], writes=[("f_po", q)])
                    P.op("dve", "scalar_tensor_tensor", dict(out=X[:, t, nh * 512:(nh + 1) * 512], in0=po[q][:],
                                                             scalar=0.5, in1=X[:, t, nh * 512:(nh + 1) * 512],
                                                             op0=ALU.mult, op1=ALU.add),
                         reads=[("f_po", q), xk], writes=[xk])
            P.op("sync", "dma_start", dict(out=xv(xdst, b), in_=X[:]), reads=[xk], writes=[("xres", b)], dma="f_st")

    def s5_phase(self, es, xsrc, xdst, L, wn, wb, ident_f, ident_b):
        nc, P = self.nc, self.P
        jj = L // 2
        A = lambda *a: es.enter_context(self.sb(*a))
        toep = A("s_toep", [128, 64, 128], BF16)
        wtr = A("s_wtr", [128, 64, 64], BF16)
        wti = A("s_wti", [128, 64, 64], BF16)
        vre = A("s_vre", [128, 32, 128], BF16)
        vim = A("s_vim", [128, 32, 128], BF16)
        dvec = A("s_dvec", [128, 64], F32)
        mcat = A("s_mcat", [128, 2, 64], F32)
        gn = A("s_gn", [128, D], F32)
        P.op("sync", "dma_start", dict(out=gn[:], in_=wn["mix_norm"][L].partition_broadcast(128)),
             writes=["s_gn"], dma="s_misc")
        with contextlib.ExitStack() as es1:
            self.s5_setup(es1, jj, wn, ident_f, toep, wtr, wti, vre, vim, dvec, mcat)
        P.barrier()
        self.s5_main(es, xsrc, xdst, jj, wb, ident_b, toep, wtr, wti, vre, vim, dvec, mcat, gn)

    def s5_setup(self, es, jj, wn, ident_f, toep, wtr, wti, vre, vim, dvec, mcat):
        nc, P = self.nc, self.P
        A = lambda n, s, d=F32: es.enter_context(self.sb(n, s, d))
        PS = lambda n, s, d=F32: es.enter_context(self.pp(n, s, d))
        kn = lambda ap: ap.tensor.name.split("__u")[0]

        def TT(out, in0, in1, op, eng="dve"):
            P.op(eng, "tensor_tensor", dict(out=out, in0=in0, in1=in1, op=op), reads=[kn(in0), kn(in1)], writes=[kn(out)])

        def TS(out, in0, s1, op0, s2=None, op1=None, eng="dve"):
            kw = dict(out=out, in0=in0, scalar1=s1, scalar2=s2, op0=op0)
            if op1 is not None:
                kw["op1"] = op1
            rd = [kn(in0)] + ([kn(s1)] if not isinstance(s1, (int, float)) else [])
            P.op(eng, "tensor_scalar", kw, reads=rd, writes=[kn(out)])

        def ACT(out, in_, func, **kw):
            P.op("act", "activation", dict(out=out, in_=in_, func=func, **kw), reads=[kn(in_)], writes=[kn(out)])

        def CP(out, in_, eng="dve"):
            P.op(eng, "tensor_copy", dict(out=out, in_=in_), reads=[kn(in_)], writes=[kn(out)])

        def DMA(out, in_, wr):
            P.op("sync", "dma_start", dict(out=out, in_=in_), writes=[wr], dma="z_ld")

        are, aim, lst = wn["s5_a_re"][jj], wn["s5_a_im"][jj], wn["s5_log_step"][jj]
        bre, bim, cre, cim, dsk = wn["s5_b_re"][jj], wn["s5_b_im"][jj], wn["s5_c_re"][jj], wn["s5_c_im"][jj], wn["s5_d"][jj]
        smask = self.consts["s5mask"]
        AR = A("z_ar", [64, 128]); AI = A("z_ai", [64, 128]); LS = A("z_ls", [64, 1])
        for h in range(2):
            DMA(AR[:, h * 64:(h + 1) * 64], are, "z_ar")
            DMA(AI[:, h * 64:(h + 1) * 64], aim, "z_ai")
        DMA(LS[:], lst.rearrange("(g o) -> g o", o=1), "z_ls")
        mask = A("z_mask", [128, 128])
        DMA(mask[:], smask, "z_mask")
        BR = A("z_br", [128, 64, 16]); BI = A("z_bi", [128, 64, 16])
        for h in range(2):
            DMA(BR[h * 64:(h + 1) * 64], bre.rearrange("g p c -> p g c"), "z_br")
            DMA(BI[h * 64:(h + 1) * 64], bim.rearrange("g p c -> p g c"), "z_bi")
        tC = [A("z_tcr", [128, 8, 128]), A("z_tci", [128, 8, 128])]
        for t, src in zip(tC, (cre, cim)):
            for h in range(2):
                DMA(t[:, :, h * 64:(h + 1) * 64], src.rearrange("(o g) c p -> (g c) o p", o=8), kn(t[:]))
        Dg = A("z_dg", [64, 16]); Dg8 = A("z_dg8", [64, 8, 16])
        DMA(Dg[:], dsk.rearrange("(g c) -> g c", c=16), "z_dg")
        CP(Dg8[:], Dg[:].unsqueeze(1).broadcast_to([64, 8, 16]))

        step = A("z_step", [64, 1])
        ACT(step[:], LS[:], AF.Exp)
        ARS = A("z_ars", [64, 128]); TH = A("z_th", [64, 128]); MAG = A("z_mag", [64, 128])
        Cc = A("z_c", [64, 128]); Sn = A("z_s", [64, 128])
        t1 = A("z_t1", [64, 128]); t2 = A("z_t2", [64, 128]); t3 = A("z_t3", [64, 128])
        TS(ARS[:], AR[:], step[:, 0:1], ALU.mult)
        TS(TH[:], AI[:], step[:, 0:1], ALU.mult)
        ACT(MAG[:], ARS[:], AF.Exp)
        ACT(Sn[:], TH[:], AF.Sin, scale=1.0 / 32)
        hpi = A("z_hpi", [64, 1])
        P.op("dve", "memset", ((hpi[:], math.pi / 2), {}), writes=["z_hpi"])
        P.op("act", "activation", dict(out=Cc[:], in_=TH[:], func=AF.Sin, scale=1.0 / 32, bias=hpi[:, 0:1]),
             reads=["z_th", "z_hpi"], writes=["z_c"])
        for it in range(5):
            TT(t1[:], Cc[:], Cc[:], ALU.mult)
            TT(t2[:], Sn[:], Sn[:], ALU.mult)
            TT(t3[:], Cc[:], Sn[:], ALU.mult)
            TT(Cc[:], t1[:], t2[:], ALU.subtract)
            TS(Sn[:], t3[:], 2.0, ALU.mult)
        PWr = A("z_pwr", [64, 16, 128]); PWi = A("z_pwi", [64, 16, 128])

        def cmul(orr, oi, ar, ai, br, bi):
            TT(t1[:], ar, br, ALU.mult)
            TT(t2[:], ai, bi, ALU.mult)
            TT(t3[:], ar, bi, ALU.mult)
            TT(orr, t1[:], t2[:], ALU.subtract)
            TT(t1[:], ai, br, ALU.mult)
            TT(oi, t3[:], t1[:], ALU.add)

        P.op("dve", "memset", ((PWr[:, 7, :], 1.0), {}), writes=["z_pwr"])
        P.op("dve", "memset", ((PWi[:, 7, :], 0.0), {}), writes=["z_pwi"])
        TT(PWr[:, 8, :], MAG[:], Cc[:], ALU.mult)
        TT(PWi[:, 8, :], MAG[:], Sn[:], ALU.mult)
        for k in range(9, 16):
            cmul(PWr[:, k, :], PWi[:, k, :], PWr[:, k - 1, :], PWi[:, k - 1, :], PWr[:, 8, :], PWi[:, 8, :])
        m2 = A("z_m2", [64, 128])
        TT(t1[:], PWr[:, 8, :], PWr[:, 8, :], ALU.mult)
        TT(t2[:], PWi[:, 8, :], PWi[:, 8, :], ALU.mult)
        TT(m2[:], t1[:], t2[:], ALU.add)
        P.op("dve", "reciprocal", dict(out=m2[:], in_=m2[:]), reads=["z_m2"], writes=["z_m2"])
        TT(PWr[:, 6, :], PWr[:, 8, :], m2[:], ALU.mult)
        TT(t1[:], PWi[:, 8, :], m2[:], ALU.mult)
        TS(PWi[:, 6, :], t1[:], -1.0, ALU.mult)
        for k in range(5, -1, -1):
            cmul(PWr[:, k, :], PWi[:, k, :], PWr[:, k + 1, :], PWi[:, k + 1, :], PWr[:, 6, :], PWi[:, 6, :])
        CRg_ = A("z_crg", [64, 128]); CIg_ = A("z_cig", [64, 128]); den = A("z_den", [64, 128]); lm1 = A("z_lm1", [64, 128])
        TT(t1[:], AR[:], AR[:], ALU.mult)
        TT(t2[:], AI[:], AI[:], ALU.mult)
        TT(den[:], t1[:], t2[:], ALU.add)
        P.op("dve", "reciprocal", dict(out=den[:], in_=den[:]), reads=["z_den"], writes=["z_den"])
        TS(lm1[:], PWr[:, 8, :], -1.0, ALU.add)
        TT(t1[:], lm1[:], AR[:], ALU.mult)
        TT(t2[:], PWi[:, 8, :], AI[:], ALU.mult)
        TT(t1[:], t1[:], t2[:], ALU.add)
        TT(CRg_[:], t1[:], den[:], ALU.mult)
        TT(t1[:], PWi[:, 8, :], AR[:], ALU.mult)
        TT(t2[:], lm1[:], AI[:], ALU.mult)
        TT(t1[:], t1[:], t2[:], ALU.subtract)
        TT(CIg_[:], t1[:], den[:], ALU.mult)

        TABr = A("z_tabr", [128, 64, 25]); TABi = A("z_tabi", [128, 64, 25])
        CRp = A("z_crp", [128, 64]); CIp = A("z_cip", [128, 64])
        pz = [PS("z_pz0", [128, 8, 64]), PS("z_pz1", [128, 8, 64])]
        slots = [7 - s for s in range(8)] + [7 + k for k in range(9)] + [14 - s for s in range(8)]
        idf64 = ident_f[0:64, 0:64]
        nb = 0
        for TAB, PW in ((TABr, PWr), (TABi, PWi)):
            for b0 in range(0, 25, 8):
                n = min(8, 25 - b0)
                pt = pz[nb % 2]
                nb += 1
                for q in range(n):
                    P.op("pe", "transpose", dict(out=pt[:, q, :], in_=PW[:, slots[b0 + q], :], identity=idf64),
                         reads=[kn(PW[:]), "ident_f"], writes=[kn(pt[:])])
                CP(TAB[:, :, b0:b0 + n].rearrange("p g s -> p s g"), pt[:, 0:n, :])
        pt = pz[nb % 2]
        P.op("pe", "transpose", dict(out=pt[:, 0, :], in_=CRg_[:], identity=idf64), reads=["z_crg", "ident_f"], writes=[kn(pt[:])])
        P.op("pe", "transpose", dict(out=pt[:, 1, :], in_=CIg_[:], identity=idf64), reads=["z_cig", "ident_f"], writes=[kn(pt[:])])
        P.op("pe", "transpose", dict(out=pt[:, 2, :], in_=Dg8[:].rearrange("g a c -> g (a c)"), identity=idf64),
             reads=["z_dg8", "ident_f"], writes=[kn(pt[:])])
        CP(CRp[:], pt[:, 0, :])
        CP(CIp[:], pt[:, 1, :])
        CP(dvec[:], pt[:, 2, :])
        for h in range(2):
            rs = slice(h * 64, (h + 1) * 64)
            mr = TABr[rs, h::2, 16]
            mi = TABi[rs, h::2, 16]
            CP(mcat[rs, 0, 0:32], mr)
            CP(mcat[rs, 0, 32:64], mr)
            TS(mcat[rs, 1, 0:32], mi, -1.0, ALU.mult)
            CP(mcat[rs, 1, 32:64], mi)
        CRg = A("z_crpg", [128, 64, 16]); CIn = A("z_cinpg", [128, 64, 16])
        pc = [PS("z_pc0", [128, 4, 128]), PS("z_pc1", [128, 4, 128])]
        nb = 0
        for t, dstC in zip(tC, (CRg, CIn)):
            for q4 in range(2):
                pt = pc[nb % 2]
                nb += 1
                for q in range(4):
                    P.op("pe", "transpose", dict(out=pt[:, q, :], in_=t[:, q4 * 4 + q, :], identity=ident_f[:]),
                         reads=[kn(t[:]), "ident_f"], writes=[kn(pt[:])])
                CP(dstC[:, q4 * 32:(q4 + 1) * 32, :].rearrange("p (a g) c -> p a (g c)", a=4), pt[:])
        TS(CIn[:], CIn[:], -1.0, ALU.mult)
        BBr = A("z_bbr", [128, 64, 16]); BBi = A("z_bbi", [128, 64, 16])
        u1 = A("z_u1", [128, 64, 16]); u2 = A("z_u2", [128, 64, 16])
        crb = CRp[:].unsqueeze(2).broadcast_to([128, 64, 16])
        cib = CIp[:].unsqueeze(2).broadcast_to([128, 64, 16])
        TT(u1[:], crb, BR[:], ALU.mult)
        TT(u2[:], cib, BI[:], ALU.mult)
        TT(BBr[:], u1[:], u2[:], ALU.subtract)
        TT(u1[:], crb, BI[:], ALU.mult)
        TT(u2[:], cib, BR[:], ALU.mult)
        TT(BBi[:], u1[:], u2[:], ALU.add)

        GH = 16
        X1 = A("z_x1", [128, GH, 8, 16]); X2 = A("z_x2", [128, GH, 8, 16])
        X3 = A("z_x3", [128, GH, 8, 16]); X4 = A("z_x4", [128, GH, 8, 16])
        T1 = A("z_T1", [128, GH, 8, 16]); T2 = A("z_T2", [128, GH, 8, 16])
        ptp = [PS("z_ptp0", [128, 4, 128]), PS("z_ptp1", [128, 4, 128])]
        pw8 = [PS("z_pw0", [128, 8, 64]), PS("z_pw1", [128, 8, 64])]

        def cprod(oa, ob, s0, Xr, Xi, g0, opa, opb, e1="dve", e2="dve"):
            pr = TABr[:, g0:g0 + GH, s0:s0 + 8].unsqueeze(3).broadcast_to([128, GH, 8, 16])
            pi = TABi[:, g0:g0 + GH, s0:s0 + 8].unsqueeze(3).broadcast_to([128, GH, 8, 16])
            xr = Xr[:, g0:g0 + GH, :].unsqueeze(2).broadcast_to([128, GH, 8, 16])
            xi = Xi[:, g0:g0 + GH, :].unsqueeze(2).broadcast_to([128, GH, 8, 16])
            TT(T1[:], pr, xr, ALU.mult, e1)
            TT(T2[:], pi, xi, ALU.mult, e1)
            TT(oa[:], T1[:], T2[:], opa, e1)
            TT(T1[:], pr, xi, ALU.mult, e2)
            TT(T2[:], pi, xr, ALU.mult, e2)
            TT(ob[:], T1[:], T2[:], opb, e2)

        nb = 0
        for gh in range(64 // GH):
            g0 = gh * GH
            cprod(X1, X2, 0, BBr, BBi, g0, ALU.subtract, ALU.add)
            cprod(X3, X4, 8, CRg, CIn, g0, ALU.add, ALU.subtract)
            for q4 in range(GH // 4):
                pt = ptp[nb % 2]
                nb += 1
                for q in range(4):
                    gl = q4 * 4 + q
                    P.op("pe", "matmul", ((pt[:, q, :],), dict(lhsT=X1[0:64, gl].rearrange("p s c -> p (s c)"),
                                                              rhs=X3[0:64, gl].rearrange("p s c -> p (s c)"),
                                                              start=True, stop=False)),
                         reads=["z_x1", "z_x3"], writes=[kn(pt[:])])
                    P.op("pe", "matmul", ((pt[:, q, :],), dict(lhsT=X2[0:64, gl].rearrange("p s c -> p (s c)"),
                                                              rhs=X4[0:64, gl].rearrange("p s c -> p (s c)"),
                                                              start=False, stop=True)),
                         reads=["z_x2", "z_x4"], writes=[kn(pt[:])])
                TT(toep[:, g0 + q4 * 4:g0 + q4 * 4 + 4, :], pt[:], mask[:].unsqueeze(1).broadcast_to([128, 4, 128]), ALU.mult)
            cprod(X1, X2, 17, BBr, BBi, g0, ALU.subtract, ALU.add)
            for Xs, wt in ((X1, wtr), (X2, wti)):
                for q8 in range(GH // 8):
                    pt = pw8[nb % 2]
                    nb += 1
                    for q in range(8):
                        gl = q8 * 8 + q
                        P.op("pe", "transpose", dict(out=pt[:, q, :], in_=Xs[0:64, gl].rearrange("p s c -> p (s c)"),
                                                     identity=idf64),
                             reads=[kn(Xs[:]), "ident_f"], writes=[kn(pt[:])])
                    CP(wt[:, g0 + q8 * 8:g0 + q8 * 8 + 8, :], pt[:])
            cprod(X3, X4, 9, CRg, CIn, g0, ALU.add, ALU.subtract)
            for Xs, vt in ((X3, vre), (X4, vim)):
                for h in range(2):
                    rs = slice(h * 64, (h + 1) * 64)
                    P.op("dve", "tensor_copy", dict(out=vt[rs, gh * (GH // 2):(gh + 1) * (GH // 2), :],
                                                     in_=Xs[rs, h::2].rearrange("p g s c -> p g (s c)")),
                         reads=[kn(Xs[:])], writes=[kn(vt[:])])

    def s5_main(self, es, xsrc, xdst, jj, wb, ident_b, toep, wtr, wti, vre, vim, dvec, mcat, gn):
        nc, P = self.nc, self.P
        A = lambda n, s, d=F32: es.enter_context(self.sb(n, s, d))
        PS = lambda n, s, d=F32: es.enter_context(self.pp(n, s, d))
        xq = A("m_xq", [128, 8, D])
        hy = A("m_hy", [128, 8192], BF16)
        uy = A("m_uy", [128, 8192], BF16)
        beta = A("m_beta", [128, 128, 64])
        XS = A("m_xs", [128, 129, 64], BF16)
        Z = [A("m_z0", [128, 2, 64]), A("m_z1", [128, 2, 64])]
        AB = A("m_ab", [128, 2, 64]); Ssum = A("m_s", [128, 64])
        wa = A("m_wa", [128, 8, 512], BF16); wbt = A("m_wb", [128, 8, 512], BF16)
        sq = A("m_sq", [128, D], BF16); ss = A("m_ss", [128, 8]); rstd = A("m_rstd", [128, 8])
        ytmp = [A(f"m_yt{i}", [128, 128]) for i in range(4)]
        yact = [A(f"m_ya{i}", [128, 128], BF16) for i in range(4)]
        sgt = A("m_sg", [128, 512]); tt = A("m_tt", [128, 512])
        pT = [PS(f"m_pT{i}", [128, 8, 128], BF16) for i in range(2)]
        pb = [PS(f"m_pb{i}", [128, 2, 128]) for i in range(2)]
        py = [PS(f"m_py{i}", [128, 128]) for i in range(2)]
        pa = PS("m_pa", [128, 512]); pbb = PS("m_pbb", [128, 512])
        hbp = hy[:].rearrange("p (g s c) -> p g s c", g=64, s=8)
        yt = hy[:].rearrange("p (s ch) -> p s ch", s=8)
        U = uy[:].rearrange("p (g j) -> p g j", g=64)
        yT = uy[:].rearrange("p (k t) -> p k t", k=8)
        wa_v = wb["s5_glu_a"][jj].rearrange("(k p) n -> p k n", p=128)
        wb_v = wb["s5_glu_b"][jj].rearrange("(k p) n -> p k n", p=128)
        wkey = f"w_s5_{jj}"

        def xv(ap, q):
            return ap[q * 1024:(q + 1) * 1024, :].rearrange("(p t) d -> p t d", t=8)

        P.op("dve", "memset", ((Z[0][:], 0.0), {}), writes=[("Z", 0)])
        P.op("dve", "memset", ((XS[:, 0, :], 0.0), {}), writes=["XS"])
        cnt = 0
        ncp = 0
        for q in range(4):
            P.op("sync", "dma_start", dict(out=xq[:], in_=xv(xsrc, q)), reads=[("xres", q)], writes=["xq"], dma="m_x")
            for t in range(8):
                P.op("act", "activation", dict(out=sq[:], in_=xq[:, t, :], func=AF.Square, accum_out=ss[:, t:t + 1]),
                     reads=["xq"], writes=["m_sq", "m_ss"])
            P.op("dve", "tensor_scalar", dict(out=rstd[:], in0=ss[:], scalar1=1.0 / D, scalar2=EPS, op0=ALU.mult, op1=ALU.add),
                 reads=["m_ss"], writes=["m_rstd"])
            P.op("act", "activation", dict(out=rstd[:], in_=rstd[:], func=AF.Sqrt), reads=["m_rstd"], writes=["m_rstd"])
            P.op("dve", "reciprocal", dict(out=rstd[:], in_=rstd[:]), reads=["m_rstd"], writes=["m_rstd"])
            for t in range(8):
                P.op("dve", "scalar_tensor_tensor", dict(out=hbp[:, :, t, :], in0=xq[:, t, :].rearrange("p (g c) -> p g c", c=16),
                                                         scalar=rstd[:, t:t + 1], in1=gn[:].rearrange("p (g c) -> p g c", c=16),
                                                         op0=ALU.mult, op1=ALU.mult),
                     reads=["xq", "m_rstd", "s_gn"], writes=["hy"])
            for g8 in range(8):
                pt = pT[g8 % 2]
                for gq in range(8):
                    g = g8 * 8 + gq
                    P.op("pe", "transpose", dict(out=pt[:, gq, :], in_=hy[:, g * 128:(g + 1) * 128], identity=ident_b[:]),
                         reads=["hy", "ident"], writes=[("m_pT", g8 % 2)])
                if g8 % 2 == 0:
                    P.op("dve", "tensor_copy", dict(out=U[:, g8 * 8:(g8 + 1) * 8, :], in_=pt[:]), reads=[("m_pT", 0)], writes=["uy"])
                else:
                    P.op("act", "copy", dict(out=U[:, g8 * 8:(g8 + 1) * 8, :], in_=pt[:]), reads=[("m_pT", 1)], writes=["uy"])
            for pr in range(32):
                pbt = pb[pr % 2]
                for ri, wt in enumerate((wtr, wti)):
                    for g2 in range(2):
                        g = 2 * pr + g2
                        P.op("pe", "matmul", ((pbt[g2 * 64:(g2 + 1) * 64, ri, :],),
                                              dict(lhsT=wt[:, g, :], rhs=U[:, g, :], start=True, stop=True)),
                             reads=["uy", "s_w"], writes=[("m_pb", pr % 2)])
                bt = beta[:, :, :]
                ov = bass.AP(tensor=bt.tensor, offset=bt.offset + pr, ap=[list(bt.ap[0]), [32, 2], [64, 128]])
                if pr % 2 == 0:
                    P.op("dve", "tensor_copy", dict(out=ov, in_=pbt[:]), reads=[("m_pb", 0)], writes=["beta"])
                else:
                    P.op("act", "copy", dict(out=ov, in_=pbt[:]), reads=[("m_pb", 1)], writes=["beta"])
            for j in range(128):
                zc, zn = Z[cnt % 2], Z[(cnt + 1) % 2]
                kc, kn_ = ("Z", cnt % 2), ("Z", (cnt + 1) % 2)
                cnt += 1
                zt = zc[:, :, :]
                win = bass.AP(tensor=zt.tensor, offset=zt.offset, ap=[list(zt.ap[0]), [32, 2], [1, 64]])
                P.op("dve", "tensor_tensor", dict(out=AB[:], in0=mcat[:], in1=win, op=ALU.mult), reads=[kc, "s_mcat"], writes=["AB"])
                P.op("dve", "tensor_tensor", dict(out=Ssum[:], in0=AB[:, 0, :], in1=AB[:, 1, :], op=ALU.add), reads=["AB"], writes=["Ssum"])
                P.op("dve", "tensor_tensor", dict(out=zn[:], in0=Ssum[:].unsqueeze(1).broadcast_to([128, 2, 64]),
                                                  in1=beta[:, j, :].unsqueeze(1).broadcast_to([128, 2, 64]), op=ALU.add),
                     reads=["Ssum", "beta"], writes=[kn_])
                P.op("act", "copy", dict(out=XS[:, j + 1, :], in_=zn[:, 0, :]), reads=[kn_], writes=["XS"])
            def grp_front(g):
                pr, g2 = g // 2, g % 2
                pyt = [py[0][:], py[1][:], pa[:, 0:128], pbb[:, 0:128]][g % 4]
                pyk = [("m_py", 0), ("m_py", 1), "m_pa", "m_pbb"][g % 4]
                rs = slice(g2 * 64, (g2 + 1) * 64)
                P.op("pe", "matmul", ((pyt,), dict(lhsT=toep[:, g, :], rhs=U[:, g, :], start=True, stop=False)),
                     reads=["uy", "s_toep"], writes=[pyk])
                P.op("pe", "matmul", ((pyt,), dict(lhsT=vre[rs, pr, :], rhs=XS[rs, 0:128, pr], start=False, stop=False)),
                     reads=["XS", "s_v"], writes=[pyk])
                P.op("pe", "matmul", ((pyt,), dict(lhsT=vim[rs, pr, :], rhs=XS[rs, 0:128, 32 + pr], start=False, stop=True)),
                     reads=["XS", "s_v"], writes=[pyk])
                P.op("dve", "scalar_tensor_tensor", dict(out=ytmp[g % 4][:], in0=U[:, g, :], scalar=dvec[:, g:g + 1], in1=pyt,
                                                         op0=ALU.mult, op1=ALU.add),
                     reads=["uy", pyk, "s_dvec"], writes=[("m_ytmp", g % 4)])
                P.op("act", "activation", dict(out=yact[g % 4][:], in_=ytmp[g % 4][:], func=AF.Gelu),
                     reads=[("m_ytmp", g % 4)], writes=[("m_yact", g % 4)])

            def grp_back(g):
                g8 = g // 8
                pt = pT[g8 % 2]
                P.op("pe", "transpose", dict(out=pt[:, g % 8, :], in_=yact[g % 4][:], identity=ident_b[:]),
                     reads=[("m_yact", g % 4), "ident"], writes=[("m_pT", g8 % 2)])
                if g % 8 == 7:
                    ov = yt[:, :, g8 * 128:(g8 + 1) * 128].rearrange("p i (g c) -> p i g c", c=16)
                    iv = pt[:].rearrange("p g (i c) -> p i g c", c=16)
                    P.op("dve", "tensor_copy", dict(out=ov, in_=iv), reads=[("m_pT", g8 % 2)], writes=["hy"])

            for g in range(64 + 2):
                if g < 64:
                    grp_front(g)
                if g >= 2:
                    grp_back(g - 2)
            P.op("act", "copy", dict(out=XS[:, 0, :], in_=XS[:, 128, :]), reads=["XS"], writes=["XS"])
            for s in range(8):
                pt = pT[s % 2]
                for k in range(8):
                    P.op("pe", "transpose", dict(out=pt[:, k, :], in_=yt[:, s, k * 128:(k + 1) * 128], identity=ident_b[:]),
                         reads=["hy", "ident"], writes=[("m_pT", s % 2)])
                if s % 2 == 0:
                    P.op("dve", "tensor_copy", dict(out=yT[:, :, s * 128:(s + 1) * 128], in_=pt[:]), reads=[("m_pT", 0)], writes=["uy"])
                else:
                    P.op("act", "copy", dict(out=yT[:, :, s * 128:(s + 1) * 128], in_=pt[:]), reads=[("m_pT", 1)], writes=["uy"])
            for nh in range(2):
                P.op("sync", "dma_start", dict(out=wa[:], in_=wa_v[:, :, nh * 512:(nh + 1) * 512]), reads=self.wkeys[wkey], writes=["m_wa"], dma="m_w")
                P.op("sync", "dma_start", dict(out=wbt[:], in_=wb_v[:, :, nh * 512:(nh + 1) * 512]), reads=self.wkeys[wkey], writes=["m_wb"], dma="m_w")
                for s in range(8):
                    for k in range(8):
                        P.op("pe", "matmul", ((pa[:],), dict(lhsT=yT[:, k, s * 128:(s + 1) * 128], rhs=wa[:, k, :],
                                                            start=(k == 0), stop=(k == 7))), reads=["uy", "m_wa"], writes=["m_pa"])
                    for k in range(8):
                        P.op("pe", "matmul", ((pbb[:],), dict(lhsT=yT[:, k, s * 128:(s + 1) * 128], rhs=wbt[:, k, :],
                                                             start=(k == 0), stop=(k == 7))), reads=["uy", "m_wb"], writes=["m_pbb"])
                    P.op("act", "activation", dict(out=sgt[:], in_=pbb[:], func=AF.Sigmoid), reads=["m_pbb"], writes=["m_sg"])
                    P.op("dve", "tensor_tensor", dict(out=tt[:], in0=sgt[:], in1=pa[:], op=ALU.mult), reads=["m_sg", "m_pa"], writes=["m_tt"])
                    xs_ = xq[:, s, nh * 512:(nh + 1) * 512]
                    P.op("dve", "tensor_tensor", dict(out=xs_, in0=xs_, in1=tt[:], op=ALU.add), reads=["m_tt", "xq"], writes=["xq"])
            P.op("sync", "dma_start", dict(out=xv(xdst, q), in_=xq[:]), reads=["xq"], writes=[("xres", q)], dma="m_st")

    def attn_phase(self, es, xsrc, xdst, L, wn, wb, ident_f, ident_b):
        nc, P = self.nc, self.P
        jj = L // 2
        lam_init = 0.8 - 0.6 * math.exp(-0.3 * L)
        A = lambda n, s, d=F32: es.enter_context(self.sb(n, s, d))
        QT, KT, HD = self.scr["QT"], self.scr["KT"], self.scr["HD"]
        win_v = wb["attn_w_in"][jj].rearrange("(k p) n -> p k n", p=128)
        wout_v = wb["attn_w_out"][jj].rearrange("(k p) n -> p k n", p=128)
        wkey = f"w_attn_{jj}"
        kn = lambda ap: ap.tensor.name.split("__u")[0]
        Vd = A("a_vd", [128, 32, 4, 129], BF16)
        Vf = A("a_vf", [128, 32, 8, 65], BF16)
        cposk = A("a_cposk", [128, 32, 8])
        P.op("dve", "memset", ((Vd[:, :, :, 128:129], 1.0), {}), writes=["Vd"])
        P.op("dve", "memset", ((Vf[:, :, :, 64:65], 1.0), {}), writes=["Vf"])

        LSP = A("a_lsp", [128, 32, 8]); R = A("a_R", [128, 33, 8])
        tri = A("a_tri", [128, 128]); onesf = A("a_onesf", [128, 128])
        with contextlib.ExitStack() as e1:
            B = lambda n, s, d=F32: e1.enter_context(self.sb(n, s, d))
            PS = lambda n, s, d=F32: e1.enter_context(self.pp(n, s, d))
            xt = [B(f"a_xt{i}", [128, 4, D]) for i in range(2)]
            hb = [B(f"a_hb{i}", [128, D], BF16) for i in range(2)]
            hT = B("a_hT", [128, 8, 512], BF16)
            win = B("a_win", [128, 8, IN_COLS], BF16)
            gn = B("a_gn", [128, D])
            sq = B("a_sq", [128, D], BF16); ss = B("a_ss", [128, 4]); rstd = B("a_rstd", [128, 4])
            qsq = [B(f"a_qsq{i}", [128, 512]) for i in range(3)]
            lnv = [B(f"a_lnv{i}", [128, 512]) for i in range(2)]
            qo = [B(f"a_qo{i}", [128, 512], BF16) for i in range(2)]
            G = B("a_G", [128, 4]); epsc = B("a_eps", [128, 1]); ones2 = B("a_ones2", [128, 128])
            fgb = B("a_fgb", [128, 8]); zt = B("a_zt", [128, 8])
            pT = PS("a_pT", [128, 8, 128], BF16)
            pq = [PS(f"a_pq{i}", [128, 512]) for i in range(3)]
            pms = PS("a_pms", [128, 512])
            pv = [PS(f"a_pv{i}", [128, 512]) for i in range(2)]
            pfl = PS("a_pfl", [128, 512])

            def DMA(out, in_, wr, rd=()):
                P.op("sync", "dma_start", dict(out=out, in_=in_), reads=list(rd), writes=[wr], dma="a_ld")

            DMA(gn[:], wn["mix_norm"][L].partition_broadcast(128), "a_gn")
            for h in range(2):
                DMA(win[:, h * 4:(h + 1) * 4, :], win_v[:, h * 4:(h + 1) * 4, :], "a_win", self.wkeys[wkey])
            for c, nm in enumerate(("diff_q_norm", "diff_k_norm", "fox_q_norm", "fox_k_norm")):
                for h in range(2):
                    DMA(G[h * 64:(h + 1) * 64, c:c + 1], wn[nm][jj].rearrange("(d o) -> d o", o=1), "a_G")
            DMA(fgb[:], wn["fg_bias"][jj].partition_broadcast(128), "a_fgb")
            DMA(ones2[:], self.consts["ones2"], "a_ones2")
            DMA(tri[:], self.consts["tri"], "a_tri")
            P.op("dve", "memset", ((epsc[:], EPS), {}), writes=["a_eps"])
            P.op("dve", "memset", ((onesf[:], 1.0), {}), writes=["a_onesf"])
            P.op("dve", "memset", ((R[:, 0, :], 0.0), {}), writes=["a_R"])
            qk_tiles = []
            for h in range(4):
                qk_tiles.append((h * 128, 0, QT, 2 * h))
            for h in range(4):
                qk_tiles.append((512 + h * 128, 1, KT, 2 * h))
            for h in range(4):
                qk_tiles.append((1536 + h * 128, 2, QT, 8 + 2 * h))
            for h in range(4):
                qk_tiles.append((2048 + h * 128, 3, KT, 8 + 2 * h))

            def xv(ap, b):
                return ap[b * 512:(b + 1) * 512, :].rearrange("(t p) d -> p t d", p=128)

            def load_x(b):
                P.op("sync", "dma_start", dict(out=xt[b % 2][:], in_=xv(xsrc, b)), reads=[("xres", b)],
                     writes=[("a_xt", b % 2)], dma=f"a_x{b % 2}")

            load_x(0)
            ev = 0
            for b in range(8):
                X = xt[b % 2]
                xk = ("a_xt", b % 2)
                if b + 1 < 8:
                    load_x(b + 1)
                for t in range(4):
                    P.op("act", "activation", dict(out=sq[:], in_=X[:, t, :], func=AF.Square, accum_out=ss[:, t:t + 1]),
                         reads=[xk], writes=["a_sq", "a_ss"])
                P.op("dve", "tensor_scalar", dict(out=rstd[:], in0=ss[:], scalar1=1.0 / D, scalar2=EPS, op0=ALU.mult, op1=ALU.add),
                     reads=["a_ss"], writes=["a_rstd"])
                P.op("act", "activation", dict(out=rstd[:], in_=rstd[:], func=AF.Sqrt), reads=["a_rstd"], writes=["a_rstd"])
                P.op("dve", "reciprocal", dict(out=rstd[:], in_=rstd[:]), reads=["a_rstd"], writes=["a_rstd"])
                for t in range(4):
                    H = hb[t % 2]
                    P.op("dve", "scalar_tensor_tensor", dict(out=H[:], in0=X[:, t, :], scalar=rstd[:, t:t + 1], in1=gn[:],
                                                             op0=ALU.mult, op1=ALU.mult),
                         reads=[xk, "a_rstd", "a_gn"], writes=[("a_hb", t % 2)])
                    for k in range(8):
                        P.op("pe", "transpose", dict(out=pT[:, k, :], in_=H[:, k * 128:(k + 1) * 128], identity=ident_b[:]),
                             reads=[("a_hb", t % 2), "ident"], writes=["a_pT"])
                    P.op("dve", "tensor_copy", dict(out=hT[:, :, t * 128:(t + 1) * 128], in_=pT[:]), reads=["a_pT"], writes=["a_hT"])
                def qk_front(i):
                    c0, gc, dstT, m0 = qk_tiles[i]
                    q = (b * 16 + i) % 3
                    for k in range(8):
                        P.op("pe", "matmul", ((pq[q][:],), dict(lhsT=win[:, k, c0:c0 + 128], rhs=hT[:, k, :],
                                                                start=(k == 0), stop=(k == 7))),
                             reads=["a_win", "a_hT"], writes=[("a_pq", q)])
                    P.op("act", "activation", dict(out=qsq[q][:], in_=pq[q][:], func=AF.Square), reads=[("a_pq", q)], writes=[("a_qsq", q)])

                def qk_back(i):
                    c0, gc, dstT, m0 = qk_tiles[i]
                    q = (b * 16 + i) % 3
                    r = (b * 16 + i) % 2
                    P.op("pe", "matmul", ((pms[:],), dict(lhsT=ones2[:], rhs=qsq[q][:], start=True, stop=True)),
                         reads=["a_ones2", ("a_qsq", q)], writes=["a_pms"])
                    P.op("act", "activation", dict(out=lnv[r][:], in_=pms[:], func=AF.Ln, bias=epsc[:, 0:1]),
                         reads=["a_pms", "a_eps"], writes=[("a_lnv", r)])
                    P.op("act", "activation", dict(out=lnv[r][:], in_=lnv[r][:], func=AF.Exp, scale=-0.5),
                         reads=[("a_lnv", r)], writes=[("a_lnv", r)])
                    P.op("dve", "scalar_tensor_tensor", dict(out=qo[r][:], in0=pq[q][:], scalar=G[:, gc:gc + 1], in1=lnv[r][:],
                                                             op0=ALU.mult, op1=ALU.mult),
                         reads=[("a_pq", q), ("a_lnv", r), "a_G"], writes=[("a_qo", r)])
                    for hh in range(2):
                        P.op("sync", "dma_start", dict(out=dstT[m0 + hh, 0:64, b * 512:(b + 1) * 512],
                                                       in_=qo[r][hh * 64:(hh + 1) * 64, :]),
                             reads=[("a_qo", r)], writes=[(kn(dstT), m0 + hh, b)], dma="a_qst")

                for i in range(17):
                    if i < 16:
                        qk_front(i)
                    if i >= 1:
                        qk_back(i - 1)
                for t in range(4):
                    blk = b * 4 + t
                    for vi, (c0, Vt, nh, vd) in enumerate(((1024, Vd, 4, 128), (2560, Vf, 8, 64))):
                        q = vi
                        for k in range(8):
                            P.op("pe", "matmul", ((pv[q][:],), dict(lhsT=hT[:, k, t * 128:(t + 1) * 128], rhs=win[:, k, c0:c0 + 512],
                                                                    start=(k == 0), stop=(k == 7))),
                                 reads=["a_win", "a_hT"], writes=[("a_pv", q)])
                        P.op("act" if vi == 0 else "dve", "copy" if vi == 0 else "tensor_copy",
                             dict(out=Vt[:, blk, :, 0:vd], in_=pv[q][:].rearrange("p (h v) -> p h v", h=nh)),
                             reads=[("a_pv", q)], writes=["Vd" if vi == 0 else "Vf"])
                    for k in range(8):
                        P.op("pe", "matmul", ((pfl[:, 0:8],), dict(lhsT=hT[:, k, t * 128:(t + 1) * 128], rhs=win[:, k, 3072:3080],
                                                                   start=(k == 0), stop=(k == 7))),
                             reads=["a_win", "a_hT"], writes=["a_pfl"])
                    P.op("dve", "tensor_tensor", dict(out=zt[:], in0=pfl[:, 0:8], in1=fgb[:], op=ALU.add),
                         reads=["a_pfl", "a_fgb"], writes=["a_zt"])
                    P.op("act", "activation", dict(out=zt[:], in_=zt[:], func=AF.Exp, scale=-1.0), reads=["a_zt"], writes=["a_zt"])
                    P.op("act", "activation", dict(out=LSP[:, blk, :], in_=zt[:], func=AF.Ln, bias=1.0), reads=["a_zt"], writes=["a_lsp"])
                    P.op("dve", "tensor_tensor", dict(out=R[:, blk + 1, :], in0=R[:, blk, :], in1=LSP[:, blk, :], op=ALU.add),
                         reads=["a_lsp", "a_R"], writes=["a_R"])
        P.barrier()
        self.uid += 1
        with contextlib.ExitStack() as e1:
            B = lambda n, s, d=F32: e1.enter_context(self.sb(n, s, d))
            PS = lambda n, s, d=F32: e1.enter_context(self.pp(n, s, d))
            cT = B("a_cT", [8, S]); rT = B("a_rT", [8, S])
            a123 = [B(f"a_a{i}", [8, S], BF16) for i in range(3)]
            onesb = B("a_onesb", [8, S], BF16)
            pv = [PS(f"a_pv{i}", [128, 512]) for i in range(2)]
            pq = [PS(f"a_pq{i}", [128, 512]) for i in range(2)]
            P.op("dve", "memset", ((onesb[:], 1.0), {}), writes=["a_onesb"])
            for blk in range(32):
                q = blk % 2
                P.op("pe", "matmul", ((pv[q][:, 0:8],), dict(lhsT=tri[:], rhs=LSP[:, blk, :], start=True, stop=False)),
                     reads=["a_tri", "a_lsp"], writes=[("a_pv", q)])
                P.op("pe", "matmul", ((pv[q][:, 0:8],), dict(lhsT=onesf[:], rhs=R[:, blk, :], start=False, stop=True)),
                     reads=["a_onesf", "a_R"], writes=[("a_pv", q)])
                P.op("dve", "tensor_copy", dict(out=cposk[:, blk, :], in_=pv[q][:, 0:8]), reads=[("a_pv", q)], writes=["cposk"])
                P.op("pe", "matmul", ((pq[q][0:8, 0:128],), dict(lhsT=LSP[:, blk, :], rhs=tri[:], start=True, stop=False)),
                     reads=["a_tri", "a_lsp"], writes=[("a_pq", q)])
                P.op("pe", "matmul", ((pq[q][0:8, 0:128],), dict(lhsT=R[:, blk, :], rhs=onesf[:], start=False, stop=True)),
                     reads=["a_onesf", "a_R"], writes=[("a_pq", q)])
                P.op("act", "activation", dict(out=cT[:, blk * 128:(blk + 1) * 128], in_=pq[q][0:8, 0:128], func=AF.Copy, scale=-8.0),
                     reads=[("a_pq", q)], writes=["a_cT"])
            P.op("dve", "tensor_copy", dict(out=a123[0][:], in_=cT[:]), reads=["a_cT"], writes=["a_a0"])
            P.op("dve", "tensor_tensor", dict(out=rT[:], in0=cT[:], in1=a123[0][:], op=ALU.subtract), reads=["a_cT", "a_a0"], writes=["a_rT"])
            P.op("dve", "tensor_copy", dict(out=a123[1][:], in_=rT[:]), reads=["a_rT"], writes=["a_a1"])
            P.op("dve", "tensor_tensor", dict(out=cT[:], in0=rT[:], in1=a123[1][:], op=ALU.subtract), reads=["a_rT", "a_a1"], writes=["a_cT"])
            P.op("dve", "tensor_copy", dict(out=a123[2][:], in_=cT[:]), reads=["a_cT"], writes=["a_a2"])
            for i in range(3):
                P.op("sync", "dma_start", dict(out=QT[8:16, 64 + i, :], in_=a123[i][:]), reads=[f"a_a{i}"],
                     writes=[("QTaug", i)], dma="a_qst")
                P.op("sync", "dma_start", dict(out=KT[8:16, 64 + i, :], in_=onesb[:]), reads=["a_onesb"],
                     writes=[("KTaug", i)], dma="a_qst")
        self.dump("lsp", LSP[:], [])
        self.dump("cposk", cposk[:], [])
        self.dump("vd", Vd[:, 0:2], [])
        self.dump("vf", Vf[:, 30:32], [])
        self.dump("qt", QT[:, :, 0:512], [])
        self.dump("kt", KT[:, :, 3584:4096], [])
        P.barrier()

        Ocat = A("b_ocat", [128, 32, D], BF16)
        with contextlib.ExitStack() as e2:
            B = lambda n, s, d=F32: e2.enter_context(self.sb(n, s, d))
            PS = lambda n, s, d=F32: e2.enter_context(self.pp(n, s, d))
            qT = [B(f"b_qT{i}", [67, S], BF16) for i in range(2)]
            kT = [B(f"b_kT{i}", [67, S], BF16) for i in range(2)]
            NPS, NPE = 4, 7
            Pe = [B(f"b_pe{i}", [128, 512], BF16) for i in range(NPE)]
            BT = B("b_BT", [128, 5, 2, 128]); BT8 = B("b_BT8", [128, 5, 2, 128])
            b31 = B("b_b31", [128, 4])
            n0 = B("b_n0", [128, 32, 128])
            rb33 = B("b_rb33", [33, 5]); rbl = B("b_rbl", [33, 128]); OH = B("b_oh", [33, 384]); hrep = B("b_hrep", [128, 384])
            lq = [B(f"b_lq{i}", [128, 64]) for i in range(4)]
            lp = B("b_lp", [128, 64]); e12 = B("b_e12", [128, 2]); nlam = B("b_nlam", [128, 1])
            SW = B("b_sw", [128, 128])
            ssqa = B("b_ssqa", [128, 32]); rinv = B("b_rinv", [128, 1]); odt = B("b_odt", [128, 128]); ssq = B("b_ssq", [128, 1]); junk = B("b_junk", [128, 128], BF16)
            ps = [PS(f"b_ps{i}", [128, 512]) for i in range(NPS)]
            po = [PS(f"b_po{i}", [128, 4, 256]) for i in range(2)]

            def DMA(out, in_, wr, rd=(), sem="b_ld"):
                P.op("sync", "dma_start", dict(out=out, in_=in_), reads=list(rd), writes=[wr], dma=sem)

            P.op("dve", "memset", ((rb33[:], 0.0), {}), writes=["b_rb33"])
            P.op("dve", "memset", ((rb33[32:33, :], NEG), {}), writes=["b_rb33"])
            DMA(rb33[0:32, 0:4], wn["rel_bias"], "b_rb33")
            DMA(OH[:], self.consts["relOH"], "b_oh")
            for i, nm in enumerate(("diff_lambda_q1", "diff_lambda_k1", "diff_lambda_q2", "diff_lambda_k2")):
                DMA(lq[i][:], wn[nm][jj].partition_broadcast(128), f"b_lq{i}")
            DMA(SW[:], wn["diff_subln"][jj].partition_broadcast(128), "b_sw")
            for h in range(5):
                P.op("dve", "tensor_copy", dict(out=rbl[:], in_=rb33[:, h:h + 1].broadcast_to([33, 128])), reads=["b_rb33"], writes=["b_rbl"])
                P.op("pe", "matmul", ((ps[0][:, 0:384],), dict(lhsT=rbl[:], rhs=OH[:], start=True, stop=True)),
                     reads=["b_rbl", "b_oh"], writes=[("b_ps", 0)])
                P.op("dve", "tensor_copy", dict(out=hrep[:], in_=ps[0][:, 0:384]), reads=[("b_ps", 0)], writes=["b_hrep"])
                if h < 4:
                    P.op("dve", "tensor_copy", dict(out=b31[:, h:h + 1], in_=hrep[:, 383:384]), reads=["b_hrep"], writes=["b_b31"])
                DMA(HD[h], hrep[:], ("HD", h), ["b_hrep"], sem="b_hd")
                hd = HD[h]
                src = bass.AP(tensor=hd.tensor, offset=hd.offset + 127, ap=[[383, 128], [128, 2], [1, 128]])
                DMA(BT[:, h, :, :], src, "b_BT", [("HD", h)], sem="b_hd2")
            for h in range(5):
                if h < 4:
                    P.op("dve", "tensor_scalar", dict(out=BT8[:, h], in0=BT[:, h], scalar1=b31[:, h:h + 1], scalar2=8.0,
                                                      op0=ALU.subtract, op1=ALU.mult), reads=["b_BT", "b_b31"], writes=["b_BT8"])
                else:
                    P.op("dve", "tensor_scalar", dict(out=BT8[:, h], in0=BT[:, h], scalar1=8.0, scalar2=None, op0=ALU.mult),
                         reads=["b_BT"], writes=["b_BT8"])
            for i in range(2):
                P.op("dve", "tensor_tensor", dict(out=lp[:], in0=lq[2 * i][:], in1=lq[2 * i + 1][:], op=ALU.mult),
                     reads=[f"b_lq{2 * i}", f"b_lq{2 * i + 1}"], writes=["b_lp"])
                P.op("dve", "tensor_reduce", dict(out=e12[:, i:i + 1], in_=lp[:], op=ALU.add, axis=AX.X), reads=["b_lp"], writes=["b_e12"])
            P.op("act", "activation", dict(out=e12[:], in_=e12[:], func=AF.Exp), reads=["b_e12"], writes=["b_e12"])
            P.op("dve", "scalar_tensor_tensor", dict(out=nlam[:], in0=e12[:, 1:2], scalar=-lam_init, in1=e12[:, 0:1],
                                                     op0=ALU.add, op1=ALU.subtract), reads=["b_e12"], writes=["b_nlam"])
            P.op("dve", "tensor_scalar", dict(out=SW[:], in0=SW[:], scalar1=1.0 - lam_init, scalar2=None, op0=ALU.mult),
                 reads=["b_sw"], writes=["b_sw"])

            maps = [(2 * h + m, "d", h, m) for h in range(4) for m in range(2)] + [(8 + f, "f", f, 0) for f in range(8)]
            steps = [(mi, mp, I, J) for mi, mp in enumerate(maps) for I in range(8) for J in range(4 * I + 4)]
            LAG = 4
            started = {}

            def front(idx):
                mi, (mapi, kind, hh, mm), I, J = steps[idx]
                sl = mi % 2
                K = 64 if kind == "d" else 67
                if I == 0 and J == 0:
                    for c4 in range(2):
                        cs = slice(c4 * 2048, (c4 + 1) * 2048)
                        DMA(qT[sl][0:K, cs], QT[mapi, 0:K, cs], ("b_qT", sl), [], sem=f"b_q{sl}")
                        DMA(kT[sl][0:K, cs], KT[mapi, 0:K, cs], ("b_kT", sl), [], sem=f"b_q{sl}")
                bth = hh if kind == "d" else 4
                qlo = max(4 * I, J)
                c0 = (qlo - 4 * I) * 128
                pst, psk = ps[idx % NPS], ("b_ps", idx % NPS)
                pet, pek = Pe[idx % NPE], ("b_pe", idx % NPE)
                P.op("pe", "matmul", ((pst[:, c0:512],), dict(lhsT=kT[sl][0:K, J * 128:(J + 1) * 128],
                                                              rhs=qT[sl][0:K, I * 512 + c0:(I + 1) * 512],
                                                              start=True, stop=True)),
                     reads=[("b_qT", sl), ("b_kT", sl)], writes=[psk])
                if kind == "d":
                    fbias, frd = b31[:, hh:hh + 1], "b_b31"
                else:
                    fbias, frd = cposk[:, J, hh:hh + 1], "cposk"
                nnear = 2 if kind == "d" else 1
                for dist in range(nnear):
                    qt = J + dist
                    if qt < qlo or qt >= 4 * I + 4:
                        continue
                    cc = (qt - 4 * I) * 128
                    P.op("dve", "tensor_tensor", dict(out=pst[:, cc:cc + 128], in0=pst[:, cc:cc + 128], in1=BT8[:, bth, dist, :],
                                                      op=ALU.add), reads=[psk, "b_BT8"], writes=[psk])
                P.op("act", "activation", dict(out=pet[:, c0:512], in_=pst[:, c0:512], func=AF.Exp, scale=0.125, bias=fbias),
                     reads=[psk, frd], writes=[pek])

            def back(idx):
                mi, (mapi, kind, hh, mm), I, J = steps[idx]
                sp = mi * 8 + I
                pot, pok = po[sp % 2], ("b_po", sp % 2)
                pet, pek = Pe[idx % NPE], ("b_pe", idx % NPE)
                Vt, vd = (Vd, 128) if kind == "d" else (Vf, 64)
                vkey = "Vd" if kind == "d" else "Vf"
                qlo = max(4 * I, J)
                for qt in range(qlo, 4 * I + 4):
                    ql = qt - 4 * I
                    cc = ql * 128
                    st = (sp, ql // 2) not in started
                    started[(sp, ql // 2)] = True
                    P.op("pe", "matmul", ((pot[:, ql, 0:vd + 1],), dict(lhsT=pet[:, cc:cc + 128], rhs=Vt[:, J, hh, 0:vd + 1],
                                                                       start=st, stop=(J == qt), skip_group_check=True)),
                         reads=[pek, vkey], writes=[pok])
                if J != 4 * I + 3:
                    return
                for ql in range(4):
                    qt = 4 * I + ql
                    P.op("dve", "reciprocal", dict(out=rinv[:], in_=pot[:, ql, vd:vd + 1]), reads=[pok], writes=["b_rinv"])
                    if kind == "f":
                        P.op("dve", "tensor_scalar", dict(out=Ocat[:, qt, 512 + hh * 64:512 + (hh + 1) * 64], in0=pot[:, ql, 0:64],
                                                          scalar1=rinv[:, 0:1], scalar2=None, op0=ALU.mult),
                             reads=[pok, "b_rinv"], writes=[("b_ocat", qt)])
                    elif mm == 0:
                        P.op("dve", "tensor_scalar", dict(out=n0[:, qt, :], in0=pot[:, ql, 0:128], scalar1=rinv[:, 0:1], scalar2=None,
                                                          op0=ALU.mult), reads=[pok, "b_rinv"], writes=["b_n0"])
                    else:
                        P.op("dve", "tensor_tensor", dict(out=rinv[:], in0=rinv[:], in1=nlam[:], op=ALU.mult),
                             reads=["b_rinv", "b_nlam"], writes=["b_rinv"])
                        P.op("dve", "scalar_tensor_tensor", dict(out=n0[:, qt, :], in0=pot[:, ql, 0:128], scalar=rinv[:, 0:1], in1=n0[:, qt, :],
                                                                 op0=ALU.mult, op1=ALU.add), reads=[pok, "b_rinv", "b_n0"], writes=["b_n0"])
                        P.op("dve", "tensor_tensor", dict(out=odt[:], in0=n0[:, qt, :], in1=n0[:, qt, :], op=ALU.mult),
                             reads=["b_n0"], writes=["b_odt"])
                        P.op("dve", "tensor_reduce", dict(out=ssqa[:, qt:qt + 1], in_=odt[:], op=ALU.add, axis=AX.X),
                             reads=["b_odt"], writes=["b_ssqa"])
                if kind == "d" and mm == 1 and I == 7:
                    P.op("dve", "tensor_scalar", dict(out=ssqa[:], in0=ssqa[:], scalar1=1.0 / 128, scalar2=EPS, op0=ALU.mult, op1=ALU.add),
                         reads=["b_ssqa"], writes=["b_ssqa"])
                    P.op("act", "activation", dict(out=ssqa[:], in_=ssqa[:], func=AF.Sqrt), reads=["b_ssqa"], writes=["b_ssqa"])
                    P.op("dve", "reciprocal", dict(out=ssqa[:], in_=ssqa[:]), reads=["b_ssqa"], writes=["b_ssqa"])
                    for qt in range(32):
                        P.op("dve", "scalar_tensor_tensor", dict(out=Ocat[:, qt, hh * 128:(hh + 1) * 128], in0=n0[:, qt, :],
                                                                 scalar=ssqa[:, qt:qt + 1], in1=SW[:], op0=ALU.mult, op1=ALU.mult),
                             reads=["b_n0", "b_ssqa", "b_sw"], writes=[("b_ocat", qt)])

            for idx in range(len(steps) + LAG):
                if idx < len(steps):
                    front(idx)
                if idx >= LAG:
                    back(idx - LAG)
            self.dump("bt", BT[:], [])
            self.dump("n0", n0[:], [])
            self.dump("nlam", nlam[:], [])
            self.dump("ocat", Ocat[:, 0:2, :], [])
            self.dump("ocat2", Ocat[:, 30:32, :], [])
        P.barrier()
        self.uid += 1
        with contextlib.ExitStack() as e3:
            B = lambda n, s, d=F32: e3.enter_context(self.sb(n, s, d))
            PS = lambda n, s, d=F32: e3.enter_context(self.pp(n, s, d))
            wout = B("b_wout", [128, 8, D], BF16)
            oT = [B(f"b_oT{i}", [128, 8, 128], BF16) for i in range(2)]
            xo = [B(f"b_xo{i}", [128, D]) for i in range(2)]
            pT = PS("b_pT", [128, 8, 128], BF16)
            po2 = PS("b_po2", [128, 512])

            def DMA(out, in_, wr, rd=(), sem="b_ld"):
                P.op("sync", "dma_start", dict(out=out, in_=in_), reads=list(rd), writes=[wr], dma=sem)

            for h in range(2):
                DMA(wout[:, h * 4:(h + 1) * 4, :], wout_v[:, h * 4:(h + 1) * 4, :], "b_wout", self.wkeys[wkey])
            def xrow(ap, blk):
                return ap[blk * 128:(blk + 1) * 128, :]
            for blk in range(32):
                sl = blk % 2
                DMA(xo[sl][:], xrow(xsrc, blk), ("b_xo", sl), [("xres", blk // 4)], sem=f"b_x{sl}")
                for k in range(8):
                    P.op("pe", "transpose", dict(out=pT[:, k, :], in_=Ocat[:, blk, k * 128:(k + 1) * 128], identity=ident_b[:]),
                         reads=[("b_ocat", blk), "ident"], writes=["b_pT"])
                P.op("act", "copy", dict(out=oT[sl][:], in_=pT[:]), reads=["b_pT"], writes=[("b_oT", sl)])
                for nh in range(2):
                    for k in range(8):
                        P.op("pe", "matmul", ((po2[:],), dict(lhsT=oT[sl][:, k, :], rhs=wout[:, k, nh * 512:(nh + 1) * 512],
                                                             start=(k == 0), stop=(k == 7))),
                             reads=[("b_oT", sl), "b_wout"], writes=["b_po2"])
                    P.op("dve", "tensor_tensor", dict(out=xo[sl][:, nh * 512:(nh + 1) * 512], in0=xo[sl][:, nh * 512:(nh + 1) * 512],
                                                      in1=po2[:], op=ALU.add), reads=["b_po2", ("b_xo", sl)], writes=[("b_xo", sl)])
                P.op("sync", "dma_start", dict(out=xrow(xdst, blk), in_=xo[sl][:]), reads=[("b_xo", sl)], writes=[("xres_o", blk)], dma="b_st")

    def build(self):
        nc, P = self.nc, self.P
        x_in = self.din("x", [S, D])
        ident = self.din("ident", [128, 128])
        wn = {}
        for nm, shp in [("ffn1_norm", [DEPTH, D]), ("ffn1_gate", [DEPTH, D, DFF]), ("ffn1_up", [DEPTH, D, DFF]),
                        ("ffn1_down", [DEPTH, DFF, D]), ("mix_norm", [DEPTH, D]), ("ffn2_norm", [DEPTH, D]),
                        ("ffn2_gate", [DEPTH, D, DFF]), ("ffn2_up", [DEPTH, D, DFF]), ("ffn2_down", [DEPTH, DFF, D])]:
            wn[nm] = self.din(nm, shp)
        for nm, shp in [("s5_a_re", [2, 64, 64]), ("s5_a_im", [2, 64, 64]), ("s5_log_step", [2, 64]),
                        ("s5_b_re", [2, 64, 64, 16]), ("s5_b_im", [2, 64, 64, 16]), ("s5_c_re", [2, 64, 16, 64]),
                        ("s5_c_im", [2, 64, 16, 64]), ("s5_d", [2, D]), ("s5_glu_a", [2, D, D]), ("s5_glu_b", [2, D, D])]:
            wn[nm] = self.din(nm, shp)
        for nm, shp in [("attn_w_in", [2, D, IN_COLS]), ("attn_w_out", [2, D, D]), ("fg_bias", [2, 8]),
                        ("diff_q_norm", [2, 64]), ("diff_k_norm", [2, 64]), ("diff_lambda_q1", [2, 64]),
                        ("diff_lambda_k1", [2, 64]), ("diff_lambda_q2", [2, 64]), ("diff_lambda_k2", [2, 64]),
                        ("diff_subln", [2, 128]), ("fox_q_norm", [2, 64]), ("fox_k_norm", [2, 64]), ("rel_bias", [32, 4])]:
            wn[nm] = self.din(nm, shp)
        self.consts = {k: self.din(k, list(v.shape)) for k, v in host_consts().items() if k != "ident"}
        self.scr = {"QT": self.dscr("QT", [16, 67, S], BF16), "KT": self.dscr("KT", [16, 67, S], BF16),
                    "HD": self.dscr("HD", [5, 128, 384], F32)}
        out = nc.dram_tensor("out", [S, D], F32, kind="ExternalOutput").ap()
        xres = self.dscr("xres", [S, D], F32)
        wb = {}
        for f in ("ffn1", "ffn2"):
            wb[f + "_gate"] = self.dscr(f + "_gate_b", [DEPTH, D, DFF], BF16)
            wb[f + "_up"] = self.dscr(f + "_up_b", [DEPTH, D, DFF], BF16)
            wb[f + "_down"] = self.dscr(f + "_down_b", [DEPTH, DFF, D], BF16)
        wb["attn_w_in"] = self.dscr("attn_w_in_b", [2, D, IN_COLS], BF16)
        wb["attn_w_out"] = self.dscr("attn_w_out_b", [2, D, D], BF16)
        wb["s5_glu_a"] = self.dscr("s5_glu_a_b", [2, D, D], BF16)
        wb["s5_glu_b"] = self.dscr("s5_glu_b_b", [2, D, D], BF16)

        with contextlib.ExitStack() as es0:
            ident_f = es0.enter_context(self.sb("ident_f", [128, 128], F32))
            ident_b = es0.enter_context(self.sb("ident_b", [128, 128], BF16))
            P.op("sync", "dma_start", dict(out=ident_f[:], in_=ident), writes=["ident_f"], dma="c_misc")
            P.op("dve", "tensor_copy", dict(out=ident_b[:], in_=ident_f[:]), reads=["ident_f"], writes=["ident"])
            need = set(k for k, _ in (self.phases or [("ffn1", 0), ("ffn2", 0), ("s5", 1), ("attn", 0)]))
            for L in range(DEPTH):
                for f in ("ffn1", "ffn2"):
                    if f == "ffn2":
                        if L % 2 == 0 and "attn" in need:
                            for w in ("attn_w_in", "attn_w_out"):
                                self.cast_w(wn[w][L // 2], wb[w][L // 2], f"cast_attn_{L // 2}", f"w_attn_{L // 2}")
                        if L % 2 == 1 and "s5" in need:
                            for w in ("s5_glu_a", "s5_glu_b"):
                                self.cast_w(wn[w][L // 2], wb[w][L // 2], f"cast_s5_{L // 2}", f"w_s5_{L // 2}")
                    if f not in need:
                        continue
                    for w in ("gate", "up", "down"):
                        self.cast_w(wn[f"{f}_{w}"][L], wb[f"{f}_{w}"][L], f"cast_{f}_{L}", f"w_{f}_{L}")
            phases = self.phases
            if phases is None:
                phases = []
                for L in range(DEPTH):
                    phases += [("ffn1", L), ("attn" if L % 2 == 0 else "s5", L), ("ffn2", L)]
            src = x_in
            for i, (kind, L) in enumerate(phases):
                dst = out if i == len(phases) - 1 else xres
                P.barrier()
                self.uid += 1
                with contextlib.ExitStack() as es:
                    if kind in ("ffn1", "ffn2"):
                        f = kind
                        self.ffn_phase(es, src, dst, wn[f + "_norm"][L], wb[f + "_gate"][L], wb[f + "_up"][L],
                                       wb[f + "_down"][L], f"w_{f}_{L}", ident_b)
                    elif kind == "s5":
                        self.s5_phase(es, src, dst, L, wn, wb, ident_f, ident_b)
                    elif kind == "attn":
                        self.attn_phase(es, src, dst, L, wn, wb, ident_f, ident_b)
                src = xres
            P.barrier(final=True)
            P.emit()
        return nc


def host_consts():
    idx = np.arange(128) // 16
    s5mask = (idx[None, :] >= idx[:, None]).astype(np.float32)
    n = np.arange(384) - 127
    nn = np.maximum(n, 0)
    nf = np.maximum(nn, 1).astype(np.float32)
    large = 16 + (np.log(nf / np.float32(16)) / np.float32(math.log(128 / 16)) * np.float32(16)).astype(np.int32)
    large = np.minimum(large, 31)
    bucket = np.where(nn < 16, nn, large)
    oh = np.zeros((33, 384), np.float32)
    for i in range(384):
        if n[i] >= 0:
            oh[bucket[i], i] = 1.0
        else:
            oh[32, i] = 1.0
    ones2 = np.zeros((128, 128), np.float32)
    ones2[:64, :64] = 1.0 / 64
    ones2[64:, 64:] = 1.0 / 64
    tri = (np.arange(128)[:, None] <= np.arange(128)[None, :]).astype(np.float32)
    return {"ident": np.eye(128, dtype=np.float32), "s5mask": s5mask, "relOH": oh, "ones2": ones2, "tri": tri}


_CACHE = {}


def kernel(**inputs):
    if "b" not in _CACHE:
        b = Builder()
        b.build()
        _CACHE["b"] = b
    b = _CACHE["b"]
    consts = host_consts()
    in_maps = []
    for c in range(NCORES):
        m = {}
        for k in b.inputs:
            if k == "x":
                m[k] = np.ascontiguousarray(inputs["x"][c])
            elif k in consts:
                m[k] = consts[k]
            else:
                m[k] = np.ascontiguousarray(inputs[k])
        in_maps.append(m)
    res = run_bass_kernel_spmd(b.nc, in_maps, core_ids=list(range(NCORES)))
    return np.stack([np.asarray(r["out"]) for r in res.results], axis=0).astype(np.float32)
```

```python
import bisect
import contextlib
import math

import numpy as np
import concourse.bass as bass
import concourse.mybir as mybir
from concourse.bass_utils import run_bass_kernel_spmd

F32 = mybir.dt.float32
BF16 = mybir.dt.bfloat16
AF = mybir.ActivationFunctionType
ALU = mybir.AluOpType
AX = mybir.AxisListType

D = 1024
S = 4096
DFF = 2816
DEPTH = 4
NCORES = 8
EPS = 1e-6
IN_COLS = 3080
NEG = -80.0


class Op:
    __slots__ = ("eng", "fn", "dma", "deps", "marked", "cum", "seq")

    def __init__(self, eng, fn, dma):
        self.eng = eng
        self.fn = fn
        self.dma = dma
        self.deps = []
        self.marked = False
        self.cum = 0
        self.seq = 0


class Prog:
    ENGS = ["sync", "act", "dve", "pe", "pool"]
    BLK = {"sync": "sync", "act": "scalar", "dve": "vector", "pe": "tensor", "pool": "gpsimd"}
    CH = 30000

    def __init__(self, nc):
        self.nc = nc
        self.ops = []
        self.lastw = {}
        self.readers = {}
        self.last_on = {}

    @staticmethod
    def stream(o):
        return o.dma if o.dma else o.eng

    def op(self, eng, name, kw, reads=(), writes=(), dma=None):
        args = ()
        if isinstance(kw, tuple):
            args, kw = kw
        fn = (lambda e, name=name, args=args, kw=kw: getattr(e, name)(*args, **kw))
        o = Op(eng, fn, dma)
        o.seq = len(self.ops)
        deps = {}

        def add(d, raw=False):
            if d is None:
                return
            if (not d.dma) and (not o.dma) and d.eng == o.eng:
                if o.eng == "pe":
                    return
            st = self.stream(d)
            if st not in deps or deps[st].seq < d.seq:
                deps[st] = d

        for k in reads:
            add(self.lastw.get(k), raw=True)
        for k in writes:
            add(self.lastw.get(k))
            for d in self.readers.get(k, {}).values():
                add(d)
        o.deps = list(deps.values())
        for k in writes:
            self.lastw[k] = o
            self.readers[k] = {}
        for k in reads:
            self.readers.setdefault(k, {})[self.stream(o)] = o
        self.ops.append(o)
        if fn is not None:
            self.last_on[self.stream(o)] = o
        return o

    def barrier(self, final=False):
        lasts = {st: d for st, d in self.last_on.items() if final or not st.startswith("cast_")}
        keep = {k: v for k, v in self.lastw.items() if isinstance(k, tuple) and isinstance(k[0], str) and k[0].startswith("w_")}
        for e in self.ENGS:
            o = Op(e, None, None)
            o.seq = len(self.ops)
            o.deps = [d for st, d in lasts.items() if not ((not d.dma) and d.eng == e)]
            self.ops.append(o)
        self.lastw = {} if final else keep
        self.readers = {}

    def emit(self):
        nc = self.nc
        for o in self.ops:
            for d in o.deps:
                d.marked = True
        cnt = {}
        dma_seqs = {}
        for o in self.ops:
            if o.fn is None:
                continue
            if o.dma:
                cnt[o.dma] = cnt.get(o.dma, 0) + 1
                o.cum = cnt[o.dma]
                dma_seqs.setdefault(o.dma, []).append(o.seq)
            elif o.marked:
                cnt[o.eng] = cnt.get(o.eng, 0) + 1
                o.cum = cnt[o.eng]
        for k, v in cnt.items():
            if k not in self.ENGS:
                assert v * 16 < 60000, (k, v)
        with contextlib.ExitStack() as es:
            sems = {}

            def sem(name):
                if name not in sems:
                    sems[name] = es.enter_context(nc.semaphore("s_" + name))
                return sems[name]

            for k, v in cnt.items():
                if k in self.ENGS:
                    for c in range((v - 1) // self.CH + 1):
                        sem(f"{k}{c}")
                else:
                    sem(k)

            bar_seqs = [o.seq for o in self.ops if o.fn is None]
            self.partial = {}

            def resolve(d, o):
                if d.dma:
                    n = bisect.bisect_left(dma_seqs[d.dma], o.seq)
                    nb = bar_seqs[bisect.bisect_left(bar_seqs, o.seq)] if bisect.bisect_left(bar_seqs, o.seq) < len(bar_seqs) else 1 << 60
                    n_epoch = bisect.bisect_left(dma_seqs[d.dma], nb)
                    if n_epoch > n:
                        self.partial[d.dma] = self.partial.get(d.dma, 0) + 1
                    assert n >= d.cum
                    return d.dma, n * 16
                c = (d.cum - 1) // self.CH
                return f"{d.eng}{c}", (d.cum - 1) % self.CH + 1

            block = es.enter_context(nc.Block())
            for eng in self.ENGS:
                ops_e = [o for o in self.ops if o.eng == eng]

                def body(e, ops_e=ops_e):
                    waited = {}
                    for o in ops_e:
                        for d in o.deps:
                            sn, val = resolve(d, o)
                            if waited.get(sn, 0) < val:
                                e.wait_ge(sem(sn), val)
                                waited[sn] = val
                        if o.fn is None:
                            continue
                        inst = o.fn(e)
                        if o.dma:
                            inst.then_inc(sem(o.dma), 16)
                        elif o.marked:
                            c = (o.cum - 1) // self.CH
                            inst.then_inc(sem(f"{o.eng}{c}"), 1)

                getattr(block, self.BLK[eng])(body)


class Builder:
    def __init__(self, phases=None):
        self.nc = bass.Bass("TRN2", target_bir_lowering=False)
        self.P = Prog(self.nc)
        self.phases = phases
        self.inputs = {}
        self.wkeys = {}

    uid = 0
    debug = False

    def dump(self, name, ap, reads):
        if not self.debug:
            return
        o = self.nc.dram_tensor("dbg_" + name, list(ap.shape), ap.dtype, kind="ExternalOutput").ap()
        self.P.op("sync", "dma_start", dict(out=o, in_=ap), reads=list(reads), writes=[("dbg", name)], dma="dbg")

    def sb(self, name, shape, dt=F32):
        return self.nc.sbuf_tensor(f"{name}__u{self.uid}", shape, dt)

    def pp(self, name, shape, dt=F32):
        return self.nc.psum_tensor(f"{name}__u{self.uid}", shape, dt)

    def din(self, name, shape, dt=F32):
        ap = self.nc.dram_tensor(name, list(shape), dt, kind="ExternalInput").ap()
        self.inputs[name] = ap
        return ap

    def dscr(self, name, shape, dt):
        return self.nc.dram_tensor(name, list(shape), dt, kind="Internal").ap()

    def cast_w(self, src, dst, semname, key, nsplit=4):
        P = self.P
        rows, cols = src.shape
        b = cols
        for cand in range(1, 9):
            if cols % cand == 0 and cols // cand <= 1024:
                b = cols // cand
                break
        rs = rows // nsplit
        for i in range(nsplit):
            s_ap = src[i * rs:(i + 1) * rs, :].rearrange("k (a b) -> k a b", b=b)
            d_ap = dst[i * rs:(i + 1) * rs, :].rearrange("k (a b) -> k a b", b=b)
            self.wkeys.setdefault(key, [])
            kk = (key, len(self.wkeys[key]))
            self.wkeys[key].append(kk)
            P.op("pool", "dma_start", dict(out=d_ap, in_=s_ap), writes=[kk], dma=semname)

    def ffn_phase(self, es, xsrc, xdst, gnorm_ap, wg, wu, wd, wkey, ident_b):
        nc, P = self.nc, self.P
        TB = 1024
        NB = S // TB
        KT = D // 128
        MT = DFF // 128
        CG = 256
        NCG = DFF // CG
        A = lambda *a: es.enter_context(self.sb(*a))
        PS = lambda *a: es.enter_context(self.pp(*a))
        xt = [A(f"f_xt{i}", [128, 8, D], F32) for i in range(2)]
        hb = [A(f"f_hb{i}", [128, D], BF16) for i in range(2)]
        hT = A("f_hT", [128, KT, TB], BF16)
        aT = A("f_aT", [128, MT, TB], BF16)
        wdt = A("f_wd", [128, MT, D], BF16)
        wgt = [A(f"f_wg{i}", [128, KT, CG], BF16) for i in range(2)]
        wut = [A(f"f_wu{i}", [128, KT, CG], BF16) for i in range(2)]
        gn = A("f_gn", [128, D], F32)
        sq = A("f_sq", [128, D], BF16)
        ss = A("f_ss", [128, 8], F32)
        rstd = A("f_rstd", [128, 8], F32)
        sg = [A(f"f_sg{i}", [128, 512], F32) for i in range(2)]
        pT = [PS(f"f_pT{i}", [128, 8, 128], BF16) for i in range(2)]
        pg = [PS(f"f_pg{i}", [128, 512], F32) for i in range(2)]
        pu = [PS(f"f_pu{i}", [128, 512], F32) for i in range(2)]
        po = [PS(f"f_po{i}", [128, 512], F32) for i in range(2)]

        P.op("sync", "dma_start", dict(out=gn[:], in_=gnorm_ap.partition_broadcast(128)),
             writes=["f_gn"], dma="f_misc")
        wd_v = wd.rearrange("(k p) n -> p k n", p=128)
        for h in range(2):
            P.op("sync", "dma_start", dict(out=wdt[:, h * 11:(h + 1) * 11, :], in_=wd_v[:, h * 11:(h + 1) * 11, :]),
                 reads=self.wkeys[wkey + "_down"], writes=["f_wd"], dma="f_misc")
        wg_v = wg.rearrange("(k p) n -> p k n", p=128)
        wu_v = wu.rearrange("(k p) n -> p k n", p=128)

        def xv(ap, b):
            return ap[b * TB:(b + 1) * TB, :].rearrange("(p t) d -> p t d", t=8)

        def load_x(b):
            P.op("sync", "dma_start", dict(out=xt[b % 2][:], in_=xv(xsrc, b)),
                 reads=[("xres", b)], writes=[("f_xt", b % 2)], dma=f"f_x{b % 2}")

        load_x(0)
        wl = 0
        for b in range(NB):
            X = xt[b % 2]
            xk = ("f_xt", b % 2)
            if b + 1 < NB:
                load_x(b + 1)
            for t in range(8):
                P.op("act", "activation", dict(out=sq[:], in_=X[:, t, :], func=AF.Square, accum_out=ss[:, t:t + 1]),
                     reads=[xk], writes=["f_sq", ("f_ss", t)])
            P.op("dve", "tensor_scalar", dict(out=rstd[:], in0=ss[:], scalar1=1.0 / D, scalar2=EPS,
                                              op0=ALU.mult, op1=ALU.add),
                 reads=[("f_ss", t) for t in range(8)], writes=["f_rstd"])
            P.op("act", "activation", dict(out=rstd[:], in_=rstd[:], func=AF.Sqrt), reads=["f_rstd"], writes=["f_rstd"])
            P.op("dve", "reciprocal", dict(out=rstd[:], in_=rstd[:]), reads=["f_rstd"], writes=["f_rstd"])
            for t in range(8):
                H = hb[t % 2]
                P.op("dve", "scalar_tensor_tensor", dict(out=H[:], in0=X[:, t, :], scalar=rstd[:, t:t + 1], in1=gn[:],
                                                         op0=ALU.mult, op1=ALU.mult),
                     reads=[xk, "f_rstd", "f_gn"], writes=[("f_hb", t % 2)])
                pt = pT[t % 2]
                for k in range(KT):
                    P.op("pe", "transpose", dict(out=pt[:, k, :], in_=H[:, k * 128:(k + 1) * 128], identity=ident_b[:]),
                         reads=[("f_hb", t % 2), "ident"], writes=[("f_pT", t % 2)])
                if t % 2 == 0:
                    P.op("dve", "tensor_copy", dict(out=hT[:, :, t * 128:(t + 1) * 128], in_=pt[:]),
                         reads=[("f_pT", t % 2)], writes=[("f_hT", t)])
                else:
                    P.op("act", "copy", dict(out=hT[:, :, t * 128:(t + 1) * 128], in_=pt[:]),
                         reads=[("f_pT", t % 2)], writes=[("f_hT", t)])
            hT_keys = [("f_hT", t) for t in range(8)]
            ev = 0
            for cg in range(NCG):
                sl = wl % 2
                wl += 1
                P.op("sync", "dma_start", dict(out=wgt[sl][:], in_=wg_v[:, :, cg * CG:(cg + 1) * CG]),
                     reads=self.wkeys[wkey + "_gate"], writes=[("f_wg", sl)], dma=f"f_w{sl}")
                P.op("sync", "dma_start", dict(out=wut[sl][:], in_=wu_v[:, :, cg * CG:(cg + 1) * CG]),
                     reads=self.wkeys[wkey + "_up"], writes=[("f_wu", sl)], dma=f"f_w{sl}")
                for mi in range(CG // 128):
                    m = cg * (CG // 128) + mi
                    for hf in range(TB // 512):
                        q = ev % 2
                        ev += 1
                        for k in range(KT):
                            P.op("pe", "matmul", ((pg[q][:],), dict(lhsT=wgt[sl][:, k, mi * 128:(mi + 1) * 128],
                                                                   rhs=hT[:, k, hf * 512:(hf + 1) * 512],
                                                                   start=(k == 0), stop=(k == KT - 1))),
                                 reads=[("f_wg", sl)] + hT_keys, writes=[("f_pg", q)])
                        for k in range(KT):
                            P.op("pe", "matmul", ((pu[q][:],), dict(lhsT=wut[sl][:, k, mi * 128:(mi + 1) * 128],
                                                                   rhs=hT[:, k, hf * 512:(hf + 1) * 512],
                                                                   start=(k == 0), stop=(k == KT - 1))),
                                 reads=[("f_wu", sl)] + hT_keys, writes=[("f_pu", q)])
                        P.op("act", "activation", dict(out=sg[q][:], in_=pg[q][:], func=AF.Silu),
                             reads=[("f_pg", q)], writes=[("f_sg", q)])
                        P.op("dve", "tensor_tensor", dict(out=aT[:, m, hf * 512:(hf + 1) * 512], in0=sg[q][:],
                                                          in1=pu[q][:], op=ALU.mult),
                             reads=[("f_sg", q), ("f_pu", q)], writes=[("f_aT", m)])
            aT_keys = [("f_aT", m) for m in range(MT)]
            for t in range(8):
                for nh in range(2):
                    q = (t * 2 + nh) % 2
                    for k in range(MT):
                        P.op("pe", "matmul", ((po[q][:],), dict(lhsT=aT[:, k, t * 128:(t + 1) * 128],
                                                               rhs=wdt[:, k, nh * 512:(nh + 1) * 512],
                                                               start=(k == 0), stop=(k == MT - 1))),
                             reads=aT_keys + ["f_wd"], writes=[("f_po", q)])
                    P.op("dve", "scalar_tensor_tensor", dict(out=X[:, t, nh * 512:(nh + 1) * 512], in0=po[q][:],
                                                             scalar=0.5, in1=X[:, t, nh * 512:(nh + 1) * 512],
                                                             op0=ALU.mult, op1=ALU.add),
                         reads=[("f_po", q), xk], writes=[xk])
            P.op("sync", "dma_start", dict(out=xv(xdst, b), in_=X[:]), reads=[xk], writes=[("xres", b)], dma="f_st")

    def s5_phase(self, es, xsrc, xdst, L, wn, wb, ident_f, ident_b):
        nc, P = self.nc, self.P
        jj = L // 2
        A = lambda *a: es.enter_context(self.sb(*a))
        toep = A("s_toep", [128, 64, 128], BF16)
        wtr = A("s_wtr", [128, 64, 64], BF16)
        wti = A("s_wti", [128, 64, 64], BF16)
        vre = A("s_vre", [128, 32, 128], BF16)
        vim = A("s_vim", [128, 32, 128], BF16)
        dvec = A("s_dvec", [128, 64], F32)
        mcat = A("s_mcat", [128, 2, 64], F32)
        gn = A("s_gn", [128, D], F32)
        P.op("sync", "dma_start", dict(out=gn[:], in_=wn["mix_norm"][L].partition_broadcast(128)),
             writes=["s_gn"], dma="s_misc")
        with contextlib.ExitStack() as es1:
            self.s5_setup(es1, jj, wn, ident_f, toep, wtr, wti, vre, vim, dvec, mcat)
        P.barrier()
        self.s5_main(es, xsrc, xdst, jj, wb, ident_b, toep, wtr, wti, vre, vim, dvec, mcat, gn)

    def s5_setup(self, es, jj, wn, ident_f, toep, wtr, wti, vre, vim, dvec, mcat):
        nc, P = self.nc, self.P
        A = lambda n, s, d=F32: es.enter_context(self.sb(n, s, d))
        PS = lambda n, s, d=F32: es.enter_context(self.pp(n, s, d))
        kn = lambda ap: ap.tensor.name.split("__u")[0]

        def TT(out, in0, in1, op, eng="dve"):
            P.op(eng, "tensor_tensor", dict(out=out, in0=in0, in1=in1, op=op), reads=[kn(in0), kn(in1)], writes=[kn(out)])

        def TS(out, in0, s1, op0, s2=None, op1=None, eng="dve"):
            kw = dict(out=out, in0=in0, scalar1=s1, scalar2=s2, op0=op0)
            if op1 is not None:
                kw["op1"] = op1
            rd = [kn(in0)] + ([kn(s1)] if not isinstance(s1, (int, float)) else [])
            P.op(eng, "tensor_scalar", kw, reads=rd, writes=[kn(out)])

        def ACT(out, in_, func, **kw):
            P.op("act", "activation", dict(out=out, in_=in_, func=func, **kw), reads=[kn(in_)], writes=[kn(out)])

        def CP(out, in_, eng="dve"):
            P.op(eng, "tensor_copy", dict(out=out, in_=in_), reads=[kn(in_)], writes=[kn(out)])

        def DMA(out, in_, wr):
            P.op("sync", "dma_start", dict(out=out, in_=in_), writes=[wr], dma="z_ld")

        are, aim, lst = wn["s5_a_re"][jj], wn["s5_a_im"][jj], wn["s5_log_step"][jj]
        bre, bim, cre, cim, dsk = wn["s5_b_re"][jj], wn["s5_b_im"][jj], wn["s5_c_re"][jj], wn["s5_c_im"][jj], wn["s5_d"][jj]
        smask = self.consts["s5mask"]
        AR = A("z_ar", [64, 128]); AI = A("z_ai", [64, 128]); LS = A("z_ls", [64, 1])
        for h in range(2):
            DMA(AR[:, h * 64:(h + 1) * 64], are, "z_ar")
            DMA(AI[:, h * 64:(h + 1) * 64], aim, "z_ai")
        DMA(LS[:], lst.rearrange("(g o) -> g o", o=1), "z_ls")
        mask = A("z_mask", [128, 128])
        DMA(mask[:], smask, "z_mask")
        BR = A("z_br", [128, 64, 16]); BI = A("z_bi", [128, 64, 16])
        for h in range(2):
            DMA(BR[h * 64:(h + 1) * 64], bre.rearrange("g p c -> p g c"), "z_br")
            DMA(BI[h * 64:(h + 1) * 64], bim.rearrange("g p c -> p g c"), "z_bi")
        tC = [A("z_tcr", [128, 8, 128]), A("z_tci", [128, 8, 128])]
        for t, src in zip(tC, (cre, cim)):
            for h in range(2):
                DMA(t[:, :, h * 64:(h + 1) * 64], src.rearrange("(o g) c p -> (g c) o p", o=8), kn(t[:]))
        Dg = A("z_dg", [64, 16]); Dg8 = A("z_dg8", [64, 8, 16])
        DMA(Dg[:], dsk.rearrange("(g c) -> g c", c=16), "z_dg")
        CP(Dg8[:], Dg[:].unsqueeze(1).broadcast_to([64, 8, 16]))

        step = A("z_step", [64, 1])
        ACT(step[:], LS[:], AF.Exp)
        ARS = A("z_ars", [64, 128]); TH = A("z_th", [64, 128]); MAG = A("z_mag", [64, 128])
        Cc = A("z_c", [64, 128]); Sn = A("z_s", [64, 128])
        t1 = A("z_t1", [64, 128]); t2 = A("z_t2", [64, 128]); t3 = A("z_t3", [64, 128])
        TS(ARS[:], AR[:], step[:, 0:1], ALU.mult)
        TS(TH[:], AI[:], step[:, 0:1], ALU.mult)
        ACT(MAG[:], ARS[:], AF.Exp)
        ACT(Sn[:], TH[:], AF.Sin, scale=1.0 / 32)
        hpi = A("z_hpi", [64, 1])
        P.op("dve", "memset", ((hpi[:], math.pi / 2), {}), writes=["z_hpi"])
        P.op("act", "activation", dict(out=Cc[:], in_=TH[:], func=AF.Sin, scale=1.0 / 32, bias=hpi[:, 0:1]),
             reads=["z_th", "z_hpi"], writes=["z_c"])
        for it in range(5):
            TT(t1[:], Cc[:], Cc[:], ALU.mult)
            TT(t2[:], Sn[:], Sn[:], ALU.mult)
            TT(t3[:], Cc[:], Sn[:], ALU.mult)
            TT(Cc[:], t1[:], t2[:], ALU.subtract)
            TS(Sn[:], t3[:], 2.0, ALU.mult)
        PWr = A("z_pwr", [64, 16, 128]); PWi = A("z_pwi", [64, 16, 128])

        def cmul(orr, oi, ar, ai, br, bi):
            TT(t1[:], ar, br, ALU.mult)
            TT(t2[:], ai, bi, ALU.mult)
            TT(t3[:], ar, bi, ALU.mult)
            TT(orr, t1[:], t2[:], ALU.subtract)
            TT(t1[:], ai, br, ALU.mult)
            TT(oi, t3[:], t1[:], ALU.add)

        P.op("dve", "memset", ((PWr[:, 7, :], 1.0), {}), writes=["z_pwr"])
        P.op("dve", "memset", ((PWi[:, 7, :], 0.0), {}), writes=["z_pwi"])
        TT(PWr[:, 8, :], MAG[:], Cc[:], ALU.mult)
        TT(PWi[:, 8, :], MAG[:], Sn[:], ALU.mult)
        for k in range(9, 16):
            cmul(PWr[:, k, :], PWi[:, k, :], PWr[:, k - 1, :], PWi[:, k - 1, :], PWr[:, 8, :], PWi[:, 8, :])
        m2 = A("z_m2", [64, 128])
        TT(t1[:], PWr[:, 8, :], PWr[:, 8, :], ALU.mult)
        TT(t2[:], PWi[:, 8, :], PWi[:, 8, :], ALU.mult)
        TT(m2[:], t1[:], t2[:], ALU.add)
        P.op("dve", "reciprocal", dict(out=m2[:], in_=m2[:]), reads=["z_m2"], writes=["z_m2"])
        TT(PWr[:, 6, :], PWr[:, 8, :], m2[:], ALU.mult)
        TT(t1[:], PWi[:, 8, :], m2[:], ALU.mult)
        TS(PWi[:, 6, :], t1[:], -1.0, ALU.mult)
        for k in range(5, -1, -1):
            cmul(PWr[:, k, :], PWi[:, k, :], PWr[:, k + 1, :], PWi[:, k + 1, :], PWr[:, 6, :], PWi[:, 6, :])
        CRg_ = A("z_crg", [64, 128]); CIg_ = A("z_cig", [64, 128]); den = A("z_den", [64, 128]); lm1 = A("z_lm1", [64, 128])
        TT(t1[:], AR[:], AR[:], ALU.mult)
        TT(t2[:], AI[:], AI[:], ALU.mult)
        TT(den[:], t1[:], t2[:], ALU.add)
        P.op("dve", "reciprocal", dict(out=den[:], in_=den[:]), reads=["z_den"], writes=["z_den"])
        TS(lm1[:], PWr[:, 8, :], -1.0, ALU.add)
        TT(t1[:], lm1[:], AR[:], ALU.mult)
        TT(t2[:], PWi[:, 8, :], AI[:], ALU.mult)
        TT(t1[:], t1[:], t2[:], ALU.add)
        TT(CRg_[:], t1[:], den[:], ALU.mult)
        TT(t1[:], PWi[:, 8, :], AR[:], ALU.mult)
        TT(t2[:], lm1[:], AI[:], ALU.mult)
        TT(t1[:], t1[:], t2[:], ALU.subtract)
        TT(CIg_[:], t1[:], den[:], ALU.mult)

        TABr = A("z_tabr", [128, 64, 25]); TABi = A("z_tabi", [128, 64, 25])
        CRp = A("z_crp", [128, 64]); CIp = A("z_cip", [128, 64])
        pz = [PS("z_pz0", [128, 8, 64]), PS("z_pz1", [128, 8, 64])]
        slots = [7 - s for s in range(8)] + [7 + k for k in range(9)] + [14 - s for s in range(8)]
        idf64 = ident_f[0:64, 0:64]
        nb = 0
        for TAB, PW in ((TABr, PWr), (TABi, PWi)):
            for b0 in range(0, 25, 8):
                n = min(8, 25 - b0)
                pt = pz[nb % 2]
                nb += 1
                for q in range(n):
                    P.op("pe", "transpose", dict(out=pt[:, q, :], in_=PW[:, slots[b0 + q], :], identity=idf64),
                         reads=[kn(PW[:]), "ident_f"], writes=[kn(pt[:])])
                CP(TAB[:, :, b0:b0 + n].rearrange("p g s -> p s g"), pt[:, 0:n, :])
        pt = pz[nb % 2]
        P.op("pe", "transpose", dict(out=pt[:, 0, :], in_=CRg_[:], identity=idf64), reads=["z_crg", "ident_f"], writes=[kn(pt[:])])
        P.op("pe", "transpose", dict(out=pt[:, 1, :], in_=CIg_[:], identity=idf64), reads=["z_cig", "ident_f"], writes=[kn(pt[:])])
        P.op("pe", "transpose", dict(out=pt[:, 2, :], in_=Dg8[:].rearrange("g a c -> g (a c)"), identity=idf64),
             reads=["z_dg8", "ident_f"], writes=[kn(pt[:])])
        CP(CRp[:], pt[:, 0, :])
        CP(CIp[:], pt[:, 1, :])
        CP(dvec[:], pt[:, 2, :])
        for h in range(2):
            rs = slice(h * 64, (h + 1) * 64)
            mr = TABr[rs, h::2, 16]
            mi = TABi[rs, h::2, 16]
            CP(mcat[rs, 0, 0:32], mr)
            CP(mcat[rs, 0, 32:64], mr)
            TS(mcat[rs, 1, 0:32], mi, -1.0, ALU.mult)
            CP(mcat[rs, 1, 32:64], mi)
        CRg = A("z_crpg", [128, 64, 16]); CIn = A("z_cinpg", [128, 64, 16])
        pc = [PS("z_pc0", [128, 4, 128]), PS("z_pc1", [128, 4, 128])]
        nb = 0
        for t, dstC in zip(tC, (CRg, CIn)):
            for q4 in range(2):
                pt = pc[nb % 2]
                nb += 1
                for q in range(4):
                    P.op("pe", "transpose", dict(out=pt[:, q, :], in_=t[:, q4 * 4 + q, :], identity=ident_f[:]),
                         reads=[kn(t[:]), "ident_f"], writes=[kn(pt[:])])
                CP(dstC[:, q4 * 32:(q4 + 1) * 32, :].rearrange("p (a g) c -> p a (g c)", a=4), pt[:])
        TS(CIn[:], CIn[:], -1.0, ALU.mult)
        BBr = A("z_bbr", [128, 64, 16]); BBi = A("z_bbi", [128, 64, 16])
        u1 = A("z_u1", [128, 64, 16]); u2 = A("z_u2", [128, 64, 16])
        crb = CRp[:].unsqueeze(2).broadcast_to([128, 64, 16])
        cib = CIp[:].unsqueeze(2).broadcast_to([128, 64, 16])
        TT(u1[:], crb, BR[:], ALU.mult)
        TT(u2[:], cib, BI[:], ALU.mult)
        TT(BBr[:], u1[:], u2[:], ALU.subtract)
        TT(u1[:], crb, BI[:], ALU.mult)
        TT(u2[:], cib, BR[:], ALU.mult)
        TT(BBi[:], u1[:], u2[:], ALU.add)

        GH = 16
        X1 = A("z_x1", [128, GH, 8, 16]); X2 = A("z_x2", [128, GH, 8, 16])
        X3 = A("z_x3", [128, GH, 8, 16]); X4 = A("z_x4", [128, GH, 8, 16])
        T1 = A("z_T1", [128, GH, 8, 16]); T2 = A("z_T2", [128, GH, 8, 16])
        ptp = [PS("z_ptp0", [128, 4, 128]), PS("z_ptp1", [128, 4, 128])]
        pw8 = [PS("z_pw0", [128, 8, 64]), PS("z_pw1", [128, 8, 64])]

        def cprod(oa, ob, s0, Xr, Xi, g0, opa, opb, e1="dve", e2="dve"):
            pr = TABr[:, g0:g0 + GH, s0:s0 + 8].unsqueeze(3).broadcast_to([128, GH, 8, 16])
            pi = TABi[:, g0:g0 + GH, s0:s0 + 8].unsqueeze(3).broadcast_to([128, GH, 8, 16])
            xr = Xr[:, g0:g0 + GH, :].unsqueeze(2).broadcast_to([128, GH, 8, 16])
            xi = Xi[:, g0:g0 + GH, :].unsqueeze(2).broadcast_to([128, GH, 8, 16])
            TT(T1[:], pr, xr, ALU.mult, e1)
            TT(T2[:], pi, xi, ALU.mult, e1)
            TT(oa[:], T1[:], T2[:], opa, e1)
            TT(T1[:], pr, xi, ALU.mult, e2)
            TT(T2[:], pi, xr, ALU.mult, e2)
            TT(ob[:], T1[:], T2[:], opb, e2)

        nb = 0
        for gh in range(64 // GH):
            g0 = gh * GH
            cprod(X1, X2, 0, BBr, BBi, g0, ALU.subtract, ALU.add)
            cprod(X3, X4, 8, CRg, CIn, g0, ALU.add, ALU.subtract)
            for q4 in range(GH // 4):
                pt = ptp[nb % 2]
                nb += 1
                for q in range(4):
                    gl = q4 * 4 + q
                    P.op("pe", "matmul", ((pt[:, q, :],), dict(lhsT=X1[0:64, gl].rearrange("p s c -> p (s c)"),
                                                              rhs=X3[0:64, gl].rearrange("p s c -> p (s c)"),
                                                              start=True, stop=False)),
                         reads=["z_x1", "z_x3"], writes=[kn(pt[:])])
                    P.op("pe", "matmul", ((pt[:, q, :],), dict(lhsT=X2[0:64, gl].rearrange("p s c -> p (s c)"),
                                                              rhs=X4[0:64, gl].rearrange("p s c -> p (s c)"),
                                                              start=False, stop=True)),
                         reads=["z_x2", "z_x4"], writes=[kn(pt[:])])
                TT(toep[:, g0 + q4 * 4:g0 + q4 * 4 + 4, :], pt[:], mask[:].unsqueeze(1).broadcast_to([128, 4, 128]), ALU.mult)
            cprod(X1, X2, 17, BBr, BBi, g0, ALU.subtract, ALU.add)
            for Xs, wt in ((X1, wtr), (X2, wti)):
                for q8 in range(GH // 8):
                    pt = pw8[nb % 2]
                    nb += 1
                    for q in range(8):
                        gl = q8 * 8 + q
                        P.op("pe", "transpose", dict(out=pt[:, q, :], in_=Xs[0:64, gl].rearrange("p s c -> p (s c)"),
                                                     identity=idf64),
                             reads=[kn(Xs[:]), "ident_f"], writes=[kn(pt[:])])
                    CP(wt[:, g0 + q8 * 8:g0 + q8 * 8 + 8, :], pt[:])
            cprod(X3, X4, 9, CRg, CIn, g0, ALU.add, ALU.subtract)
            for Xs, vt in ((X3, vre), (X4, vim)):
                for h in range(2):
                    rs = slice(h * 64, (h + 1) * 64)
                    P.op("dve", "tensor_copy", dict(out=vt[rs, gh * (GH // 2):(gh + 1) * (GH // 2), :],
                                                     in_=Xs[rs, h::2].rearrange("p g s c -> p g (s c)")),
                         reads=[kn(Xs[:])], writes=[kn(vt[:])])

    def s5_main(self, es, xsrc, xdst, jj, wb, ident_b, toep, wtr, wti, vre, vim, dvec, mcat, gn):
        nc, P = self.nc, self.P
        A = lambda n, s, d=F32: es.enter_context(self.sb(n, s, d))
        PS = lambda n, s, d=F32: es.enter_context(self.pp(n, s, d))
        xq = A("m_xq", [128, 8, D])
        hy = A("m_hy", [128, 8192], BF16)
        uy = A("m_uy", [128, 8192], BF16)
        beta = A("m_beta", [128, 128, 64])
        XS = A("m_xs", [128, 129, 64], BF16)
        Z = [A("m_z0", [128, 2, 64]), A("m_z1", [128, 2, 64])]
        AB3 = [A(f"m_ab{i}", [128, 3, 64]) for i in range(4)]
        wa = A("m_wa", [128, 8, 512], BF16); wbt = A("m_wb", [128, 8, 512], BF16)
        sq = A("m_sq", [128, D], BF16); ss = A("m_ss", [128, 8]); rstd = A("m_rstd", [128, 8])
        ytmp = [A(f"m_yt{i}", [128, 128]) for i in range(4)]
        yact = [A(f"m_ya{i}", [128, 128], BF16) for i in range(4)]
        sgt = A("m_sg", [128, 512]); tt = A("m_tt", [128, 512])
        pT = [PS(f"m_pT{i}", [128, 8, 128], BF16) for i in range(2)]
        pb = [PS(f"m_pb{i}", [128, 2, 128]) for i in range(2)]
        py = [PS(f"m_py{i}", [128, 128]) for i in range(2)]
        pa = PS("m_pa", [128, 512]); pbb = PS("m_pbb", [128, 512])
        hbp = hy[:].rearrange("p (g s c) -> p g s c", g=64, s=8)
        yt = hy[:].rearrange("p (s ch) -> p s ch", s=8)
        U = uy[:].rearrange("p (g j) -> p g j", g=64)
        yT = uy[:].rearrange("p (k t) -> p k t", k=8)
        wa_v = wb["s5_glu_a"][jj].rearrange("(k p) n -> p k n", p=128)
        wb_v = wb["s5_glu_b"][jj].rearrange("(k p) n -> p k n", p=128)
        wkey = f"w_s5_{jj}"

        def xv(ap, q):
            return ap[q * 1024:(q + 1) * 1024, :].rearrange("(p t) d -> p t d", t=8)

        P.op("dve", "memset", ((Z[0][:], 0.0), {}), writes=[("Z", 0)])
        P.op("dve", "memset", ((XS[:, 0, :], 0.0), {}), writes=["XS"])
        cnt = 0
        ncp = 0
        for q in range(4):
            P.op("sync", "dma_start", dict(out=xq[:], in_=xv(xsrc, q)), reads=[("xres", q)], writes=["xq"], dma="m_x")
            for t in range(8):
                P.op("act", "activation", dict(out=sq[:], in_=xq[:, t, :], func=AF.Square, accum_out=ss[:, t:t + 1]),
                     reads=["xq"], writes=["m_sq", "m_ss"])
            P.op("dve", "tensor_scalar", dict(out=rstd[:], in0=ss[:], scalar1=1.0 / D, scalar2=EPS, op0=ALU.mult, op1=ALU.add),
                 reads=["m_ss"], writes=["m_rstd"])
            P.op("act", "activation", dict(out=rstd[:], in_=rstd[:], func=AF.Sqrt), reads=["m_rstd"], writes=["m_rstd"])
            P.op("dve", "reciprocal", dict(out=rstd[:], in_=rstd[:]), reads=["m_rstd"], writes=["m_rstd"])
            for t in range(8):
                P.op("dve", "scalar_tensor_tensor", dict(out=hbp[:, :, t, :], in0=xq[:, t, :].rearrange("p (g c) -> p g c", c=16),
                                                         scalar=rstd[:, t:t + 1], in1=gn[:].rearrange("p (g c) -> p g c", c=16),
                                                         op0=ALU.mult, op1=ALU.mult),
                     reads=["xq", "m_rstd", "s_gn"], writes=["hy"])
            for g8 in range(8):
                pt = pT[g8 % 2]
                for gq in range(8):
                    g = g8 * 8 + gq
                    P.op("pe", "transpose", dict(out=pt[:, gq, :], in_=hy[:, g * 128:(g + 1) * 128], identity=ident_b[:]),
                         reads=["hy", "ident"], writes=[("m_pT", g8 % 2)])
                if g8 % 2 == 0:
                    P.op("dve", "tensor_copy", dict(out=U[:, g8 * 8:(g8 + 1) * 8, :], in_=pt[:]), reads=[("m_pT", 0)], writes=["uy"])
                else:
                    P.op("act", "copy", dict(out=U[:, g8 * 8:(g8 + 1) * 8, :], in_=pt[:]), reads=[("m_pT", 1)], writes=["uy"])
            for pr in range(32):
                pbt = pb[pr % 2]
                for ri, wt in enumerate((wtr, wti)):
                    for g2 in range(2):
                        g = 2 * pr + g2
                        P.op("pe", "matmul", ((pbt[g2 * 64:(g2 + 1) * 64, ri, :],),
                                              dict(lhsT=wt[:, g, :], rhs=U[:, g, :], start=True, stop=True)),
                             reads=["uy", "s_w"], writes=[("m_pb", pr % 2)])
                bt = beta[:, :, :]
                ov = bass.AP(tensor=bt.tensor, offset=bt.offset + pr, ap=[list(bt.ap[0]), [32, 2], [64, 128]])
                if pr % 2 == 0:
                    P.op("dve", "tensor_copy", dict(out=ov, in_=pbt[:]), reads=[("m_pb", 0)], writes=["beta"])
                else:
                    P.op("act", "copy", dict(out=ov, in_=pbt[:]), reads=[("m_pb", 1)], writes=["beta"])
            for j0 in range(2):
                P.op("act", "copy", dict(out=AB3[(cnt + 1 + j0) % 4][:, 2, :], in_=beta[:, j0, :]), reads=["beta"],
                     writes=[("ABb", (cnt + 1 + j0) % 4)])
            for j in range(128):
                zc, zn = Z[cnt % 2], Z[(cnt + 1) % 2]
                kc, kn_ = ("Z", cnt % 2), ("Z", (cnt + 1) % 2)
                cnt += 1
                zt = zc[:, :, :]
                win = bass.AP(tensor=zt.tensor, offset=zt.offset, ap=[list(zt.ap[0]), [32, 2], [1, 64]])
                ab = AB3[cnt % 4]
                abk, abbk = ("AB", cnt % 4), ("ABb", cnt % 4)
                if j + 2 < 128:
                    P.op("act", "copy", dict(out=AB3[(cnt + 2) % 4][:, 2, :], in_=beta[:, j + 2, :]), reads=["beta"],
                         writes=[("ABb", (cnt + 2) % 4)])
                P.op("dve", "tensor_tensor", dict(out=ab[:, 0:2, :], in0=mcat[:], in1=win, op=ALU.mult), reads=[kc, "s_mcat"], writes=[abk])
                abt = ab[:, :, :]
                rin = bass.AP(tensor=abt.tensor, offset=abt.offset, ap=[list(abt.ap[0]), [0, 2], [1, 64], [64, 3]])
                P.op("dve", "tensor_reduce", dict(out=zn[:], in_=rin, op=ALU.add, axis=AX.X), reads=[abk, abbk], writes=[kn_])
                P.op("act", "copy", dict(out=XS[:, j + 1, :], in_=zn[:, 0, :]), reads=[kn_], writes=["XS"])
            def grp_front(g):
                pr, g2 = g // 2, g % 2
                pyt = [py[0][:], py[1][:], pa[:, 0:128], pbb[:, 0:128]][g % 4]
                pyk = [("m_py", 0), ("m_py", 1), "m_pa", "m_pbb"][g % 4]
                rs = slice(g2 * 64, (g2 + 1) * 64)
                P.op("pe", "matmul", ((pyt,), dict(lhsT=toep[:, g, :], rhs=U[:, g, :], start=True, stop=False)),
                     reads=["uy", "s_toep"], writes=[pyk])
                P.op("pe", "matmul", ((pyt,), dict(lhsT=vre[rs, pr, :], rhs=XS[rs, 0:128, pr], start=False, stop=False)),
                     reads=["XS", "s_v"], writes=[pyk])
                P.op("pe", "matmul", ((pyt,), dict(lhsT=vim[rs, pr, :], rhs=XS[rs, 0:128, 32 + pr], start=False, stop=True)),
                     reads=["XS", "s_v"], writes=[pyk])
                P.op("dve", "scalar_tensor_tensor", dict(out=ytmp[g % 4][:], in0=U[:, g, :], scalar=dvec[:, g:g + 1], in1=pyt,
                                                         op0=ALU.mult, op1=ALU.add),
                     reads=["uy", pyk, "s_dvec"], writes=[("m_ytmp", g % 4)])
                P.op("act", "activation", dict(out=yact[g % 4][:], in_=ytmp[g % 4][:], func=AF.Gelu),
                     reads=[("m_ytmp", g % 4)], writes=[("m_yact", g % 4)])

            def grp_back(g):
                g8 = g // 8
                pt = pT[g8 % 2]
                P.op("pe", "transpose", dict(out=pt[:, g % 8, :], in_=yact[g % 4][:], identity=ident_b[:]),
                     reads=[("m_yact", g % 4), "ident"], writes=[("m_pT", g8 % 2)])
                if g % 8 == 7:
                    ov = yt[:, :, g8 * 128:(g8 + 1) * 128].rearrange("p i (g c) -> p i g c", c=16)
                    iv = pt[:].rearrange("p g (i c) -> p i g c", c=16)
                    P.op("dve", "tensor_copy", dict(out=ov, in_=iv), reads=[("m_pT", g8 % 2)], writes=["hy"])

            for g in range(64 + 2):
                if g < 64:
                    grp_front(g)
                if g >= 2:
                    grp_back(g - 2)
            P.op("act", "copy", dict(out=XS[:, 0, :], in_=XS[:, 128, :]), reads=["XS"], writes=["XS"])
            for s in range(8):
                pt = pT[s % 2]
                for k in range(8):
                    P.op("pe", "transpose", dict(out=pt[:, k, :], in_=yt[:, s, k * 128:(k + 1) * 128], identity=ident_b[:]),
                         reads=["hy", "ident"], writes=[("m_pT", s % 2)])
                if s % 2 == 0:
                    P.op("dve", "tensor_copy", dict(out=yT[:, :, s * 128:(s + 1) * 128], in_=pt[:]), reads=[("m_pT", 0)], writes=["uy"])
                else:
                    P.op("act", "copy", dict(out=yT[:, :, s * 128:(s + 1) * 128], in_=pt[:]), reads=[("m_pT", 1)], writes=["uy"])
            for nh in range(2):
                P.op("sync", "dma_start", dict(out=wa[:], in_=wa_v[:, :, nh * 512:(nh + 1) * 512]), reads=self.wkeys[wkey], writes=["m_wa"], dma="m_w")
                P.op("sync", "dma_start", dict(out=wbt[:], in_=wb_v[:, :, nh * 512:(nh + 1) * 512]), reads=self.wkeys[wkey], writes=["m_wb"], dma="m_w")
                for s in range(8):
                    for k in range(8):
                        P.op("pe", "matmul", ((pa[:],), dict(lhsT=yT[:, k, s * 128:(s + 1) * 128], rhs=wa[:, k, :],
                                                            start=(k == 0), stop=(k == 7))), reads=["uy", "m_wa"], writes=["m_pa"])
                    for k in range(8):
                        P.op("pe", "matmul", ((pbb[:],), dict(lhsT=yT[:, k, s * 128:(s + 1) * 128], rhs=wbt[:, k, :],
                                                             start=(k == 0), stop=(k == 7))), reads=["uy", "m_wb"], writes=["m_pbb"])
                    P.op("act", "activation", dict(out=sgt[:], in_=pbb[:], func=AF.Sigmoid), reads=["m_pbb"], writes=["m_sg"])
                    P.op("dve", "tensor_tensor", dict(out=tt[:], in0=sgt[:], in1=pa[:], op=ALU.mult), reads=["m_sg", "m_pa"], writes=["m_tt"])
                    xs_ = xq[:, s, nh * 512:(nh + 1) * 512]
                    P.op("dve", "tensor_tensor", dict(out=xs_, in0=xs_, in1=tt[:], op=ALU.add), reads=["m_tt", "xq"], writes=["xq"])
            P.op("sync", "dma_start", dict(out=xv(xdst, q), in_=xq[:]), reads=["xq"], writes=[("xres", q)], dma="m_st")

    def attn_phase(self, es, xsrc, xdst, L, wn, wb, ident_f, ident_b):
        nc, P = self.nc, self.P
        jj = L // 2
        lam_init = 0.8 - 0.6 * math.exp(-0.3 * L)
        A = lambda n, s, d=F32: es.enter_context(self.sb(n, s, d))
        QT, KT, HD = self.scr["QT"], self.scr["KT"], self.scr["HD"]
        win_v = wb["attn_w_in"][jj].rearrange("(k p) n -> p k n", p=128)
        wout_v = wb["attn_w_out"][jj].rearrange("(k p) n -> p k n", p=128)
        wkey = f"w_attn_{jj}"
        kn = lambda ap: ap.tensor.name.split("__u")[0]
        Vd = A("a_vd", [128, 32, 4, 129], BF16)
        Vf = A("a_vf", [128, 32, 8, 65], BF16)
        cposk = A("a_cposk", [128, 32, 8])
        P.op("dve", "memset", ((Vd[:, :, :, 128:129], 1.0), {}), writes=["Vd"])
        P.op("dve", "memset", ((Vf[:, :, :, 64:65], 1.0), {}), writes=["Vf"])

        LSP = A("a_lsp", [128, 32, 8]); R = A("a_R", [128, 33, 8])
        tri = A("a_tri", [128, 128]); onesf = A("a_onesf", [128, 128])
        with contextlib.ExitStack() as e1:
            B = lambda n, s, d=F32: e1.enter_context(self.sb(n, s, d))
            PS = lambda n, s, d=F32: e1.enter_context(self.pp(n, s, d))
            xt = [B(f"a_xt{i}", [128, 4, D]) for i in range(2)]
            hb = [B(f"a_hb{i}", [128, D], BF16) for i in range(2)]
            hT = B("a_hT", [128, 8, 512], BF16)
            win = B("a_win", [128, 8, IN_COLS], BF16)
            gn = B("a_gn", [128, D])
            sq = B("a_sq", [128, D], BF16); ss = B("a_ss", [128, 4]); rstd = B("a_rstd", [128, 4])
            qsq = [B(f"a_qsq{i}", [128, 512]) for i in range(3)]
            lnv = [B(f"a_lnv{i}", [128, 512]) for i in range(2)]
            qo = [B(f"a_qo{i}", [128, 512], BF16) for i in range(2)]
            G = B("a_G", [128, 4]); epsc = B("a_eps", [128, 1]); ones2 = B("a_ones2", [128, 128])
            fgb = B("a_fgb", [128, 8]); zt = B("a_zt", [128, 8])
            pT = PS("a_pT", [128, 8, 128], BF16)
            pq = [PS(f"a_pq{i}", [128, 512]) for i in range(3)]
            pms = PS("a_pms", [128, 512])
            pv = [PS(f"a_pv{i}", [128, 512]) for i in range(2)]
            pfl = PS("a_pfl", [128, 512])

            def DMA(out, in_, wr, rd=()):
                P.op("sync", "dma_start", dict(out=out, in_=in_), reads=list(rd), writes=[wr], dma="a_ld")

            DMA(gn[:], wn["mix_norm"][L].partition_broadcast(128), "a_gn")
            for h in range(2):
                DMA(win[:, h * 4:(h + 1) * 4, :], win_v[:, h * 4:(h + 1) * 4, :], "a_win", self.wkeys[wkey])
            for c, nm in enumerate(("diff_q_norm", "diff_k_norm", "fox_q_norm", "fox_k_norm")):
                for h in range(2):
                    DMA(G[h * 64:(h + 1) * 64, c:c + 1], wn[nm][jj].rearrange("(d o) -> d o", o=1), "a_G")
            DMA(fgb[:], wn["fg_bias"][jj].partition_broadcast(128), "a_fgb")
            DMA(ones2[:], self.consts["ones2"], "a_ones2")
            DMA(tri[:], self.consts["tri"], "a_tri")
            P.op("dve", "memset", ((epsc[:], EPS), {}), writes=["a_eps"])
            P.op("dve", "memset", ((onesf[:], 1.0), {}), writes=["a_onesf"])
            P.op("dve", "memset", ((R[:, 0, :], 0.0), {}), writes=["a_R"])
            qk_tiles = []
            for h in range(4):
                qk_tiles.append((h * 128, 0, QT, 2 * h))
            for h in range(4):
                qk_tiles.append((512 + h * 128, 1, KT, 2 * h))
            for h in range(4):
                qk_tiles.append((1536 + h * 128, 2, QT, 8 + 2 * h))
            for h in range(4):
                qk_tiles.append((2048 + h * 128, 3, KT, 8 + 2 * h))

            def xv(ap, b):
                return ap[b * 512:(b + 1) * 512, :].rearrange("(t p) d -> p t d", p=128)

            def load_x(b):
                P.op("sync", "dma_start", dict(out=xt[b % 2][:], in_=xv(xsrc, b)), reads=[("xres", b)],
                     writes=[("a_xt", b % 2)], dma=f"a_x{b % 2}")

            load_x(0)
            ev = 0
            for b in range(8):
                X = xt[b % 2]
                xk = ("a_xt", b % 2)
                if b + 1 < 8:
                    load_x(b + 1)
                for t in range(4):
                    P.op("act", "activation", dict(out=sq[:], in_=X[:, t, :], func=AF.Square, accum_out=ss[:, t:t + 1]),
                         reads=[xk], writes=["a_sq", "a_ss"])
                P.op("dve", "tensor_scalar", dict(out=rstd[:], in0=ss[:], scalar1=1.0 / D, scalar2=EPS, op0=ALU.mult, op1=ALU.add),
                     reads=["a_ss"], writes=["a_rstd"])
                P.op("act", "activation", dict(out=rstd[:], in_=rstd[:], func=AF.Sqrt), reads=["a_rstd"], writes=["a_rstd"])
                P.op("dve", "reciprocal", dict(out=rstd[:], in_=rstd[:]), reads=["a_rstd"], writes=["a_rstd"])
                for t in range(4):
                    H = hb[t % 2]
                    P.op("dve", "scalar_tensor_tensor", dict(out=H[:], in0=X[:, t, :], scalar=rstd[:, t:t + 1], in1=gn[:],
                                                             op0=ALU.mult, op1=ALU.mult),
                         reads=[xk, "a_rstd", "a_gn"], writes=[("a_hb", t % 2)])
                    for k in range(8):
                        P.op("pe", "transpose", dict(out=pT[:, k, :], in_=H[:, k * 128:(k + 1) * 128], identity=ident_b[:]),
                             reads=[("a_hb", t % 2), "ident"], writes=["a_pT"])
                    P.op("dve", "tensor_copy", dict(out=hT[:, :, t * 128:(t + 1) * 128], in_=pT[:]), reads=["a_pT"], writes=["a_hT"])
                def qk_front(i):
                    c0, gc, dstT, m0 = qk_tiles[i]
                    q = (b * 16 + i) % 3
                    for k in range(8):
                        P.op("pe", "matmul", ((pq[q][:],), dict(lhsT=win[:, k, c0:c0 + 128], rhs=hT[:, k, :],
                                                                start=(k == 0), stop=(k == 7))),
                             reads=["a_win", "a_hT"], writes=[("a_pq", q)])
                    P.op("act", "activation", dict(out=qsq[q][:], in_=pq[q][:], func=AF.Square), reads=[("a_pq", q)], writes=[("a_qsq", q)])

                def qk_back(i):
                    c0, gc, dstT, m0 = qk_tiles[i]
                    q = (b * 16 + i) % 3
                    r = (b * 16 + i) % 2
                    P.op("pe", "matmul", ((pms[:],), dict(lhsT=ones2[:], rhs=qsq[q][:], start=True, stop=True)),
                         reads=["a_ones2", ("a_qsq", q)], writes=["a_pms"])
                    P.op("act", "activation", dict(out=lnv[r][:], in_=pms[:], func=AF.Ln, bias=epsc[:, 0:1]),
                         reads=["a_pms", "a_eps"], writes=[("a_lnv", r)])
                    P.op("act", "activation", dict(out=lnv[r][:], in_=lnv[r][:], func=AF.Exp, scale=-0.5),
                         reads=[("a_lnv", r)], writes=[("a_lnv", r)])
                    P.op("dve", "scalar_tensor_tensor", dict(out=qo[r][:], in0=pq[q][:], scalar=G[:, gc:gc + 1], in1=lnv[r][:],
                                                             op0=ALU.mult, op1=ALU.mult),
                         reads=[("a_pq", q), ("a_lnv", r), "a_G"], writes=[("a_qo", r)])
                    for hh in range(2):
                        P.op("sync", "dma_start", dict(out=dstT[m0 + hh, 0:64, b * 512:(b + 1) * 512],
                                                       in_=qo[r][hh * 64:(hh + 1) * 64, :]),
                             reads=[("a_qo", r)], writes=[(kn(dstT), m0 + hh, b)], dma="a_qst")

                for i in range(17):
                    if i < 16:
                        qk_front(i)
                    if i >= 1:
                        qk_back(i - 1)
                for t in range(4):
                    blk = b * 4 + t
                    for vi, (c0, Vt, nh, vd) in enumerate(((1024, Vd, 4, 128), (2560, Vf, 8, 64))):
                        q = vi
                        for k in range(8):
                            P.op("pe", "matmul", ((pv[q][:],), dict(lhsT=hT[:, k, t * 128:(t + 1) * 128], rhs=win[:, k, c0:c0 + 512],
                                                                    start=(k == 0), stop=(k == 7))),
                                 reads=["a_win", "a_hT"], writes=[("a_pv", q)])
                        P.op("act" if vi == 0 else "dve", "copy" if vi == 0 else "tensor_copy",
                             dict(out=Vt[:, blk, :, 0:vd], in_=pv[q][:].rearrange("p (h v) -> p h v", h=nh)),
                             reads=[("a_pv", q)], writes=["Vd" if vi == 0 else "Vf"])
                    for k in range(8):
                        P.op("pe", "matmul", ((pfl[:, 0:8],), dict(lhsT=hT[:, k, t * 128:(t + 1) * 128], rhs=win[:, k, 3072:3080],
                                                                   start=(k == 0), stop=(k == 7))),
                             reads=["a_win", "a_hT"], writes=["a_pfl"])
                    P.op("dve", "tensor_tensor", dict(out=zt[:], in0=pfl[:, 0:8], in1=fgb[:], op=ALU.add),
                         reads=["a_pfl", "a_fgb"], writes=["a_zt"])
                    P.op("act", "activation", dict(out=zt[:], in_=zt[:], func=AF.Exp, scale=-1.0), reads=["a_zt"], writes=["a_zt"])
                    P.op("act", "activation", dict(out=LSP[:, blk, :], in_=zt[:], func=AF.Ln, bias=1.0), reads=["a_zt"], writes=["a_lsp"])
                    P.op("dve", "tensor_tensor", dict(out=R[:, blk + 1, :], in0=R[:, blk, :], in1=LSP[:, blk, :], op=ALU.add),
                         reads=["a_lsp", "a_R"], writes=["a_R"])
        P.barrier()
        self.uid += 1
        with contextlib.ExitStack() as e1:
            B = lambda n, s, d=F32: e1.enter_context(self.sb(n, s, d))
            PS = lambda n, s, d=F32: e1.enter_context(self.pp(n, s, d))
            cT = B("a_cT", [8, S]); rT = B("a_rT", [8, S])
            a123 = [B(f"a_a{i}", [8, S], BF16) for i in range(3)]
            onesb = B("a_onesb", [8, S], BF16)
            pv = [PS(f"a_pv{i}", [128, 512]) for i in range(2)]
            pq = [PS(f"a_pq{i}", [128, 512]) for i in range(2)]
            P.op("dve", "memset", ((onesb[:], 1.0), {}), writes=["a_onesb"])
            for blk in range(32):
                q = blk % 2
                P.op("pe", "matmul", ((pv[q][:, 0:8],), dict(lhsT=tri[:], rhs=LSP[:, blk, :], start=True, stop=False)),
                     reads=["a_tri", "a_lsp"], writes=[("a_pv", q)])
                P.op("pe", "matmul", ((pv[q][:, 0:8],), dict(lhsT=onesf[:], rhs=R[:, blk, :], start=False, stop=True)),
                     reads=["a_onesf", "a_R"], writes=[("a_pv", q)])
                P.op("dve", "tensor_copy", dict(out=cposk[:, blk, :], in_=pv[q][:, 0:8]), reads=[("a_pv", q)], writes=["cposk"])
                P.op("pe", "matmul", ((pq[q][0:8, 0:128],), dict(lhsT=LSP[:, blk, :], rhs=tri[:], start=True, stop=False)),
                     reads=["a_tri", "a_lsp"], writes=[("a_pq", q)])
                P.op("pe", "matmul", ((pq[q][0:8, 0:128],), dict(lhsT=R[:, blk, :], rhs=onesf[:], start=False, stop=True)),
                     reads=["a_onesf", "a_R"], writes=[("a_pq", q)])
                P.op("act", "activation", dict(out=cT[:, blk * 128:(blk + 1) * 128], in_=pq[q][0:8, 0:128], func=AF.Copy, scale=-8.0),
                     reads=[("a_pq", q)], writes=["a_cT"])
            P.op("dve", "tensor_copy", dict(out=a123[0][:], in_=cT[:]), reads=["a_cT"], writes=["a_a0"])
            P.op("dve", "tensor_tensor", dict(out=rT[:], in0=cT[:], in1=a123[0][:], op=ALU.subtract), reads=["a_cT", "a_a0"], writes=["a_rT"])
            P.op("dve", "tensor_copy", dict(out=a123[1][:], in_=rT[:]), reads=["a_rT"], writes=["a_a1"])
            P.op("dve", "tensor_tensor", dict(out=cT[:], in0=rT[:], in1=a123[1][:], op=ALU.subtract), reads=["a_rT", "a_a1"], writes=["a_cT"])
            P.op("dve", "tensor_copy", dict(out=a123[2][:], in_=cT[:]), reads=["a_cT"], writes=["a_a2"])
            for i in range(3):
                P.op("sync", "dma_start", dict(out=QT[8:16, 64 + i, :], in_=a123[i][:]), reads=[f"a_a{i}"],
                     writes=[("QTaug", i)], dma="a_qst")
                P.op("sync", "dma_start", dict(out=KT[8:16, 64 + i, :], in_=onesb[:]), reads=["a_onesb"],
                     writes=[("KTaug", i)], dma="a_qst")
        self.dump("lsp", LSP[:], [])
        self.dump("cposk", cposk[:], [])
        self.dump("vd", Vd[:, 0:2], [])
        self.dump("vf", Vf[:, 30:32], [])
        self.dump("qt", QT[:, :, 0:512], [])
        self.dump("kt", KT[:, :, 3584:4096], [])
        P.barrier()

        Ocat = A("b_ocat", [128, 32, D], BF16)
        with contextlib.ExitStack() as e2:
            B = lambda n, s, d=F32: e2.enter_context(self.sb(n, s, d))
            PS = lambda n, s, d=F32: e2.enter_context(self.pp(n, s, d))
            qT = [B(f"b_qT{i}", [67, S], BF16) for i in range(2)]
            kT = [B(f"b_kT{i}", [67, S], BF16) for i in range(2)]
            NPS, NPE = 4, 7
            Pe = [B(f"b_pe{i}", [128, 512], BF16) for i in range(NPE)]
            BT = B("b_BT", [128, 5, 2, 128]); BT8 = B("b_BT8", [128, 5, 2, 128])
            b31 = B("b_b31", [128, 4])
            n0 = B("b_n0", [128, 32, 128])
            rb33 = B("b_rb33", [33, 5]); rbl = B("b_rbl", [33, 128]); OH = B("b_oh", [33, 384]); hrep = B("b_hrep", [128, 384])
            lq = [B(f"b_lq{i}", [128, 64]) for i in range(4)]
            lp = B("b_lp", [128, 64]); e12 = B("b_e12", [128, 2]); nlam = B("b_nlam", [128, 1])
            SW = B("b_sw", [128, 128])
            ssqa = B("b_ssqa", [128, 32]); rinv = B("b_rinv", [128, 1]); odt = B("b_odt", [128, 128]); ssq = B("b_ssq", [128, 1]); junk = B("b_junk", [128, 128], BF16)
            ps = [PS(f"b_ps{i}", [128, 512]) for i in range(NPS)]
            po = [PS(f"b_po{i}", [128, 4, 256]) for i in range(2)]

            def DMA(out, in_, wr, rd=(), sem="b_ld"):
                P.op("sync", "dma_start", dict(out=out, in_=in_), reads=list(rd), writes=[wr], dma=sem)

            P.op("dve", "memset", ((rb33[:], 0.0), {}), writes=["b_rb33"])
            P.op("dve", "memset", ((rb33[32:33, :], NEG), {}), writes=["b_rb33"])
            DMA(rb33[0:32, 0:4], wn["rel_bias"], "b_rb33")
            DMA(OH[:], self.consts["relOH"], "b_oh")
            for i, nm in enumerate(("diff_lambda_q1", "diff_lambda_k1", "diff_lambda_q2", "diff_lambda_k2")):
                DMA(lq[i][:], wn[nm][jj].partition_broadcast(128), f"b_lq{i}")
            DMA(SW[:], wn["diff_subln"][jj].partition_broadcast(128), "b_sw")
            for h in range(5):
                P.op("dve", "tensor_copy", dict(out=rbl[:], in_=rb33[:, h:h + 1].broadcast_to([33, 128])), reads=["b_rb33"], writes=["b_rbl"])
                P.op("pe", "matmul", ((ps[0][:, 0:384],), dict(lhsT=rbl[:], rhs=OH[:], start=True, stop=True)),
                     reads=["b_rbl", "b_oh"], writes=[("b_ps", 0)])
                P.op("dve", "tensor_copy", dict(out=hrep[:], in_=ps[0][:, 0:384]), reads=[("b_ps", 0)], writes=["b_hrep"])
                if h < 4:
                    P.op("dve", "tensor_copy", dict(out=b31[:, h:h + 1], in_=hrep[:, 383:384]), reads=["b_hrep"], writes=["b_b31"])
                DMA(HD[h], hrep[:], ("HD", h), ["b_hrep"], sem="b_hd")
                hd = HD[h]
                src = bass.AP(tensor=hd.tensor, offset=hd.offset + 127, ap=[[383, 128], [128, 2], [1, 128]])
                DMA(BT[:, h, :, :], src, "b_BT", [("HD", h)], sem="b_hd2")
            for h in range(5):
                if h < 4:
                    P.op("dve", "tensor_scalar", dict(out=BT8[:, h], in0=BT[:, h], scalar1=b31[:, h:h + 1], scalar2=8.0,
                                                      op0=ALU.subtract, op1=ALU.mult), reads=["b_BT", "b_b31"], writes=["b_BT8"])
                else:
                    P.op("dve", "tensor_scalar", dict(out=BT8[:, h], in0=BT[:, h], scalar1=8.0, scalar2=None, op0=ALU.mult),
                         reads=["b_BT"], writes=["b_BT8"])
            for i in range(2):
                P.op("dve", "tensor_tensor", dict(out=lp[:], in0=lq[2 * i][:], in1=lq[2 * i + 1][:], op=ALU.mult),
                     reads=[f"b_lq{2 * i}", f"b_lq{2 * i + 1}"], writes=["b_lp"])
                P.op("dve", "tensor_reduce", dict(out=e12[:, i:i + 1], in_=lp[:], op=ALU.add, axis=AX.X), reads=["b_lp"], writes=["b_e12"])
            P.op("act", "activation", dict(out=e12[:], in_=e12[:], func=AF.Exp), reads=["b_e12"], writes=["b_e12"])
            P.op("dve", "scalar_tensor_tensor", dict(out=nlam[:], in0=e12[:, 1:2], scalar=-lam_init, in1=e12[:, 0:1],
                                                     op0=ALU.add, op1=ALU.subtract), reads=["b_e12"], writes=["b_nlam"])
            P.op("dve", "tensor_scalar", dict(out=SW[:], in0=SW[:], scalar1=1.0 - lam_init, scalar2=None, op0=ALU.mult),
                 reads=["b_sw"], writes=["b_sw"])

            maps = [(2 * h + m, "d", h, m) for h in range(4) for m in range(2)] + [(8 + f, "f", f, 0) for f in range(8)]
            steps = [(mi, mp, I, J) for mi, mp in enumerate(maps) for I in range(8) for J in range(4 * I + 4)]
            LAG = 4
            started = {}

            def front(idx):
                mi, (mapi, kind, hh, mm), I, J = steps[idx]
                sl = mi % 2
                K = 64 if kind == "d" else 67
                if I == 0 and J == 0:
                    for c4 in range(2):
                        cs = slice(c4 * 2048, (c4 + 1) * 2048)
                        DMA(qT[sl][0:K, cs], QT[mapi, 0:K, cs], ("b_qT", sl), [], sem=f"b_q{sl}")
                        DMA(kT[sl][0:K, cs], KT[mapi, 0:K, cs], ("b_kT", sl), [], sem=f"b_q{sl}")
                bth = hh if kind == "d" else 4
                qlo = max(4 * I, J)
                c0 = (qlo - 4 * I) * 128
                pst, psk = ps[idx % NPS], ("b_ps", idx % NPS)
                pet, pek = Pe[idx % NPE], ("b_pe", idx % NPE)
                P.op("pe", "matmul", ((pst[:, c0:512],), dict(lhsT=kT[sl][0:K, J * 128:(J + 1) * 128],
                                                              rhs=qT[sl][0:K, I * 512 + c0:(I + 1) * 512],
                                                              start=True, stop=True)),
                     reads=[("b_qT", sl), ("b_kT", sl)], writes=[psk])
                if kind == "d":
                    fbias, frd = b31[:, hh:hh + 1], "b_b31"
                else:
                    fbias, frd = cposk[:, J, hh:hh + 1], "cposk"
                nnear = 2 if kind == "d" else 1
                for dist in range(nnear):
                    qt = J + dist
                    if qt < qlo or qt >= 4 * I + 4:
                        continue
                    cc = (qt - 4 * I) * 128
                    P.op("dve", "tensor_tensor", dict(out=pst[:, cc:cc + 128], in0=pst[:, cc:cc + 128], in1=BT8[:, bth, dist, :],
                                                      op=ALU.add), reads=[psk, "b_BT8"], writes=[psk])
                P.op("act", "activation", dict(out=pet[:, c0:512], in_=pst[:, c0:512], func=AF.Exp, scale=0.125, bias=fbias),
                     reads=[psk, frd], writes=[pek])

            def back(idx):
                mi, (mapi, kind, hh, mm), I, J = steps[idx]
                sp = mi * 8 + I
                pot, pok = po[sp % 2], ("b_po", sp % 2)
                pet, pek = Pe[idx % NPE], ("b_pe", idx % NPE)
                Vt, vd = (Vd, 128) if kind == "d" else (Vf, 64)
                vkey = "Vd" if kind == "d" else "Vf"
                qlo = max(4 * I, J)
                for qt in range(qlo, 4 * I + 4):
                    ql = qt - 4 * I
                    cc = ql * 128
                    st = (sp, ql // 2) not in started
                    started[(sp, ql // 2)] = True
                    P.op("pe", "matmul", ((pot[:, ql, 0:vd + 1],), dict(lhsT=pet[:, cc:cc + 128], rhs=Vt[:, J, hh, 0:vd + 1],
                                                                       start=st, stop=(J == qt), skip_group_check=True)),
                         reads=[pek, vkey], writes=[pok])
                if J != 4 * I + 3:
                    return
                for ql in range(4):
                    qt = 4 * I + ql
                    P.op("dve", "reciprocal", dict(out=rinv[:], in_=pot[:, ql, vd:vd + 1]), reads=[pok], writes=["b_rinv"])
                    if kind == "f":
                        P.op("dve", "tensor_scalar", dict(out=Ocat[:, qt, 512 + hh * 64:512 + (hh + 1) * 64], in0=pot[:, ql, 0:64],
                                                          scalar1=rinv[:, 0:1], scalar2=None, op0=ALU.mult),
                             reads=[pok, "b_rinv"], writes=[("b_ocat", qt)])
                    elif mm == 0:
                        P.op("dve", "tensor_scalar", dict(out=n0[:, qt, :], in0=pot[:, ql, 0:128], scalar1=rinv[:, 0:1], scalar2=None,
                                                          op0=ALU.mult), reads=[pok, "b_rinv"], writes=["b_n0"])
                    else:
                        P.op("dve", "tensor_tensor", dict(out=rinv[:], in0=rinv[:], in1=nlam[:], op=ALU.mult),
                             reads=["b_rinv", "b_nlam"], writes=["b_rinv"])
                        P.op("dve", "scalar_tensor_tensor", dict(out=n0[:, qt, :], in0=pot[:, ql, 0:128], scalar=rinv[:, 0:1], in1=n0[:, qt, :],
                                                                 op0=ALU.mult, op1=ALU.add), reads=[pok, "b_rinv", "b_n0"], writes=["b_n0"])
                        P.op("dve", "tensor_tensor", dict(out=odt[:], in0=n0[:, qt, :], in1=n0[:, qt, :], op=ALU.mult),
                             reads=["b_n0"], writes=["b_odt"])
                        P.op("dve", "tensor_reduce", dict(out=ssqa[:, qt:qt + 1], in_=odt[:], op=ALU.add, axis=AX.X),
                             reads=["b_odt"], writes=["b_ssqa"])
                if kind == "d" and mm == 1 and I == 7:
                    P.op("dve", "tensor_scalar", dict(out=ssqa[:], in0=ssqa[:], scalar1=1.0 / 128, scalar2=EPS, op0=ALU.mult, op1=ALU.add),
                         reads=["b_ssqa"], writes=["b_ssqa"])
                    P.op("act", "activation", dict(out=ssqa[:], in_=ssqa[:], func=AF.Sqrt), reads=["b_ssqa"], writes=["b_ssqa"])
                    P.op("dve", "reciprocal", dict(out=ssqa[:], in_=ssqa[:]), reads=["b_ssqa"], writes=["b_ssqa"])
                    for qt in range(32):
                        P.op("dve", "scalar_tensor_tensor", dict(out=Ocat[:, qt, hh * 128:(hh + 1) * 128], in0=n0[:, qt, :],
                                                                 scalar=ssqa[:, qt:qt + 1], in1=SW[:], op0=ALU.mult, op1=ALU.mult),
                             reads=["b_n0", "b_ssqa", "b_sw"], writes=[("b_ocat", qt)])

            for idx in range(len(steps) + LAG):
                if idx < len(steps):
                    front(idx)
                if idx >= LAG:
                    back(idx - LAG)
            self.dump("bt", BT[:], [])
            self.dump("n0", n0[:], [])
            self.dump("nlam", nlam[:], [])
            self.dump("ocat", Ocat[:, 0:2, :], [])
            self.dump("ocat2", Ocat[:, 30:32, :], [])
        P.barrier()
        self.uid += 1
        with contextlib.ExitStack() as e3:
            B = lambda n, s, d=F32: e3.enter_context(self.sb(n, s, d))
            PS = lambda n, s, d=F32: e3.enter_context(self.pp(n, s, d))
            wout = B("b_wout", [128, 8, D], BF16)
            oT = [B(f"b_oT{i}", [128, 8, 128], BF16) for i in range(2)]
            xo = [B(f"b_xo{i}", [128, D]) for i in range(2)]
            pT = PS("b_pT", [128, 8, 128], BF16)
            po2 = PS("b_po2", [128, 512])

            def DMA(out, in_, wr, rd=(), sem="b_ld"):
                P.op("sync", "dma_start", dict(out=out, in_=in_), reads=list(rd), writes=[wr], dma=sem)

            for h in range(2):
                DMA(wout[:, h * 4:(h + 1) * 4, :], wout_v[:, h * 4:(h + 1) * 4, :], "b_wout", self.wkeys[wkey])
            def xrow(ap, blk):
                return ap[blk * 128:(blk + 1) * 128, :]
            for blk in range(32):
                sl = blk % 2
                DMA(xo[sl][:], xrow(xsrc, blk), ("b_xo", sl), [("xres", blk // 4)], sem=f"b_x{sl}")
                for k in range(8):
                    P.op("pe", "transpose", dict(out=pT[:, k, :], in_=Ocat[:, blk, k * 128:(k + 1) * 128], identity=ident_b[:]),
                         reads=[("b_ocat", blk), "ident"], writes=["b_pT"])
                P.op("act", "copy", dict(out=oT[sl][:], in_=pT[:]), reads=["b_pT"], writes=[("b_oT", sl)])
                for nh in range(2):
                    for k in range(8):
                        P.op("pe", "matmul", ((po2[:],), dict(lhsT=oT[sl][:, k, :], rhs=wout[:, k, nh * 512:(nh + 1) * 512],
                                                             start=(k == 0), stop=(k == 7))),
                             reads=[("b_oT", sl), "b_wout"], writes=["b_po2"])
                    P.op("dve", "tensor_tensor", dict(out=xo[sl][:, nh * 512:(nh + 1) * 512], in0=xo[sl][:, nh * 512:(nh + 1) * 512],
                                                      in1=po2[:], op=ALU.add), reads=["b_po2", ("b_xo", sl)], writes=[("b_xo", sl)])
                P.op("sync", "dma_start", dict(out=xrow(xdst, blk), in_=xo[sl][:]), reads=[("b_xo", sl)], writes=[("xres_o", blk)], dma="b_st")

    def build(self):
        nc, P = self.nc, self.P
        x_in = self.din("x", [S, D])
        ident = self.din("ident", [128, 128])
        wn = {}
        for nm, shp in [("ffn1_norm", [DEPTH, D]), ("ffn1_gate", [DEPTH, D, DFF]), ("ffn1_up", [DEPTH, D, DFF]),
                        ("ffn1_down", [DEPTH, DFF, D]), ("mix_norm", [DEPTH, D]), ("ffn2_norm", [DEPTH, D]),
                        ("ffn2_gate", [DEPTH, D, DFF]), ("ffn2_up", [DEPTH, D, DFF]), ("ffn2_down", [DEPTH, DFF, D])]:
            wn[nm] = self.din(nm, shp)
        for nm, shp in [("s5_a_re", [2, 64, 64]), ("s5_a_im", [2, 64, 64]), ("s5_log_step", [2, 64]),
                        ("s5_b_re", [2, 64, 64, 16]), ("s5_b_im", [2, 64, 64, 16]), ("s5_c_re", [2, 64, 16, 64]),
                        ("s5_c_im", [2, 64, 16, 64]), ("s5_d", [2, D]), ("s5_glu_a", [2, D, D]), ("s5_glu_b", [2, D, D])]:
            wn[nm] = self.din(nm, shp)
        for nm, shp in [("attn_w_in", [2, D, IN_COLS]), ("attn_w_out", [2, D, D]), ("fg_bias", [2, 8]),
                        ("diff_q_norm", [2, 64]), ("diff_k_norm", [2, 64]), ("diff_lambda_q1", [2, 64]),
                        ("diff_lambda_k1", [2, 64]), ("diff_lambda_q2", [2, 64]), ("diff_lambda_k2", [2, 64]),
                        ("diff_subln", [2, 128]), ("fox_q_norm", [2, 64]), ("fox_k_norm", [2, 64]), ("rel_bias", [32, 4])]:
            wn[nm] = self.din(nm, shp)
        self.consts = {k: self.din(k, list(v.shape)) for k, v in host_consts().items() if k != "ident"}
        self.scr = {"QT": self.dscr("QT", [16, 67, S], BF16), "KT": self.dscr("KT", [16, 67, S], BF16),
                    "HD": self.dscr("HD", [5, 128, 384], F32)}
        out = nc.dram_tensor("out", [S, D], F32, kind="ExternalOutput").ap()
        xres = self.dscr("xres", [S, D], F32)
        wb = {}
        for f in ("ffn1", "ffn2"):
            wb[f + "_gate"] = self.dscr(f + "_gate_b", [DEPTH, D, DFF], BF16)
            wb[f + "_up"] = self.dscr(f + "_up_b", [DEPTH, D, DFF], BF16)
            wb[f + "_down"] = self.dscr(f + "_down_b", [DEPTH, DFF, D], BF16)
        wb["attn_w_in"] = self.dscr("attn_w_in_b", [2, D, IN_COLS], BF16)
        wb["attn_w_out"] = self.dscr("attn_w_out_b", [2, D, D], BF16)
        wb["s5_glu_a"] = self.dscr("s5_glu_a_b", [2, D, D], BF16)
        wb["s5_glu_b"] = self.dscr("s5_glu_b_b", [2, D, D], BF16)

        with contextlib.ExitStack() as es0:
            ident_f = es0.enter_context(self.sb("ident_f", [128, 128], F32))
            ident_b = es0.enter_context(self.sb("ident_b", [128, 128], BF16))
            P.op("sync", "dma_start", dict(out=ident_f[:], in_=ident), writes=["ident_f"], dma="c_misc")
            P.op("dve", "tensor_copy", dict(out=ident_b[:], in_=ident_f[:]), reads=["ident_f"], writes=["ident"])
            need = set(k for k, _ in (self.phases or [("ffn1", 0), ("ffn2", 0), ("s5", 1), ("attn", 0)]))
            for L in range(DEPTH):
                for f in ("ffn1", "ffn2"):
                    if f == "ffn2":
                        if L % 2 == 0 and "attn" in need:
                            for w in ("attn_w_in", "attn_w_out"):
                                self.cast_w(wn[w][L // 2], wb[w][L // 2], f"cast_attn_{L // 2}", f"w_attn_{L // 2}")
                        if L % 2 == 1 and "s5" in need:
                            for w in ("s5_glu_a", "s5_glu_b"):
                                self.cast_w(wn[w][L // 2], wb[w][L // 2], f"cast_s5_{L // 2}", f"w_s5_{L // 2}")
                    if f not in need:
                        continue
                    for w in ("gate", "up", "down"):
                        self.cast_w(wn[f"{f}_{w}"][L], wb[f"{f}_{w}"][L], f"cast_{f}_{L}_{w}", f"w_{f}_{L}_{w}")
            phases = self.phases
            if phases is None:
                phases = []
                for L in range(DEPTH):
                    phases += [("ffn1", L), ("attn" if L % 2 == 0 else "s5", L), ("ffn2", L)]
            src = x_in
            for i, (kind, L) in enumerate(phases):
                dst = out if i == len(phases) - 1 else xres
                P.barrier()
                self.uid += 1
                with contextlib.ExitStack() as es:
                    if kind in ("ffn1", "ffn2"):
                        f = kind
                        self.ffn_phase(es, src, dst, wn[f + "_norm"][L], wb[f + "_gate"][L], wb[f + "_up"][L],
                                       wb[f + "_down"][L], f"w_{f}_{L}", ident_b)
                    elif kind == "s5":
                        self.s5_phase(es, src, dst, L, wn, wb, ident_f, ident_b)
                    elif kind == "attn":
                        self.attn_phase(es, src, dst, L, wn, wb, ident_f, ident_b)
                src = xres
            P.barrier(final=True)
            P.emit()
        return nc


def host_consts():
    idx = np.arange(128) // 16
    s5mask = (idx[None, :] >= idx[:, None]).astype(np.float32)
    n = np.arange(384) - 127
    nn = np.maximum(n, 0)
    nf = np.maximum(nn, 1).astype(np.float32)
    large = 16 + (np.log(nf / np.float32(16)) / np.float32(math.log(128 / 16)) * np.float32(16)).astype(np.int32)
    large = np.minimum(large, 31)
    bucket = np.where(nn < 16, nn, large)
    oh = np.zeros((33, 384), np.float32)
    for i in range(384):
        if n[i] >= 0:
            oh[bucket[i], i] = 1.0
        else:
            oh[32, i] = 1.0
    ones2 = np.zeros((128, 128), np.float32)
    ones2[:64, :64] = 1.0 / 64
    ones2[64:, 64:] = 1.0 / 64
    tri = (np.arange(128)[:, None] <= np.arange(128)[None, :]).astype(np.float32)
    return {"ident": np.eye(128, dtype=np.float32), "s5mask": s5mask, "relOH": oh, "ones2": ones2, "tri": tri}


_CACHE = {}


def kernel(**inputs):
    if "b" not in _CACHE:
        b = Builder()
        b.build()
        _CACHE["b"] = b
    b = _CACHE["b"]
    consts = host_consts()
    in_maps = []
    for c in range(NCORES):
        m = {}
        for k in b.inputs:
            if k == "x":
                m[k] = np.ascontiguousarray(inputs["x"][c])
            elif k in consts:
                m[k] = consts[k]
            else:
                m[k] = np.ascontiguousarray(inputs[k])
        in_maps.append(m)
    res = run_bass_kernel_spmd(b.nc, in_maps, core_ids=list(range(NCORES)))
    return np.stack([np.asarray(r["out"]) for r in res.results], axis=0).astype(np.float32)
```

```python
import bisect
import contextlib
import math

import numpy as np
import concourse.bass as bass
import concourse.mybir as mybir
from concourse.bass_utils import run_bass_kernel_spmd

F32 = mybir.dt.float32
BF16 = mybir.dt.bfloat16
AF = mybir.ActivationFunctionType
ALU = mybir.AluOpType
AX = mybir.AxisListType

D = 1024
S = 4096
DFF = 2816
DEPTH = 4
NCORES = 8
EPS = 1e-6
IN_COLS = 3080
NEG = -80.0


class Op:
    __slots__ = ("eng", "fn", "dma", "deps", "marked", "cum", "seq")

    def __init__(self, eng, fn, dma):
        self.eng = eng
        self.fn = fn
        self.dma = dma
        self.deps = []
        self.marked = False
        self.cum = 0
        self.seq = 0


class Prog:
    ENGS = ["sync", "act", "dve", "pe", "pool"]
    BLK = {"sync": "sync", "act": "scalar", "dve": "vector", "pe": "tensor", "pool": "gpsimd"}
    CH = 30000

    def __init__(self, nc):
        self.nc = nc
        self.ops = []
        self.lastw = {}
        self.readers = {}
        self.last_on = {}

    @staticmethod
    def stream(o):
        return o.dma if o.dma else o.eng

    def op(self, eng, name, kw, reads=(), writes=(), dma=None):
        args = ()
        if isinstance(kw, tuple):
            args, kw = kw
        fn = (lambda e, name=name, args=args, kw=kw: getattr(e, name)(*args, **kw))
        o = Op(eng, fn, dma)
        o.seq = len(self.ops)
        deps = {}

        def add(d, raw=False):
            if d is None:
                return
            if (not d.dma) and (not o.dma) and d.eng == o.eng:
                if o.eng == "pe":
                    return
            st = self.stream(d)
            if st not in deps or deps[st].seq < d.seq:
                deps[st] = d

        for k in reads:
            add(self.lastw.get(k), raw=True)
        for k in writes:
            add(self.lastw.get(k))
            for d in self.readers.get(k, {}).values():
                add(d)
        o.deps = list(deps.values())
        for k in writes:
            self.lastw[k] = o
            self.readers[k] = {}
        for k in reads:
            self.readers.setdefault(k, {})[self.stream(o)] = o
        self.ops.append(o)
        if fn is not None:
            self.last_on[self.stream(o)] = o
        return o

    def barrier(self, final=False):
        lasts = {st: d for st, d in self.last_on.items() if final or not st.startswith("cast_")}
        keep = {k: v for k, v in self.lastw.items() if isinstance(k, tuple) and isinstance(k[0], str) and k[0].startswith("w_")}
        for e in self.ENGS:
            o = Op(e, None, None)
            o.seq = len(self.ops)
            o.deps = [d for st, d in lasts.items() if not ((not d.dma) and d.eng == e)]
            self.ops.append(o)
        self.lastw = {} if final else keep
        self.readers = {}

    def emit(self):
        nc = self.nc
        for o in self.ops:
            for d in o.deps:
                d.marked = True
        cnt = {}
        dma_seqs = {}
        for o in self.ops:
            if o.fn is None:
                continue
            if o.dma:
                cnt[o.dma] = cnt.get(o.dma, 0) + 1
                o.cum = cnt[o.dma]
                dma_seqs.setdefault(o.dma, []).append(o.seq)
            elif o.marked:
                cnt[o.eng] = cnt.get(o.eng, 0) + 1
                o.cum = cnt[o.eng]
        for k, v in cnt.items():
            if k not in self.ENGS:
                assert v * 16 < 60000, (k, v)
        with contextlib.ExitStack() as es:
            sems = {}

            def sem(name):
                if name not in sems:
                    sems[name] = es.enter_context(nc.semaphore("s_" + name))
                return sems[name]

            for k, v in cnt.items():
                if k in self.ENGS:
                    for c in range((v - 1) // self.CH + 1):
                        sem(f"{k}{c}")
                else:
                    sem(k)

            bar_seqs = [o.seq for o in self.ops if o.fn is None]
            self.partial = {}

            def resolve(d, o):
                if d.dma:
                    n = bisect.bisect_left(dma_seqs[d.dma], o.seq)
                    nb = bar_seqs[bisect.bisect_left(bar_seqs, o.seq)] if bisect.bisect_left(bar_seqs, o.seq) < len(bar_seqs) else 1 << 60
                    n_epoch = bisect.bisect_left(dma_seqs[d.dma], nb)
                    if n_epoch > n:
                        self.partial[d.dma] = self.partial.get(d.dma, 0) + 1
                    assert n >= d.cum
                    return d.dma, n * 16
                c = (d.cum - 1) // self.CH
                return f"{d.eng}{c}", (d.cum - 1) % self.CH + 1

            block = es.enter_context(nc.Block())
            for eng in self.ENGS:
                ops_e = [o for o in self.ops if o.eng == eng]

                def body(e, ops_e=ops_e):
                    waited = {}
                    for o in ops_e:
                        for d in o.deps:
                            sn, val = resolve(d, o)
                            if waited.get(sn, 0) < val:
                                e.wait_ge(sem(sn), val)
                                waited[sn] = val
                        if o.fn is None:
                            continue
                        inst = o.fn(e)
                        if o.dma:
                            inst.then_inc(sem(o.dma), 16)
                        elif o.marked:
                            c = (o.cum - 1) // self.CH
                            inst.then_inc(sem(f"{o.eng}{c}"), 1)

                getattr(block, self.BLK[eng])(body)


class Builder:
    def __init__(self, phases=None):
        self.nc = bass.Bass("TRN2", target_bir_lowering=False)
        self.P = Prog(self.nc)
        self.phases = phases
        self.inputs = {}
        self.wkeys = {}

    uid = 0
    debug = False

    def dump(self, name, ap, reads):
        if not self.debug:
            return
        o = self.nc.dram_tensor("dbg_" + name, list(ap.shape), ap.dtype, kind="ExternalOutput").ap()
        self.P.op("sync", "dma_start", dict(out=o, in_=ap), reads=list(reads), writes=[("dbg", name)], dma="dbg")

    def sb(self, name, shape, dt=F32):
        return self.nc.sbuf_tensor(f"{name}__u{self.uid}", shape, dt)

    def pp(self, name, shape, dt=F32):
        return self.nc.psum_tensor(f"{name}__u{self.uid}", shape, dt)

    def din(self, name, shape, dt=F32):
        ap = self.nc.dram_tensor(name, list(shape), dt, kind="ExternalInput").ap()
        self.inputs[name] = ap
        return ap

    def dscr(self, name, shape, dt):
        return self.nc.dram_tensor(name, list(shape), dt, kind="Internal").ap()

    def cast_w(self, src, dst, semname, key, nsplit=4):
        P = self.P
        rows, cols = src.shape
        b = cols
        for cand in range(1, 9):
            if cols % cand == 0 and cols // cand <= 1024:
                b = cols // cand
                break
        rs = rows // nsplit
        for i in range(nsplit):
            s_ap = src[i * rs:(i + 1) * rs, :].rearrange("k (a b) -> k a b", b=b)
            d_ap = dst[i * rs:(i + 1) * rs, :].rearrange("k (a b) -> k a b", b=b)
            self.wkeys.setdefault(key, [])
            kk = (key, len(self.wkeys[key]))
            self.wkeys[key].append(kk)
            P.op("pool", "dma_start", dict(out=d_ap, in_=s_ap), writes=[kk], dma=semname)

    def ffn_phase(self, es, xsrc, xdst, gnorm_ap, wg, wu, wd, wkey, ident_b):
        nc, P = self.nc, self.P
        TB = 1024
        NB = S // TB
        KT = D // 128
        MT = DFF // 128
        CG = 256
        NCG = DFF // CG
        A = lambda *a: es.enter_context(self.sb(*a))
        PS = lambda *a: es.enter_context(self.pp(*a))
        xt = [A(f"f_xt{i}", [128, 8, D], F32) for i in range(2)]
        hb = [A(f"f_hb{i}", [128, D], BF16) for i in range(2)]
        hT = A("f_hT", [128, KT, TB], BF16)
        aT = A("f_aT", [128, MT, TB], BF16)
        wdt = A("f_wd", [128, MT, D], BF16)
        wgt = [A(f"f_wg{i}", [128, KT, CG], BF16) for i in range(2)]
        wut = [A(f"f_wu{i}", [128, KT, CG], BF16) for i in range(2)]
        gn = A("f_gn", [128, D], F32)
        sq = A("f_sq", [128, D], BF16)
        ss = A("f_ss", [128, 8], F32)
        rstd = A("f_rstd", [128, 8], F32)
        sg = [A(f"f_sg{i}", [128, 512], F32) for i in range(2)]
        pT = [PS(f"f_pT{i}", [128, 8, 128], BF16) for i in range(2)]
        pg = [PS(f"f_pg{i}", [128, 512], F32) for i in range(2)]
        pu = [PS(f"f_pu{i}", [128, 512], F32) for i in range(2)]
        po = [PS(f"f_po{i}", [128, 512], F32) for i in range(2)]

        P.op("sync", "dma_start", dict(out=gn[:], in_=gnorm_ap.partition_broadcast(128)),
             writes=["f_gn"], dma="f_misc")
        wd_v = wd.rearrange("(k p) n -> p k n", p=128)
        for h in range(2):
            P.op("sync", "dma_start", dict(out=wdt[:, h * 11:(h + 1) * 11, :], in_=wd_v[:, h * 11:(h + 1) * 11, :]),
                 reads=self.wkeys[wkey + "_down"], writes=["f_wd"], dma="f_misc")
        wg_v = wg.rearrange("(k p) n -> p k n", p=128)
        wu_v = wu.rearrange("(k p) n -> p k n", p=128)

        def xv(ap, b):
            return ap[b * TB:(b + 1) * TB, :].rearrange("(p t) d -> p t d", t=8)

        def load_x(b):
            P.op("sync", "dma_start", dict(out=xt[b % 2][:], in_=xv(xsrc, b)),
                 reads=[("xres", b)], writes=[("f_xt", b % 2)], dma=f"f_x{b % 2}")

        load_x(0)
        wl = 0
        for b in range(NB):
            X = xt[b % 2]
            xk = ("f_xt", b % 2)
            if b + 1 < NB:
                load_x(b + 1)
            for t in range(8):
                P.op("act", "activation", dict(out=sq[:], in_=X[:, t, :], func=AF.Square, accum_out=ss[:, t:t + 1]),
                     reads=[xk], writes=["f_sq", ("f_ss", t)])
            P.op("dve", "tensor_scalar", dict(out=rstd[:], in0=ss[:], scalar1=1.0 / D, scalar2=EPS,
                                              op0=ALU.mult, op1=ALU.add),
                 reads=[("f_ss", t) for t in range(8)], writes=["f_rstd"])
            P.op("act", "activation", dict(out=rstd[:], in_=rstd[:], func=AF.Sqrt), reads=["f_rstd"], writes=["f_rstd"])
            P.op("dve", "reciprocal", dict(out=rstd[:], in_=rstd[:]), reads=["f_rstd"], writes=["f_rstd"])
            for t in range(8):
                H = hb[t % 2]
                P.op("dve", "scalar_tensor_tensor", dict(out=H[:], in0=X[:, t, :], scalar=rstd[:, t:t + 1], in1=gn[:],
                                                         op0=ALU.mult, op1=ALU.mult),
                     reads=[xk, "f_rstd", "f_gn"], writes=[("f_hb", t % 2)])
                pt = pT[t % 2]
                for k in range(KT):
                    P.op("pe", "transpose", dict(out=pt[:, k, :], in_=H[:, k * 128:(k + 1) * 128], identity=ident_b[:]),
                         reads=[("f_hb", t % 2), "ident"], writes=[("f_pT", t % 2)])
                if t % 2 == 0:
                    P.op("dve", "tensor_copy", dict(out=hT[:, :, t * 128:(t + 1) * 128], in_=pt[:]),
                         reads=[("f_pT", t % 2)], writes=[("f_hT", t)])
                else:
                    P.op("act", "copy", dict(out=hT[:, :, t * 128:(t + 1) * 128], in_=pt[:]),
                         reads=[("f_pT", t % 2)], writes=[("f_hT", t)])
            hT_keys = [("f_hT", t) for t in range(8)]
            ev = 0
            for cg in range(NCG):
                sl = wl % 2
                wl += 1
                P.op("sync", "dma_start", dict(out=wgt[sl][:], in_=wg_v[:, :, cg * CG:(cg + 1) * CG]),
                     reads=self.wkeys[wkey + "_gate"], writes=[("f_wg", sl)], dma=f"f_w{sl}")
                P.op("sync", "dma_start", dict(out=wut[sl][:], in_=wu_v[:, :, cg * CG:(cg + 1) * CG]),
                     reads=self.wkeys[wkey + "_up"], writes=[("f_wu", sl)], dma=f"f_w{sl}")
                for mi in range(CG // 128):
                    m = cg * (CG // 128) + mi
                    for hf in range(TB // 512):
                        q = ev % 2
                        ev += 1
                        for k in range(KT):
                            P.op("pe", "matmul", ((pg[q][:],), dict(lhsT=wgt[sl][:, k, mi * 128:(mi + 1) * 128],
                                                                   rhs=hT[:, k, hf * 512:(hf + 1) * 512],
                                                                   start=(k == 0), stop=(k == KT - 1))),
                                 reads=[("f_wg", sl)] + hT_keys, writes=[("f_pg", q)])
                        for k in range(KT):
                            P.op("pe", "matmul", ((pu[q][:],), dict(lhsT=wut[sl][:, k, mi * 128:(mi + 1) * 128],
                                                                   rhs=hT[:, k, hf * 512:(hf + 1) * 512],
                                                                   start=(k == 0), stop=(k == KT - 1))),
                                 reads=[("f_wu", sl)] + hT_keys, writes=[("f_pu", q)])
                        P.op("act", "activation", dict(out=sg[q][:], in_=pg[q][:], func=AF.Silu),
                             reads=[("f_pg", q)], writes=[("f_sg", q)])
                        P.op("dve", "tensor_tensor", dict(out=aT[:, m, hf * 512:(hf + 1) * 512], in0=sg[q][:],
                                                          in1=pu[q][:], op=ALU.mult),
                             reads=[("f_sg", q), ("f_pu", q)], writes=[("f_aT", m)])
            aT_keys = [("f_aT", m) for m in range(MT)]
            for t in range(8):
                for nh in range(2):
                    q = (t * 2 + nh) % 2
                    for k in range(MT):
                        P.op("pe", "matmul", ((po[q][:],), dict(lhsT=aT[:, k, t * 128:(t + 1) * 128],
                                                               rhs=wdt[:, k, nh * 512:(nh + 1) * 512],
                                                               start=(k == 0), stop=(k == MT - 1))),
                             reads=aT_keys + ["f_wd"], writes=[("f_po", q)])
                    P.op("dve", "scalar_tensor_tensor", dict(out=X[:, t, nh * 512:(nh + 1) * 512], in0=po[q][:],
                                                             scalar=0.5, in1=X[:, t, nh * 512:(nh + 1) * 512],
                                                             op0=ALU.mult, op1=ALU.add),
                         reads=[("f_po", q), xk], writes=[xk])
            P.op("sync", "dma_start", dict(out=xv(xdst, b), in_=X[:]), reads=[xk], writes=[("xres", b)], dma="f_st")

    def s5_phase(self, es, xsrc, xdst, L, wn, wb, ident_f, ident_b):
        nc, P = self.nc, self.P
        jj = L // 2
        A = lambda *a: es.enter_context(self.sb(*a))
        toep = A("s_toep", [128, 64, 128], BF16)
        wtr = A("s_wtr", [128, 64, 64], BF16)
        wti = A("s_wti", [128, 64, 64], BF16)
        vre = A("s_vre", [128, 32, 128], BF16)
        vim = A("s_vim", [128, 32, 128], BF16)
        dvec = A("s_dvec", [128, 64], F32)
        mcat = A("s_mcat", [128, 2, 64], F32)
        gn = A("s_gn", [128, D], F32)
        P.op("sync", "dma_start", dict(out=gn[:], in_=wn["mix_norm"][L].partition_broadcast(128)),
             writes=["s_gn"], dma="s_misc")
        with contextlib.ExitStack() as es1:
            self.s5_setup(es1, jj, wn, ident_f, toep, wtr, wti, vre, vim, dvec, mcat)
        P.barrier()
        self.s5_main(es, xsrc, xdst, jj, wb, ident_b, toep, wtr, wti, vre, vim, dvec, mcat, gn)

    def s5_setup(self, es, jj, wn, ident_f, toep, wtr, wti, vre, vim, dvec, mcat):
        nc, P = self.nc, self.P
        A = lambda n, s, d=F32: es.enter_context(self.sb(n, s, d))
        PS = lambda n, s, d=F32: es.enter_context(self.pp(n, s, d))
        kn = lambda ap: ap.tensor.name.split("__u")[0]

        def TT(out, in0, in1, op, eng="dve"):
            P.op(eng, "tensor_tensor", dict(out=out, in0=in0, in1=in1, op=op), reads=[kn(in0), kn(in1)], writes=[kn(out)])

        def TS(out, in0, s1, op0, s2=None, op1=None, eng="dve"):
            kw = dict(out=out, in0=in0, scalar1=s1, scalar2=s2, op0=op0)
            if op1 is not None:
                kw["op1"] = op1
            rd = [kn(in0)] + ([kn(s1)] if not isinstance(s1, (int, float)) else [])
            P.op(eng, "tensor_scalar", kw, reads=rd, writes=[kn(out)])

        def ACT(out, in_, func, **kw):
            P.op("act", "activation", dict(out=out, in_=in_, func=func, **kw), reads=[kn(in_)], writes=[kn(out)])

        def CP(out, in_, eng="dve"):
            P.op(eng, "tensor_copy", dict(out=out, in_=in_), reads=[kn(in_)], writes=[kn(out)])

        def DMA(out, in_, wr):
            P.op("sync", "dma_start", dict(out=out, in_=in_), writes=[wr], dma="z_ld")

        are, aim, lst = wn["s5_a_re"][jj], wn["s5_a_im"][jj], wn["s5_log_step"][jj]
        bre, bim, cre, cim, dsk = wn["s5_b_re"][jj], wn["s5_b_im"][jj], wn["s5_c_re"][jj], wn["s5_c_im"][jj], wn["s5_d"][jj]
        smask = self.consts["s5mask"]
        AR = A("z_ar", [64, 128]); AI = A("z_ai", [64, 128]); LS = A("z_ls", [64, 1])
        for h in range(2):
            DMA(AR[:, h * 64:(h + 1) * 64], are, "z_ar")
            DMA(AI[:, h * 64:(h + 1) * 64], aim, "z_ai")
        DMA(LS[:], lst.rearrange("(g o) -> g o", o=1), "z_ls")
        mask = A("z_mask", [128, 128])
        DMA(mask[:], smask, "z_mask")
        BR = A("z_br", [128, 64, 16]); BI = A("z_bi", [128, 64, 16])
        for h in range(2):
            DMA(BR[h * 64:(h + 1) * 64], bre.rearrange("g p c -> p g c"), "z_br")
            DMA(BI[h * 64:(h + 1) * 64], bim.rearrange("g p c -> p g c"), "z_bi")
        tC = [A("z_tcr", [128, 8, 128]), A("z_tci", [128, 8, 128])]
        for t, src in zip(tC, (cre, cim)):
            for h in range(2):
                DMA(t[:, :, h * 64:(h + 1) * 64], src.rearrange("(o g) c p -> (g c) o p", o=8), kn(t[:]))
        Dg = A("z_dg", [64, 16]); Dg8 = A("z_dg8", [64, 8, 16])
        DMA(Dg[:], dsk.rearrange("(g c) -> g c", c=16), "z_dg")
        CP(Dg8[:], Dg[:].unsqueeze(1).broadcast_to([64, 8, 16]))

        step = A("z_step", [64, 1])
        ACT(step[:], LS[:], AF.Exp)
        ARS = A("z_ars", [64, 128]); TH = A("z_th", [64, 128]); MAG = A("z_mag", [64, 128])
        Cc = A("z_c", [64, 128]); Sn = A("z_s", [64, 128])
        t1 = A("z_t1", [64, 128]); t2 = A("z_t2", [64, 128]); t3 = A("z_t3", [64, 128])
        TS(ARS[:], AR[:], step[:, 0:1], ALU.mult)
        TS(TH[:], AI[:], step[:, 0:1], ALU.mult)
        ACT(MAG[:], ARS[:], AF.Exp)
        ACT(Sn[:], TH[:], AF.Sin, scale=1.0 / 32)
        hpi = A("z_hpi", [64, 1])
        P.op("dve", "memset", ((hpi[:], math.pi / 2), {}), writes=["z_hpi"])
        P.op("act", "activation", dict(out=Cc[:], in_=TH[:], func=AF.Sin, scale=1.0 / 32, bias=hpi[:, 0:1]),
             reads=["z_th", "z_hpi"], writes=["z_c"])
        for it in range(5):
            TT(t1[:], Cc[:], Cc[:], ALU.mult)
            TT(t2[:], Sn[:], Sn[:], ALU.mult)
            TT(t3[:], Cc[:], Sn[:], ALU.mult)
            TT(Cc[:], t1[:], t2[:], ALU.subtract)
            TS(Sn[:], t3[:], 2.0, ALU.mult)
        PWr = A("z_pwr", [64, 16, 128]); PWi = A("z_pwi", [64, 16, 128])

        def cmul(orr, oi, ar, ai, br, bi):
            TT(t1[:], ar, br, ALU.mult)
            TT(t2[:], ai, bi, ALU.mult)
            TT(t3[:], ar, bi, ALU.mult)
            TT(orr, t1[:], t2[:], ALU.subtract)
            TT(t1[:], ai, br, ALU.mult)
            TT(oi, t3[:], t1[:], ALU.add)

        P.op("dve", "memset", ((PWr[:, 7, :], 1.0), {}), writes=["z_pwr"])
        P.op("dve", "memset", ((PWi[:, 7, :], 0.0), {}), writes=["z_pwi"])
        TT(PWr[:, 8, :], MAG[:], Cc[:], ALU.mult)
        TT(PWi[:, 8, :], MAG[:], Sn[:], ALU.mult)
        for k in range(9, 16):
            cmul(PWr[:, k, :], PWi[:, k, :], PWr[:, k - 1, :], PWi[:, k - 1, :], PWr[:, 8, :], PWi[:, 8, :])
        m2 = A("z_m2", [64, 128])
        TT(t1[:], PWr[:, 8, :], PWr[:, 8, :], ALU.mult)
        TT(t2[:], PWi[:, 8, :], PWi[:, 8, :], ALU.mult)
        TT(m2[:], t1[:], t2[:], ALU.add)
        P.op("dve", "reciprocal", dict(out=m2[:], in_=m2[:]), reads=["z_m2"], writes=["z_m2"])
        TT(PWr[:, 6, :], PWr[:, 8, :], m2[:], ALU.mult)
        TT(t1[:], PWi[:, 8, :], m2[:], ALU.mult)
        TS(PWi[:, 6, :], t1[:], -1.0, ALU.mult)
        for k in range(5, -1, -1):
            cmul(PWr[:, k, :], PWi[:, k, :], PWr[:, k + 1, :], PWi[:, k + 1, :], PWr[:, 6, :], PWi[:, 6, :])
        CRg_ = A("z_crg", [64, 128]); CIg_ = A("z_cig", [64, 128]); den = A("z_den", [64, 128]); lm1 = A("z_lm1", [64, 128])
        TT(t1[:], AR[:], AR[:], ALU.mult)
        TT(t2[:], AI[:], AI[:], ALU.mult)
        TT(den[:], t1[:], t2[:], ALU.add)
        P.op("dve", "reciprocal", dict(out=den[:], in_=den[:]), reads=["z_den"], writes=["z_den"])
        TS(lm1[:], PWr[:, 8, :], -1.0, ALU.add)
        TT(t1[:], lm1[:], AR[:], ALU.mult)
        TT(t2[:], PWi[:, 8, :], AI[:], ALU.mult)
        TT(t1[:], t1[:], t2[:], ALU.add)
        TT(CRg_[:], t1[:], den[:], ALU.mult)
        TT(t1[:], PWi[:, 8, :], AR[:], ALU.mult)
        TT(t2[:], lm1[:], AI[:], ALU.mult)
        TT(t1[:], t1[:], t2[:], ALU.subtract)
        TT(CIg_[:], t1[:], den[:], ALU.mult)

        TABr = A("z_tabr", [128, 64, 25]); TABi = A("z_tabi", [128, 64, 25])
        CRp = A("z_crp", [128, 64]); CIp = A("z_cip", [128, 64])
        pz = [PS("z_pz0", [128, 8, 64]), PS("z_pz1", [128, 8, 64])]
        slots = [7 - s for s in range(8)] + [7 + k for k in range(9)] + [14 - s for s in range(8)]
        idf64 = ident_f[0:64, 0:64]
        nb = 0
        for TAB, PW in ((TABr, PWr), (TABi, PWi)):
            for b0 in range(0, 25, 8):
                n = min(8, 25 - b0)
                pt = pz[nb % 2]
                nb += 1
                for q in range(n):
                    P.op("pe", "transpose", dict(out=pt[:, q, :], in_=PW[:, slots[b0 + q], :], identity=idf64),
                         reads=[kn(PW[:]), "ident_f"], writes=[kn(pt[:])])
                CP(TAB[:, :, b0:b0 + n].rearrange("p g s -> p s g"), pt[:, 0:n, :])
        pt = pz[nb % 2]
        P.op("pe", "transpose", dict(out=pt[:, 0, :], in_=CRg_[:], identity=idf64), reads=["z_crg", "ident_f"], writes=[kn(pt[:])])
        P.op("pe", "transpose", dict(out=pt[:, 1, :], in_=CIg_[:], identity=idf64), reads=["z_cig", "ident_f"], writes=[kn(pt[:])])
        P.op("pe", "transpose", dict(out=pt[:, 2, :], in_=Dg8[:].rearrange("g a c -> g (a c)"), identity=idf64),
             reads=["z_dg8", "ident_f"], writes=[kn(pt[:])])
        CP(CRp[:], pt[:, 0, :])
        CP(CIp[:], pt[:, 1, :])
        CP(dvec[:], pt[:, 2, :])
        for h in range(2):
            rs = slice(h * 64, (h + 1) * 64)
            mr = TABr[rs, h::2, 16]
            mi = TABi[rs, h::2, 16]
            CP(mcat[rs, 0, 0:32], mr)
            CP(mcat[rs, 0, 32:64], mr)
            TS(mcat[rs, 1, 0:32], mi, -1.0, ALU.mult)
            CP(mcat[rs, 1, 32:64], mi)
        CRg = A("z_crpg", [128, 64, 16]); CIn = A("z_cinpg", [128, 64, 16])
        pc = [PS("z_pc0", [128, 4, 128]), PS("z_pc1", [128, 4, 128])]
        nb = 0
        for t, dstC in zip(tC, (CRg, CIn)):
            for q4 in range(2):
                pt = pc[nb % 2]
                nb += 1
                for q in range(4):
                    P.op("pe", "transpose", dict(out=pt[:, q, :], in_=t[:, q4 * 4 + q, :], identity=ident_f[:]),
                         reads=[kn(t[:]), "ident_f"], writes=[kn(pt[:])])
                CP(dstC[:, q4 * 32:(q4 + 1) * 32, :].rearrange("p (a g) c -> p a (g c)", a=4), pt[:])
        TS(CIn[:], CIn[:], -1.0, ALU.mult)
        BBr = A("z_bbr", [128, 64, 16]); BBi = A("z_bbi", [128, 64, 16])
        u1 = A("z_u1", [128, 64, 16]); u2 = A("z_u2", [128, 64, 16])
        crb = CRp[:].unsqueeze(2).broadcast_to([128, 64, 16])
        cib = CIp[:].unsqueeze(2).broadcast_to([128, 64, 16])
        TT(u1[:], crb, BR[:], ALU.mult)
        TT(u2[:], cib, BI[:], ALU.mult)
        TT(BBr[:], u1[:], u2[:], ALU.subtract)
        TT(u1[:], crb, BI[:], ALU.mult)
        TT(u2[:], cib, BR[:], ALU.mult)
        TT(BBi[:], u1[:], u2[:], ALU.add)

        GH = 16
        X1 = A("z_x1", [128, GH, 8, 16]); X2 = A("z_x2", [128, GH, 8, 16])
        X3 = A("z_x3", [128, GH, 8, 16]); X4 = A("z_x4", [128, GH, 8, 16])
        T1 = A("z_T1", [128, GH, 8, 16]); T2 = A("z_T2", [128, GH, 8, 16])
        ptp = [PS("z_ptp0", [128, 4, 128]), PS("z_ptp1", [128, 4, 128])]
        pw8 = [PS("z_pw0", [128, 8, 64]), PS("z_pw1", [128, 8, 64])]

        def cprod(oa, ob, s0, Xr, Xi, g0, opa, opb, e1="dve", e2="dve"):
            pr = TABr[:, g0:g0 + GH, s0:s0 + 8].unsqueeze(3).broadcast_to([128, GH, 8, 16])
            pi = TABi[:, g0:g0 + GH, s0:s0 + 8].unsqueeze(3).broadcast_to([128, GH, 8, 16])
            xr = Xr[:, g0:g0 + GH, :].unsqueeze(2).broadcast_to([128, GH, 8, 16])
            xi = Xi[:, g0:g0 + GH, :].unsqueeze(2).broadcast_to([128, GH, 8, 16])
            TT(T1[:], pr, xr, ALU.mult, e1)
            TT(T2[:], pi, xi, ALU.mult, e1)
            TT(oa[:], T1[:], T2[:], opa, e1)
            TT(T1[:], pr, xi, ALU.mult, e2)
            TT(T2[:], pi, xr, ALU.mult, e2)
            TT(ob[:], T1[:], T2[:], opb, e2)

        nb = 0
        for gh in range(64 // GH):
            g0 = gh * GH
            cprod(X1, X2, 0, BBr, BBi, g0, ALU.subtract, ALU.add)
            cprod(X3, X4, 8, CRg, CIn, g0, ALU.add, ALU.subtract)
            for q4 in range(GH // 4):
                pt = ptp[nb % 2]
                nb += 1
                for q in range(4):
                    gl = q4 * 4 + q
                    P.op("pe", "matmul", ((pt[:, q, :],), dict(lhsT=X1[0:64, gl].rearrange("p s c -> p (s c)"),
                                                              rhs=X3[0:64, gl].rearrange("p s c -> p (s c)"),
                                                              start=True, stop=False)),
                         reads=["z_x1", "z_x3"], writes=[kn(pt[:])])
                    P.op("pe", "matmul", ((pt[:, q, :],), dict(lhsT=X2[0:64, gl].rearrange("p s c -> p (s c)"),
                                                              rhs=X4[0:64, gl].rearrange("p s c -> p (s c)"),
                                                              start=False, stop=True)),
                         reads=["z_x2", "z_x4"], writes=[kn(pt[:])])
                TT(toep[:, g0 + q4 * 4:g0 + q4 * 4 + 4, :], pt[:], mask[:].unsqueeze(1).broadcast_to([128, 4, 128]), ALU.mult)
            cprod(X1, X2, 17, BBr, BBi, g0, ALU.subtract, ALU.add)
            for Xs, wt in ((X1, wtr), (X2, wti)):
                for q8 in range(GH // 8):
                    pt = pw8[nb % 2]
                    nb += 1
                    for q in range(8):
                        gl = q8 * 8 + q
                        P.op("pe", "transpose", dict(out=pt[:, q, :], in_=Xs[0:64, gl].rearrange("p s c -> p (s c)"),
                                                     identity=idf64),
                             reads=[kn(Xs[:]), "ident_f"], writes=[kn(pt[:])])
                    CP(wt[:, g0 + q8 * 8:g0 + q8 * 8 + 8, :], pt[:])
            cprod(X3, X4, 9, CRg, CIn, g0, ALU.add, ALU.subtract)
            for Xs, vt in ((X3, vre), (X4, vim)):
                for h in range(2):
                    rs = slice(h * 64, (h + 1) * 64)
                    P.op("dve", "tensor_copy", dict(out=vt[rs, gh * (GH // 2):(gh + 1) * (GH // 2), :],
                                                     in_=Xs[rs, h::2].rearrange("p g s c -> p g (s c)")),
                         reads=[kn(Xs[:])], writes=[kn(vt[:])])

    def s5_main(self, es, xsrc, xdst, jj, wb, ident_b, toep, wtr, wti, vre, vim, dvec, mcat, gn):
        nc, P = self.nc, self.P
        A = lambda n, s, d=F32: es.enter_context(self.sb(n, s, d))
        PS = lambda n, s, d=F32: es.enter_context(self.pp(n, s, d))
        xq = A("m_xq", [128, 8, D])
        hy = A("m_hy", [128, 8192], BF16)
        uy = A("m_uy", [128, 8192], BF16)
        beta = A("m_beta", [128, 128, 64])
        XS = A("m_xs", [128, 129, 64], BF16)
        Z = [A("m_z0", [128, 2, 64]), A("m_z1", [128, 2, 64])]
        AB3 = [A(f"m_ab{i}", [128, 3, 64]) for i in range(4)]
        wa = A("m_wa", [128, 8, 512], BF16); wbt = A("m_wb", [128, 8, 512], BF16)
        sq = A("m_sq", [128, D], BF16); ss = A("m_ss", [128, 8]); rstd = A("m_rstd", [128, 8])
        ytmp = [A(f"m_yt{i}", [128, 128]) for i in range(4)]
        yact = [A(f"m_ya{i}", [128, 128], BF16) for i in range(4)]
        sgt = A("m_sg", [128, 512]); tt = A("m_tt", [128, 512])
        pT = [PS(f"m_pT{i}", [128, 8, 128], BF16) for i in range(2)]
        pb = [PS(f"m_pb{i}", [128, 2, 128]) for i in range(2)]
        py = [PS(f"m_py{i}", [128, 128]) for i in range(2)]
        pa = PS("m_pa", [128, 512]); pbb = PS("m_pbb", [128, 512])
        hbp = hy[:].rearrange("p (g s c) -> p g s c", g=64, s=8)
        yt = hy[:].rearrange("p (s ch) -> p s ch", s=8)
        U = uy[:].rearrange("p (g j) -> p g j", g=64)
        yT = uy[:].rearrange("p (k t) -> p k t", k=8)
        wa_v = wb["s5_glu_a"][jj].rearrange("(k p) n -> p k n", p=128)
        wb_v = wb["s5_glu_b"][jj].rearrange("(k p) n -> p k n", p=128)
        wkey = f"w_s5_{jj}"

        def xv(ap, q):
            return ap[q * 1024:(q + 1) * 1024, :].rearrange("(p t) d -> p t d", t=8)

        P.op("dve", "memset", ((Z[0][:], 0.0), {}), writes=[("Z", 0)])
        P.op("dve", "memset", ((XS[:, 0, :], 0.0), {}), writes=["XS"])
        cnt = 0
        ncp = 0
        for q in range(4):
            P.op("sync", "dma_start", dict(out=xq[:], in_=xv(xsrc, q)), reads=[("xres", q)], writes=["xq"], dma="m_x")
            for t in range(8):
                P.op("act", "activation", dict(out=sq[:], in_=xq[:, t, :], func=AF.Square, accum_out=ss[:, t:t + 1]),
                     reads=["xq"], writes=["m_sq", "m_ss"])
            P.op("dve", "tensor_scalar", dict(out=rstd[:], in0=ss[:], scalar1=1.0 / D, scalar2=EPS, op0=ALU.mult, op1=ALU.add),
                 reads=["m_ss"], writes=["m_rstd"])
            P.op("act", "activation", dict(out=rstd[:], in_=rstd[:], func=AF.Sqrt), reads=["m_rstd"], writes=["m_rstd"])
            P.op("dve", "reciprocal", dict(out=rstd[:], in_=rstd[:]), reads=["m_rstd"], writes=["m_rstd"])
            for t in range(8):
                P.op("dve", "scalar_tensor_tensor", dict(out=hbp[:, :, t, :], in0=xq[:, t, :].rearrange("p (g c) -> p g c", c=16),
                                                         scalar=rstd[:, t:t + 1], in1=gn[:].rearrange("p (g c) -> p g c", c=16),
                                                         op0=ALU.mult, op1=ALU.mult),
                     reads=["xq", "m_rstd", "s_gn"], writes=["hy"])
            for g8 in range(8):
                pt = pT[g8 % 2]
                for gq in range(8):
                    g = g8 * 8 + gq
                    P.op("pe", "transpose", dict(out=pt[:, gq, :], in_=hy[:, g * 128:(g + 1) * 128], identity=ident_b[:]),
                         reads=["hy", "ident"], writes=[("m_pT", g8 % 2)])
                if g8 % 2 == 0:
                    P.op("dve", "tensor_copy", dict(out=U[:, g8 * 8:(g8 + 1) * 8, :], in_=pt[:]), reads=[("m_pT", 0)], writes=["uy"])
                else:
                    P.op("act", "copy", dict(out=U[:, g8 * 8:(g8 + 1) * 8, :], in_=pt[:]), reads=[("m_pT", 1)], writes=["uy"])
            for pr in range(32):
                pbt = pb[pr % 2]
                for ri, wt in enumerate((wtr, wti)):
                    for g2 in range(2):
                        g = 2 * pr + g2
                        P.op("pe", "matmul", ((pbt[g2 * 64:(g2 + 1) * 64, ri, :],),
                                              dict(lhsT=wt[:, g, :], rhs=U[:, g, :], start=True, stop=True)),
                             reads=["uy", "s_w"], writes=[("m_pb", pr % 2)])
                bt = beta[:, :, :]
                ov = bass.AP(tensor=bt.tensor, offset=bt.offset + pr, ap=[list(bt.ap[0]), [32, 2], [64, 128]])
                if pr % 2 == 0:
                    P.op("dve", "tensor_copy", dict(out=ov, in_=pbt[:]), reads=[("m_pb", 0)], writes=["beta"])
                else:
                    P.op("act", "copy", dict(out=ov, in_=pbt[:]), reads=[("m_pb", 1)], writes=["beta"])
            for j0 in range(2):
                P.op("act", "copy", dict(out=AB3[(cnt + 1 + j0) % 4][:, 2, :], in_=beta[:, j0, :]), reads=["beta"],
                     writes=[("ABb", (cnt + 1 + j0) % 4)])
            for j in range(128):
                zc, zn = Z[cnt % 2], Z[(cnt + 1) % 2]
                kc, kn_ = ("Z", cnt % 2), ("Z", (cnt + 1) % 2)
                cnt += 1
                zt = zc[:, :, :]
                win = bass.AP(tensor=zt.tensor, offset=zt.offset, ap=[list(zt.ap[0]), [32, 2], [1, 64]])
                ab = AB3[cnt % 4]
                abk, abbk = ("AB", cnt % 4), ("ABb", cnt % 4)
                if j + 2 < 128:
                    P.op("act", "copy", dict(out=AB3[(cnt + 2) % 4][:, 2, :], in_=beta[:, j + 2, :]), reads=["beta"],
                         writes=[("ABb", (cnt + 2) % 4)])
                P.op("dve", "tensor_tensor", dict(out=ab[:, 0:2, :], in0=mcat[:], in1=win, op=ALU.mult), reads=[kc, "s_mcat"], writes=[abk])
                abt = ab[:, :, :]
                rin = bass.AP(tensor=abt.tensor, offset=abt.offset, ap=[list(abt.ap[0]), [0, 2], [1, 64], [64, 3]])
                P.op("dve", "tensor_reduce", dict(out=zn[:], in_=rin, op=ALU.add, axis=AX.X), reads=[abk, abbk], writes=[kn_])
                P.op("act", "copy", dict(out=XS[:, j + 1, :], in_=zn[:, 0, :]), reads=[kn_], writes=["XS"])
            def grp_front(g):
                pr, g2 = g // 2, g % 2
                pyt = [py[0][:], py[1][:], pa[:, 0:128], pbb[:, 0:128]][g % 4]
                pyk = [("m_py", 0), ("m_py", 1), "m_pa", "m_pbb"][g % 4]
                rs = slice(g2 * 64, (g2 + 1) * 64)
                P.op("pe", "matmul", ((pyt,), dict(lhsT=toep[:, g, :], rhs=U[:, g, :], start=True, stop=False)),
                     reads=["uy", "s_toep"], writes=[pyk])
                P.op("pe", "matmul", ((pyt,), dict(lhsT=vre[rs, pr, :], rhs=XS[rs, 0:128, pr], start=False, stop=False)),
                     reads=["XS", "s_v"], writes=[pyk])
                P.op("pe", "matmul", ((pyt,), dict(lhsT=vim[rs, pr, :], rhs=XS[rs, 0:128, 32 + pr], start=False, stop=True)),
                     reads=["XS", "s_v"], writes=[pyk])
                P.op("dve", "scalar_tensor_tensor", dict(out=ytmp[g % 4][:], in0=U[:, g, :], scalar=dvec[:, g:g + 1], in1=pyt,
                                                         op0=ALU.mult, op1=ALU.add),
                     reads=["uy", pyk, "s_dvec"], writes=[("m_ytmp", g % 4)])
                P.op("act", "activation", dict(out=yact[g % 4][:], in_=ytmp[g % 4][:], func=AF.Gelu),
                     reads=[("m_ytmp", g % 4)], writes=[("m_yact", g % 4)])

            def grp_back(g):
                g8 = g // 8
                pt = pT[g8 % 2]
                P.op("pe", "transpose", dict(out=pt[:, g % 8, :], in_=yact[g % 4][:], identity=ident_b[:]),
                     reads=[("m_yact", g % 4), "ident"], writes=[("m_pT", g8 % 2)])
                if g % 8 == 7:
                    ov = yt[:, :, g8 * 128:(g8 + 1) * 128].rearrange("p i (g c) -> p i g c", c=16)
                    iv = pt[:].rearrange("p g (i c) -> p i g c", c=16)
                    P.op("dve", "tensor_copy", dict(out=ov, in_=iv), reads=[("m_pT", g8 % 2)], writes=["hy"])

            for g in range(64 + 2):
                if g < 64:
                    grp_front(g)
                if g >= 2:
                    grp_back(g - 2)
            P.op("act", "copy", dict(out=XS[:, 0, :], in_=XS[:, 128, :]), reads=["XS"], writes=["XS"])
            for s in range(8):
                pt = pT[s % 2]
                for k in range(8):
                    P.op("pe", "transpose", dict(out=pt[:, k, :], in_=yt[:, s, k * 128:(k + 1) * 128], identity=ident_b[:]),
                         reads=["hy", "ident"], writes=[("m_pT", s % 2)])
                if s % 2 == 0:
                    P.op("dve", "tensor_copy", dict(out=yT[:, :, s * 128:(s + 1) * 128], in_=pt[:]), reads=[("m_pT", 0)], writes=["uy"])
                else:
                    P.op("act", "copy", dict(out=yT[:, :, s * 128:(s + 1) * 128], in_=pt[:]), reads=[("m_pT", 1)], writes=["uy"])
            for nh in range(2):
                P.op("sync", "dma_start", dict(out=wa[:], in_=wa_v[:, :, nh * 512:(nh + 1) * 512]), reads=self.wkeys[wkey], writes=["m_wa"], dma="m_w")
                P.op("sync", "dma_start", dict(out=wbt[:], in_=wb_v[:, :, nh * 512:(nh + 1) * 512]), reads=self.wkeys[wkey], writes=["m_wb"], dma="m_w")
                for s in range(8):
                    for k in range(8):
                        P.op("pe", "matmul", ((pa[:],), dict(lhsT=yT[:, k, s * 128:(s + 1) * 128], rhs=wa[:, k, :],
                                                            start=(k == 0), stop=(k == 7))), reads=["uy", "m_wa"], writes=["m_pa"])
                    for k in range(8):
                        P.op("pe", "matmul", ((pbb[:],), dict(lhsT=yT[:, k, s * 128:(s + 1) * 128], rhs=wbt[:, k, :],
                                                             start=(k == 0), stop=(k == 7))), reads=["uy", "m_wb"], writes=["m_pbb"])
                    P.op("act", "activation", dict(out=sgt[:], in_=pbb[:], func=AF.Sigmoid), reads=["m_pbb"], writes=["m_sg"])
                    P.op("dve", "tensor_tensor", dict(out=tt[:], in0=sgt[:], in1=pa[:], op=ALU.mult), reads=["m_sg", "m_pa"], writes=["m_tt"])
                    xs_ = xq[:, s, nh * 512:(nh + 1) * 512]
                    P.op("dve", "tensor_tensor", dict(out=xs_, in0=xs_, in1=tt[:], op=ALU.add), reads=["m_tt", "xq"], writes=["xq"])
            P.op("sync", "dma_start", dict(out=xv(xdst, q), in_=xq[:]), reads=["xq"], writes=[("xres", q)], dma="m_st")

    def attn_phase(self, es, xsrc, xdst, L, wn, wb, ident_f, ident_b):
        nc, P = self.nc, self.P
        jj = L // 2
        lam_init = 0.8 - 0.6 * math.exp(-0.3 * L)
        A = lambda n, s, d=F32: es.enter_context(self.sb(n, s, d))
        QT, KT, HD = self.scr["QT"], self.scr["KT"], self.scr["HD"]
        win_v = wb["attn_w_in"][jj].rearrange("(k p) n -> p k n", p=128)
        wout_v = wb["attn_w_out"][jj].rearrange("(k p) n -> p k n", p=128)
        wkey = f"w_attn_{jj}"
        kn = lambda ap: ap.tensor.name.split("__u")[0]
        Vd = A("a_vd", [128, 32, 4, 129], BF16)
        Vf = A("a_vf", [128, 32, 8, 65], BF16)
        cposk = A("a_cposk", [128, 32, 8])
        P.op("dve", "memset", ((Vd[:, :, :, 128:129], 1.0), {}), writes=["Vd"])
        P.op("dve", "memset", ((Vf[:, :, :, 64:65], 1.0), {}), writes=["Vf"])

        LSP = A("a_lsp", [128, 32, 8]); R = A("a_R", [128, 33, 8])
        tri = A("a_tri", [128, 128]); onesf = A("a_onesf", [128, 128])
        with contextlib.ExitStack() as e1:
            B = lambda n, s, d=F32: e1.enter_context(self.sb(n, s, d))
            PS = lambda n, s, d=F32: e1.enter_context(self.pp(n, s, d))
            xt = [B(f"a_xt{i}", [128, 4, D]) for i in range(2)]
            hb = [B(f"a_hb{i}", [128, D], BF16) for i in range(2)]
            hT = B("a_hT", [128, 8, 512], BF16)
            win = B("a_win", [128, 8, IN_COLS], BF16)
            gn = B("a_gn", [128, D])
            sq = B("a_sq", [128, D], BF16); ss = B("a_ss", [128, 4]); rstd = B("a_rstd", [128, 4])
            qsq = [B(f"a_qsq{i}", [128, 512]) for i in range(3)]
            lnv = [B(f"a_lnv{i}", [128, 512]) for i in range(2)]
            qo = [B(f"a_qo{i}", [128, 512], BF16) for i in range(2)]
            G = B("a_G", [128, 4]); epsc = B("a_eps", [128, 1]); ones2 = B("a_ones2", [128, 128])
            fgb = B("a_fgb", [128, 8]); zt = B("a_zt", [128, 8])
            pT = PS("a_pT", [128, 8, 128], BF16)
            pq = [PS(f"a_pq{i}", [128, 512]) for i in range(3)]
            pms = PS("a_pms", [128, 512])
            pv = [PS(f"a_pv{i}", [128, 512]) for i in range(2)]
            pfl = PS("a_pfl", [128, 512])

            def DMA(out, in_, wr, rd=()):
                P.op("sync", "dma_start", dict(out=out, in_=in_), reads=list(rd), writes=[wr], dma="a_ld")

            DMA(gn[:], wn["mix_norm"][L].partition_broadcast(128), "a_gn")
            for h in range(2):
                DMA(win[:, h * 4:(h + 1) * 4, :], win_v[:, h * 4:(h + 1) * 4, :], "a_win", self.wkeys[wkey])
            for c, nm in enumerate(("diff_q_norm", "diff_k_norm", "fox_q_norm", "fox_k_norm")):
                for h in range(2):
                    DMA(G[h * 64:(h + 1) * 64, c:c + 1], wn[nm][jj].rearrange("(d o) -> d o", o=1), "a_G")
            DMA(fgb[:], wn["fg_bias"][jj].partition_broadcast(128), "a_fgb")
            DMA(ones2[:], self.consts["ones2"], "a_ones2")
            DMA(tri[:], self.consts["tri"], "a_tri")
            P.op("dve", "memset", ((epsc[:], EPS), {}), writes=["a_eps"])
            P.op("dve", "memset", ((onesf[:], 1.0), {}), writes=["a_onesf"])
            P.op("dve", "memset", ((R[:, 0, :], 0.0), {}), writes=["a_R"])
            qk_tiles = []
            for h in range(4):
                qk_tiles.append((h * 128, 0, QT, 2 * h))
            for h in range(4):
                qk_tiles.append((512 + h * 128, 1, KT, 2 * h))
            for h in range(4):
                qk_tiles.append((1536 + h * 128, 2, QT, 8 + 2 * h))
            for h in range(4):
                qk_tiles.append((2048 + h * 128, 3, KT, 8 + 2 * h))

            def xv(ap, b):
                return ap[b * 512:(b + 1) * 512, :].rearrange("(t p) d -> p t d", p=128)

            def load_x(b):
                P.op("sync", "dma_start", dict(out=xt[b % 2][:], in_=xv(xsrc, b)), reads=[("xres", b)],
                     writes=[("a_xt", b % 2)], dma=f"a_x{b % 2}")

            load_x(0)
            ev = 0
            for b in range(8):
                X = xt[b % 2]
                xk = ("a_xt", b % 2)
                if b + 1 < 8:
                    load_x(b + 1)
                for t in range(4):
                    P.op("act", "activation", dict(out=sq[:], in_=X[:, t, :], func=AF.Square, accum_out=ss[:, t:t + 1]),
                         reads=[xk], writes=["a_sq", "a_ss"])
                P.op("dve", "tensor_scalar", dict(out=rstd[:], in0=ss[:], scalar1=1.0 / D, scalar2=EPS, op0=ALU.mult, op1=ALU.add),
                     reads=["a_ss"], writes=["a_rstd"])
                P.op("act", "activation", dict(out=rstd[:], in_=rstd[:], func=AF.Sqrt), reads=["a_rstd"], writes=["a_rstd"])
                P.op("dve", "reciprocal", dict(out=rstd[:], in_=rstd[:]), reads=["a_rstd"], writes=["a_rstd"])
                for t in range(4):
                    H = hb[t % 2]
                    P.op("dve", "scalar_tensor_tensor", dict(out=H[:], in0=X[:, t, :], scalar=rstd[:, t:t + 1], in1=gn[:],
                                                             op0=ALU.mult, op1=ALU.mult),
                         reads=[xk, "a_rstd", "a_gn"], writes=[("a_hb", t % 2)])
                    for k in range(8):
                        P.op("pe", "transpose", dict(out=pT[:, k, :], in_=H[:, k * 128:(k + 1) * 128], identity=ident_b[:]),
                             reads=[("a_hb", t % 2), "ident"], writes=["a_pT"])
                    P.op("dve", "tensor_copy", dict(out=hT[:, :, t * 128:(t + 1) * 128], in_=pT[:]), reads=["a_pT"], writes=["a_hT"])
                def qk_front(i):
                    c0, gc, dstT, m0 = qk_tiles[i]
                    q = (b * 16 + i) % 3
                    for k in range(8):
                        P.op("pe", "matmul", ((pq[q][:],), dict(lhsT=win[:, k, c0:c0 + 128], rhs=hT[:, k, :],
                                                                start=(k == 0), stop=(k == 7))),
                             reads=["a_win", "a_hT"], writes=[("a_pq", q)])
                    P.op("act", "activation", dict(out=qsq[q][:], in_=pq[q][:], func=AF.Square), reads=[("a_pq", q)], writes=[("a_qsq", q)])

                def qk_back(i):
                    c0, gc, dstT, m0 = qk_tiles[i]
                    q = (b * 16 + i) % 3
                    r = (b * 16 + i) % 2
                    P.op("pe", "matmul", ((pms[:],), dict(lhsT=ones2[:], rhs=qsq[q][:], start=True, stop=True)),
                         reads=["a_ones2", ("a_qsq", q)], writes=["a_pms"])
                    P.op("act", "activation", dict(out=lnv[r][:], in_=pms[:], func=AF.Ln, bias=epsc[:, 0:1]),
                         reads=["a_pms", "a_eps"], writes=[("a_lnv", r)])
                    P.op("act", "activation", dict(out=lnv[r][:], in_=lnv[r][:], func=AF.Exp, scale=-0.5),
                         reads=[("a_lnv", r)], writes=[("a_lnv", r)])
                    P.op("dve", "scalar_tensor_tensor", dict(out=qo[r][:], in0=pq[q][:], scalar=G[:, gc:gc + 1], in1=lnv[r][:],
                                                             op0=ALU.mult, op1=ALU.mult),
                         reads=[("a_pq", q), ("a_lnv", r), "a_G"], writes=[("a_qo", r)])
                    for hh in range(2):
                        P.op("sync", "dma_start", dict(out=dstT[m0 + hh, 0:64, b * 512:(b + 1) * 512],
                                                       in_=qo[r][hh * 64:(hh + 1) * 64, :]),
                             reads=[("a_qo", r)], writes=[(kn(dstT), m0 + hh, b)], dma="a_qst")

                for i in range(17):
                    if i < 16:
                        qk_front(i)
                    if i >= 1:
                        qk_back(i - 1)
                for t in range(4):
                    blk = b * 4 + t
                    for vi, (c0, Vt, nh, vd) in enumerate(((1024, Vd, 4, 128), (2560, Vf, 8, 64))):
                        q = vi
                        for k in range(8):
                            P.op("pe", "matmul", ((pv[q][:],), dict(lhsT=hT[:, k, t * 128:(t + 1) * 128], rhs=win[:, k, c0:c0 + 512],
                                                                    start=(k == 0), stop=(k == 7))),
                                 reads=["a_win", "a_hT"], writes=[("a_pv", q)])
                        P.op("act" if vi == 0 else "dve", "copy" if vi == 0 else "tensor_copy",
                             dict(out=Vt[:, blk, :, 0:vd], in_=pv[q][:].rearrange("p (h v) -> p h v", h=nh)),
                             reads=[("a_pv", q)], writes=["Vd" if vi == 0 else "Vf"])
                    for k in range(8):
                        P.op("pe", "matmul", ((pfl[:, 0:8],), dict(lhsT=hT[:, k, t * 128:(t + 1) * 128], rhs=win[:, k, 3072:3080],
                                                                   start=(k == 0), stop=(k == 7))),
                             reads=["a_win", "a_hT"], writes=["a_pfl"])
                    P.op("dve", "tensor_tensor", dict(out=zt[:], in0=pfl[:, 0:8], in1=fgb[:], op=ALU.add),
                         reads=["a_pfl", "a_fgb"], writes=["a_zt"])
                    P.op("act", "activation", dict(out=zt[:], in_=zt[:], func=AF.Exp, scale=-1.0), reads=["a_zt"], writes=["a_zt"])
                    P.op("act", "activation", dict(out=LSP[:, blk, :], in_=zt[:], func=AF.Ln, bias=1.0), reads=["a_zt"], writes=["a_lsp"])
                    P.op("dve", "tensor_tensor", dict(out=R[:, blk + 1, :], in0=R[:, blk, :], in1=LSP[:, blk, :], op=ALU.add),
                         reads=["a_lsp", "a_R"], writes=["a_R"])
        P.barrier()
        self.uid += 1
        with contextlib.ExitStack() as e1:
            B = lambda n, s, d=F32: e1.enter_context(self.sb(n, s, d))
            PS = lambda n, s, d=F32: e1.enter_context(self.pp(n, s, d))
            cT = B("a_cT", [8, S]); rT = B("a_rT", [8, S])
            a123 = [B(f"a_a{i}", [8, S], BF16) for i in range(3)]
            onesb = B("a_onesb", [8, S], BF16)
            pv = [PS(f"a_pv{i}", [128, 512]) for i in range(2)]
            pq = [PS(f"a_pq{i}", [128, 512]) for i in range(2)]
            P.op("dve", "memset", ((onesb[:], 1.0), {}), writes=["a_onesb"])
            for blk in range(32):
                q = blk % 2
                P.op("pe", "matmul", ((pv[q][:, 0:8],), dict(lhsT=tri[:], rhs=LSP[:, blk, :], start=True, stop=False)),
                     reads=["a_tri", "a_lsp"], writes=[("a_pv", q)])
                P.op("pe", "matmul", ((pv[q][:, 0:8],), dict(lhsT=onesf[:], rhs=R[:, blk, :], start=False, stop=True)),
                     reads=["a_onesf", "a_R"], writes=[("a_pv", q)])
                P.op("dve", "tensor_copy", dict(out=cposk[:, blk, :], in_=pv[q][:, 0:8]), reads=[("a_pv", q)], writes=["cposk"])
                P.op("pe", "matmul", ((pq[q][0:8, 0:128],), dict(lhsT=LSP[:, blk, :], rhs=tri[:], start=True, stop=False)),
                     reads=["a_tri", "a_lsp"], writes=[("a_pq", q)])
                P.op("pe", "matmul", ((pq[q][0:8, 0:128],), dict(lhsT=R[:, blk, :], rhs=onesf[:], start=False, stop=True)),
                     reads=["a_onesf", "a_R"], writes=[("a_pq", q)])
                P.op("act", "activation", dict(out=cT[:, blk * 128:(blk + 1) * 128], in_=pq[q][0:8, 0:128], func=AF.Copy, scale=-8.0),
                     reads=[("a_pq", q)], writes=["a_cT"])
            P.op("dve", "tensor_copy", dict(out=a123[0][:], in_=cT[:]), reads=["a_cT"], writes=["a_a0"])
            P.op("dve", "tensor_tensor", dict(out=rT[:], in0=cT[:], in1=a123[0][:], op=ALU.subtract), reads=["a_cT", "a_a0"], writes=["a_rT"])
            P.op("dve", "tensor_copy", dict(out=a123[1][:], in_=rT[:]), reads=["a_rT"], writes=["a_a1"])
            P.op("dve", "tensor_tensor", dict(out=cT[:], in0=rT[:], in1=a123[1][:], op=ALU.subtract), reads=["a_rT", "a_a1"], writes=["a_cT"])
            P.op("dve", "tensor_copy", dict(out=a123[2][:], in_=cT[:]), reads=["a_cT"], writes=["a_a2"])
            for i in range(3):
                P.op("sync", "dma_start", dict(out=QT[8:16, 64 + i, :], in_=a123[i][:]), reads=[f"a_a{i}"],
                     writes=[("QTaug", i)], dma="a_qst")
                P.op("sync", "dma_start", dict(out=KT[8:16, 64 + i, :], in_=onesb[:]), reads=["a_onesb"],
                     writes=[("KTaug", i)], dma="a_qst")
        self.dump("lsp", LSP[:], [])
        self.dump("cposk", cposk[:], [])
        self.dump("vd", Vd[:, 0:2], [])
        self.dump("vf", Vf[:, 30:32], [])
        self.dump("qt", QT[:, :, 0:512], [])
        self.dump("kt", KT[:, :, 3584:4096], [])
        P.barrier()

        Ocat = A("b_ocat", [128, 32, D], BF16)
        with contextlib.ExitStack() as e2:
            B = lambda n, s, d=F32: e2.enter_context(self.sb(n, s, d))
            PS = lambda n, s, d=F32: e2.enter_context(self.pp(n, s, d))
            qT = [B(f"b_qT{i}", [67, S], BF16) for i in range(2)]
            kT = [B(f"b_kT{i}", [67, S], BF16) for i in range(2)]
            NPS, NPE = 4, 7
            Pe = [B(f"b_pe{i}", [128, 512], BF16) for i in range(NPE)]
            BT = B("b_BT", [128, 5, 2, 128]); BT8 = B("b_BT8", [128, 5, 2, 128])
            b31 = B("b_b31", [128, 4])
            n0 = B("b_n0", [128, 32, 128])
            rb33 = B("b_rb33", [33, 5]); rbl = B("b_rbl", [33, 128]); OH = B("b_oh", [33, 384]); hrep = B("b_hrep", [128, 384])
            lq = [B(f"b_lq{i}", [128, 64]) for i in range(4)]
            lp = B("b_lp", [128, 64]); e12 = B("b_e12", [128, 2]); nlam = B("b_nlam", [128, 1])
            SW = B("b_sw", [128, 128])
            ssqa = B("b_ssqa", [128, 32]); rinv = B("b_rinv", [128, 1]); odt = B("b_odt", [128, 128]); ssq = B("b_ssq", [128, 1]); junk = B("b_junk", [128, 128], BF16)
            ps = [PS(f"b_ps{i}", [128, 512]) for i in range(NPS)]
            po = [PS(f"b_po{i}", [128, 4, 256]) for i in range(2)]

            def DMA(out, in_, wr, rd=(), sem="b_ld"):
                P.op("sync", "dma_start", dict(out=out, in_=in_), reads=list(rd), writes=[wr], dma=sem)

            P.op("dve", "memset", ((rb33[:], 0.0), {}), writes=["b_rb33"])
            P.op("dve", "memset", ((rb33[32:33, :], NEG), {}), writes=["b_rb33"])
            DMA(rb33[0:32, 0:4], wn["rel_bias"], "b_rb33")
            DMA(OH[:], self.consts["relOH"], "b_oh")
            for i, nm in enumerate(("diff_lambda_q1", "diff_lambda_k1", "diff_lambda_q2", "diff_lambda_k2")):
                DMA(lq[i][:], wn[nm][jj].partition_broadcast(128), f"b_lq{i}")
            DMA(SW[:], wn["diff_subln"][jj].partition_broadcast(128), "b_sw")
            for h in range(5):
                P.op("dve", "tensor_copy", dict(out=rbl[:], in_=rb33[:, h:h + 1].broadcast_to([33, 128])), reads=["b_rb33"], writes=["b_rbl"])
                P.op("pe", "matmul", ((ps[0][:, 0:384],), dict(lhsT=rbl[:], rhs=OH[:], start=True, stop=True)),
                     reads=["b_rbl", "b_oh"], writes=[("b_ps", 0)])
                P.op("dve", "tensor_copy", dict(out=hrep[:], in_=ps[0][:, 0:384]), reads=[("b_ps", 0)], writes=["b_hrep"])
                if h < 4:
                    P.op("dve", "tensor_copy", dict(out=b31[:, h:h + 1], in_=hrep[:, 383:384]), reads=["b_hrep"], writes=["b_b31"])
                DMA(HD[h], hrep[:], ("HD", h), ["b_hrep"], sem="b_hd")
                hd = HD[h]
                src = bass.AP(tensor=hd.tensor, offset=hd.offset + 127, ap=[[383, 128], [128, 2], [1, 128]])
                DMA(BT[:, h, :, :], src, "b_BT", [("HD", h)], sem="b_hd2")
            for h in range(5):
                if h < 4:
                    P.op("dve", "tensor_scalar", dict(out=BT8[:, h], in0=BT[:, h], scalar1=b31[:, h:h + 1], scalar2=8.0,
                                                      op0=ALU.subtract, op1=ALU.mult), reads=["b_BT", "b_b31"], writes=["b_BT8"])
                else:
                    P.op("dve", "tensor_scalar", dict(out=BT8[:, h], in0=BT[:, h], scalar1=8.0, scalar2=None, op0=ALU.mult),
                         reads=["b_BT"], writes=["b_BT8"])
            for i in range(2):
                P.op("dve", "tensor_tensor", dict(out=lp[:], in0=lq[2 * i][:], in1=lq[2 * i + 1][:], op=ALU.mult),
                     reads=[f"b_lq{2 * i}", f"b_lq{2 * i + 1}"], writes=["b_lp"])
                P.op("dve", "tensor_reduce", dict(out=e12[:, i:i + 1], in_=lp[:], op=ALU.add, axis=AX.X), reads=["b_lp"], writes=["b_e12"])
            P.op("act", "activation", dict(out=e12[:], in_=e12[:], func=AF.Exp), reads=["b_e12"], writes=["b_e12"])
            P.op("dve", "scalar_tensor_tensor", dict(out=nlam[:], in0=e12[:, 1:2], scalar=-lam_init, in1=e12[:, 0:1],
                                                     op0=ALU.add, op1=ALU.subtract), reads=["b_e12"], writes=["b_nlam"])
            P.op("dve", "tensor_scalar", dict(out=SW[:], in0=SW[:], scalar1=1.0 - lam_init, scalar2=None, op0=ALU.mult),
                 reads=["b_sw"], writes=["b_sw"])

            maps = [(2 * h + m, "d", h, m) for h in range(4) for m in range(2)] + [(8 + f, "f", f, 0) for f in range(8)]
            steps = [(mi, mp, I, J) for mi, mp in enumerate(maps) for I in range(8) for J in range(4 * I + 4)]
            LAG = 4
            started = {}

            def front(idx):
                mi, (mapi, kind, hh, mm), I, J = steps[idx]
                sl = mi % 2
                K = 64 if kind == "d" else 67
                if I == 0 and J == 0:
                    for c4 in range(2):
                        cs = slice(c4 * 2048, (c4 + 1) * 2048)
                        DMA(qT[sl][0:K, cs], QT[mapi, 0:K, cs], ("b_qT", sl), [], sem=f"b_q{sl}")
                        DMA(kT[sl][0:K, cs], KT[mapi, 0:K, cs], ("b_kT", sl), [], sem=f"b_q{sl}")
                bth = hh if kind == "d" else 4
                qlo = max(4 * I, J)
                c0 = (qlo - 4 * I) * 128
                pst, psk = ps[idx % NPS], ("b_ps", idx % NPS)
                pet, pek = Pe[idx % NPE], ("b_pe", idx % NPE)
                P.op("pe", "matmul", ((pst[:, c0:512],), dict(lhsT=kT[sl][0:K, J * 128:(J + 1) * 128],
                                                              rhs=qT[sl][0:K, I * 512 + c0:(I + 1) * 512],
                                                              start=True, stop=True)),
                     reads=[("b_qT", sl), ("b_kT", sl)], writes=[psk])
                if kind == "d":
                    fbias, frd = b31[:, hh:hh + 1], "b_b31"
                else:
                    fbias, frd = cposk[:, J, hh:hh + 1], "cposk"
                nnear = 2 if kind == "d" else 1
                for dist in range(nnear):
                    qt = J + dist
                    if qt < qlo or qt >= 4 * I + 4:
                        continue
                    cc = (qt - 4 * I) * 128
                    P.op("dve", "tensor_tensor", dict(out=pst[:, cc:cc + 128], in0=pst[:, cc:cc + 128], in1=BT8[:, bth, dist, :],
                                                      op=ALU.add), reads=[psk, "b_BT8"], writes=[psk])
                P.op("act", "activation", dict(out=pet[:, c0:512], in_=pst[:, c0:512], func=AF.Exp, scale=0.125, bias=fbias),
                     reads=[psk, frd], writes=[pek])

            def back(idx):
                mi, (mapi, kind, hh, mm), I, J = steps[idx]
                sp = mi * 8 + I
                pot, pok = po[sp % 2], ("b_po", sp % 2)
                pet, pek = Pe[idx % NPE], ("b_pe", idx % NPE)
                Vt, vd = (Vd, 128) if kind == "d" else (Vf, 64)
                vkey = "Vd" if kind == "d" else "Vf"
                qlo = max(4 * I, J)
                for qt in range(qlo, 4 * I + 4):
                    ql = qt - 4 * I
                    cc = ql * 128
                    st = (sp, ql // 2) not in started
                    started[(sp, ql // 2)] = True
                    P.op("pe", "matmul", ((pot[:, ql, 0:vd + 1],), dict(lhsT=pet[:, cc:cc + 128], rhs=Vt[:, J, hh, 0:vd + 1],
                                                                       start=st, stop=(J == qt), skip_group_check=True)),
                         reads=[pek, vkey], writes=[pok])
                if J != 4 * I + 3:
                    return
                for ql in range(4):
                    qt = 4 * I + ql
                    P.op("dve", "reciprocal", dict(out=rinv[:], in_=pot[:, ql, vd:vd + 1]), reads=[pok], writes=["b_rinv"])
                    if kind == "f":
                        P.op("dve", "tensor_scalar", dict(out=Ocat[:, qt, 512 + hh * 64:512 + (hh + 1) * 64], in0=pot[:, ql, 0:64],
                                                          scalar1=rinv[:, 0:1], scalar2=None, op0=ALU.mult),
                             reads=[pok, "b_rinv"], writes=[("b_ocat", qt)])
                    elif mm == 0:
                        P.op("dve", "tensor_scalar", dict(out=n0[:, qt, :], in0=pot[:, ql, 0:128], scalar1=rinv[:, 0:1], scalar2=None,
                                                          op0=ALU.mult), reads=[pok, "b_rinv"], writes=["b_n0"])
                    else:
                        P.op("dve", "tensor_tensor", dict(out=rinv[:], in0=rinv[:], in1=nlam[:], op=ALU.mult),
                             reads=["b_rinv", "b_nlam"], writes=["b_rinv"])
                        P.op("dve", "scalar_tensor_tensor", dict(out=n0[:, qt, :], in0=pot[:, ql, 0:128], scalar=rinv[:, 0:1], in1=n0[:, qt, :],
                                                                 op0=ALU.mult, op1=ALU.add), reads=[pok, "b_rinv", "b_n0"], writes=["b_n0"])
                        P.op("dve", "tensor_tensor", dict(out=odt[:], in0=n0[:, qt, :], in1=n0[:, qt, :], op=ALU.mult),
                             reads=["b_n0"], writes=["b_odt"])
                        P.op("dve", "tensor_reduce", dict(out=ssqa[:, qt:qt + 1], in_=odt[:], op=ALU.add, axis=AX.X),
                             reads=["b_odt"], writes=["b_ssqa"])
                if kind == "d" and mm == 1 and I == 7:
                    P.op("dve", "tensor_scalar", dict(out=ssqa[:], in0=ssqa[:], scalar1=1.0 / 128, scalar2=EPS, op0=ALU.mult, op1=ALU.add),
                         reads=["b_ssqa"], writes=["b_ssqa"])
                    P.op("act", "activation", dict(out=ssqa[:], in_=ssqa[:], func=AF.Sqrt), reads=["b_ssqa"], writes=["b_ssqa"])
                    P.op("dve", "reciprocal", dict(out=ssqa[:], in_=ssqa[:]), reads=["b_ssqa"], writes=["b_ssqa"])
                    for qt in range(32):
                        P.op("dve", "scalar_tensor_tensor", dict(out=Ocat[:, qt, hh * 128:(hh + 1) * 128], in0=n0[:, qt, :],
                                                                 scalar=ssqa[:, qt:qt + 1], in1=SW[:], op0=ALU.mult, op1=ALU.mult),
                             reads=["b_n0", "b_ssqa", "b_sw"], writes=[("b_ocat", qt)])

            for idx in range(len(steps) + LAG):
                if idx < len(steps):
                    front(idx)
                if idx >= LAG:
                    back(idx - LAG)
            self.dump("bt", BT[:], [])
            self.dump("n0", n0[:], [])
            self.dump("nlam", nlam[:], [])
            self.dump("ocat", Ocat[:, 0:2, :], [])
            self.dump("ocat2", Ocat[:, 30:32, :], [])
        P.barrier()
        self.uid += 1
        with contextlib.ExitStack() as e3:
            B = lambda n, s, d=F32: e3.enter_context(self.sb(n, s, d))
            PS = lambda n, s, d=F32: e3.enter_context(self.pp(n, s, d))
            wout = B("b_wout", [128, 8, D], BF16)
            oT = [B(f"b_oT{i}", [128, 8, 128], BF16) for i in range(2)]
            xo = [B(f"b_xo{i}", [128, D]) for i in range(2)]
            pTs = [PS(f"b_pT{i}", [128, 8, 128], BF16) for i in range(2)]
            po2s = [PS(f"b_po2{i}", [128, 512]) for i in range(4)]

            def DMA(out, in_, wr, rd=(), sem="b_ld"):
                P.op("sync", "dma_start", dict(out=out, in_=in_), reads=list(rd), writes=[wr], dma=sem)

            for h in range(2):
                DMA(wout[:, h * 4:(h + 1) * 4, :], wout_v[:, h * 4:(h + 1) * 4, :], "b_wout", self.wkeys[wkey])
            def xrow(ap, blk):
                return ap[blk * 128:(blk + 1) * 128, :]
            for blk in range(32):
                sl = blk % 2
                pT, pTk = pTs[sl], ("b_pT", sl)
                DMA(xo[sl][:], xrow(xsrc, blk), ("b_xo", sl), [("xres", blk // 4)], sem=f"b_x{sl}")
                for k in range(8):
                    P.op("pe", "transpose", dict(out=pT[:, k, :], in_=Ocat[:, blk, k * 128:(k + 1) * 128], identity=ident_b[:]),
                         reads=[("b_ocat", blk), "ident"], writes=[pTk])
                P.op("act", "copy", dict(out=oT[sl][:], in_=pT[:]), reads=[pTk], writes=[("b_oT", sl)])
                for nh in range(2):
                    po2, pok2 = po2s[(blk * 2 + nh) % 4], ("b_po2", (blk * 2 + nh) % 4)
                    for k in range(8):
                        P.op("pe", "matmul", ((po2[:],), dict(lhsT=oT[sl][:, k, :], rhs=wout[:, k, nh * 512:(nh + 1) * 512],
                                                             start=(k == 0), stop=(k == 7))),
                             reads=[("b_oT", sl), "b_wout"], writes=[pok2])
                    P.op("dve", "tensor_tensor", dict(out=xo[sl][:, nh * 512:(nh + 1) * 512], in0=xo[sl][:, nh * 512:(nh + 1) * 512],
                                                      in1=po2[:], op=ALU.add), reads=[pok2, ("b_xo", sl)], writes=[("b_xo", sl)])
                P.op("sync", "dma_start", dict(out=xrow(xdst, blk), in_=xo[sl][:]), reads=[("b_xo", sl)], writes=[("xres_o", blk)], dma="b_st")

    def build(self):
        nc, P = self.nc, self.P
        x_in = self.din("x", [S, D])
        ident = self.din("ident", [128, 128])
        wn = {}
        for nm, shp in [("ffn1_norm", [DEPTH, D]), ("ffn1_gate", [DEPTH, D, DFF]), ("ffn1_up", [DEPTH, D, DFF]),
                        ("ffn1_down", [DEPTH, DFF, D]), ("mix_norm", [DEPTH, D]), ("ffn2_norm", [DEPTH, D]),
                        ("ffn2_gate", [DEPTH, D, DFF]), ("ffn2_up", [DEPTH, D, DFF]), ("ffn2_down", [DEPTH, DFF, D])]:
            wn[nm] = self.din(nm, shp)
        for nm, shp in [("s5_a_re", [2, 64, 64]), ("s5_a_im", [2, 64, 64]), ("s5_log_step", [2, 64]),
                        ("s5_b_re", [2, 64, 64, 16]), ("s5_b_im", [2, 64, 64, 16]), ("s5_c_re", [2, 64, 16, 64]),
                        ("s5_c_im", [2, 64, 16, 64]), ("s5_d", [2, D]), ("s5_glu_a", [2, D, D]), ("s5_glu_b", [2, D, D])]:
            wn[nm] = self.din(nm, shp)
        for nm, shp in [("attn_w_in", [2, D, IN_COLS]), ("attn_w_out", [2, D, D]), ("fg_bias", [2, 8]),
                        ("diff_q_norm", [2, 64]), ("diff_k_norm", [2, 64]), ("diff_lambda_q1", [2, 64]),
                        ("diff_lambda_k1", [2, 64]), ("diff_lambda_q2", [2, 64]), ("diff_lambda_k2", [2, 64]),
                        ("diff_subln", [2, 128]), ("fox_q_norm", [2, 64]), ("fox_k_norm", [2, 64]), ("rel_bias", [32, 4])]:
            wn[nm] = self.din(nm, shp)
        self.consts = {k: self.din(k, list(v.shape)) for k, v in host_consts().items() if k != "ident"}
        self.scr = {"QT": self.dscr("QT", [16, 67, S], BF16), "KT": self.dscr("KT", [16, 67, S], BF16),
                    "HD": self.dscr("HD", [5, 128, 384], F32)}
        out = nc.dram_tensor("out", [S, D], F32, kind="ExternalOutput").ap()
        xres = self.dscr("xres", [S, D], F32)
        wb = {}
        for f in ("ffn1", "ffn2"):
            wb[f + "_gate"] = self.dscr(f + "_gate_b", [DEPTH, D, DFF], BF16)
            wb[f + "_up"] = self.dscr(f + "_up_b", [DEPTH, D, DFF], BF16)
            wb[f + "_down"] = self.dscr(f + "_down_b", [DEPTH, DFF, D], BF16)
        wb["attn_w_in"] = self.dscr("attn_w_in_b", [2, D, IN_COLS], BF16)
        wb["attn_w_out"] = self.dscr("attn_w_out_b", [2, D, D], BF16)
        wb["s5_glu_a"] = self.dscr("s5_glu_a_b", [2, D, D], BF16)
        wb["s5_glu_b"] = self.dscr("s5_glu_b_b", [2, D, D], BF16)

        with contextlib.ExitStack() as es0:
            ident_f = es0.enter_context(self.sb("ident_f", [128, 128], F32))
            ident_b = es0.enter_context(self.sb("ident_b", [128, 128], BF16))
            P.op("sync", "dma_start", dict(out=ident_f[:], in_=ident), writes=["ident_f"], dma="c_misc")
            P.op("dve", "tensor_copy", dict(out=ident_b[:], in_=ident_f[:]), reads=["ident_f"], writes=["ident"])
            need = set(k for k, _ in (self.phases or [("ffn1", 0), ("ffn2", 0), ("s5", 1), ("attn", 0)]))
            for L in range(DEPTH):
                for f in ("ffn1", "ffn2"):
                    if f == "ffn2":
                        if L % 2 == 0 and "attn" in need:
                            for w in ("attn_w_in", "attn_w_out"):
                                self.cast_w(wn[w][L // 2], wb[w][L // 2], f"cast_attn_{L // 2}", f"w_attn_{L // 2}")
                        if L % 2 == 1 and "s5" in need:
                            for w in ("s5_glu_a", "s5_glu_b"):
                                self.cast_w(wn[w][L // 2], wb[w][L // 2], f"cast_s5_{L // 2}", f"w_s5_{L // 2}")
                    if f not in need:
                        continue
                    for w in ("gate", "up", "down"):
                        self.cast_w(wn[f"{f}_{w}"][L], wb[f"{f}_{w}"][L], f"cast_{f}_{L}_{w}", f"w_{f}_{L}_{w}")
            phases = self.phases
            if phases is None:
                phases = []
                for L in range(DEPTH):
                    phases += [("ffn1", L), ("attn" if L % 2 == 0 else "s5", L), ("ffn2", L)]
            src = x_in
            for i, (kind, L) in enumerate(phases):
                dst = out if i == len(phases) - 1 else xres
                P.barrier()
                self.uid += 1
                with contextlib.ExitStack() as es:
                    if kind in ("ffn1", "ffn2"):
                        f = kind
                        self.ffn_phase(es, src, dst, wn[f + "_norm"][L], wb[f + "_gate"][L], wb[f + "_up"][L],
                                       wb[f + "_down"][L], f"w_{f}_{L}", ident_b)
                    elif kind == "s5":
                        self.s5_phase(es, src, dst, L, wn, wb, ident_f, ident_b)
                    elif kind == "attn":
                        self.attn_phase(es, src, dst, L, wn, wb, ident_f, ident_b)
                src = xres
            P.barrier(final=True)
            P.emit()
        return nc


def host_consts():
    idx = np.arange(128) // 16
    s5mask = (idx[None, :] >= idx[:, None]).astype(np.float32)
    n = np.arange(384) - 127
    nn = np.maximum(n, 0)
    nf = np.maximum(nn, 1).astype(np.float32)
    large = 16 + (np.log(nf / np.float32(16)) / np.float32(math.log(128 / 16)) * np.float32(16)).astype(np.int32)
    large = np.minimum(large, 31)
    bucket = np.where(nn < 16, nn, large)
    oh = np.zeros((33, 384), np.float32)
    for i in range(384):
        if n[i] >= 0:
            oh[bucket[i], i] = 1.0
        else:
            oh[32, i] = 1.0
    ones2 = np.zeros((128, 128), np.float32)
    ones2[:64, :64] = 1.0 / 64
    ones2[64:, 64:] = 1.0 / 64
    tri = (np.arange(128)[:, None] <= np.arange(128)[None, :]).astype(np.float32)
    return {"ident": np.eye(128, dtype=np.float32), "s5mask": s5mask, "relOH": oh, "ones2": ones2, "tri": tri}


_CACHE = {}


def kernel(**inputs):
    if "b" not in _CACHE:
        b = Builder()
        b.build()
        _CACHE["b"] = b
    b = _CACHE["b"]
    consts = host_consts()
    in_maps = []
    for c in range(NCORES):
        m = {}
        for k in b.inputs:
            if k == "x":
                m[k] = np.ascontiguousarray(inputs["x"][c])
            elif k in consts:
                m[k] = consts[k]
            else:
                m[k] = np.ascontiguousarray(inputs[k])
        in_maps.append(m)
    res = run_bass_kernel_spmd(b.nc, in_maps, core_ids=list(range(NCORES)))
    return np.stack([np.asarray(r["out"]) for r in res.results], axis=0).astype(np.float32)
```

```python
import bisect
import contextlib
import math

import numpy as np
import concourse.bass as bass
import concourse.mybir as mybir
from concourse.bass_utils import run_bass_kernel_spmd

F32 = mybir.dt.float32
BF16 = mybir.dt.bfloat16
AF = mybir.ActivationFunctionType
ALU = mybir.AluOpType
AX = mybir.AxisListType

D = 1024
S = 4096
DFF = 2816
DEPTH = 4
NCORES = 8
EPS = 1e-6
IN_COLS = 3080
NEG = -80.0


class Op:
    __slots__ = ("eng", "fn", "dma", "deps", "marked", "cum", "seq")

    def __init__(self, eng, fn, dma):
        self.eng = eng
        self.fn = fn
        self.dma = dma
        self.deps = []
        self.marked = False
        self.cum = 0
        self.seq = 0


class Prog:
    ENGS = ["sync", "act", "dve", "pe", "pool"]
    BLK = {"sync": "sync", "act": "scalar", "dve": "vector", "pe": "tensor", "pool": "gpsimd"}
    CH = 30000

    def __init__(self, nc):
        self.nc = nc
        self.ops = []
        self.lastw = {}
        self.readers = {}
        self.last_on = {}

    @staticmethod
    def stream(o):
        return o.dma if o.dma else o.eng

    def op(self, eng, name, kw, reads=(), writes=(), dma=None):
        args = ()
        if isinstance(kw, tuple):
            args, kw = kw
        fn = (lambda e, name=name, args=args, kw=kw: getattr(e, name)(*args, **kw))
        o = Op(eng, fn, dma)
        o.seq = len(self.ops)
        deps = {}

        def add(d, raw=False):
            if d is None:
                return
            if (not d.dma) and (not o.dma) and d.eng == o.eng:
                if o.eng == "pe":
                    return
            st = self.stream(d)
            if st not in deps or deps[st].seq < d.seq:
                deps[st] = d

        for k in reads:
            add(self.lastw.get(k), raw=True)
        for k in writes:
            add(self.lastw.get(k))
            for d in self.readers.get(k, {}).values():
                add(d)
        o.deps = list(deps.values())
        for k in writes:
            self.lastw[k] = o
            self.readers[k] = {}
        for k in reads:
            self.readers.setdefault(k, {})[self.stream(o)] = o
        self.ops.append(o)
        if fn is not None:
            self.last_on[self.stream(o)] = o
        return o

    def barrier(self, final=False):
        lasts = {st: d for st, d in self.last_on.items() if final or not st.startswith("cast_")}
        keep = {k: v for k, v in self.lastw.items() if isinstance(k, tuple) and isinstance(k[0], str) and k[0].startswith("w_")}
        for e in self.ENGS:
            o = Op(e, None, None)
            o.seq = len(self.ops)
            o.deps = [d for st, d in lasts.items() if not ((not d.dma) and d.eng == e)]
            self.ops.append(o)
        self.lastw = {} if final else keep
        self.readers = {}

    def emit(self):
        nc = self.nc
        for o in self.ops:
            for d in o.deps:
                d.marked = True
        cnt = {}
        dma_seqs = {}
        for o in self.ops:
            if o.fn is None:
                continue
            if o.dma:
                cnt[o.dma] = cnt.get(o.dma, 0) + 1
                o.cum = cnt[o.dma]
                dma_seqs.setdefault(o.dma, []).append(o.seq)
            elif o.marked:
                cnt[o.eng] = cnt.get(o.eng, 0) + 1
                o.cum = cnt[o.eng]
        for k, v in cnt.items():
            if k not in self.ENGS:
                assert v * 16 < 60000, (k, v)
        with contextlib.ExitStack() as es:
            sems = {}

            def sem(name):
                if name not in sems:
                    sems[name] = es.enter_context(nc.semaphore("s_" + name))
                return sems[name]

            for k, v in cnt.items():
                if k in self.ENGS:
                    for c in range((v - 1) // self.CH + 1):
                        sem(f"{k}{c}")
                else:
                    sem(k)

            bar_seqs = [o.seq for o in self.ops if o.fn is None]
            self.partial = {}

            def resolve(d, o):
                if d.dma:
                    n = bisect.bisect_left(dma_seqs[d.dma], o.seq)
                    nb = bar_seqs[bisect.bisect_left(bar_seqs, o.seq)] if bisect.bisect_left(bar_seqs, o.seq) < len(bar_seqs) else 1 << 60
                    n_epoch = bisect.bisect_left(dma_seqs[d.dma], nb)
                    if n_epoch > n:
                        self.partial[d.dma] = self.partial.get(d.dma, 0) + 1
                    assert n >= d.cum
                    return d.dma, n * 16
                c = (d.cum - 1) // self.CH
                return f"{d.eng}{c}", (d.cum - 1) % self.CH + 1

            block = es.enter_context(nc.Block())
            for eng in self.ENGS:
                ops_e = [o for o in self.ops if o.eng == eng]

                def body(e, ops_e=ops_e):
                    waited = {}
                    for o in ops_e:
                        for d in o.deps:
                            sn, val = resolve(d, o)
                            if waited.get(sn, 0) < val:
                                e.wait_ge(sem(sn), val)
                                waited[sn] = val
                        if o.fn is None:
                            continue
                        inst = o.fn(e)
                        if o.dma:
                            inst.then_inc(sem(o.dma), 16)
                        elif o.marked:
                            c = (o.cum - 1) // self.CH
                            inst.then_inc(sem(f"{o.eng}{c}"), 1)

                getattr(block, self.BLK[eng])(body)


class Builder:
    def __init__(self, phases=None):
        self.nc = bass.Bass("TRN2", target_bir_lowering=False)
        self.P = Prog(self.nc)
        self.phases = phases
        self.inputs = {}
        self.wkeys = {}

    uid = 0
    debug = False

    def dump(self, name, ap, reads):
        if not self.debug:
            return
        o = self.nc.dram_tensor("dbg_" + name, list(ap.shape), ap.dtype, kind="ExternalOutput").ap()
        self.P.op("sync", "dma_start", dict(out=o, in_=ap), reads=list(reads), writes=[("dbg", name)], dma="dbg")

    def sb(self, name, shape, dt=F32):
        return self.nc.sbuf_tensor(f"{name}__u{self.uid}", shape, dt)

    def pp(self, name, shape, dt=F32):
        return self.nc.psum_tensor(f"{name}__u{self.uid}", shape, dt)

    def din(self, name, shape, dt=F32):
        ap = self.nc.dram_tensor(name, list(shape), dt, kind="ExternalInput").ap()
        self.inputs[name] = ap
        return ap

    def dscr(self, name, shape, dt):
        return self.nc.dram_tensor(name, list(shape), dt, kind="Internal").ap()

    def cast_w(self, src, dst, semname, key, nsplit=4):
        P = self.P
        rows, cols = src.shape
        b = cols
        for cand in range(1, 9):
            if cols % cand == 0 and cols // cand <= 1024:
                b = cols // cand
                break
        rs = rows // nsplit
        for i in range(nsplit):
            s_ap = src[i * rs:(i + 1) * rs, :].rearrange("k (a b) -> k a b", b=b)
            d_ap = dst[i * rs:(i + 1) * rs, :].rearrange("k (a b) -> k a b", b=b)
            self.wkeys.setdefault(key, [])
            kk = (key, len(self.wkeys[key]))
            self.wkeys[key].append(kk)
            P.op("pool", "dma_start", dict(out=d_ap, in_=s_ap), writes=[kk], dma=semname)

    def ffn_phase(self, es, xsrc, xdst, gnorm_ap, wg, wu, wd, wkey, ident_b):
        nc, P = self.nc, self.P
        TB = 1024
        NB = S // TB
        KT = D // 128
        MT = DFF // 128
        CG = 256
        NCG = DFF // CG
        A = lambda *a: es.enter_context(self.sb(*a))
        PS = lambda *a: es.enter_context(self.pp(*a))
        xt = [A(f"f_xt{i}", [128, 8, D], F32) for i in range(2)]
        hb = [A(f"f_hb{i}", [128, D], BF16) for i in range(2)]
        hT = A("f_hT", [128, KT, TB], BF16)
        aT = A("f_aT", [128, MT, TB], BF16)
        wdt = A("f_wd", [128, MT, D], BF16)
        wgt = [A(f"f_wg{i}", [128, KT, CG], BF16) for i in range(2)]
        wut = [A(f"f_wu{i}", [128, KT, CG], BF16) for i in range(2)]
        gn = A("f_gn", [128, D], F32)
        sq = A("f_sq", [128, D], BF16)
        ss = A("f_ss", [128, 8], F32)
        rstd = A("f_rstd", [128, 8], F32)
        sg = [A(f"f_sg{i}", [128, 512], F32) for i in range(2)]
        pT = [PS(f"f_pT{i}", [128, 8, 128], BF16) for i in range(2)]
        pg = [PS(f"f_pg{i}", [128, 512], F32) for i in range(2)]
        pu = [PS(f"f_pu{i}", [128, 512], F32) for i in range(2)]
        po = [PS(f"f_po{i}", [128, 512], F32) for i in range(2)]

        P.op("sync", "dma_start", dict(out=gn[:], in_=gnorm_ap.partition_broadcast(128)),
             writes=["f_gn"], dma="f_misc")
        wd_v = wd.rearrange("(k p) n -> p k n", p=128)
        for h in range(2):
            P.op("sync", "dma_start", dict(out=wdt[:, h * 11:(h + 1) * 11, :], in_=wd_v[:, h * 11:(h + 1) * 11, :]),
                 reads=self.wkeys[wkey + "_down"], writes=["f_wd"], dma="f_misc")
        wg_v = wg.rearrange("(k p) n -> p k n", p=128)
        wu_v = wu.rearrange("(k p) n -> p k n", p=128)

        def xv(ap, b):
            return ap[b * TB:(b + 1) * TB, :].rearrange("(p t) d -> p t d", t=8)

        def load_x(b):
            P.op("sync", "dma_start", dict(out=xt[b % 2][:], in_=xv(xsrc, b)),
                 reads=[("xres", b)], writes=[("f_xt", b % 2)], dma=f"f_x{b % 2}")

        load_x(0)
        wl = 0
        for b in range(NB):
            X = xt[b % 2]
            xk = ("f_xt", b % 2)
            if b + 1 < NB:
                load_x(b + 1)
            for t in range(8):
                P.op("act", "activation", dict(out=sq[:], in_=X[:, t, :], func=AF.Square, accum_out=ss[:, t:t + 1]),
                     reads=[xk], writes=["f_sq", ("f_ss", t)])
            P.op("dve", "tensor_scalar", dict(out=rstd[:], in0=ss[:], scalar1=1.0 / D, scalar2=EPS,
                                              op0=ALU.mult, op1=ALU.add),
                 reads=[("f_ss", t) for t in range(8)], writes=["f_rstd"])
            P.op("act", "activation", dict(out=rstd[:], in_=rstd[:], func=AF.Sqrt), reads=["f_rstd"], writes=["f_rstd"])
            P.op("dve", "reciprocal", dict(out=rstd[:], in_=rstd[:]), reads=["f_rstd"], writes=["f_rstd"])
            for t in range(8):
                H = hb[t % 2]
                P.op("dve", "scalar_tensor_tensor", dict(out=H[:], in0=X[:, t, :], scalar=rstd[:, t:t + 1], in1=gn[:],
                                                         op0=ALU.mult, op1=ALU.mult),
                     reads=[xk, "f_rstd", "f_gn"], writes=[("f_hb", t % 2)])
                pt = pT[t % 2]
                for k in range(KT):
                    P.op("pe", "transpose", dict(out=pt[:, k, :], in_=H[:, k * 128:(k + 1) * 128], identity=ident_b[:]),
                         reads=[("f_hb", t % 2), "ident"], writes=[("f_pT", t % 2)])
                if t % 2 == 0:
                    P.op("dve", "tensor_copy", dict(out=hT[:, :, t * 128:(t + 1) * 128], in_=pt[:]),
                         reads=[("f_pT", t % 2)], writes=[("f_hT", t)])
                else:
                    P.op("act", "copy", dict(out=hT[:, :, t * 128:(t + 1) * 128], in_=pt[:]),
                         reads=[("f_pT", t % 2)], writes=[("f_hT", t)])
            hT_keys = [("f_hT", t) for t in range(8)]
            ev = 0
            for cg in range(NCG):
                sl = wl % 2
                wl += 1
                P.op("sync", "dma_start", dict(out=wgt[sl][:], in_=wg_v[:, :, cg * CG:(cg + 1) * CG]),
                     reads=self.wkeys[wkey + "_gate"], writes=[("f_wg", sl)], dma=f"f_w{sl}")
                P.op("sync", "dma_start", dict(out=wut[sl][:], in_=wu_v[:, :, cg * CG:(cg + 1) * CG]),
                     reads=self.wkeys[wkey + "_up"], writes=[("f_wu", sl)], dma=f"f_w{sl}")
                for mi in range(CG // 128):
                    m = cg * (CG // 128) + mi
                    for hf in range(TB // 512):
                        q = ev % 2
                        ev += 1
                        for k in range(KT):
                            P.op("pe", "matmul", ((pg[q][:],), dict(lhsT=wgt[sl][:, k, mi * 128:(mi + 1) * 128],
                                                                   rhs=hT[:, k, hf * 512:(hf + 1) * 512],
                                                                   start=(k == 0), stop=(k == KT - 1))),
                                 reads=[("f_wg", sl)] + hT_keys, writes=[("f_pg", q)])
                        for k in range(KT):
                            P.op("pe", "matmul", ((pu[q][:],), dict(lhsT=wut[sl][:, k, mi * 128:(mi + 1) * 128],
                                                                   rhs=hT[:, k, hf * 512:(hf + 1) * 512],
                                                                   start=(k == 0), stop=(k == KT - 1))),
                                 reads=[("f_wu", sl)] + hT_keys, writes=[("f_pu", q)])
                        P.op("act", "activation", dict(out=sg[q][:], in_=pg[q][:], func=AF.Silu),
                             reads=[("f_pg", q)], writes=[("f_sg", q)])
                        P.op("dve", "tensor_tensor", dict(out=aT[:, m, hf * 512:(hf + 1) * 512], in0=sg[q][:],
                                                          in1=pu[q][:], op=ALU.mult),
                             reads=[("f_sg", q), ("f_pu", q)], writes=[("f_aT", m)])
            aT_keys = [("f_aT", m) for m in range(MT)]
            for t in range(8):
                for nh in range(2):
                    q = (t * 2 + nh) % 2
                    for k in range(MT):
                        P.op("pe", "matmul", ((po[q][:],), dict(lhsT=aT[:, k, t * 128:(t + 1) * 128],
                                                               rhs=wdt[:, k, nh * 512:(nh + 1) * 512],
                                                               start=(k == 0), stop=(k == MT - 1))),
                             reads=aT_keys + ["f_wd"], writes=[("f_po", q)])
                    P.op("dve", "scalar_tensor_tensor", dict(out=X[:, t, nh * 512:(nh + 1) * 512], in0=po[q][:],
                                                             scalar=0.5, in1=X[:, t, nh * 512:(nh + 1) * 512],
                                                             op0=ALU.mult, op1=ALU.add),
                         reads=[("f_po", q), xk], writes=[xk])
            P.op("sync", "dma_start", dict(out=xv(xdst, b), in_=X[:]), reads=[xk], writes=[("xres", b)], dma="f_st")

    def s5_phase(self, es, xsrc, xdst, L, wn, wb, ident_f, ident_b):
        nc, P = self.nc, self.P
        jj = L // 2
        A = lambda *a: es.enter_context(self.sb(*a))
        toep = A("s_toep", [128, 64, 128], BF16)
        wtr = A("s_wtr", [128, 64, 64], BF16)
        wti = A("s_wti", [128, 64, 64], BF16)
        vre = A("s_vre", [128, 32, 128], BF16)
        vim = A("s_vim", [128, 32, 128], BF16)
        dvec = A("s_dvec", [128, 64], F32)
        mcat = A("s_mcat", [128, 2, 64], F32)
        gn = A("s_gn", [128, D], F32)
        P.op("sync", "dma_start", dict(out=gn[:], in_=wn["mix_norm"][L].partition_broadcast(128)),
             writes=["s_gn"], dma="s_misc")
        with contextlib.ExitStack() as es1:
            self.s5_setup(es1, jj, wn, ident_f, toep, wtr, wti, vre, vim, dvec, mcat)
        P.barrier()
        self.s5_main(es, xsrc, xdst, jj, wb, ident_b, toep, wtr, wti, vre, vim, dvec, mcat, gn)

    def s5_setup(self, es, jj, wn, ident_f, toep, wtr, wti, vre, vim, dvec, mcat):
        nc, P = self.nc, self.P
        A = lambda n, s, d=F32: es.enter_context(self.sb(n, s, d))
        PS = lambda n, s, d=F32: es.enter_context(self.pp(n, s, d))
        kn = lambda ap: ap.tensor.name.split("__u")[0]

        def TT(out, in0, in1, op, eng="dve"):
            P.op(eng, "tensor_tensor", dict(out=out, in0=in0, in1=in1, op=op), reads=[kn(in0), kn(in1)], writes=[kn(out)])

        def TS(out, in0, s1, op0, s2=None, op1=None, eng="dve"):
            kw = dict(out=out, in0=in0, scalar1=s1, scalar2=s2, op0=op0)
            if op1 is not None:
                kw["op1"] = op1
            rd = [kn(in0)] + ([kn(s1)] if not isinstance(s1, (int, float)) else [])
            P.op(eng, "tensor_scalar", kw, reads=rd, writes=[kn(out)])

        def ACT(out, in_, func, **kw):
            P.op("act", "activation", dict(out=out, in_=in_, func=func, **kw), reads=[kn(in_)], writes=[kn(out)])

        def CP(out, in_, eng="dve"):
            P.op(eng, "tensor_copy", dict(out=out, in_=in_), reads=[kn(in_)], writes=[kn(out)])

        def DMA(out, in_, wr):
            P.op("sync", "dma_start", dict(out=out, in_=in_), writes=[wr], dma="z_ld")

        are, aim, lst = wn["s5_a_re"][jj], wn["s5_a_im"][jj], wn["s5_log_step"][jj]
        bre, bim, cre, cim, dsk = wn["s5_b_re"][jj], wn["s5_b_im"][jj], wn["s5_c_re"][jj], wn["s5_c_im"][jj], wn["s5_d"][jj]
        smask = self.consts["s5mask"]
        AR = A("z_ar", [64, 128]); AI = A("z_ai", [64, 128]); LS = A("z_ls", [64, 1])
        for h in range(2):
            DMA(AR[:, h * 64:(h + 1) * 64], are, "z_ar")
            DMA(AI[:, h * 64:(h + 1) * 64], aim, "z_ai")
        DMA(LS[:], lst.rearrange("(g o) -> g o", o=1), "z_ls")
        mask = A("z_mask", [128, 128])
        DMA(mask[:], smask, "z_mask")
        BR = A("z_br", [128, 64, 16]); BI = A("z_bi", [128, 64, 16])
        for h in range(2):
            DMA(BR[h * 64:(h + 1) * 64], bre.rearrange("g p c -> p g c"), "z_br")
            DMA(BI[h * 64:(h + 1) * 64], bim.rearrange("g p c -> p g c"), "z_bi")
        tC = [A("z_tcr", [128, 8, 128]), A("z_tci", [128, 8, 128])]
        for t, src in zip(tC, (cre, cim)):
            for h in range(2):
                DMA(t[:, :, h * 64:(h + 1) * 64], src.rearrange("(o g) c p -> (g c) o p", o=8), kn(t[:]))
        Dg = A("z_dg", [64, 16]); Dg8 = A("z_dg8", [64, 8, 16])
        DMA(Dg[:], dsk.rearrange("(g c) -> g c", c=16), "z_dg")
        CP(Dg8[:], Dg[:].unsqueeze(1).broadcast_to([64, 8, 16]))

        step = A("z_step", [64, 1])
        ACT(step[:], LS[:], AF.Exp)
        ARS = A("z_ars", [64, 128]); TH = A("z_th", [64, 128]); MAG = A("z_mag", [64, 128])
        Cc = A("z_c", [64, 128]); Sn = A("z_s", [64, 128])
        t1 = A("z_t1", [64, 128]); t2 = A("z_t2", [64, 128]); t3 = A("z_t3", [64, 128])
        TS(ARS[:], AR[:], step[:, 0:1], ALU.mult)
        TS(TH[:], AI[:], step[:, 0:1], ALU.mult)
        ACT(MAG[:], ARS[:], AF.Exp)
        ACT(Sn[:], TH[:], AF.Sin, scale=1.0 / 32)
        hpi = A("z_hpi", [64, 1])
        P.op("dve", "memset", ((hpi[:], math.pi / 2), {}), writes=["z_hpi"])
        P.op("act", "activation", dict(out=Cc[:], in_=TH[:], func=AF.Sin, scale=1.0 / 32, bias=hpi[:, 0:1]),
             reads=["z_th", "z_hpi"], writes=["z_c"])
        for it in range(5):
            TT(t1[:], Cc[:], Cc[:], ALU.mult)
            TT(t2[:], Sn[:], Sn[:], ALU.mult)
            TT(t3[:], Cc[:], Sn[:], ALU.mult)
            TT(Cc[:], t1[:], t2[:], ALU.subtract)
            TS(Sn[:], t3[:], 2.0, ALU.mult)
        PWr = A("z_pwr", [64, 16, 128]); PWi = A("z_pwi", [64, 16, 128])

        def cmul(orr, oi, ar, ai, br, bi):
            TT(t1[:], ar, br, ALU.mult)
            TT(t2[:], ai, bi, ALU.mult)
            TT(t3[:], ar, bi, ALU.mult)
            TT(orr, t1[:], t2[:], ALU.subtract)
            TT(t1[:], ai, br, ALU.mult)
            TT(oi, t3[:], t1[:], ALU.add)

        P.op("dve", "memset", ((PWr[:, 7, :], 1.0), {}), writes=["z_pwr"])
        P.op("dve", "memset", ((PWi[:, 7, :], 0.0), {}), writes=["z_pwi"])
        TT(PWr[:, 8, :], MAG[:], Cc[:], ALU.mult)
        TT(PWi[:, 8, :], MAG[:], Sn[:], ALU.mult)
        for k in range(9, 16):
            cmul(PWr[:, k, :], PWi[:, k, :], PWr[:, k - 1, :], PWi[:, k - 1, :], PWr[:, 8, :], PWi[:, 8, :])
        m2 = A("z_m2", [64, 128])
        TT(t1[:], PWr[:, 8, :], PWr[:, 8, :], ALU.mult)
        TT(t2[:], PWi[:, 8, :], PWi[:, 8, :], ALU.mult)
        TT(m2[:], t1[:], t2[:], ALU.add)
        P.op("dve", "reciprocal", dict(out=m2[:], in_=m2[:]), reads=["z_m2"], writes=["z_m2"])
        TT(PWr[:, 6, :], PWr[:, 8, :], m2[:], ALU.mult)
        TT(t1[:], PWi[:, 8, :], m2[:], ALU.mult)
        TS(PWi[:, 6, :], t1[:], -1.0, ALU.mult)
        for k in range(5, -1, -1):
            cmul(PWr[:, k, :], PWi[:, k, :], PWr[:, k + 1, :], PWi[:, k + 1, :], PWr[:, 6, :], PWi[:, 6, :])
        CRg_ = A("z_crg", [64, 128]); CIg_ = A("z_cig", [64, 128]); den = A("z_den", [64, 128]); lm1 = A("z_lm1", [64, 128])
        TT(t1[:], AR[:], AR[:], ALU.mult)
        TT(t2[:], AI[:], AI[:], ALU.mult)
        TT(den[:], t1[:], t2[:], ALU.add)
        P.op("dve", "reciprocal", dict(out=den[:], in_=den[:]), reads=["z_den"], writes=["z_den"])
        TS(lm1[:], PWr[:, 8, :], -1.0, ALU.add)
        TT(t1[:], lm1[:], AR[:], ALU.mult)
        TT(t2[:], PWi[:, 8, :], AI[:], ALU.mult)
        TT(t1[:], t1[:], t2[:], ALU.add)
        TT(CRg_[:], t1[:], den[:], ALU.mult)
        TT(t1[:], PWi[:, 8, :], AR[:], ALU.mult)
        TT(t2[:], lm1[:], AI[:], ALU.mult)
        TT(t1[:], t1[:], t2[:], ALU.subtract)
        TT(CIg_[:], t1[:], den[:], ALU.mult)

        TABr = A("z_tabr", [128, 64, 25]); TABi = A("z_tabi", [128, 64, 25])
        CRp = A("z_crp", [128, 64]); CIp = A("z_cip", [128, 64])
        pz = [PS("z_pz0", [128, 8, 64]), PS("z_pz1", [128, 8, 64])]
        slots = [7 - s for s in range(8)] + [7 + k for k in range(9)] + [14 - s for s in range(8)]
        idf64 = ident_f[0:64, 0:64]
        nb = 0
        for TAB, PW in ((TABr, PWr), (TABi, PWi)):
            for b0 in range(0, 25, 8):
                n = min(8, 25 - b0)
                pt = pz[nb % 2]
                nb += 1
                for q in range(n):
                    P.op("pe", "transpose", dict(out=pt[:, q, :], in_=PW[:, slots[b0 + q], :], identity=idf64),
                         reads=[kn(PW[:]), "ident_f"], writes=[kn(pt[:])])
                CP(TAB[:, :, b0:b0 + n].rearrange("p g s -> p s g"), pt[:, 0:n, :])
        pt = pz[nb % 2]
        P.op("pe", "transpose", dict(out=pt[:, 0, :], in_=CRg_[:], identity=idf64), reads=["z_crg", "ident_f"], writes=[kn(pt[:])])
        P.op("pe", "transpose", dict(out=pt[:, 1, :], in_=CIg_[:], identity=idf64), reads=["z_cig", "ident_f"], writes=[kn(pt[:])])
        P.op("pe", "transpose", dict(out=pt[:, 2, :], in_=Dg8[:].rearrange("g a c -> g (a c)"), identity=idf64),
             reads=["z_dg8", "ident_f"], writes=[kn(pt[:])])
        CP(CRp[:], pt[:, 0, :])
        CP(CIp[:], pt[:, 1, :])
        CP(dvec[:], pt[:, 2, :])
        for h in range(2):
            rs = slice(h * 64, (h + 1) * 64)
            mr = TABr[rs, h::2, 16]
            mi = TABi[rs, h::2, 16]
            CP(mcat[rs, 0, 0:32], mr)
            CP(mcat[rs, 0, 32:64], mr)
            TS(mcat[rs, 1, 0:32], mi, -1.0, ALU.mult)
            CP(mcat[rs, 1, 32:64], mi)
        CRg = A("z_crpg", [128, 64, 16]); CIn = A("z_cinpg", [128, 64, 16])
        pc = [PS("z_pc0", [128, 4, 128]), PS("z_pc1", [128, 4, 128])]
        nb = 0
        for t, dstC in zip(tC, (CRg, CIn)):
            for q4 in range(2):
                pt = pc[nb % 2]
                nb += 1
                for q in range(4):
                    P.op("pe", "transpose", dict(out=pt[:, q, :], in_=t[:, q4 * 4 + q, :], identity=ident_f[:]),
                         reads=[kn(t[:]), "ident_f"], writes=[kn(pt[:])])
                CP(dstC[:, q4 * 32:(q4 + 1) * 32, :].rearrange("p (a g) c -> p a (g c)", a=4), pt[:])
        TS(CIn[:], CIn[:], -1.0, ALU.mult)
        BBr = A("z_bbr", [128, 64, 16]); BBi = A("z_bbi", [128, 64, 16])
        u1 = A("z_u1", [128, 64, 16]); u2 = A("z_u2", [128, 64, 16])
        crb = CRp[:].unsqueeze(2).broadcast_to([128, 64, 16])
        cib = CIp[:].unsqueeze(2).broadcast_to([128, 64, 16])
        TT(u1[:], crb, BR[:], ALU.mult)
        TT(u2[:], cib, BI[:], ALU.mult)
        TT(BBr[:], u1[:], u2[:], ALU.subtract)
        TT(u1[:], crb, BI[:], ALU.mult)
        TT(u2[:], cib, BR[:], ALU.mult)
        TT(BBi[:], u1[:], u2[:], ALU.add)

        GH = 16
        X1 = A("z_x1", [128, GH, 8, 16]); X2 = A("z_x2", [128, GH, 8, 16])
        X3 = A("z_x3", [128, GH, 8, 16]); X4 = A("z_x4", [128, GH, 8, 16])
        T1 = A("z_T1", [128, GH, 8, 16]); T2 = A("z_T2", [128, GH, 8, 16])
        ptp = [PS("z_ptp0", [128, 4, 128]), PS("z_ptp1", [128, 4, 128])]
        pw8 = [PS("z_pw0", [128, 8, 64]), PS("z_pw1", [128, 8, 64])]

        def cprod(oa, ob, s0, Xr, Xi, g0, opa, opb, e1="dve", e2="dve"):
            pr = TABr[:, g0:g0 + GH, s0:s0 + 8].unsqueeze(3).broadcast_to([128, GH, 8, 16])
            pi = TABi[:, g0:g0 + GH, s0:s0 + 8].unsqueeze(3).broadcast_to([128, GH, 8, 16])
            xr = Xr[:, g0:g0 + GH, :].unsqueeze(2).broadcast_to([128, GH, 8, 16])
            xi = Xi[:, g0:g0 + GH, :].unsqueeze(2).broadcast_to([128, GH, 8, 16])
            TT(T1[:], pr, xr, ALU.mult, e1)
            TT(T2[:], pi, xi, ALU.mult, e1)
            TT(oa[:], T1[:], T2[:], opa, e1)
            TT(T1[:], pr, xi, ALU.mult, e2)
            TT(T2[:], pi, xr, ALU.mult, e2)
            TT(ob[:], T1[:], T2[:], opb, e2)

        nb = 0
        for gh in range(64 // GH):
            g0 = gh * GH
            cprod(X1, X2, 0, BBr, BBi, g0, ALU.subtract, ALU.add)
            cprod(X3, X4, 8, CRg, CIn, g0, ALU.add, ALU.subtract)
            for q4 in range(GH // 4):
                pt = ptp[nb % 2]
                nb += 1
                for q in range(4):
                    gl = q4 * 4 + q
                    P.op("pe", "matmul", ((pt[:, q, :],), dict(lhsT=X1[0:64, gl].rearrange("p s c -> p (s c)"),
                                                              rhs=X3[0:64, gl].rearrange("p s c -> p (s c)"),
                                                              start=True, stop=False)),
                         reads=["z_x1", "z_x3"], writes=[kn(pt[:])])
                    P.op("pe", "matmul", ((pt[:, q, :],), dict(lhsT=X2[0:64, gl].rearrange("p s c -> p (s c)"),
                                                              rhs=X4[0:64, gl].rearrange("p s c -> p (s c)"),
                                                              start=False, stop=True)),
                         reads=["z_x2", "z_x4"], writes=[kn(pt[:])])
                TT(toep[:, g0 + q4 * 4:g0 + q4 * 4 + 4, :], pt[:], mask[:].unsqueeze(1).broadcast_to([128, 4, 128]), ALU.mult)
            cprod(X1, X2, 17, BBr, BBi, g0, ALU.subtract, ALU.add)
            for Xs, wt in ((X1, wtr), (X2, wti)):
                for q8 in range(GH // 8):
                    pt = pw8[nb % 2]
                    nb += 1
                    for q in range(8):
                        gl = q8 * 8 + q
                        P.op("pe", "transpose", dict(out=pt[:, q, :], in_=Xs[0:64, gl].rearrange("p s c -> p (s c)"),
                                                     identity=idf64),
                             reads=[kn(Xs[:]), "ident_f"], writes=[kn(pt[:])])
                    CP(wt[:, g0 + q8 * 8:g0 + q8 * 8 + 8, :], pt[:])
            cprod(X3, X4, 9, CRg, CIn, g0, ALU.add, ALU.subtract)
            for Xs, vt in ((X3, vre), (X4, vim)):
                for h in range(2):
                    rs = slice(h * 64, (h + 1) * 64)
                    P.op("dve", "tensor_copy", dict(out=vt[rs, gh * (GH // 2):(gh + 1) * (GH // 2), :],
                                                     in_=Xs[rs, h::2].rearrange("p g s c -> p g (s c)")),
                         reads=[kn(Xs[:])], writes=[kn(vt[:])])

    def s5_main(self, es, xsrc, xdst, jj, wb, ident_b, toep, wtr, wti, vre, vim, dvec, mcat, gn):
        nc, P = self.nc, self.P
        A = lambda n, s, d=F32: es.enter_context(self.sb(n, s, d))
        PS = lambda n, s, d=F32: es.enter_context(self.pp(n, s, d))
        xq = A("m_xq", [128, 8, D])
        hy = A("m_hy", [128, 8192], BF16)
        uy = A("m_uy", [128, 8192], BF16)
        beta = A("m_beta", [128, 128, 64])
        XS = A("m_xs", [128, 129, 64], BF16)
        Z = [A("m_z0", [128, 2, 64]), A("m_z1", [128, 2, 64])]
        AB3 = [A(f"m_ab{i}", [128, 3, 64]) for i in range(4)]
        wa = A("m_wa", [128, 8, 512], BF16); wbt = A("m_wb", [128, 8, 512], BF16)
        sq = A("m_sq", [128, D], BF16); ss = A("m_ss", [128, 8]); rstd = A("m_rstd", [128, 8])
        ytmp = [A(f"m_yt{i}", [128, 128]) for i in range(4)]
        yact = [A(f"m_ya{i}", [128, 128], BF16) for i in range(4)]
        sgts = [A(f"m_sg{i}", [128, 512]) for i in range(2)]; tts = [A(f"m_tt{i}", [128, 512]) for i in range(2)]
        pT = [PS(f"m_pT{i}", [128, 8, 128], BF16) for i in range(2)]
        pb = [PS(f"m_pb{i}", [128, 2, 128]) for i in range(2)]
        py = [PS(f"m_py{i}", [128, 512]) for i in range(2)]
        pa = PS("m_pa", [128, 512]); pbb = PS("m_pbb", [128, 512])
        hbp = hy[:].rearrange("p (g s c) -> p g s c", g=64, s=8)
        yt = hy[:].rearrange("p (s ch) -> p s ch", s=8)
        U = uy[:].rearrange("p (g j) -> p g j", g=64)
        yT = uy[:].rearrange("p (k t) -> p k t", k=8)
        wa_v = wb["s5_glu_a"][jj].rearrange("(k p) n -> p k n", p=128)
        wb_v = wb["s5_glu_b"][jj].rearrange("(k p) n -> p k n", p=128)
        wkey = f"w_s5_{jj}"

        def xv(ap, q):
            return ap[q * 1024:(q + 1) * 1024, :].rearrange("(p t) d -> p t d", t=8)

        P.op("dve", "memset", ((Z[0][:], 0.0), {}), writes=[("Z", 0)])
        P.op("dve", "memset", ((XS[:, 0, :], 0.0), {}), writes=["XS"])
        cnt = 0
        ncp = 0
        for q in range(4):
            P.op("sync", "dma_start", dict(out=xq[:], in_=xv(xsrc, q)), reads=[("xres", q)], writes=["xq"], dma="m_x")
            for t in range(8):
                P.op("act", "activation", dict(out=sq[:], in_=xq[:, t, :], func=AF.Square, accum_out=ss[:, t:t + 1]),
                     reads=["xq"], writes=["m_sq", "m_ss"])
            P.op("dve", "tensor_scalar", dict(out=rstd[:], in0=ss[:], scalar1=1.0 / D, scalar2=EPS, op0=ALU.mult, op1=ALU.add),
                 reads=["m_ss"], writes=["m_rstd"])
            P.op("act", "activation", dict(out=rstd[:], in_=rstd[:], func=AF.Sqrt), reads=["m_rstd"], writes=["m_rstd"])
            P.op("dve", "reciprocal", dict(out=rstd[:], in_=rstd[:]), reads=["m_rstd"], writes=["m_rstd"])
            for t in range(8):
                P.op("dve", "scalar_tensor_tensor", dict(out=hbp[:, :, t, :], in0=xq[:, t, :].rearrange("p (g c) -> p g c", c=16),
                                                         scalar=rstd[:, t:t + 1], in1=gn[:].rearrange("p (g c) -> p g c", c=16),
                                                         op0=ALU.mult, op1=ALU.mult),
                     reads=["xq", "m_rstd", "s_gn"], writes=["hy"])
            for g8 in range(8):
                pt = pT[g8 % 2]
                for gq in range(8):
                    g = g8 * 8 + gq
                    P.op("pe", "transpose", dict(out=pt[:, gq, :], in_=hy[:, g * 128:(g + 1) * 128], identity=ident_b[:]),
                         reads=["hy", "ident"], writes=[("m_pT", g8 % 2)])
                if g8 % 2 == 0:
                    P.op("dve", "tensor_copy", dict(out=U[:, g8 * 8:(g8 + 1) * 8, :], in_=pt[:]), reads=[("m_pT", 0)], writes=["uy"])
                else:
                    P.op("act", "copy", dict(out=U[:, g8 * 8:(g8 + 1) * 8, :], in_=pt[:]), reads=[("m_pT", 1)], writes=["uy"])
            for pr in range(32):
                pbt = pb[pr % 2]
                for ri, wt in enumerate((wtr, wti)):
                    for g2 in range(2):
                        g = 2 * pr + g2
                        P.op("pe", "matmul", ((pbt[g2 * 64:(g2 + 1) * 64, ri, :],),
                                              dict(lhsT=wt[:, g, :], rhs=U[:, g, :], start=True, stop=True)),
                             reads=["uy", "s_w"], writes=[("m_pb", pr % 2)])
                bt = beta[:, :, :]
                ov = bass.AP(tensor=bt.tensor, offset=bt.offset + pr, ap=[list(bt.ap[0]), [32, 2], [64, 128]])
                if pr % 2 == 0:
                    P.op("dve", "tensor_copy", dict(out=ov, in_=pbt[:]), reads=[("m_pb", 0)], writes=["beta"])
                else:
                    P.op("act", "copy", dict(out=ov, in_=pbt[:]), reads=[("m_pb", 1)], writes=["beta"])
            for j0 in range(2):
                P.op("act", "copy", dict(out=AB3[(cnt + 1 + j0) % 4][:, 2, :], in_=beta[:, j0, :]), reads=["beta"],
                     writes=[("ABb", (cnt + 1 + j0) % 4)])
            for j in range(128):
                zc, zn = Z[cnt % 2], Z[(cnt + 1) % 2]
                kc, kn_ = ("Z", cnt % 2), ("Z", (cnt + 1) % 2)
                cnt += 1
                zt = zc[:, :, :]
                win = bass.AP(tensor=zt.tensor, offset=zt.offset, ap=[list(zt.ap[0]), [32, 2], [1, 64]])
                ab = AB3[cnt % 4]
                abk, abbk = ("AB", cnt % 4), ("ABb", cnt % 4)
                if j + 2 < 128:
                    P.op("act", "copy", dict(out=AB3[(cnt + 2) % 4][:, 2, :], in_=beta[:, j + 2, :]), reads=["beta"],
                         writes=[("ABb", (cnt + 2) % 4)])
                P.op("dve", "tensor_tensor", dict(out=ab[:, 0:2, :], in0=mcat[:], in1=win, op=ALU.mult), reads=[kc, "s_mcat"], writes=[abk])
                abt = ab[:, :, :]
                rin = bass.AP(tensor=abt.tensor, offset=abt.offset, ap=[list(abt.ap[0]), [0, 2], [1, 64], [64, 3]])
                P.op("dve", "tensor_reduce", dict(out=zn[:], in_=rin, op=ALU.add, axis=AX.X), reads=[abk, abbk], writes=[kn_])
                P.op("act", "copy", dict(out=XS[:, j + 1, :], in_=zn[:, 0, :]), reads=[kn_], writes=["XS"])
            def grp_front(g):
                pr, g2 = g // 2, g % 2
                pyt = [py[0][:, 0:128], py[1][:, 0:128], pa[:, 0:128], pbb[:, 0:128]][g % 4]
                pyk = [("m_py", 0), ("m_py", 1), "m_pa", "m_pbb"][g % 4]
                rs = slice(g2 * 64, (g2 + 1) * 64)
                P.op("pe", "matmul", ((pyt,), dict(lhsT=toep[:, g, :], rhs=U[:, g, :], start=True, stop=False)),
                     reads=["uy", "s_toep"], writes=[pyk])
                P.op("pe", "matmul", ((pyt,), dict(lhsT=vre[rs, pr, :], rhs=XS[rs, 0:128, pr], start=False, stop=False)),
                     reads=["XS", "s_v"], writes=[pyk])
                P.op("pe", "matmul", ((pyt,), dict(lhsT=vim[rs, pr, :], rhs=XS[rs, 0:128, 32 + pr], start=False, stop=True)),
                     reads=["XS", "s_v"], writes=[pyk])
                P.op("dve", "scalar_tensor_tensor", dict(out=ytmp[g % 4][:], in0=U[:, g, :], scalar=dvec[:, g:g + 1], in1=pyt,
                                                         op0=ALU.mult, op1=ALU.add),
                     reads=["uy", pyk, "s_dvec"], writes=[("m_ytmp", g % 4)])
                P.op("act", "activation", dict(out=yact[g % 4][:], in_=ytmp[g % 4][:], func=AF.Gelu),
                     reads=[("m_ytmp", g % 4)], writes=[("m_yact", g % 4)])

            def grp_back(g):
                g8 = g // 8
                pt = pT[g8 % 2]
                P.op("pe", "transpose", dict(out=pt[:, g % 8, :], in_=yact[g % 4][:], identity=ident_b[:]),
                     reads=[("m_yact", g % 4), "ident"], writes=[("m_pT", g8 % 2)])
                if g % 8 == 7:
                    ov = yt[:, :, g8 * 128:(g8 + 1) * 128].rearrange("p i (g c) -> p i g c", c=16)
                    iv = pt[:].rearrange("p g (i c) -> p i g c", c=16)
                    P.op("dve", "tensor_copy", dict(out=ov, in_=iv), reads=[("m_pT", g8 % 2)], writes=["hy"])

            for g in range(64 + 2):
                if g < 64:
                    grp_front(g)
                if g >= 2:
                    grp_back(g - 2)
            P.op("act", "copy", dict(out=XS[:, 0, :], in_=XS[:, 128, :]), reads=["XS"], writes=["XS"])
            for s in range(8):
                pt = pT[s % 2]
                for k in range(8):
                    P.op("pe", "transpose", dict(out=pt[:, k, :], in_=yt[:, s, k * 128:(k + 1) * 128], identity=ident_b[:]),
                         reads=["hy", "ident"], writes=[("m_pT", s % 2)])
                if s % 2 == 0:
                    P.op("dve", "tensor_copy", dict(out=yT[:, :, s * 128:(s + 1) * 128], in_=pt[:]), reads=[("m_pT", 0)], writes=["uy"])
                else:
                    P.op("act", "copy", dict(out=yT[:, :, s * 128:(s + 1) * 128], in_=pt[:]), reads=[("m_pT", 1)], writes=["uy"])
            for nh in range(2):
                P.op("sync", "dma_start", dict(out=wa[:], in_=wa_v[:, :, nh * 512:(nh + 1) * 512]), reads=self.wkeys[wkey], writes=["m_wa"], dma="m_w")
                P.op("sync", "dma_start", dict(out=wbt[:], in_=wb_v[:, :, nh * 512:(nh + 1) * 512]), reads=self.wkeys[wkey], writes=["m_wb"], dma="m_w")
                for s in range(8):
                    gs = (nh * 8 + s) % 2
                    pa_t, pa_k = [(pa, "m_pa"), (py[0], ("m_py", 0))][gs]
                    pb_t, pb_k = [(pbb, "m_pbb"), (py[1], ("m_py", 1))][gs]
                    sgt, tt = sgts[gs], tts[gs]
                    for k in range(8):
                        P.op("pe", "matmul", ((pa_t[:],), dict(lhsT=yT[:, k, s * 128:(s + 1) * 128], rhs=wa[:, k, :],
                                                              start=(k == 0), stop=(k == 7))), reads=["uy", "m_wa"], writes=[pa_k])
                    for k in range(8):
                        P.op("pe", "matmul", ((pb_t[:],), dict(lhsT=yT[:, k, s * 128:(s + 1) * 128], rhs=wbt[:, k, :],
                                                              start=(k == 0), stop=(k == 7))), reads=["uy", "m_wb"], writes=[pb_k])
                    P.op("act", "activation", dict(out=sgt[:], in_=pb_t[:], func=AF.Sigmoid), reads=[pb_k], writes=[("m_sg", gs)])
                    P.op("dve", "tensor_tensor", dict(out=tt[:], in0=sgt[:], in1=pa_t[:], op=ALU.mult), reads=[("m_sg", gs), pa_k],
                         writes=[("m_tt", gs)])
                    xs_ = xq[:, s, nh * 512:(nh + 1) * 512]
                    P.op("dve", "tensor_tensor", dict(out=xs_, in0=xs_, in1=tt[:], op=ALU.add), reads=[("m_tt", gs), "xq"], writes=["xq"])
            P.op("sync", "dma_start", dict(out=xv(xdst, q), in_=xq[:]), reads=["xq"], writes=[("xres", q)], dma="m_st")

    def attn_phase(self, es, xsrc, xdst, L, wn, wb, ident_f, ident_b):
        nc, P = self.nc, self.P
        jj = L // 2
        lam_init = 0.8 - 0.6 * math.exp(-0.3 * L)
        A = lambda n, s, d=F32: es.enter_context(self.sb(n, s, d))
        QT, KT, HD = self.scr["QT"], self.scr["KT"], self.scr["HD"]
        win_v = wb["attn_w_in"][jj].rearrange("(k p) n -> p k n", p=128)
        wout_v = wb["attn_w_out"][jj].rearrange("(k p) n -> p k n", p=128)
        wkey = f"w_attn_{jj}"
        kn = lambda ap: ap.tensor.name.split("__u")[0]
        Vd = A("a_vd", [128, 32, 4, 129], BF16)
        Vf = A("a_vf", [128, 32, 8, 65], BF16)
        cposk = A("a_cposk", [128, 32, 8])
        P.op("dve", "memset", ((Vd[:, :, :, 128:129], 1.0), {}), writes=["Vd"])
        P.op("dve", "memset", ((Vf[:, :, :, 64:65], 1.0), {}), writes=["Vf"])

        LSP = A("a_lsp", [128, 32, 8]); R = A("a_R", [128, 33, 8])
        tri = A("a_tri", [128, 128]); onesf = A("a_onesf", [128, 128])
        with contextlib.ExitStack() as e1:
            B = lambda n, s, d=F32: e1.enter_context(self.sb(n, s, d))
            PS = lambda n, s, d=F32: e1.enter_context(self.pp(n, s, d))
            xt = [B(f"a_xt{i}", [128, 4, D]) for i in range(2)]
            hb = [B(f"a_hb{i}", [128, D], BF16) for i in range(2)]
            hT = B("a_hT", [128, 8, 512], BF16)
            win = B("a_win", [128, 8, IN_COLS], BF16)
            gn = B("a_gn", [128, D])
            sq = B("a_sq", [128, D], BF16); ss = B("a_ss", [128, 4]); rstd = B("a_rstd", [128, 4])
            qsq = [B(f"a_qsq{i}", [128, 512]) for i in range(3)]
            lnv = [B(f"a_lnv{i}", [128, 512]) for i in range(2)]
            qo = [B(f"a_qo{i}", [128, 512], BF16) for i in range(2)]
            G = B("a_G", [128, 4]); epsc = B("a_eps", [128, 1]); ones2 = B("a_ones2", [128, 128])
            fgb = B("a_fgb", [128, 8]); zt = B("a_zt", [128, 8])
            pT = PS("a_pT", [128, 8, 128], BF16)
            pq = [PS(f"a_pq{i}", [128, 512]) for i in range(3)]
            pms = PS("a_pms", [128, 512])
            pv = [PS(f"a_pv{i}", [128, 512]) for i in range(2)]
            pfl = PS("a_pfl", [128, 512])

            def DMA(out, in_, wr, rd=()):
                P.op("sync", "dma_start", dict(out=out, in_=in_), reads=list(rd), writes=[wr], dma="a_ld")

            DMA(gn[:], wn["mix_norm"][L].partition_broadcast(128), "a_gn")
            for h in range(2):
                DMA(win[:, h * 4:(h + 1) * 4, :], win_v[:, h * 4:(h + 1) * 4, :], "a_win", self.wkeys[wkey])
            for c, nm in enumerate(("diff_q_norm", "diff_k_norm", "fox_q_norm", "fox_k_norm")):
                for h in range(2):
                    DMA(G[h * 64:(h + 1) * 64, c:c + 1], wn[nm][jj].rearrange("(d o) -> d o", o=1), "a_G")
            DMA(fgb[:], wn["fg_bias"][jj].partition_broadcast(128), "a_fgb")
            DMA(ones2[:], self.consts["ones2"], "a_ones2")
            DMA(tri[:], self.consts["tri"], "a_tri")
            P.op("dve", "memset", ((epsc[:], EPS), {}), writes=["a_eps"])
            P.op("dve", "memset", ((onesf[:], 1.0), {}), writes=["a_onesf"])
            P.op("dve", "memset", ((R[:, 0, :], 0.0), {}), writes=["a_R"])
            qk_tiles = []
            for h in range(4):
                qk_tiles.append((h * 128, 0, QT, 2 * h))
            for h in range(4):
                qk_tiles.append((512 + h * 128, 1, KT, 2 * h))
            for h in range(4):
                qk_tiles.append((1536 + h * 128, 2, QT, 8 + 2 * h))
            for h in range(4):
                qk_tiles.append((2048 + h * 128, 3, KT, 8 + 2 * h))

            def xv(ap, b):
                return ap[b * 512:(b + 1) * 512, :].rearrange("(t p) d -> p t d", p=128)

            def load_x(b):
                P.op("sync", "dma_start", dict(out=xt[b % 2][:], in_=xv(xsrc, b)), reads=[("xres", b)],
                     writes=[("a_xt", b % 2)], dma=f"a_x{b % 2}")

            load_x(0)
            ev = 0
            for b in range(8):
                X = xt[b % 2]
                xk = ("a_xt", b % 2)
                if b + 1 < 8:
                    load_x(b + 1)
                for t in range(4):
                    P.op("act", "activation", dict(out=sq[:], in_=X[:, t, :], func=AF.Square, accum_out=ss[:, t:t + 1]),
                         reads=[xk], writes=["a_sq", "a_ss"])
                P.op("dve", "tensor_scalar", dict(out=rstd[:], in0=ss[:], scalar1=1.0 / D, scalar2=EPS, op0=ALU.mult, op1=ALU.add),
                     reads=["a_ss"], writes=["a_rstd"])
                P.op("act", "activation", dict(out=rstd[:], in_=rstd[:], func=AF.Sqrt), reads=["a_rstd"], writes=["a_rstd"])
                P.op("dve", "reciprocal", dict(out=rstd[:], in_=rstd[:]), reads=["a_rstd"], writes=["a_rstd"])
                for t in range(4):
                    H = hb[t % 2]
                    P.op("dve", "scalar_tensor_tensor", dict(out=H[:], in0=X[:, t, :], scalar=rstd[:, t:t + 1], in1=gn[:],
                                                             op0=ALU.mult, op1=ALU.mult),
                         reads=[xk, "a_rstd", "a_gn"], writes=[("a_hb", t % 2)])
                    for k in range(8):
                        P.op("pe", "transpose", dict(out=pT[:, k, :], in_=H[:, k * 128:(k + 1) * 128], identity=ident_b[:]),
                             reads=[("a_hb", t % 2), "ident"], writes=["a_pT"])
                    P.op("dve", "tensor_copy", dict(out=hT[:, :, t * 128:(t + 1) * 128], in_=pT[:]), reads=["a_pT"], writes=["a_hT"])
                def qk_front(i):
                    c0, gc, dstT, m0 = qk_tiles[i]
                    q = (b * 16 + i) % 3
                    for k in range(8):
                        P.op("pe", "matmul", ((pq[q][:],), dict(lhsT=win[:, k, c0:c0 + 128], rhs=hT[:, k, :],
                                                                start=(k == 0), stop=(k == 7))),
                             reads=["a_win", "a_hT"], writes=[("a_pq", q)])
                    P.op("act", "activation", dict(out=qsq[q][:], in_=pq[q][:], func=AF.Square), reads=[("a_pq", q)], writes=[("a_qsq", q)])

                def qk_back(i):
                    c0, gc, dstT, m0 = qk_tiles[i]
                    q = (b * 16 + i) % 3
                    r = (b * 16 + i) % 2
                    P.op("pe", "matmul", ((pms[:],), dict(lhsT=ones2[:], rhs=qsq[q][:], start=True, stop=True)),
                         reads=["a_ones2", ("a_qsq", q)], writes=["a_pms"])
                    P.op("act", "activation", dict(out=lnv[r][:], in_=pms[:], func=AF.Ln, bias=epsc[:, 0:1]),
                         reads=["a_pms", "a_eps"], writes=[("a_lnv", r)])
                    P.op("act", "activation", dict(out=lnv[r][:], in_=lnv[r][:], func=AF.Exp, scale=-0.5),
                         reads=[("a_lnv", r)], writes=[("a_lnv", r)])
                    P.op("dve", "scalar_tensor_tensor", dict(out=qo[r][:], in0=pq[q][:], scalar=G[:, gc:gc + 1], in1=lnv[r][:],
                                                             op0=ALU.mult, op1=ALU.mult),
                         reads=[("a_pq", q), ("a_lnv", r), "a_G"], writes=[("a_qo", r)])
                    for hh in range(2):
                        P.op("sync", "dma_start", dict(out=dstT[m0 + hh, 0:64, b * 512:(b + 1) * 512],
                                                       in_=qo[r][hh * 64:(hh + 1) * 64, :]),
                             reads=[("a_qo", r)], writes=[(kn(dstT), m0 + hh, b)], dma="a_qst")

                for i in range(17):
                    if i < 16:
                        qk_front(i)
                    if i >= 1:
                        qk_back(i - 1)
                for t in range(4):
                    blk = b * 4 + t
                    for vi, (c0, Vt, nh, vd) in enumerate(((1024, Vd, 4, 128), (2560, Vf, 8, 64))):
                        q = vi
                        for k in range(8):
                            P.op("pe", "matmul", ((pv[q][:],), dict(lhsT=hT[:, k, t * 128:(t + 1) * 128], rhs=win[:, k, c0:c0 + 512],
                                                                    start=(k == 0), stop=(k == 7))),
                                 reads=["a_win", "a_hT"], writes=[("a_pv", q)])
                        P.op("act" if vi == 0 else "dve", "copy" if vi == 0 else "tensor_copy",
                             dict(out=Vt[:, blk, :, 0:vd], in_=pv[q][:].rearrange("p (h v) -> p h v", h=nh)),
                             reads=[("a_pv", q)], writes=["Vd" if vi == 0 else "Vf"])
                    for k in range(8):
                        P.op("pe", "matmul", ((pfl[:, 0:8],), dict(lhsT=hT[:, k, t * 128:(t + 1) * 128], rhs=win[:, k, 3072:3080],
                                                                   start=(k == 0), stop=(k == 7))),
                             reads=["a_win", "a_hT"], writes=["a_pfl"])
                    P.op("dve", "tensor_tensor", dict(out=zt[:], in0=pfl[:, 0:8], in1=fgb[:], op=ALU.add),
                         reads=["a_pfl", "a_fgb"], writes=["a_zt"])
                    P.op("act", "activation", dict(out=zt[:], in_=zt[:], func=AF.Exp, scale=-1.0), reads=["a_zt"], writes=["a_zt"])
                    P.op("act", "activation", dict(out=LSP[:, blk, :], in_=zt[:], func=AF.Ln, bias=1.0), reads=["a_zt"], writes=["a_lsp"])
                    P.op("dve", "tensor_tensor", dict(out=R[:, blk + 1, :], in0=R[:, blk, :], in1=LSP[:, blk, :], op=ALU.add),
                         reads=["a_lsp", "a_R"], writes=["a_R"])
        P.barrier()
        self.uid += 1
        with contextlib.ExitStack() as e1:
            B = lambda n, s, d=F32: e1.enter_context(self.sb(n, s, d))
            PS = lambda n, s, d=F32: e1.enter_context(self.pp(n, s, d))
            cT = B("a_cT", [8, S]); rT = B("a_rT", [8, S])
            a123 = [B(f"a_a{i}", [8, S], BF16) for i in range(3)]
            onesb = B("a_onesb", [8, S], BF16)
            pv = [PS(f"a_pv{i}", [128, 512]) for i in range(2)]
            pq = [PS(f"a_pq{i}", [128, 512]) for i in range(2)]
            P.op("dve", "memset", ((onesb[:], 1.0), {}), writes=["a_onesb"])
            for blk in range(32):
                q = blk % 2
                P.op("pe", "matmul", ((pv[q][:, 0:8],), dict(lhsT=tri[:], rhs=LSP[:, blk, :], start=True, stop=False)),
                     reads=["a_tri", "a_lsp"], writes=[("a_pv", q)])
                P.op("pe", "matmul", ((pv[q][:, 0:8],), dict(lhsT=onesf[:], rhs=R[:, blk, :], start=False, stop=True)),
                     reads=["a_onesf", "a_R"], writes=[("a_pv", q)])
                P.op("dve", "tensor_copy", dict(out=cposk[:, blk, :], in_=pv[q][:, 0:8]), reads=[("a_pv", q)], writes=["cposk"])
                P.op("pe", "matmul", ((pq[q][0:8, 0:128],), dict(lhsT=LSP[:, blk, :], rhs=tri[:], start=True, stop=False)),
                     reads=["a_tri", "a_lsp"], writes=[("a_pq", q)])
                P.op("pe", "matmul", ((pq[q][0:8, 0:128],), dict(lhsT=R[:, blk, :], rhs=onesf[:], start=False, stop=True)),
                     reads=["a_onesf", "a_R"], writes=[("a_pq", q)])
                P.op("act", "activation", dict(out=cT[:, blk * 128:(blk + 1) * 128], in_=pq[q][0:8, 0:128], func=AF.Copy, scale=-8.0),
                     reads=[("a_pq", q)], writes=["a_cT"])
            P.op("dve", "tensor_copy", dict(out=a123[0][:], in_=cT[:]), reads=["a_cT"], writes=["a_a0"])
            P.op("dve", "tensor_tensor", dict(out=rT[:], in0=cT[:], in1=a123[0][:], op=ALU.subtract), reads=["a_cT", "a_a0"], writes=["a_rT"])
            P.op("dve", "tensor_copy", dict(out=a123[1][:], in_=rT[:]), reads=["a_rT"], writes=["a_a1"])
            P.op("dve", "tensor_tensor", dict(out=cT[:], in0=rT[:], in1=a123[1][:], op=ALU.subtract), reads=["a_rT", "a_a1"], writes=["a_cT"])
            P.op("dve", "tensor_copy", dict(out=a123[2][:], in_=cT[:]), reads=["a_cT"], writes=["a_a2"])
            for i in range(3):
                P.op("sync", "dma_start", dict(out=QT[8:16, 64 + i, :], in_=a123[i][:]), reads=[f"a_a{i}"],
                     writes=[("QTaug", i)], dma="a_qst")
                P.op("sync", "dma_start", dict(out=KT[8:16, 64 + i, :], in_=onesb[:]), reads=["a_onesb"],
                     writes=[("KTaug", i)], dma="a_qst")
        self.dump("lsp", LSP[:], [])
        self.dump("cposk", cposk[:], [])
        self.dump("vd", Vd[:, 0:2], [])
        self.dump("vf", Vf[:, 30:32], [])
        self.dump("qt", QT[:, :, 0:512], [])
        self.dump("kt", KT[:, :, 3584:4096], [])
        P.barrier()

        Ocat = A("b_ocat", [128, 32, D], BF16)
        with contextlib.ExitStack() as e2:
            B = lambda n, s, d=F32: e2.enter_context(self.sb(n, s, d))
            PS = lambda n, s, d=F32: e2.enter_context(self.pp(n, s, d))
            qT = [B(f"b_qT{i}", [67, S], BF16) for i in range(2)]
            kT = [B(f"b_kT{i}", [67, S], BF16) for i in range(2)]
            NPS, NPE = 4, 7
            Pe = [B(f"b_pe{i}", [128, 512], BF16) for i in range(NPE)]
            BT = B("b_BT", [128, 5, 2, 128]); BT8 = B("b_BT8", [128, 5, 2, 128])
            b31 = B("b_b31", [128, 4])
            n0 = B("b_n0", [128, 32, 128])
            rb33 = B("b_rb33", [33, 5]); rbl = B("b_rbl", [33, 128]); OH = B("b_oh", [33, 384]); hrep = B("b_hrep", [128, 384])
            lq = [B(f"b_lq{i}", [128, 64]) for i in range(4)]
            lp = B("b_lp", [128, 64]); e12 = B("b_e12", [128, 2]); nlam = B("b_nlam", [128, 1])
            SW = B("b_sw", [128, 128])
            ssqa = B("b_ssqa", [128, 32]); rinv = B("b_rinv", [128, 1]); odt = B("b_odt", [128, 128]); ssq = B("b_ssq", [128, 1]); junk = B("b_junk", [128, 128], BF16)
            ps = [PS(f"b_ps{i}", [128, 512]) for i in range(NPS)]
            po = [PS(f"b_po{i}", [128, 4, 256]) for i in range(2)]

            def DMA(out, in_, wr, rd=(), sem="b_ld"):
                P.op("sync", "dma_start", dict(out=out, in_=in_), reads=list(rd), writes=[wr], dma=sem)

            P.op("dve", "memset", ((rb33[:], 0.0), {}), writes=["b_rb33"])
            P.op("dve", "memset", ((rb33[32:33, :], NEG), {}), writes=["b_rb33"])
            DMA(rb33[0:32, 0:4], wn["rel_bias"], "b_rb33")
            DMA(OH[:], self.consts["relOH"], "b_oh")
            for i, nm in enumerate(("diff_lambda_q1", "diff_lambda_k1", "diff_lambda_q2", "diff_lambda_k2")):
                DMA(lq[i][:], wn[nm][jj].partition_broadcast(128), f"b_lq{i}")
            DMA(SW[:], wn["diff_subln"][jj].partition_broadcast(128), "b_sw")
            for h in range(5):
                P.op("dve", "tensor_copy", dict(out=rbl[:], in_=rb33[:, h:h + 1].broadcast_to([33, 128])), reads=["b_rb33"], writes=["b_rbl"])
                P.op("pe", "matmul", ((ps[0][:, 0:384],), dict(lhsT=rbl[:], rhs=OH[:], start=True, stop=True)),
                     reads=["b_rbl", "b_oh"], writes=[("b_ps", 0)])
                P.op("dve", "tensor_copy", dict(out=hrep[:], in_=ps[0][:, 0:384]), reads=[("b_ps", 0)], writes=["b_hrep"])
                if h < 4:
                    P.op("dve", "tensor_copy", dict(out=b31[:, h:h + 1], in_=hrep[:, 383:384]), reads=["b_hrep"], writes=["b_b31"])
                DMA(HD[h], hrep[:], ("HD", h), ["b_hrep"], sem="b_hd")
                hd = HD[h]
                src = bass.AP(tensor=hd.tensor, offset=hd.offset + 127, ap=[[383, 128], [128, 2], [1, 128]])
                DMA(BT[:, h, :, :], src, "b_BT", [("HD", h)], sem="b_hd2")
            for h in range(5):
                if h < 4:
                    P.op("dve", "tensor_scalar", dict(out=BT8[:, h], in0=BT[:, h], scalar1=b31[:, h:h + 1], scalar2=8.0,
                                                      op0=ALU.subtract, op1=ALU.mult), reads=["b_BT", "b_b31"], writes=["b_BT8"])
                else:
                    P.op("dve", "tensor_scalar", dict(out=BT8[:, h], in0=BT[:, h], scalar1=8.0, scalar2=None, op0=ALU.mult),
                         reads=["b_BT"], writes=["b_BT8"])
            for i in range(2):
                P.op("dve", "tensor_tensor", dict(out=lp[:], in0=lq[2 * i][:], in1=lq[2 * i + 1][:], op=ALU.mult),
                     reads=[f"b_lq{2 * i}", f"b_lq{2 * i + 1}"], writes=["b_lp"])
                P.op("dve", "tensor_reduce", dict(out=e12[:, i:i + 1], in_=lp[:], op=ALU.add, axis=AX.X), reads=["b_lp"], writes=["b_e12"])
            P.op("act", "activation", dict(out=e12[:], in_=e12[:], func=AF.Exp), reads=["b_e12"], writes=["b_e12"])
            P.op("dve", "scalar_tensor_tensor", dict(out=nlam[:], in0=e12[:, 1:2], scalar=-lam_init, in1=e12[:, 0:1],
                                                     op0=ALU.add, op1=ALU.subtract), reads=["b_e12"], writes=["b_nlam"])
            P.op("dve", "tensor_scalar", dict(out=SW[:], in0=SW[:], scalar1=1.0 - lam_init, scalar2=None, op0=ALU.mult),
                 reads=["b_sw"], writes=["b_sw"])

            maps = [(2 * h + m, "d", h, m) for h in range(4) for m in range(2)] + [(8 + f, "f", f, 0) for f in range(8)]
            steps = [(mi, mp, I, J) for mi, mp in enumerate(maps) for I in range(8) for J in range(4 * I + 4)]
            LAG = 4
            started = {}

            def front(idx):
                mi, (mapi, kind, hh, mm), I, J = steps[idx]
                sl = mi % 2
                K = 64 if kind == "d" else 67
                if I == 0 and J == 0:
                    for c4 in range(2):
                        cs = slice(c4 * 2048, (c4 + 1) * 2048)
                        DMA(qT[sl][0:K, cs], QT[mapi, 0:K, cs], ("b_qT", sl), [], sem=f"b_q{sl}")
                        DMA(kT[sl][0:K, cs], KT[mapi, 0:K, cs], ("b_kT", sl), [], sem=f"b_q{sl}")
                bth = hh if kind == "d" else 4
                qlo = max(4 * I, J)
                c0 = (qlo - 4 * I) * 128
                pst, psk = ps[idx % NPS], ("b_ps", idx % NPS)
                pet, pek = Pe[idx % NPE], ("b_pe", idx % NPE)
                P.op("pe", "matmul", ((pst[:, c0:512],), dict(lhsT=kT[sl][0:K, J * 128:(J + 1) * 128],
                                                              rhs=qT[sl][0:K, I * 512 + c0:(I + 1) * 512],
                                                              start=True, stop=True)),
                     reads=[("b_qT", sl), ("b_kT", sl)], writes=[psk])
                if kind == "d":
                    fbias, frd = b31[:, hh:hh + 1], "b_b31"
                else:
                    fbias, frd = cposk[:, J, hh:hh + 1], "cposk"
                nnear = 2 if kind == "d" else 1
                for dist in range(nnear):
                    qt = J + dist
                    if qt < qlo or qt >= 4 * I + 4:
                        continue
                    cc = (qt - 4 * I) * 128
                    P.op("dve", "tensor_tensor", dict(out=pst[:, cc:cc + 128], in0=pst[:, cc:cc + 128], in1=BT8[:, bth, dist, :],
                                                      op=ALU.add), reads=[psk, "b_BT8"], writes=[psk])
                P.op("act", "activation", dict(out=pet[:, c0:512], in_=pst[:, c0:512], func=AF.Exp, scale=0.125, bias=fbias),
                     reads=[psk, frd], writes=[pek])

            def back(idx):
                mi, (mapi, kind, hh, mm), I, J = steps[idx]
                sp = mi * 8 + I
                pot, pok = po[sp % 2], ("b_po", sp % 2)
                pet, pek = Pe[idx % NPE], ("b_pe", idx % NPE)
                Vt, vd = (Vd, 128) if kind == "d" else (Vf, 64)
                vkey = "Vd" if kind == "d" else "Vf"
                qlo = max(4 * I, J)
                for qt in range(qlo, 4 * I + 4):
                    ql = qt - 4 * I
                    cc = ql * 128
                    st = (sp, ql // 2) not in started
                    started[(sp, ql // 2)] = True
                    P.op("pe", "matmul", ((pot[:, ql, 0:vd + 1],), dict(lhsT=pet[:, cc:cc + 128], rhs=Vt[:, J, hh, 0:vd + 1],
                                                                       start=st, stop=(J == qt), skip_group_check=True)),
                         reads=[pek, vkey], writes=[pok])
                if J != 4 * I + 3:
                    return
                for ql in range(4):
                    qt = 4 * I + ql
                    P.op("dve", "reciprocal", dict(out=rinv[:], in_=pot[:, ql, vd:vd + 1]), reads=[pok], writes=["b_rinv"])
                    if kind == "f":
                        P.op("dve", "tensor_scalar", dict(out=Ocat[:, qt, 512 + hh * 64:512 + (hh + 1) * 64], in0=pot[:, ql, 0:64],
                                                          scalar1=rinv[:, 0:1], scalar2=None, op0=ALU.mult),
                             reads=[pok, "b_rinv"], writes=[("b_ocat", qt)])
                    elif mm == 0:
                        P.op("dve", "tensor_scalar", dict(out=n0[:, qt, :], in0=pot[:, ql, 0:128], scalar1=rinv[:, 0:1], scalar2=None,
                                                          op0=ALU.mult), reads=[pok, "b_rinv"], writes=["b_n0"])
                    else:
                        P.op("dve", "tensor_tensor", dict(out=rinv[:], in0=rinv[:], in1=nlam[:], op=ALU.mult),
                             reads=["b_rinv", "b_nlam"], writes=["b_rinv"])
                        P.op("dve", "scalar_tensor_tensor", dict(out=n0[:, qt, :], in0=pot[:, ql, 0:128], scalar=rinv[:, 0:1], in1=n0[:, qt, :],
                                                                 op0=ALU.mult, op1=ALU.add), reads=[pok, "b_rinv", "b_n0"], writes=["b_n0"])
                        P.op("dve", "tensor_tensor", dict(out=odt[:], in0=n0[:, qt, :], in1=n0[:, qt, :], op=ALU.mult),
                             reads=["b_n0"], writes=["b_odt"])
                        P.op("dve", "tensor_reduce", dict(out=ssqa[:, qt:qt + 1], in_=odt[:], op=ALU.add, axis=AX.X),
                             reads=["b_odt"], writes=["b_ssqa"])
                if kind == "d" and mm == 1 and I == 7:
                    P.op("dve", "tensor_scalar", dict(out=ssqa[:], in0=ssqa[:], scalar1=1.0 / 128, scalar2=EPS, op0=ALU.mult, op1=ALU.add),
                         reads=["b_ssqa"], writes=["b_ssqa"])
                    P.op("act", "activation", dict(out=ssqa[:], in_=ssqa[:], func=AF.Sqrt), reads=["b_ssqa"], writes=["b_ssqa"])
                    P.op("dve", "reciprocal", dict(out=ssqa[:], in_=ssqa[:]), reads=["b_ssqa"], writes=["b_ssqa"])
                    for qt in range(32):
                        P.op("dve", "scalar_tensor_tensor", dict(out=Ocat[:, qt, hh * 128:(hh + 1) * 128], in0=n0[:, qt, :],
                                                                 scalar=ssqa[:, qt:qt + 1], in1=SW[:], op0=ALU.mult, op1=ALU.mult),
                             reads=["b_n0", "b_ssqa", "b_sw"], writes=[("b_ocat", qt)])

            for idx in range(len(steps) + LAG):
                if idx < len(steps):
                    front(idx)
                if idx >= LAG:
                    back(idx - LAG)
            self.dump("bt", BT[:], [])
            self.dump("n0", n0[:], [])
            self.dump("nlam", nlam[:], [])
            self.dump("ocat", Ocat[:, 0:2, :], [])
            self.dump("ocat2", Ocat[:, 30:32, :], [])
        P.barrier()
        self.uid += 1
        with contextlib.ExitStack() as e3:
            B = lambda n, s, d=F32: e3.enter_context(self.sb(n, s, d))
            PS = lambda n, s, d=F32: e3.enter_context(self.pp(n, s, d))
            wout = B("b_wout", [128, 8, D], BF16)
            oT = [B(f"b_oT{i}", [128, 8, 128], BF16) for i in range(2)]
            xo = [B(f"b_xo{i}", [128, D]) for i in range(2)]
            pTs = [PS(f"b_pT{i}", [128, 8, 128], BF16) for i in range(2)]
            po2s = [PS(f"b_po2{i}", [128, 512]) for i in range(4)]

            def DMA(out, in_, wr, rd=(), sem="b_ld"):
                P.op("sync", "dma_start", dict(out=out, in_=in_), reads=list(rd), writes=[wr], dma=sem)

            for h in range(2):
                DMA(wout[:, h * 4:(h + 1) * 4, :], wout_v[:, h * 4:(h + 1) * 4, :], "b_wout", self.wkeys[wkey])
            def xrow(ap, blk):
                return ap[blk * 128:(blk + 1) * 128, :]
            for blk in range(32):
                sl = blk % 2
                pT, pTk = pTs[sl], ("b_pT", sl)
                DMA(xo[sl][:], xrow(xsrc, blk), ("b_xo", sl), [("xres", blk // 4)], sem=f"b_x{sl}")
                for k in range(8):
                    P.op("pe", "transpose", dict(out=pT[:, k, :], in_=Ocat[:, blk, k * 128:(k + 1) * 128], identity=ident_b[:]),
                         reads=[("b_ocat", blk), "ident"], writes=[pTk])
                P.op("act", "copy", dict(out=oT[sl][:], in_=pT[:]), reads=[pTk], writes=[("b_oT", sl)])
                for nh in range(2):
                    po2, pok2 = po2s[(blk * 2 + nh) % 4], ("b_po2", (blk * 2 + nh) % 4)
                    for k in range(8):
                        P.op("pe", "matmul", ((po2[:],), dict(lhsT=oT[sl][:, k, :], rhs=wout[:, k, nh * 512:(nh + 1) * 512],
                                                             start=(k == 0), stop=(k == 7))),
                             reads=[("b_oT", sl), "b_wout"], writes=[pok2])
                    P.op("dve", "tensor_tensor", dict(out=xo[sl][:, nh * 512:(nh + 1) * 512], in0=xo[sl][:, nh * 512:(nh + 1) * 512],
                                                      in1=po2[:], op=ALU.add), reads=[pok2, ("b_xo", sl)], writes=[("b_xo", sl)])
                P.op("sync", "dma_start", dict(out=xrow(xdst, blk), in_=xo[sl][:]), reads=[("b_xo", sl)], writes=[("xres_o", blk)], dma="b_st")

    def build(self):
        nc, P = self.nc, self.P
        x_in = self.din("x", [S, D])
        ident = self.din("ident", [128, 128])
        wn = {}
        for nm, shp in [("ffn1_norm", [DEPTH, D]), ("ffn1_gate", [DEPTH, D, DFF]), ("ffn1_up", [DEPTH, D, DFF]),
                        ("ffn1_down", [DEPTH, DFF, D]), ("mix_norm", [DEPTH, D]), ("ffn2_norm", [DEPTH, D]),
                        ("ffn2_gate", [DEPTH, D, DFF]), ("ffn2_up", [DEPTH, D, DFF]), ("ffn2_down", [DEPTH, DFF, D])]:
            wn[nm] = self.din(nm, shp)
        for nm, shp in [("s5_a_re", [2, 64, 64]), ("s5_a_im", [2, 64, 64]), ("s5_log_step", [2, 64]),
                        ("s5_b_re", [2, 64, 64, 16]), ("s5_b_im", [2, 64, 64, 16]), ("s5_c_re", [2, 64, 16, 64]),
                        ("s5_c_im", [2, 64, 16, 64]), ("s5_d", [2, D]), ("s5_glu_a", [2, D, D]), ("s5_glu_b", [2, D, D])]:
            wn[nm] = self.din(nm, shp)
        for nm, shp in [("attn_w_in", [2, D, IN_COLS]), ("attn_w_out", [2, D, D]), ("fg_bias", [2, 8]),
                        ("diff_q_norm", [2, 64]), ("diff_k_norm", [2, 64]), ("diff_lambda_q1", [2, 64]),
                        ("diff_lambda_k1", [2, 64]), ("diff_lambda_q2", [2, 64]), ("diff_lambda_k2", [2, 64]),
                        ("diff_subln", [2, 128]), ("fox_q_norm", [2, 64]), ("fox_k_norm", [2, 64]), ("rel_bias", [32, 4])]:
            wn[nm] = self.din(nm, shp)
        self.consts = {k: self.din(k, list(v.shape)) for k, v in host_consts().items() if k != "ident"}
        self.scr = {"QT": self.dscr("QT", [16, 67, S], BF16), "KT": self.dscr("KT", [16, 67, S], BF16),
                    "HD": self.dscr("HD", [5, 128, 384], F32)}
        out = nc.dram_tensor("out", [S, D], F32, kind="ExternalOutput").ap()
        xres = self.dscr("xres", [S, D], F32)
        wb = {}
        for f in ("ffn1", "ffn2"):
            wb[f + "_gate"] = self.dscr(f + "_gate_b", [DEPTH, D, DFF], BF16)
            wb[f + "_up"] = self.dscr(f + "_up_b", [DEPTH, D, DFF], BF16)
            wb[f + "_down"] = self.dscr(f + "_down_b", [DEPTH, DFF, D], BF16)
        wb["attn_w_in"] = self.dscr("attn_w_in_b", [2, D, IN_COLS], BF16)
        wb["attn_w_out"] = self.dscr("attn_w_out_b", [2, D, D], BF16)
        wb["s5_glu_a"] = self.dscr("s5_glu_a_b", [2, D, D], BF16)
        wb["s5_glu_b"] = self.dscr("s5_glu_b_b", [2, D, D], BF16)

        with contextlib.ExitStack() as es0:
            ident_f = es0.enter_context(self.sb("ident_f", [128, 128], F32))
            ident_b = es0.enter_context(self.sb("ident_b", [128, 128], BF16))
            P.op("sync", "dma_start", dict(out=ident_f[:], in_=ident), writes=["ident_f"], dma="c_misc")
            P.op("dve", "tensor_copy", dict(out=ident_b[:], in_=ident_f[:]), reads=["ident_f"], writes=["ident"])
            need = set(k for k, _ in (self.phases or [("ffn1", 0), ("ffn2", 0), ("s5", 1), ("attn", 0)]))
            for L in range(DEPTH):
                for f in ("ffn1", "ffn2"):
                    if f == "ffn2":
                        if L % 2 == 0 and "attn" in need:
                            for w in ("attn_w_in", "attn_w_out"):
                                self.cast_w(wn[w][L // 2], wb[w][L // 2], f"cast_attn_{L // 2}", f"w_attn_{L // 2}")
                        if L % 2 == 1 and "s5" in need:
                            for w in ("s5_glu_a", "s5_glu_b"):
                                self.cast_w(wn[w][L // 2], wb[w][L // 2], f"cast_s5_{L // 2}", f"w_s5_{L // 2}")
                    if f not in need:
                        continue
                    for w in ("gate", "up", "down"):
                        self.cast_w(wn[f"{f}_{w}"][L], wb[f"{f}_{w}"][L], f"cast_{f}_{L}_{w}", f"w_{f}_{L}_{w}")
            phases = self.phases
            if phases is None:
                phases = []
                for L in range(DEPTH):
                    phases += [("ffn1", L), ("attn" if L % 2 == 0 else "s5", L), ("ffn2", L)]
            src = x_in
            for i, (kind, L) in enumerate(phases):
                dst = out if i == len(phases) - 1 else xres
                P.barrier()
                self.uid += 1
                with contextlib.ExitStack() as es:
                    if kind in ("ffn1", "ffn2"):
                        f = kind
                        self.ffn_phase(es, src, dst, wn[f + "_norm"][L], wb[f + "_gate"][L], wb[f + "_up"][L],
                                       wb[f + "_down"][L], f"w_{f}_{L}", ident_b)
                    elif kind == "s5":
                        self.s5_phase(es, src, dst, L, wn, wb, ident_f, ident_b)
                    elif kind == "attn":
                        self.attn_phase(es, src, dst, L, wn, wb, ident_f, ident_b)
                src = xres
            P.barrier(final=True)
            P.emit()
        return nc


def host_consts():
    idx = np.arange(128) // 16
    s5mask = (idx[None, :] >= idx[:, None]).astype(np.float32)
    n = np.arange(384) - 127
    nn = np.maximum(n, 0)
    nf = np.maximum(nn, 1).astype(np.float32)
    large = 16 + (np.log(nf / np.float32(16)) / np.float32(math.log(128 / 16)) * np.float32(16)).astype(np.int32)
    large = np.minimum(large, 31)
    bucket = np.where(nn < 16, nn, large)
    oh = np.zeros((33, 384), np.float32)
    for i in range(384):
        if n[i] >= 0:
            oh[bucket[i], i] = 1.0
        else:
            oh[32, i] = 1.0
    ones2 = np.zeros((128, 128), np.float32)
    ones2[:64, :64] = 1.0 / 64
    ones2[64:, 64:] = 1.0 / 64
    tri = (np.arange(128)[:, None] <= np.arange(128)[None, :]).astype(np.float32)
    return {"ident": np.eye(128, dtype=np.float32), "s5mask": s5mask, "relOH": oh, "ones2": ones2, "tri": tri}


_CACHE = {}


def kernel(**inputs):
    if "b" not in _CACHE:
        b = Builder()
        b.build()
        _CACHE["b"] = b
    b = _CACHE["b"]
    consts = host_consts()
    in_maps = []
    for c in range(NCORES):
        m = {}
        for k in b.inputs:
            if k == "x":
                m[k] = np.ascontiguousarray(inputs["x"][c])
            elif k in consts:
                m[k] = consts[k]
            else:
                m[k] = np.ascontiguousarray(inputs[k])
        in_maps.append(m)
    res = run_bass_kernel_spmd(b.nc, in_maps, core_ids=list(range(NCORES)))
    return np.stack([np.asarray(r["out"]) for r in res.results], axis=0).astype(np.float32)
```
